# Optimizing a Trainium2 kernel written in Bass

```python
import jax, jax.numpy as jnp
from jax import lax
import numpy as np

D_MODEL = 1024
BATCH = 2
SEQ = 8192
DEPTH = 4

GRID_W = 64
CTX_LEN = 256
EPS = 1e-6
ROPE_BASE = 10000.0

A_HEADS = 8
A_KV_HEADS = 2
A_HEAD_DIM = 64
A_WINDOW = 128
A_BLOCK = 128
A_Q = A_HEADS * A_HEAD_DIM
A_KV = A_KV_HEADS * A_HEAD_DIM
A_OUT = A_Q

B_HEADS = 8
B_Q_RANK = 256
B_KV_RANK = 128
B_NOPE_DIM = 64
B_ROPE_DIM = 32
B_V_DIM = 64
B_QBLOCK = 128
B_OUT = B_HEADS * B_V_DIM

C_HEADS = 4
C_KEY_DIM = 64
C_VAL_DIM = 128
C_GATE_RANK = 16
C_GATE_TAU = 16.0
C_CHUNK = 64
C_QK = C_HEADS * C_KEY_DIM
C_V = C_HEADS * C_VAL_DIM
C_OUT = C_V

FFN_DIM = 2816
CONV_W = 3

IN_DIM = A_Q + 2 * A_KV + B_Q_RANK + B_KV_RANK + B_ROPE_DIM + 2 * C_QK + 2 * C_V + 2 * C_GATE_RANK + 3 * D_MODEL

kernel_name = 'hybrid_dit_gqa_mla_gla_convffn'


def _rmsnorm(x, w):
    xf = x.astype(jnp.float32)
    y = xf * lax.rsqrt(jnp.mean(xf * xf, axis=-1, keepdims=True) + EPS)
    return (y * w.astype(jnp.float32)).astype(x.dtype)


def _rope_2d(n_tokens, rot_dim):
    rows = n_tokens // GRID_W
    row = jnp.broadcast_to(jnp.arange(rows)[:, None], (rows, GRID_W)).reshape(-1).astype(jnp.float32)
    col = jnp.broadcast_to(jnp.arange(GRID_W)[None, :], (rows, GRID_W)).reshape(-1).astype(jnp.float32)
    n_freq = rot_dim // 4
    inv = ROPE_BASE ** (-jnp.arange(n_freq, dtype=jnp.float32) / n_freq)
    ang = jnp.concatenate([row[:, None] * inv, col[:, None] * inv], axis=-1)
    return jnp.cos(ang), jnp.sin(ang)


def _apply_rope(t, cos, sin):
    half = t.shape[-1] // 2
    tf = t.astype(jnp.float32)
    t1, t2 = tf[..., :half], tf[..., half:]
    return jnp.concatenate([t1 * cos - t2 * sin, t1 * sin + t2 * cos], axis=-1).astype(t.dtype)


def _split_in(proj):
    sizes = [A_Q, A_KV, A_KV, B_Q_RANK, B_KV_RANK, B_ROPE_DIM, C_QK, C_QK, C_V, C_V, 2 * C_GATE_RANK]
    return jnp.split(proj, np.cumsum(sizes).tolist(), axis=-1)


def _sink_softmax(s, sink):
    m = jnp.maximum(jnp.max(s, axis=-1, keepdims=True), sink)
    p = jnp.exp(s - m)
    return p / (jnp.sum(p, axis=-1, keepdims=True) + jnp.exp(sink - m))


def _window_gqa(q, k, v, kc, vc, sink):
    B, S, Hq, Dh = q.shape
    Hkv = k.shape[2]
    G = Hq // Hkv
    nb = S // A_BLOCK
    scale = Dh ** -0.5
    qb = q.reshape(B, nb, A_BLOCK, Hkv, G, Dh)

    def band(t):
        tp = jnp.pad(t, ((0, 0), (A_BLOCK, A_BLOCK), (0, 0), (0, 0))).reshape(B, nb + 2, A_BLOCK, Hkv, Dh)
        return jnp.concatenate([tp[:, :-2], tp[:, 1:-1], tp[:, 2:]], axis=2)

    kb, vb = band(k), band(v)
    s_loc = jnp.einsum('bnqhgd,bnkhd->bnhgqk', qb, kb, preferred_element_type=jnp.float32) * scale
    s_ctx = jnp.einsum('bnqhgd,bkhd->bnhgqk', qb, kc, preferred_element_type=jnp.float32) * scale
    n_loc = 3 * A_BLOCK
    rel = jnp.arange(n_loc)[None, :] - A_BLOCK - jnp.arange(A_BLOCK)[:, None]
    kpos = jnp.arange(nb)[:, None] * A_BLOCK - A_BLOCK + jnp.arange(n_loc)[None, :]
    valid = (jnp.abs(rel) <= A_WINDOW)[None] & ((kpos >= 0) & (kpos < S))[:, None, :]
    s_loc = jnp.where(valid[None, :, None, None], s_loc, -jnp.inf)
    p = _sink_softmax(jnp.concatenate([s_loc, s_ctx], axis=-1),
                      sink.astype(jnp.float32).reshape(1, 1, Hkv, G, 1, 1)).astype(v.dtype)
    o = (jnp.einsum('bnhgqk,bnkhd->bnqhgd', p[..., :n_loc], vb)
         + jnp.einsum('bnhgqk,bkhd->bnqhgd', p[..., n_loc:], vc))
    return o.reshape(B, S, Hq * Dh)


def _ctx_gqa(qc, kc, vc, sink):
    B, L, Hq, Dh = qc.shape
    Hkv = kc.shape[2]
    G = Hq // Hkv
    q = qc.reshape(B, L, Hkv, G, Dh)
    s = jnp.einsum('bqhgd,bkhd->bhgqk', q, kc, preferred_element_type=jnp.float32) * (Dh ** -0.5)
    p = _sink_softmax(s, sink.astype(jnp.float32).reshape(1, Hkv, G, 1, 1)).astype(vc.dtype)
    return jnp.einsum('bhgqk,bkhd->bqhgd', p, vc).reshape(B, L, Hq * Dh)


def _mla_queries(cq, P):
    B, T, _ = cq.shape
    q = (_rmsnorm(cq, P['b_q_norm']) @ P['b_w_uq']).reshape(B, T, B_HEADS, B_NOPE_DIM + B_ROPE_DIM)
    return q[..., :B_NOPE_DIM], q[..., B_NOPE_DIM:]


def _mla_keys_values(ckv, P):
    B, T, _ = ckv.shape
    kv = (_rmsnorm(ckv, P['b_kv_norm']) @ P['b_w_ukv']).reshape(B, T, B_HEADS, B_NOPE_DIM + B_V_DIM)
    return kv[..., :B_NOPE_DIM], kv[..., B_NOPE_DIM:]


def _mla_attend(qn, qr, kn, kr, v):
    s = (jnp.einsum('bqhd,bkhd->bhqk', qn, kn, preferred_element_type=jnp.float32)
         + jnp.einsum('bqhd,bkd->bhqk', qr, kr, preferred_element_type=jnp.float32)) * ((B_NOPE_DIM + B_ROPE_DIM) ** -0.5)
    p = jax.nn.softmax(s, axis=-1).astype(v.dtype)
    return jnp.einsum('bhqk,bkhd->bqhd', p, v)


def _mla_latent(qn, qr, kn, kr, v, kn_c, kr_c, v_c):
    B, S, H, _ = qn.shape
    nb = S // B_QBLOCK
    keys_n = jnp.concatenate([kn_c, kn], axis=1)
    keys_r = jnp.concatenate([kr_c, kr], axis=1)
    vals = jnp.concatenate([v_c, v], axis=1)

    def to_blocks(t):
        return jnp.moveaxis(t.reshape(B, nb, B_QBLOCK, *t.shape[2:]), 1, 0)

    o = lax.map(lambda a: _mla_attend(a[0], a[1], keys_n, keys_r, vals), (to_blocks(qn), to_blocks(qr)))
    return jnp.moveaxis(o, 0, 1).reshape(B, S, H * B_V_DIM)


def _heads(t, H):
    B, T, _ = t.shape
    return t.reshape(B, T, H, -1).transpose(0, 2, 1, 3).astype(jnp.float32)


def _flip(t):
    return jnp.flip(t, axis=2)


def _gla_log_decay(g_low, P, d):
    logit = g_low @ P['c_w_gate'][d] + P['c_b_gate'][d]
    return _heads(jax.nn.log_sigmoid(logit.astype(jnp.float32)) / C_GATE_TAU, C_HEADS)


def _gla_scan(q, k, v, g, s0, with_out):
    B, H, T, _ = q.shape
    nc = T // C_CHUNK
    causal = jnp.tril(jnp.ones((C_CHUNK, C_CHUNK), dtype=bool))

    def chunks(t):
        return t.reshape(B, H, nc, C_CHUNK, t.shape[-1]).transpose(2, 0, 1, 3, 4)

    def step(state, inp):
        qc, kc, vc, gc = inp
        b = jnp.cumsum(gc, axis=-2)
        b_last = b[..., -1:, :]
        new_state = (jnp.exp(b_last)[..., 0, :, None] * state
                     + jnp.einsum('bhcd,bhce->bhde', kc * jnp.exp(b_last - b), vc))
        if not with_out:
            return new_state, None
        o_inter = jnp.einsum('bhcd,bhde->bhce', qc * jnp.exp(b), state)
        diff = b[..., :, None, :] - b[..., None, :, :]
        decay = jnp.exp(jnp.where(causal[:, :, None], diff, -jnp.inf))
        att = jnp.einsum('bhid,bhjd,bhijd->bhij', qc, kc, decay)
        return new_state, o_inter + jnp.einsum('bhij,bhje->bhie', att, vc)

    s_fin, o = lax.scan(step, s0, (chunks(q), chunks(k), chunks(v), chunks(g)))
    if with_out:
        o = o.transpose(1, 2, 0, 3, 4).reshape(B, H, T, v.shape[-1])
    return s_fin, o


def _gla_out(o, r, gain):
    B, H, T, dv = o.shape
    o = o.transpose(0, 2, 1, 3)
    o = o * lax.rsqrt(jnp.mean(o * o, axis=-1, keepdims=True) + EPS) * gain.astype(jnp.float32).reshape(H, dv)
    return (o.reshape(B, T, H * dv) * jax.nn.silu(r.astype(jnp.float32))).astype(r.dtype)


def _merge(ya, yb, yc, gate_logits, P):
    ga, gb, gc = jnp.split(jax.nn.sigmoid(gate_logits), 3, axis=-1)
    m = ga * (ya @ P['w_br_a']) + gb * (yb @ P['w_br_b']) + gc * (yc @ P['w_br_c'])
    return m @ P['w_out']


def _mixer(hl, hc, P, rope_a, rope_b, ctx_out):
    B, S, _ = hl.shape
    Lc = hc.shape[1]
    aq_l, ak_l, av_l, bq_l, bkv_l, bkr_l, cq_l, ck_l, cv_l, cr_l, cg_l, gate_l = _split_in(hl @ P['w_in'])
    aq_c, ak_c, av_c, bq_c, bkv_c, bkr_c, cq_c, ck_c, cv_c, cr_c, cg_c, gate_c = _split_in(hc @ P['w_in'])

    cos_a, sin_a = rope_a
    qa = _apply_rope(aq_l.reshape(B, S, A_HEADS, A_HEAD_DIM), cos_a[:, None], sin_a[:, None])
    ka = _apply_rope(ak_l.reshape(B, S, A_KV_HEADS, A_HEAD_DIM), cos_a[:, None], sin_a[:, None])
    va = av_l.reshape(B, S, A_KV_HEADS, A_HEAD_DIM)
    ka_c = ak_c.reshape(B, Lc, A_KV_HEADS, A_HEAD_DIM)
    va_c = av_c.reshape(B, Lc, A_KV_HEADS, A_HEAD_DIM)
    ya_l = _window_gqa(qa, ka, va, ka_c, va_c, P['a_sink'])

    cos_b, sin_b = rope_b
    qn_l, qr_l = _mla_queries(bq_l, P)
    qr_l = _apply_rope(qr_l, cos_b[:, None], sin_b[:, None])
    kn_l, vb_l = _mla_keys_values(bkv_l, P)
    kr_l = _apply_rope(bkr_l, cos_b, sin_b)
    kn_c, vb_c = _mla_keys_values(bkv_c, P)
    yb_l = _mla_latent(qn_l, qr_l, kn_l, kr_l, vb_l, kn_c, bkr_c, vb_c)

    qs = C_KEY_DIM ** -0.5
    q_l, k_l, v_l = _heads(cq_l, C_HEADS) * qs, _heads(ck_l, C_HEADS), _heads(cv_l, C_HEADS)
    q_c, k_c, v_c = _heads(cq_c, C_HEADS) * qs, _heads(ck_c, C_HEADS), _heads(cv_c, C_HEADS)
    gf_l = _gla_log_decay(cg_l[..., :C_GATE_RANK], P, 0)
    gb_l = _gla_log_decay(cg_l[..., C_GATE_RANK:], P, 1)
    gf_c = _gla_log_decay(cg_c[..., :C_GATE_RANK], P, 0)
    gb_c = _gla_log_decay(cg_c[..., C_GATE_RANK:], P, 1)
    s0 = jnp.zeros((B, C_HEADS, C_KEY_DIM, C_VAL_DIM), jnp.float32)
    sf_c, of_c = _gla_scan(q_c, k_c, v_c, gf_c, s0, ctx_out)
    sb_c, ob_c = _gla_scan(_flip(q_c), _flip(k_c), _flip(v_c), _flip(gb_c), s0, ctx_out)
    _, of_l = _gla_scan(q_l, k_l, v_l, gf_l, sf_c, True)
    _, ob_l = _gla_scan(_flip(q_l), _flip(k_l), _flip(v_l), _flip(gb_l), sb_c, True)
    yc_l = _gla_out(of_l + _flip(ob_l), cr_l, P['c_head_norm'])

    y_l = _merge(ya_l, yb_l, yc_l, gate_l, P)
    if not ctx_out:
        return y_l, None
    ya_c = _ctx_gqa(aq_c.reshape(B, Lc, A_HEADS, A_HEAD_DIM), ka_c, va_c, P['a_sink'])
    qn_c, qr_c = _mla_queries(bq_c, P)
    yb_c = _mla_attend(qn_c, qr_c, kn_c, bkr_c, vb_c).reshape(B, Lc, B_OUT)
    yc_c = _gla_out(of_c + _flip(ob_c), cr_c, P['c_head_norm'])
    y_c = _merge(ya_c, yb_c, yc_c, gate_c, P)
    return y_l, y_c


def _conv_ffn(h, P):
    u = h @ P['w_up']
    up = jnp.pad(u, ((0, 0), (1, 1), (0, 0)))
    cw = P['conv_w']
    u = up[:, :-2] * cw[0] + up[:, 1:-1] * cw[1] + up[:, 2:] * cw[2] + P['conv_b']
    g, val = jnp.split(u, 2, axis=-1)
    return (jax.nn.silu(g) * val) @ P['w_down']


def _layer(xl, xc, mod_l, mod_c, P, rope_a, rope_b, ctx_out):
    sh1, sc1, g1, sh2, sc2, g2 = jnp.split(mod_l, 6, axis=-1)
    csh1, csc1, cg1, csh2, csc2, cg2 = jnp.split(mod_c, 6, axis=-1)
    hl = _rmsnorm(xl, P['norm_mix']) * (1 + sc1) + sh1
    hc = _rmsnorm(xc, P['norm_mix']) * (1 + csc1) + csh1
    y_l, y_c = _mixer(hl, hc, P, rope_a, rope_b, ctx_out)
    xl = xl + g1 * y_l
    xl = xl + g2 * _conv_ffn(_rmsnorm(xl, P['norm_ffn']) * (1 + sc2) + sh2, P)
    if ctx_out:
        xc = xc + cg1 * y_c
        xc = xc + cg2 * _conv_ffn(_rmsnorm(xc, P['norm_ffn']) * (1 + csc2) + csh2, P)
    return xl, xc


def setup_inputs(seed: int = 0) -> dict:
    key = jax.random.key(seed)
    ks = jax.random.split(key, 28)
    L, D = DEPTH, D_MODEL

    def nrm(k, shape, scale):
        return jax.random.normal(k, shape, jnp.float32) * scale

    def gain(k, shape):
        return 1.0 + 0.1 * jax.random.normal(k, shape, jnp.float32)

    return {
        'x': nrm(ks[0], (BATCH, SEQ, D), 1.0),
        'c': nrm(ks[1], (BATCH, D), 1.0),
        'ctx': nrm(ks[2], (BATCH, CTX_LEN, D), 1.0),
        'c_ctx': nrm(ks[3], (D,), 1.0),
        'w_mod': nrm(ks[4], (L, D, 6 * D), 0.5 * D ** -0.5),
        'b_mod': nrm(ks[5], (L, 6 * D), 0.02),
        'norm_mix': gain(ks[6], (L, D)),
        'norm_ffn': gain(ks[7], (L, D)),
        'w_in': nrm(ks[8], (L, D, IN_DIM), D ** -0.5),
        'a_sink': nrm(ks[9], (L, A_HEADS), 0.5),
        'b_q_norm': gain(ks[10], (L, B_Q_RANK)),
        'b_kv_norm': gain(ks[11], (L, B_KV_RANK)),
        'b_w_uq': nrm(ks[12], (L, B_Q_RANK, B_HEADS * (B_NOPE_DIM + B_ROPE_DIM)), B_Q_RANK ** -0.5),
        'b_w_ukv': nrm(ks[13], (L, B_KV_RANK, B_HEADS * (B_NOPE_DIM + B_V_DIM)), B_KV_RANK ** -0.5),
        'c_w_gate': nrm(ks[14], (L, 2, C_GATE_RANK, C_QK), C_GATE_RANK ** -0.5),
        'c_b_gate': nrm(ks[15], (L, 2, C_QK), 0.02),
        'c_head_norm': gain(ks[16], (L, C_V)),
        'w_br_a': nrm(ks[17], (L, A_OUT, D), A_OUT ** -0.5),
        'w_br_b': nrm(ks[18], (L, B_OUT, D), B_OUT ** -0.5),
        'w_br_c': nrm(ks[19], (L, C_OUT, D), C_OUT ** -0.5),
        'w_out': nrm(ks[20], (L, D, D), D ** -0.5),
        'w_up': nrm(ks[21], (L, D, 2 * FFN_DIM), D ** -0.5),
        'conv_w': nrm(ks[22], (L, CONV_W, 2 * FFN_DIM), CONV_W ** -0.5),
        'conv_b': nrm(ks[23], (L, 2 * FFN_DIM), 0.02),
        'w_down': nrm(ks[24], (L, FFN_DIM, D), FFN_DIM ** -0.5),
        'final_norm': gain(ks[25], (D,)),
    }


def reference(x, c, ctx, c_ctx, w_mod, b_mod, norm_mix, norm_ffn, w_in, a_sink, b_q_norm, b_kv_norm,
              b_w_uq, b_w_ukv, c_w_gate, c_b_gate, c_head_norm, w_br_a, w_br_b, w_br_c, w_out,
              w_up, conv_w, conv_b, w_down, final_norm):
    S = x.shape[1]
    rope_a = _rope_2d(S, A_HEAD_DIM)
    rope_b = _rope_2d(S, B_ROPE_DIM)
    silu_c = jax.nn.silu(c)
    silu_cc = jax.nn.silu(c_ctx)
    xl, xc = x, ctx
    for l in range(DEPTH):
        P = {'norm_mix': norm_mix[l], 'norm_ffn': norm_ffn[l], 'w_in': w_in[l], 'a_sink': a_sink[l],
             'b_q_norm': b_q_norm[l], 'b_kv_norm': b_kv_norm[l], 'b_w_uq': b_w_uq[l], 'b_w_ukv': b_w_ukv[l],
             'c_w_gate': c_w_gate[l], 'c_b_gate': c_b_gate[l], 'c_head_norm': c_head_norm[l],
             'w_br_a': w_br_a[l], 'w_br_b': w_br_b[l], 'w_br_c': w_br_c[l], 'w_out': w_out[l],
             'w_up': w_up[l], 'conv_w': conv_w[l], 'conv_b': conv_b[l], 'w_down': w_down[l]}
        mod_l = (silu_c @ w_mod[l] + b_mod[l])[:, None, :]
        mod_c = (silu_cc @ w_mod[l] + b_mod[l])[None, None, :]
        xl, xc = _layer(xl, xc, mod_l, mod_c, P, rope_a, rope_b, l < DEPTH - 1)
    return _rmsnorm(xl, final_norm)
```

```python
import contextlib
import numpy as np
import ml_dtypes
import concourse.bass as bass
import concourse.mybir as mybir
from concourse.bass_utils import run_bass_kernel_spmd

F32 = mybir.dt.float32
BF16 = mybir.dt.bfloat16
AF = mybir.ActivationFunctionType
ALU = mybir.AluOpType
NPBF = ml_dtypes.bfloat16

NCORES = 8
D = 1024
SEQ = 8192
CTX = 256
DEPTH = 4
NTOT = CTX + SEQ
NKB = NTOT // 128
EPS = 1e-6
IN_DIM = 5824
O_AQ, O_AK, O_AV, O_BQ, O_BKV, O_BKR, O_CQ, O_CK, O_CV, O_CR, O_CG, O_GATE = (
    0, 512, 640, 768, 1024, 1152, 1184, 1440, 1696, 2208, 2720, 2752)
FFN = 2816
TT = [(0, CTX)] + [(CTX + 512 * i, 512) for i in range(16)]
SEGS = [TT[0:5], TT[5:9], TT[9:13], TT[13:17]]

SAME_ENGINE_SYNC = True
_STOP = None


class _Stop(Exception):
    pass


def chk(name):
    if _STOP == name:
        raise _Stop()


class Buf:
    __slots__ = ("t", "name", "lw", "rd")

    def __init__(self, t, name=""):
        self.t = t
        self.name = name
        self.lw = {}
        self.rd = {}

    def __getitem__(self, idx):
        return self.t[idx]


class Sched:
    def __init__(self, nc, stack):
        self.nc = nc
        self.stack = stack
        self.root = stack
        self.engs = {"pe": nc.tensor, "act": nc.scalar, "dve": nc.vector,
                     "pool": nc.gpsimd, "sp": nc.sync}
        self.sems = {}
        self.cnt = {}
        self.seen = {e: {} for e in self.engs}
        for e in ("pe", "act", "dve", "pool"):
            self.sems[e] = stack.enter_context(nc.semaphore("s_" + e))
            self.cnt[e] = 0
        self.ninst = 0
        self._uid = 0
        self.sem_bufs = {}
        self._free = []
        self._scoped = [[]]

    def uid(self, p):
        self._uid += 1
        return "%s_%d" % (p, self._uid)

    @contextlib.contextmanager
    def scope(self):
        old = self.stack
        self._scoped.append([])
        with contextlib.ExitStack() as st:
            self.stack = st
            try:
                yield
            finally:
                self.barrier()
                self.stack = old
                self._free.extend(self._scoped.pop())

    def sbuf(self, name, shape, dt):
        return Buf(self.stack.enter_context(self.nc.sbuf_tensor(self.uid(name), shape, dt)), name)

    def psum(self, name, shape, dt):
        return Buf(self.stack.enter_context(self.nc.psum_tensor(self.uid(name), shape, dt)), name)

    def dram(self, name, shape, dt):
        return Buf(self.nc.dram_tensor(self.uid(name), list(shape), dt).ap(), name)

    def dma_sem(self, name):
        if self._free:
            key = self._free.pop()
            for sfx in ("~hw", "~sw"):
                if key + sfx in self.sem_bufs:
                    self.sem_bufs[key + sfx] = []
        else:
            key = self.uid("d")
        self._scoped[-1].append(key)
        return key

    def _sem(self, key):
        if key not in self.sems:
            self.sems[key] = self.root.enter_context(self.nc.semaphore(key.replace("~", "_")))
            self.cnt[key] = 0
            self.sem_bufs[key] = []
        return self.sems[key]

    def _wait(self, eng, deps):
        e = self.engs[eng]
        for (k, c) in sorted(deps, key=lambda x: str(x[0])):
            if k == eng and not SAME_ENGINE_SYNC:
                continue
            if k == "pe" and eng == "pe":
                continue
            if self.seen[eng].get(k, 0) >= c:
                continue
            e.wait_ge(self.sems[k], c)
            self.seen[eng][k] = c

    @staticmethod
    def _deps(reads, writes):
        deps = set()
        for b in reads:
            deps.update(b.lw.items())
        for b in writes:
            deps.update(b.lw.items())
            deps.update(b.rd.items())
        return deps

    def op(self, eng, fn, reads=(), writes=()):
        self._wait(eng, self._deps(reads, writes))
        ins = fn(self.engs[eng])
        self.cnt[eng] += 1
        ins.then_inc(self.sems[eng], 1)
        for b in reads:
            b.rd[eng] = self.cnt[eng]
        for b in writes:
            b.lw[eng] = self.cnt[eng]
            b.rd = {}
        self.ninst += 1
        return ins

    def mm(self, fn, reads=(), writes=(), last=True):
        self._wait("pe", self._deps(reads, writes))
        ins = fn(self.engs["pe"])
        self.ninst += 1
        if last:
            self.cnt["pe"] += 1
            ins.then_inc(self.sems["pe"], 1)
            for b in writes:
                b.lw["pe"] = self.cnt["pe"]
                b.rd = {}
        for b in reads:
            b.rd["pe"] = self.cnt["pe"] + (0 if last else 1)
        return ins

    def dma(self, q, semkey, out_ap, in_ap, reads=(), writes=(), **kw):
        semkey = semkey + ("~sw" if q == "pool" else "~hw")
        sem = self._sem(semkey)
        self._wait(q, self._deps(reads, writes))
        ins = self.engs[q].dma_start(out=out_ap, in_=in_ap, **kw)
        self.cnt[semkey] += 16
        c = self.cnt[semkey]
        ins.then_inc(sem, 16)
        for b in self.sem_bufs[semkey]:
            if semkey in b.lw:
                b.lw[semkey] = c
            if semkey in b.rd:
                b.rd[semkey] = c
        for b in reads:
            b.rd[semkey] = c
            if b not in self.sem_bufs[semkey]:
                self.sem_bufs[semkey].append(b)
        for b in writes:
            b.lw[semkey] = c
            b.rd = {}
            if b not in self.sem_bufs[semkey]:
                self.sem_bufs[semkey].append(b)
        return ins

    def barrier(self):
        snap = [(k, c) for k, c in self.cnt.items() if c > 0]
        for eng in ("pe", "act", "dve", "pool", "sp"):
            e = self.engs[eng]
            for (k, c) in snap:
                if self.seen[eng].get(k, 0) >= c:
                    continue
                e.wait_ge(self.sems[k], c)
                self.seen[eng][k] = c

    def finish(self, eng="sp"):
        for k, c in self.cnt.items():
            if c > 0:
                self._wait(eng, {(k, c)})

    def wait_all(self, eng, bufs):
        deps = set()
        for b in bufs:
            deps.update(b.lw.items())
            deps.update(b.rd.items())
        self._wait(eng, deps)


class Rot:
    def __init__(self, S, name, n, shape, dt, psum=False, dma=True):
        self.bufs = [(S.psum if psum else S.sbuf)(name + str(i), shape, dt) for i in range(n)]
        self.sems = [S.dma_sem(name + str(i)) for i in range(n)] if dma else [None] * n
        self.i = 0

    def next(self):
        b, s = self.bufs[self.i], self.sems[self.i]
        self.i = (self.i + 1) % len(self.bufs)
        return b, s


class Out:
    def __init__(self):
        self.bufs = []

    def add(self, b):
        if b not in self.bufs:
            self.bufs.append(b)


def dram_in(nc, name, shape, dt):
    return nc.dram_tensor(name, list(shape), dt, kind="ExternalInput").ap()


def dram_out(nc, name, shape, dt):
    return nc.dram_tensor(name, list(shape), dt, kind="ExternalOutput").ap()


def _rope_tables(pos, is_ctx):
    n = pos.shape[0]
    row = (pos // 64).astype(np.float32)
    col = (pos % 64).astype(np.float32)

    def tab(nfreq):
        inv = (np.float32(10000.0) ** (-np.arange(nfreq, dtype=np.float32) / np.float32(nfreq))).astype(np.float32)
        ang = np.concatenate([row[:, None] * inv, col[:, None] * inv], axis=-1).astype(np.float32)
        c, s = np.cos(ang).astype(np.float32), np.sin(ang).astype(np.float32)
        c[is_ctx] = 1.0
        s[is_ctx] = 0.0
        return c, s

    ca, sa = tab(16)
    cb, sb = tab(8)
    cosA = np.empty((128, n), np.float32)
    sinA = np.empty((128, n), np.float32)
    for p in range(128):
        d = p % 64
        i = d % 32
        cosA[p] = ca[:, i]
        sinA[p] = (-sa[:, i]) if d < 32 else sa[:, i]
    cosB = np.ones((96, n), np.float32)
    sinB = np.zeros((96, n), np.float32)
    cosK = np.empty((32, n), np.float32)
    sinK = np.empty((32, n), np.float32)
    for r in range(32):
        i = r % 16
        cosK[r] = cb[:, i]
        sinK[r] = (-sb[:, i]) if r < 16 else sb[:, i]
    cosB[64:96] = cosK
    sinB[64:96] = sinK
    return dict(cosA=cosA, sinA=sinA, cosB=cosB, sinB=sinB, cosK=cosK, sinK=sinK)


def _perm_consts():
    permA = np.zeros((128, 128), np.float32)
    for m in range(128):
        d = m % 64
        permA[m + 32 if d < 32 else m - 32, m] = 1.0
    permB = np.zeros((96, 96), np.float32)
    for m in range(64, 96):
        r = m - 64
        permB[m + 16 if r < 16 else m - 16, m] = 1.0
    permK = np.zeros((32, 32), np.float32)
    for m in range(32):
        permK[m + 16 if m < 16 else m - 16, m] = 1.0
    sel = np.zeros((32, 96), np.float32)
    for k in range(32):
        sel[k, 64 + k] = 1.0
    return dict(permA=permA.astype(NPBF), permB=permB.astype(NPBF), permK=permK.astype(NPBF),
                sel=sel.astype(NPBF))


LAYER_W = [
    ("w_mod", (D, 6 * D), F32), ("b_mod", (128, 48), F32), ("norm_mix", (128, 8), F32), ("norm_ffn", (128, 8), F32),
    ("w_in", (D, IN_DIM), F32), ("a_sink", (1, 8), F32), ("bqn", (128, 2), F32), ("bkvn", (128, 1), F32),
    ("w_uq", (256, 768), F32), ("w_ukv", (128, 1024), F32), ("w_gate", (2, 16, 256), F32), ("b_gate", (128, 2, 2), F32),
    ("c_hn", (128, 4), F32), ("w_br_a", (512, D), F32), ("w_br_b", (512, D), F32), ("w_br_c", (512, D), F32),
    ("w_out", (D, D), F32), ("w_up", (D, 2 * FFN), F32), ("conv_w", (128, 44, 3), F32), ("conv_b", (128, 44), F32),
    ("w_down", (FFN, D), F32),
]
CONSTS = [
    ("cvec", (128, 8, 2), F32), ("final_norm", (128, 8), F32),
    ("cosA", (128, NTOT), F32), ("sinA", (128, NTOT), F32), ("cosB", (96, NTOT), F32), ("sinB", (96, NTOT), F32),
    ("cosK", (32, NTOT), F32), ("sinK", (32, NTOT), F32),
    ("permA", (128, 128), BF16), ("permB", (96, 96), BF16), ("permK", (32, 32), BF16), ("sel", (32, 96), BF16),
    ("maskLo", (128, 512), BF16), ("maskHi", (128, 512), BF16), ("triF", (64, 512), BF16), ("triB", (64, 512), BF16),
    ("rmask", (64, NTOT), F32), ("ident", (64, 64), BF16), ("sv", (1, 128), BF16),
]
SCRATCH = [
    ("modT", (128, 48, 2), F32),
    ("qaT", (128, 4, NTOT), BF16), ("kaT", (128, NTOT), BF16), ("vaA", (NTOT, 2, 128), BF16),
    ("qbT", (96, 8, NTOT), BF16), ("kbT", (96, 8, NTOT), BF16), ("vbA", (NTOT, 8, 128), BF16),
    ("cqT", (128, 2, NTOT), BF16), ("ckT", (128, 2, NTOT), BF16), ("cvv", (NTOT, 512), BF16),
    ("crT", (128, 4, NTOT), BF16), ("gfT", (128, 2, NTOT), F32), ("gbT", (128, 2, NTOT), F32),
    ("gatesT", (128, 24, NTOT), BF16),
    ("yaT", (128, 4, NTOT), BF16), ("ybT", (128, 4, NTOT), BF16), ("ycT", (128, 4, NTOT), BF16),
    ("x1T", (D, NTOT), F32), ("h2T", (128, 8, NTOT), BF16), ("aT", (128, 22, NTOT), BF16),
    ("XA", (D, NTOT), F32), ("XB", (D, NTOT), F32),
]


class P:
    pass


def evac(S, eng, out_ap, in_ap, reads, writes, func=None, scale=1.0):
    if eng == "act":
        return S.op("act", lambda e: e.activation(out=out_ap, in_=in_ap, func=func or AF.Copy, scale=scale),
                    reads=reads, writes=writes)
    return S.op(eng, lambda e: e.tensor_copy(out=out_ap, in_=in_ap), reads=reads, writes=writes)


def rstd_from_ps(S, ps, n, dim, eps_t, out):
    S.op("act", lambda e: e.activation(out=out[:, 0:n], in_=ps[:, 0:n], func=AF.Ln, scale=1.0 / dim, bias=eps_t[:]),
         reads=[ps, eps_t], writes=[out])
    S.op("act", lambda e: e.activation(out=out[:, 0:n], in_=out[:, 0:n], func=AF.Exp, scale=-0.5), reads=[out], writes=[out])


def phase_A(p, L, Xin):
    S, T, C = p.S, p.T, p.C
    nc = p.nc
    eps_t, one_t, ones_bf = p.eps_t, p.one_t, p.ones_bf
    with S.scope():
        ld = S.dma_sem("ldc")
        bmod = S.sbuf("bmod", [128, 48], F32)
        nmix = S.sbuf("nmix", [128, 8], F32)
        bqn = S.sbuf("bqn", [128, 2], F32)
        bkvn = S.sbuf("bkvn", [128, 1], F32)
        bg = S.sbuf("bg", [128, 2, 2], F32)
        for b_, n_ in ((bmod, "b_mod"), (nmix, "norm_mix"), (bqn, "bqn"), (bkvn, "bkvn"), (bg, "b_gate")):
            S.dma("sp", ld, b_[:], L[n_], writes=[b_])
        wuq = S.sbuf("wuq", [128, 2, 768], BF16)
        S.dma("pool", ld, wuq[:], L["w_uq"].rearrange("(k p) n -> p k n", p=128), writes=[wuq])
        wkn = S.sbuf("wkn", [128, 8, 96], BF16)
        S.op("dve", lambda e: e.memset(wkn[:], 0.0), writes=[wkn])
        S.dma("pool", ld, wkn[:, :, 0:64], L["w_ukv"].rearrange("p (h c) -> p h c", c=128)[:, :, 0:64], writes=[wkn])
        wvb = S.sbuf("wvb", [128, 8, 64], BF16)
        S.dma("pool", ld, wvb[:], L["w_ukv"].rearrange("p (h c) -> p h c", c=128)[:, :, 64:128], writes=[wvb])
        wg = S.sbuf("wg", [32, 2, 256], BF16)
        S.op("dve", lambda e: e.memset(wg[:], 0.0), writes=[wg])
        S.dma("pool", ld, wg[0:16, 0, :], L["w_gate"][0], writes=[wg])
        S.dma("pool", ld, wg[16:32, 1, :], L["w_gate"][1], writes=[wg])
        nbg = S.sbuf("nbg", [128, 2, 2], F32)
        S.op("dve", lambda e: e.tensor_scalar_mul(out=nbg[:], in0=bg[:], scalar1=-1.0), reads=[bg], writes=[nbg])

        PS = Rot(S, "ps", 7, [128, 512], F32, psum=True, dma=False)

        modT = S.sbuf("modT", [128, 48, 2], F32)
        sc = S.sbuf("silu_c", [128, 8, 2], F32)
        tmp8 = S.sbuf("tmp8", [128, 8, 2], F32)
        cv = p.cv
        S.op("act", lambda e: e.activation(out=tmp8[:], in_=cv[:], func=AF.Exp, scale=-1.0), reads=[cv], writes=[tmp8])
        S.op("dve", lambda e: e.tensor_scalar_add(out=tmp8[:], in0=tmp8[:], scalar1=1.0), reads=[tmp8], writes=[tmp8])
        S.op("dve", lambda e: e.reciprocal(out=tmp8[:], in_=tmp8[:]), reads=[tmp8], writes=[tmp8])
        S.op("dve", lambda e: e.tensor_tensor(out=sc[:], in0=tmp8[:], in1=cv[:], op=ALU.mult), reads=[tmp8, cv], writes=[sc])
        scb = S.sbuf("silu_cb", [128, 8, 2], BF16)
        S.op("dve", lambda e: e.tensor_copy(out=scb[:], in_=sc[:]), reads=[sc], writes=[scb])
        with S.scope():
            WM = Rot(S, "wm", 2, [128, 8, 512], BF16)
            for piece in range(12):
                wb, ws = WM.next()
                S.dma("pool", ws, wb[:], L["w_mod"][:, piece * 512:(piece + 1) * 512].rearrange("(k p) n -> p k n", p=128),
                      writes=[wb])
                ps, _ = PS.next()
                for j in range(4):
                    for k in range(8):
                        S.mm(lambda e, j=j, k=k: e.matmul(ps[:, j * 2:j * 2 + 2], lhsT=wb[:, k, j * 128:(j + 1) * 128],
                                                          rhs=scb[:, k, :], start=(k == 0), stop=(k == 7)),
                             reads=[wb, scb], writes=[ps], last=(j == 3 and k == 7))
                S.op("dve", lambda e, piece=piece: e.tensor_tensor(
                    out=modT[:, piece * 4:(piece + 1) * 4, :], in0=ps[:, 0:8].rearrange("p (j v) -> p j v", v=2),
                    in1=bmod[:, piece * 4:(piece + 1) * 4].unsqueeze(2).to_broadcast([128, 4, 2]), op=ALU.add),
                    reads=[ps, bmod], writes=[modT])
        msem = S.dma_sem("modst")
        if p.stop == "mod":
            S.dma("act", msem, T["modT"][:], modT[:], reads=[modT], writes=[T["modT"]])
            return
        S.dma("act", msem, T["modT"][:], modT[:], reads=[modT], writes=[T["modT"]])
        A1 = S.sbuf("A1", [128, 8, 2], F32)
        S.op("dve", lambda e: e.tensor_scalar_add(out=A1[:], in0=modT[:, 8:16, :], scalar1=1.0), reads=[modT], writes=[A1])
        S.op("dve", lambda e: e.tensor_tensor(out=A1[:], in0=A1[:], in1=nmix[:].unsqueeze(2).to_broadcast([128, 8, 2]),
                                              op=ALU.mult), reads=[A1, nmix], writes=[A1])

        hT = S.sbuf("hT", [128, 8, 2304], BF16)
        WP = Rot(S, "wp", 3, [128, 8, 512], BF16)
        STB = Rot(S, "stb", 4, [128, 512], BF16)
        STF = Rot(S, "stf", 3, [128, 512], F32)
        TAB = Rot(S, "tab", 4, [128, 512], F32)
        XT = Rot(S, "xt", 2, [128, 8, 512], F32)
        sq = S.sbuf("sq", [128, 8, 512], BF16)
        rstd = S.sbuf("rstd", [128, 512], F32)
        xs = S.sbuf("xs", [128, 512], F32)
        cq = S.sbuf("cq", [128, 2, 512], F32)
        sq2 = S.sbuf("sq2", [128, 2, 512], BF16)
        rs2 = S.sbuf("rs2", [128, 512], F32)
        cqn = S.sbuf("cqn", [128, 2, 512], BF16)
        ckvn = S.sbuf("ckvn", [128, 512], BF16)
        vst = S.sbuf("vst", [128, 8, 128], BF16)
        S.op("dve", lambda e: e.memset(vst[:], 1.0), writes=[vst])
        vsem = S.dma_sem("vst")
        cvst = S.sbuf("cvst", [128, 512], BF16)
        csem = S.dma_sem("cvst")
        cg = S.sbuf("cg", [32, 512], BF16)
        krb = S.sbuf("krb", [32, 512], BF16)

        def load_w(col0, ncols):
            wb, ws = WP.next()
            S.dma("pool", ws, wb[:, :, 0:ncols], L["w_in"][:, col0:col0 + ncols].rearrange("(k p) n -> p k n", p=128),
                  writes=[wb])
            return wb

        def store(dst, dst_ap, buf, src_ap, sem):
            S.dma("act", sem, dst_ap, src_ap, reads=[buf], writes=[dst])

        for seg in SEGS:
            loc = {}
            l0 = 0
            for (t0, n) in seg:
                loc[t0] = l0
                l0 += n

            for (t0, n) in seg:
                v = 1 if t0 == 0 else 0
                lo = loc[t0]
                xb, xsem = XT.next()
                S.dma("sp", xsem, xb[:, :, 0:n], Xin[:][:, t0:t0 + n].rearrange("(k p) n -> p k n", p=128),
                      reads=[Xin], writes=[xb])
                S.op("act", lambda e: e.activation(out=sq[:, :, 0:n], in_=xb[:, :, 0:n], func=AF.Square), reads=[xb], writes=[sq])
                ps, _ = PS.next()
                for k in range(8):
                    S.mm(lambda e, k=k: e.matmul(ps[:, 0:n], lhsT=ones_bf[:], rhs=sq[:, k, 0:n], start=(k == 0), stop=(k == 7)),
                         reads=[ones_bf, sq], writes=[ps], last=(k == 7))
                rstd_from_ps(S, ps, n, D, eps_t, rstd)
                for k in range(8):
                    S.op("dve", lambda e, k=k: e.scalar_tensor_tensor(out=xs[:, 0:n], in0=xb[:, k, 0:n], scalar=A1[:, k, v:v + 1],
                                                                      in1=rstd[:, 0:n], op0=ALU.mult, op1=ALU.mult),
                         reads=[xb, A1, rstd], writes=[xs])
                    S.op("act", lambda e, k=k: e.activation(out=hT[:, k, lo:lo + n], in_=xs[:, 0:n], func=AF.Identity,
                                                            bias=modT[:, k, v:v + 1]),
                         reads=[xs, modT], writes=[hT])

            def proj_fm(ps, wb, c0, m, t0, n):
                lo = loc[t0]
                for k in range(8):
                    S.mm(lambda e, k=k: e.matmul(ps[0:m, 0:n], lhsT=wb[:, k, c0:c0 + m], rhs=hT[:, k, lo:lo + n],
                                                 start=(k == 0), stop=(k == 7)),
                         reads=[wb, hT], writes=[ps], last=(k == 7))

            def rope(ps, m, t0, n, perm, cosn, sinn, dst, dst_ap, obuf=None):
                pre, _ = STB.next()
                evac(S, "act", pre[0:m, 0:n], ps[0:m, 0:n], [ps], [pre])
                ct, cs = TAB.next()
                S.dma("sp", cs, ct[0:m, 0:n], C[cosn][:, t0:t0 + n], writes=[ct])
                stt, ss = TAB.next()
                S.dma("sp", ss, stt[0:m, 0:n], C[sinn][:, t0:t0 + n], writes=[stt])
                ps2, _ = PS.next()
                S.mm(lambda e: e.matmul(ps2[0:m, 0:n], lhsT=perm[:], rhs=pre[0:m, 0:n], start=True, stop=True),
                     reads=[perm, pre], writes=[ps2])
                a, _ = STF.next()
                S.op("dve", lambda e: e.tensor_tensor(out=a[0:m, 0:n], in0=pre[0:m, 0:n], in1=ct[0:m, 0:n], op=ALU.mult),
                     reads=[pre, ct], writes=[a])
                b, _ = STF.next()
                S.op("dve", lambda e: e.tensor_tensor(out=b[0:m, 0:n], in0=ps2[0:m, 0:n], in1=stt[0:m, 0:n], op=ALU.mult),
                     reads=[ps2, stt], writes=[b])
                if obuf is not None:
                    o, osm = obuf, None
                else:
                    o, osm = STB.next()
                S.op("dve", lambda e: e.tensor_tensor(out=o[0:m, 0:n], in0=a[0:m, 0:n], in1=b[0:m, 0:n], op=ALU.add),
                     reads=[a, b], writes=[o])
                if dst is not None:
                    store(dst, dst_ap, o, o[0:m, 0:n], osm)
                return o

            def simple_group(col0, nchunks, dst, func=AF.Copy, scale=1.0):
                for c0 in range(0, nchunks, 4):
                    ncn = min(4, nchunks - c0)
                    wb = load_w(col0 + c0 * 128, ncn * 128)
                    for (t0, n) in seg:
                        for j in range(ncn):
                            ps, _ = PS.next()
                            proj_fm(ps, wb, j * 128, 128, t0, n)
                            o, osm = STB.next()
                            evac(S, "act", o[:, 0:n], ps[:, 0:n], [ps], [o], func=func, scale=scale)
                            store(dst, dst[:][:, c0 + j, t0:t0 + n], o, o[:, 0:n], osm)

            if p.stop == "hT":
                return
            wb, ws = WP.next()
            for half in range(2):
                for g in range(4):
                    c0 = O_AQ + (half * 4 + g) * 64
                    S.dma("pool", ws, wb[:, :, g * 128 + half * 64:g * 128 + (half + 1) * 64],
                          L["w_in"][:, c0:c0 + 64].rearrange("(k p) c -> p k c", p=128), writes=[wb])
            for (t0, n) in seg:
                for g in range(4):
                    ps, _ = PS.next()
                    proj_fm(ps, wb, g * 128, 128, t0, n)
                    rope(ps, 128, t0, n, p.permA, "cosA", "sinA", T["qaT"], T["qaT"][:][:, g, t0:t0 + n])
            if p.stop == "G1":
                return
            wb = load_w(O_AK, 256)
            for (t0, n) in seg:
                lo = loc[t0]
                ps, _ = PS.next()
                proj_fm(ps, wb, 0, 128, t0, n)
                rope(ps, 128, t0, n, p.permA, "cosA", "sinA", T["kaT"], T["kaT"][:][:, t0:t0 + n])
                for s0 in range(0, n, 128):
                    ps, _ = PS.next()
                    for k in range(8):
                        S.mm(lambda e, k=k: e.matmul(ps[:, 0:128], lhsT=hT[:, k, lo + s0:lo + s0 + 128], rhs=wb[:, k, 128:256],
                                                     start=(k == 0), stop=(k == 7)),
                             reads=[wb, hT], writes=[ps], last=(k == 7))
                    S.op("act", lambda e: e.activation(out=vst[:, 0:2, 0:64],
                                                       in_=ps[:, 0:128].rearrange("p (h c) -> p h c", c=64), func=AF.Copy),
                         reads=[ps], writes=[vst])
                    store(T["vaA"], T["vaA"][:][t0 + s0:t0 + s0 + 128], vst, vst[:, 0:2, :], vsem)
            if p.stop == "G3":
                return
            wb = load_w(O_BQ, 256)
            wb2 = load_w(O_BKV, 256)
            for (t0, n) in seg:
                for c in range(2):
                    ps, _ = PS.next()
                    proj_fm(ps, wb, c * 128, 128, t0, n)
                    S.op("dve", lambda e, c=c: e.tensor_copy(out=cq[:, c, 0:n], in_=ps[:, 0:n]), reads=[ps], writes=[cq])
                    S.op("act", lambda e, c=c: e.activation(out=sq2[:, c, 0:n], in_=cq[:, c, 0:n], func=AF.Square),
                         reads=[cq], writes=[sq2])
                ps, _ = PS.next()
                for c in range(2):
                    S.mm(lambda e, c=c: e.matmul(ps[:, 0:n], lhsT=ones_bf[:], rhs=sq2[:, c, 0:n], start=(c == 0), stop=(c == 1)),
                         reads=[ones_bf, sq2], writes=[ps], last=(c == 1))
                rstd_from_ps(S, ps, n, 256, eps_t, rs2)
                for c in range(2):
                    S.op("dve", lambda e, c=c: e.scalar_tensor_tensor(out=cqn[:, c, 0:n], in0=cq[:, c, 0:n], scalar=bqn[:, c:c + 1],
                                                                      in1=rs2[:, 0:n], op0=ALU.mult, op1=ALU.mult),
                         reads=[cq, bqn, rs2], writes=[cqn])
                for h in range(8):
                    ps, _ = PS.next()
                    for c in range(2):
                        S.mm(lambda e, c=c, h=h: e.matmul(ps[0:96, 0:n], lhsT=wuq[:, c, h * 96:(h + 1) * 96], rhs=cqn[:, c, 0:n],
                                                          start=(c == 0), stop=(c == 1)),
                             reads=[wuq, cqn], writes=[ps], last=(c == 1))
                    rope(ps, 96, t0, n, p.permB, "cosB", "sinB", T["qbT"], T["qbT"][:][:, h, t0:t0 + n])
                if p.stop == "G4":
                    return
                ps, _ = PS.next()
                if p.stop == "G5x":
                    return
                proj_fm(ps, wb2, 0, 128, t0, n)
                if p.stop == "G5a0":
                    return
                S.op("dve", lambda e: e.tensor_copy(out=cq[:, 0, 0:n], in_=ps[:, 0:n]), reads=[ps], writes=[cq])
                S.op("act", lambda e: e.activation(out=sq2[:, 0, 0:n], in_=cq[:, 0, 0:n], func=AF.Square), reads=[cq], writes=[sq2])
                if p.stop in ("G5a1", "G5nosq"):
                    return
                ps, _ = PS.next()
                S.mm(lambda e: e.matmul(ps[:, 0:n], lhsT=ones_bf[:], rhs=sq2[:, 0, 0:n], start=True, stop=True),
                     reads=[ones_bf, sq2], writes=[ps])
                if p.stop == "G5a2":
                    return
                rstd_from_ps(S, ps, n, 128, eps_t, rs2)
                if p.stop == "G5a3":
                    return
                S.op("dve", lambda e: e.scalar_tensor_tensor(out=ckvn[:, 0:n], in0=cq[:, 0, 0:n], scalar=bkvn[:, 0:1],
                                                             in1=rs2[:, 0:n], op0=ALU.mult, op1=ALU.mult),
                     reads=[cq, bkvn, rs2], writes=[ckvn])
                if p.stop == "G5a":
                    return
                ps, _ = PS.next()
                proj_fm(ps, wb2, 128, 128, t0, n)
                if p.stop == "G5":
                    return
                kr = rope(ps, 32, t0, n, p.permK, "cosK", "sinK", None, None, obuf=krb)
                if p.stop == "G5b":
                    return
                for h in range(8):
                    ps, _ = PS.next()
                    S.mm(lambda e, h=h: e.matmul(ps[0:96, 0:n], lhsT=wkn[:, h, :], rhs=ckvn[:, 0:n], start=True, stop=False),
                         reads=[wkn, ckvn], writes=[ps], last=False)
                    S.mm(lambda e: e.matmul(ps[0:96, 0:n], lhsT=p.sel[:], rhs=kr[0:32, 0:n], start=False, stop=True),
                         reads=[p.sel, kr], writes=[ps], last=True)
                    o, osm = STB.next()
                    evac(S, "act", o[0:96, 0:n], ps[0:96, 0:n], [ps], [o])
                    store(T["kbT"], T["kbT"][:][:, h, t0:t0 + n], o, o[0:96, 0:n], osm)
                if p.stop == "G5c":
                    return
                for s0 in range(0, n, 128):
                    ps, _ = PS.next()
                    S.mm(lambda e: e.matmul(ps[:, 0:512], lhsT=ckvn[:, s0:s0 + 128], rhs=wvb[:].rearrange("p h c -> p (h c)"),
                                            start=True, stop=True), reads=[ckvn, wvb], writes=[ps])
                    S.op("act", lambda e: e.activation(out=vst[:, :, 0:64],
                                                       in_=ps[:, 0:512].rearrange("p (h c) -> p h c", c=64), func=AF.Copy),
                         reads=[ps], writes=[vst])
                    store(T["vbA"], T["vbA"][:][t0 + s0:t0 + s0 + 128], vst, vst[:, :, :], vsem)
            if p.stop == "G6":
                return
            simple_group(O_CQ, 2, T["cqT"], scale=0.125)
            simple_group(O_CK, 2, T["ckT"])
            wb = load_w(O_CV, 512)
            for (t0, n) in seg:
                lo = loc[t0]
                for s0 in range(0, n, 128):
                    ps, _ = PS.next()
                    for k in range(8):
                        S.mm(lambda e, k=k: e.matmul(ps[:, 0:512], lhsT=hT[:, k, lo + s0:lo + s0 + 128], rhs=wb[:, k, 0:512],
                                                     start=(k == 0), stop=(k == 7)),
                             reads=[wb, hT], writes=[ps], last=(k == 7))
                    evac(S, "act", cvst[:, :], ps[:, 0:512], [ps], [cvst])
                    store(T["cvv"], T["cvv"][:][t0 + s0:t0 + s0 + 128, :], cvst, cvst[:, :], csem)
            wb = load_w(O_CG, 128)
            for (t0, n) in seg:
                ps, _ = PS.next()
                proj_fm(ps, wb, 0, 128, t0, n)
                evac(S, "act", cg[:, 0:n], ps[0:32, 0:n], [ps], [cg])
                for dr, dst in ((0, T["gfT"]), (1, T["gbT"])):
                    for pr in range(2):
                        ps, _ = PS.next()
                        S.mm(lambda e, dr=dr, pr=pr: e.matmul(ps[:, 0:n], lhsT=wg[:, dr, pr * 128:(pr + 1) * 128], rhs=cg[:, 0:n],
                                                              start=True, stop=True), reads=[wg, cg], writes=[ps])
                        o, osm = STF.next()
                        S.op("act", lambda e, dr=dr, pr=pr: e.activation(out=o[:, 0:n], in_=ps[:, 0:n], func=AF.Exp, scale=-1.0,
                                                                         bias=nbg[:, dr, pr:pr + 1]), reads=[ps, nbg], writes=[o])
                        S.op("act", lambda e: e.activation(out=o[:, 0:n], in_=o[:, 0:n], func=AF.Ln, bias=one_t[:]),
                             reads=[o, one_t], writes=[o])
                        S.op("dve", lambda e: e.tensor_scalar_mul(out=o[:, 0:n], in0=o[:, 0:n], scalar1=-1.0 / 16.0),
                             reads=[o], writes=[o])
                        store(dst, dst[:][:, pr, t0:t0 + n], o, o[:, 0:n], osm)
            if p.stop == "G11":
                return
            simple_group(O_CR, 4, T["crT"], func=AF.Silu)
            simple_group(O_GATE, 24, T["gatesT"], func=AF.Sigmoid)


def phase_B1(p, L):
    S, T, C = p.S, p.T, p.C
    with S.scope():
        ld = S.dma_sem("ld")
        ka = S.sbuf("ka", [128, NTOT], BF16)
        va = S.sbuf("va", [128, NKB, 2, 128], BF16)
        S.dma("sp", ld, ka[:], T["kaT"][:], reads=[T["kaT"]], writes=[ka])
        for i in range(0, NKB, 6):
            S.dma("sp", ld, va[:, i:i + 6], T["vaA"][:].rearrange("(b p) h c -> p b h c", p=128)[:, i:i + 6],
                  reads=[T["vaA"]], writes=[va])
        sk = S.sbuf("sk", [1, 8], F32)
        S.dma("sp", ld, sk[:], L["a_sink"], writes=[sk])
        zrow = S.sbuf("zrow", [1, 128], F32)
        S.op("dve", lambda e: e.memset(zrow[:], 0.0), writes=[zrow])
        esrow = S.sbuf("esrow", [1, 2, 512], BF16)
        for h in range(8):
            S.op("act", lambda e, h=h: e.activation(out=esrow[0:1, h // 4, (h % 4) * 128:(h % 4 + 1) * 128], in_=zrow[:],
                                                    func=AF.Exp, bias=sk[0:1, h:h + 1]), reads=[zrow, sk], writes=[esrow])
        PSS = Rot(S, "pss", 4, [128, 512], F32, psum=True, dma=False)
        PSO = Rot(S, "pso", 2, [128, 512], F32, psum=True, dma=False)
        QT = Rot(S, "qt", 2, [128, 4, 512], BF16)
        PT = Rot(S, "pt", 4, [128, 512], BF16, dma=False)
        RC = Rot(S, "rc", 2, [64, 512], F32, dma=False)
        OS = Rot(S, "os", 3, [64, 512], BF16)
        scale = 64 ** -0.5

        for (t0, n) in TT:
            qt, qsem = QT.next()
            S.dma("sp", qsem, qt[:, :, 0:n], T["qaT"][:][:, :, t0:t0 + n], reads=[T["qaT"]], writes=[qt])
            for qb in range(n // 128):
                q0 = t0 + qb * 128
                blk = q0 // 128
                if t0 == 0:
                    kbs = [(0, None), (1, None)]
                else:
                    kbs = [(0, None), (1, None)]
                    if blk - 1 >= 2:
                        kbs.append((blk - 1, "maskLo"))
                    kbs.append((blk, None))
                    if blk + 1 < NKB:
                        kbs.append((blk + 1, "maskHi"))
                for kvh in range(2):
                    pb = kvh * 64
                    pso, _ = PSO.next()
                    for i, (kb, msk) in enumerate(kbs):
                        pss, _ = PSS.next()
                        S.mm(lambda e, kb=kb: e.matmul(pss[:, 0:512].rearrange("p (g q) -> p g q", g=4),
                                                       lhsT=ka[pb:pb + 64, kb * 128:(kb + 1) * 128],
                                                       rhs=qt[pb:pb + 64, :, qb * 128:(qb + 1) * 128], start=True, stop=True),
                             reads=[ka, qt], writes=[pss])
                        pt, _ = PT.next()
                        S.op("act", lambda e: e.activation(out=pt[:, :], in_=pss[:, :], func=AF.Exp, scale=scale),
                             reads=[pss], writes=[pt])
                        if msk is not None:
                            mk = p.maskLo if msk == "maskLo" else p.maskHi
                            S.op("dve", lambda e, mk=mk: e.tensor_tensor(out=pt[:, :], in0=pt[:, :], in1=mk[:, :], op=ALU.mult),
                                 reads=[pt, mk], writes=[pt])
                        S.mm(lambda e, kb=kb, i=i: e.matmul(pso[:, :], lhsT=va[:, kb, kvh, :], rhs=pt[:, :], start=(i == 0), stop=False),
                             reads=[va, pt], writes=[pso], last=False)
                    S.mm(lambda e: e.matmul(pso[:, :], lhsT=p.sv[:], rhs=esrow[0:1, kvh, :], start=False, stop=True),
                         reads=[p.sv, esrow], writes=[pso], last=True)
                    rc, _ = RC.next()
                    S.op("dve", lambda e: e.reciprocal(out=rc[:, :], in_=pso[64:128, :]), reads=[pso], writes=[rc])
                    o, osm = OS.next()
                    S.op("dve", lambda e: e.tensor_tensor(out=o[:, :], in0=pso[0:64, :], in1=rc[:, :], op=ALU.mult),
                         reads=[pso, rc], writes=[o])
                    for gp in range(2):
                        S.dma("act", osm, T["yaT"][:][gp * 64:(gp + 1) * 64, kvh * 2:kvh * 2 + 2, q0:q0 + 128],
                              o[:, :].rearrange("p (g2 gp q) -> p g2 gp q", g2=2, gp=2)[:, :, gp, :],
                              reads=[o], writes=[T["yaT"]])


def phase_B2(p, L):
    S, T, C = p.S, p.T, p.C
    with S.scope():
        KS = Rot(S, "ks", 2, [96, NTOT], BF16)
        VS = Rot(S, "vs", 2, [128, NKB, 128], BF16)
        PSS = Rot(S, "pss", 4, [128, 512], F32, psum=True, dma=False)
        PSO = Rot(S, "pso", 2, [128, 512], F32, psum=True, dma=False)
        QT = Rot(S, "qt", 2, [96, 512], BF16)
        PT = Rot(S, "pt", 4, [128, 512], BF16, dma=False)
        RC = Rot(S, "rc", 2, [64, 512], F32, dma=False)
        OS = Rot(S, "os", 3, [64, 512], BF16)
        scale = 96 ** -0.5
        for h in range(8):
            ks, ksem = KS.next()
            vs, vsem = VS.next()
            S.dma("sp", ksem, ks[:], T["kbT"][:][:, h, :], reads=[T["kbT"]], writes=[ks])
            for i in range(0, NKB, 6):
                S.dma("sp", vsem, vs[:, i:i + 6], T["vbA"][:].rearrange("(b p) h c -> p b h c", p=128)[:, i:i + 6, h, :],
                      reads=[T["vbA"]], writes=[vs])
            for (t0, n) in TT:
                qt, qsem = QT.next()
                S.dma("sp", qsem, qt[:, 0:n], T["qbT"][:][:, h, t0:t0 + n], reads=[T["qbT"]], writes=[qt])
                nkb = 2 if t0 == 0 else NKB
                pso, _ = PSO.next()
                for kb in range(nkb):
                    pss, _ = PSS.next()
                    S.mm(lambda e, kb=kb: e.matmul(pss[:, 0:n], lhsT=ks[:, kb * 128:(kb + 1) * 128], rhs=qt[:, 0:n],
                                                   start=True, stop=True), reads=[ks, qt], writes=[pss])
                    pt, _ = PT.next()
                    S.op("act", lambda e: e.activation(out=pt[:, 0:n], in_=pss[:, 0:n], func=AF.Exp, scale=scale),
                         reads=[pss], writes=[pt])
                    S.mm(lambda e, kb=kb: e.matmul(pso[:, 0:n], lhsT=vs[:, kb, :], rhs=pt[:, 0:n], start=(kb == 0),
                                                   stop=(kb == nkb - 1)), reads=[vs, pt], writes=[pso], last=(kb == nkb - 1))
                rc, _ = RC.next()
                S.op("dve", lambda e: e.reciprocal(out=rc[:, 0:n], in_=pso[64:128, 0:n]), reads=[pso], writes=[rc])
                o, osm = OS.next()
                S.op("dve", lambda e: e.tensor_tensor(out=o[:, 0:n], in0=pso[0:64, 0:n], in1=rc[:, 0:n], op=ALU.mult),
                     reads=[pso, rc], writes=[o])
                S.dma("act", osm, T["ybT"][:][(h % 2) * 64:(h % 2 + 1) * 64, h // 2, t0:t0 + n], o[:, 0:n],
                      reads=[o], writes=[T["ybT"]])


def phase_B3(p, L):
    S, T, C = p.S, p.T, p.C
    eps_t, ones_bf = p.eps_t, p.ones_bf
    NCH = NTOT // 64
    PIECE = 2112
    with S.scope():
        ld = S.dma_sem("ld")
        hn = S.sbuf("hn", [128, 4], F32)
        S.dma("sp", ld, hn[:], L["c_hn"], writes=[hn])
        vsb = S.sbuf("vsb", [64, NCH, 128], BF16)
        qtb = S.sbuf("qtb", [64, NTOT], BF16)
        ktT = S.sbuf("ktT", [64, NCH, 64], BF16)
        attT = S.sbuf("attT", [64, NCH, 64], BF16)
        ebl = S.sbuf("ebl", [64, NCH], F32)
        oT = S.sbuf("oT", [128, NTOT], F32)
        qp = S.sbuf("qp", [64, PIECE], BF16)
        kp = S.sbuf("kp", [64, PIECE], BF16)
        gp = S.sbuf("gp", [64, PIECE], F32)
        bp = S.sbuf("bp", [64, PIECE], F32)
        rm = S.sbuf("rm", [64, PIECE], F32)
        ep = S.sbuf("ep", [64, PIECE], BF16)
        ktp = S.sbuf("ktp", [64, PIECE], BF16)
        S.dma("sp", ld, rm[:], C["rmask"][:, 0:PIECE], writes=[rm])
        Sf = [S.sbuf("Sf%d" % i, [64, 128], F32) for i in range(2)]
        Sb = [S.sbuf("Sb%d" % i, [64, 128], BF16) for i in range(2)]
        S1 = S.sbuf("S1", [64, 128], F32)
        PST = Rot(S, "pst", 2, [64, 512], BF16, psum=True, dma=False)
        PSA = Rot(S, "psa", 2, [64, 512], F32, psum=True, dma=False)
        PSO = Rot(S, "pso", 2, [128, 512], F32, psum=True, dma=False)
        PSK = Rot(S, "psk", 2, [64, 128], F32, psum=True, dma=False)
        sqb = S.sbuf("sqb", [128, 512], BF16)
        rs = S.sbuf("rs", [128, 512], F32)
        tt = S.sbuf("tt", [128, 512], F32)
        CR = Rot(S, "cr", 2, [128, 512], BF16)
        YO = Rot(S, "yo", 2, [128, 512], BF16)

        for h in range(4):
            hb = (h % 2) * 64
            for i in range(0, NCH, 11):
                S.dma("sp", ld, vsb[:, i:i + 11, :],
                      T["cvv"][:].rearrange("(c j) (h e) -> j c h e", j=64, e=128)[:, i:i + 11, h, :],
                      reads=[T["cvv"]], writes=[vsb])
            for dr in range(2):
                gsrc = T["gfT"] if dr == 0 else T["gbT"]
                tri = p.triF if dr == 0 else p.triB
                for pc in range(NTOT // PIECE):
                    a0 = pc * PIECE
                    c0 = a0 // 64
                    S.dma("sp", ld, qp[:], T["cqT"][:][hb:hb + 64, h // 2, a0:a0 + PIECE], reads=[T["cqT"]], writes=[qp])
                    S.dma("sp", ld, kp[:], T["ckT"][:][hb:hb + 64, h // 2, a0:a0 + PIECE], reads=[T["ckT"]], writes=[kp])
                    S.dma("sp", ld, gp[:], gsrc[:][hb:hb + 64, h // 2, a0:a0 + PIECE], reads=[gsrc], writes=[gp])
                    S.op("dve", lambda e: e.tensor_tensor_scan(out=bp[:], data0=rm[:], data1=gp[:], initial=0.0,
                                                               op0=ALU.mult, op1=ALU.add), reads=[rm, gp], writes=[bp])
                    if dr == 1:
                        b3 = bp[:].rearrange("p (c j) -> p c j", j=64)
                        S.op("dve", lambda e: e.tensor_tensor(out=gp[:], in0=gp[:], in1=bp[:], op=ALU.subtract),
                             reads=[gp, bp], writes=[gp])
                        S.op("dve", lambda e: e.tensor_tensor(out=bp[:].rearrange("p (c j) -> p c j", j=64),
                                                              in0=gp[:].rearrange("p (c j) -> p c j", j=64),
                                                              in1=b3[:, :, 63:64].to_broadcast([64, PIECE // 64, 64]), op=ALU.add),
                             reads=[gp, bp], writes=[bp])
                    last_col = 63 if dr == 0 else 0
                    S.op("act", lambda e: e.activation(out=ebl[:, c0:c0 + PIECE // 64],
                                                       in_=bp[:].rearrange("p (c j) -> p c j", j=64)[:, :, last_col],
                                                       func=AF.Exp), reads=[bp], writes=[ebl])
                    S.op("act", lambda e: e.activation(out=ep[:], in_=bp[:], func=AF.Exp), reads=[bp], writes=[ep])
                    S.op("dve", lambda e: e.tensor_tensor(out=qtb[:, a0:a0 + PIECE], in0=qp[:], in1=ep[:], op=ALU.mult),
                         reads=[qp, ep], writes=[qtb])
                    S.op("act", lambda e: e.activation(out=ep[:], in_=bp[:], func=AF.Exp, scale=-1.0), reads=[bp], writes=[ep])
                    S.op("dve", lambda e: e.tensor_tensor(out=ktp[:], in0=kp[:], in1=ep[:], op=ALU.mult),
                         reads=[kp, ep], writes=[ktp])
                    for c8 in range(0, PIECE // 64, 8):
                        nn = min(8, PIECE // 64 - c8)
                        pst, _ = PST.next()
                        psa, _ = PSA.next()
                        for j in range(nn):
                            cl = c8 + j
                            S.mm(lambda e, cl=cl, j=j: e.transpose(out=pst[:, j * 64:(j + 1) * 64], in_=ktp[:, cl * 64:(cl + 1) * 64],
                                                                   identity=p.ident[:]),
                                 reads=[ktp, p.ident], writes=[pst], last=(j == nn - 1))
                        for j in range(nn):
                            cl = c8 + j
                            S.mm(lambda e, cl=cl, j=j: e.matmul(psa[:, j * 64:(j + 1) * 64], lhsT=ktp[:, cl * 64:(cl + 1) * 64],
                                                                rhs=qtb[:, a0 + cl * 64:a0 + (cl + 1) * 64], start=True, stop=True),
                                 reads=[ktp, qtb], writes=[psa], last=(j == nn - 1))
                        S.op("act", lambda e: e.activation(out=ktT[:, c0 + c8:c0 + c8 + nn, :].rearrange("p c d -> p (c d)"),
                                                           in_=pst[:, 0:nn * 64], func=AF.Copy), reads=[pst], writes=[ktT])
                        S.op("dve", lambda e: e.tensor_tensor(out=attT[:, c0 + c8:c0 + c8 + nn, :].rearrange("p c d -> p (c d)"),
                                                              in0=psa[:, 0:nn * 64], in1=tri[:, 0:nn * 64], op=ALU.mult),
                             reads=[psa, tri], writes=[attT])
                order = list(range(NCH)) if dr == 0 else ([3, 2, 1, 0] + list(range(NCH - 1, 3, -1)))
                cur = 0
                S.op("dve", lambda e: e.memset(Sf[0][:], 0.0), writes=[Sf[0]])
                S.op("dve", lambda e: e.memset(Sb[0][:], 0.0), writes=[Sb[0]])
                groups = [order[0:4]] + [order[i:i + 8] for i in range(4, NCH, 8)]
                for grp in groups:
                    ng = len(grp)
                    lo_c = min(grp)
                    pso, _ = PSO.next()
                    for j, c in enumerate(grp):
                        col = (c - lo_c) * 64
                        S.mm(lambda e, c=c, col=col: e.matmul(pso[:, col:col + 64], lhsT=vsb[:, c, :], rhs=attT[:, c, :],
                                                              start=True, stop=False),
                             reads=[vsb, attT], writes=[pso], last=False)
                        S.mm(lambda e, c=c, col=col, cur=cur: e.matmul(pso[:, col:col + 64], lhsT=Sb[cur][:],
                                                                       rhs=qtb[:, c * 64:(c + 1) * 64], start=False, stop=True),
                             reads=[Sb[cur], qtb], writes=[pso], last=(j == ng - 1))
                        psk, _ = PSK.next()
                        S.mm(lambda e, c=c: e.matmul(psk[:, :], lhsT=ktT[:, c, :], rhs=vsb[:, c, :], start=True, stop=True),
                             reads=[ktT, vsb], writes=[psk])
                        nxt = 1 - cur
                        S.op("dve", lambda e, cur=cur: e.tensor_tensor(out=S1[:], in0=psk[:, :], in1=Sf[cur][:], op=ALU.add),
                             reads=[psk, Sf[cur]], writes=[S1])
                        S.op("dve", lambda e, c=c, nxt=nxt: e.tensor_scalar_mul(out=Sf[nxt][:], in0=S1[:], scalar1=ebl[:, c:c + 1]),
                             reads=[S1, ebl], writes=[Sf[nxt]])
                        S.op("act", lambda e, c=c, nxt=nxt: e.activation(out=Sb[nxt][:], in_=S1[:], func=AF.Copy,
                                                                         scale=ebl[:, c:c + 1]),
                             reads=[S1, ebl], writes=[Sb[nxt]])
                        cur = nxt
                    w = ng * 64
                    if dr == 0:
                        evac(S, "act", oT[:, lo_c * 64:lo_c * 64 + w], pso[:, 0:w], [pso], [oT])
                    else:
                        S.op("dve", lambda e, lo_c=lo_c, w=w: e.tensor_tensor(out=oT[:, lo_c * 64:lo_c * 64 + w],
                                                                             in0=oT[:, lo_c * 64:lo_c * 64 + w],
                                                                             in1=pso[:, 0:w], op=ALU.add),
                             reads=[oT, pso], writes=[oT])
            for (t0, n) in TT:
                S.op("act", lambda e: e.activation(out=sqb[:, 0:n], in_=oT[:, t0:t0 + n], func=AF.Square), reads=[oT], writes=[sqb])
                pso, _ = PSO.next()
                S.mm(lambda e: e.matmul(pso[:, 0:n], lhsT=ones_bf[:], rhs=sqb[:, 0:n], start=True, stop=True),
                     reads=[ones_bf, sqb], writes=[pso])
                rstd_from_ps(S, pso, n, 128, eps_t, rs)
                cr, crs = CR.next()
                S.dma("sp", crs, cr[:, 0:n], T["crT"][:][:, h, t0:t0 + n], reads=[T["crT"]], writes=[cr])
                S.op("dve", lambda e: e.scalar_tensor_tensor(out=tt[:, 0:n], in0=oT[:, t0:t0 + n], scalar=hn[:, h:h + 1],
                                                             in1=rs[:, 0:n], op0=ALU.mult, op1=ALU.mult),
                     reads=[oT, hn, rs], writes=[tt])
                yo, ys = YO.next()
                S.op("dve", lambda e: e.tensor_tensor(out=yo[:, 0:n], in0=tt[:, 0:n], in1=cr[:, 0:n], op=ALU.mult),
                     reads=[tt, cr], writes=[yo])
                S.dma("act", ys, T["ycT"][:][:, h, t0:t0 + n], yo[:, 0:n], reads=[yo], writes=[T["ycT"]])


def phase_C1(p, L, Xin):
    S, T, C = p.S, p.T, p.C
    eps_t, ones_bf = p.eps_t, p.ones_bf
    with S.scope():
        ld = S.dma_sem("ld")
        modT = S.sbuf("modT", [128, 48, 2], F32)
        S.dma("sp", ld, modT[:], T["modT"][:], reads=[T["modT"]], writes=[modT])
        nffn = S.sbuf("nffn", [128, 8], F32)
        S.dma("sp", ld, nffn[:], L["norm_ffn"], writes=[nffn])
        wbr = [S.sbuf("wbr%d" % i, [128, 4, D], BF16) for i in range(3)]
        for i, nm in enumerate(("w_br_a", "w_br_b", "w_br_c")):
            S.dma("pool", ld, wbr[i][:], L[nm].rearrange("(k p) n -> p k n", p=128), writes=[wbr[i]])
        wo = S.sbuf("wo", [128, 8, D], BF16)
        for k0 in range(0, 8, 4):
            S.dma("pool", ld, wo[:, k0:k0 + 4, :], L["w_out"].rearrange("(k p) n -> p k n", p=128)[:, k0:k0 + 4, :], writes=[wo])
        A2 = S.sbuf("A2", [128, 8, 2], F32)
        S.op("dve", lambda e: e.tensor_scalar_add(out=A2[:], in0=modT[:, 32:40, :], scalar1=1.0), reads=[modT], writes=[A2])
        S.op("dve", lambda e: e.tensor_tensor(out=A2[:], in0=A2[:], in1=nffn[:].unsqueeze(2).to_broadcast([128, 8, 2]),
                                              op=ALU.mult), reads=[A2, nffn], writes=[A2])
        PS = Rot(S, "ps", 7, [128, 512], F32, psum=True, dma=False)
        YA = Rot(S, "ya", 2, [128, 12, 512], BF16)
        GT = Rot(S, "gt", 1, [128, 24, 512], BF16)
        XT = Rot(S, "xt", 2, [128, 8, 512], F32)
        mT = S.sbuf("mT", [128, 8, 512], BF16)
        t1 = S.sbuf("t1", [128, 512], F32)
        t2 = S.sbuf("t2", [128, 512], F32)
        sq = S.sbuf("sq", [128, 8, 512], BF16)
        rstd = S.sbuf("rstd", [128, 512], F32)
        xs = S.sbuf("xs", [128, 512], F32)
        H2 = Rot(S, "h2", 1, [128, 8, 512], BF16)
        for (t0, n) in TT:
            v = 1 if t0 == 0 else 0
            ya, yas = YA.next()
            for i, nm in enumerate(("yaT", "ybT", "ycT")):
                S.dma("sp", yas, ya[:, i * 4:(i + 1) * 4, 0:n], T[nm][:][:, :, t0:t0 + n], reads=[T[nm]], writes=[ya])
            gt, gts = GT.next()
            for i in range(3):
                S.dma("sp", gts, gt[:, i * 8:(i + 1) * 8, 0:n], T["gatesT"][:][:, i * 8:(i + 1) * 8, t0:t0 + n],
                      reads=[T["gatesT"]], writes=[gt])
            xb, xsem = XT.next()
            S.dma("sp", xsem, xb[:, :, 0:n], Xin[:][:, t0:t0 + n].rearrange("(k p) n -> p k n", p=128), reads=[Xin], writes=[xb])
            for m in range(8):
                pss = []
                for i in range(3):
                    ps, _ = PS.next()
                    for k in range(4):
                        S.mm(lambda e, i=i, k=k, m=m: e.matmul(ps[:, 0:n], lhsT=wbr[i][:, k, m * 128:(m + 1) * 128],
                                                               rhs=ya[:, i * 4 + k, 0:n], start=(k == 0), stop=(k == 3)),
                             reads=[wbr[i], ya], writes=[ps], last=(k == 3))
                    pss.append(ps)
                S.op("dve", lambda e, m=m: e.tensor_tensor(out=t1[:, 0:n], in0=pss[0][:, 0:n], in1=gt[:, m, 0:n], op=ALU.mult),
                     reads=[pss[0], gt], writes=[t1])
                S.op("dve", lambda e, m=m: e.tensor_tensor(out=t2[:, 0:n], in0=pss[1][:, 0:n], in1=gt[:, 8 + m, 0:n], op=ALU.mult),
                     reads=[pss[1], gt], writes=[t2])
                S.op("pool", lambda e: e.tensor_tensor(out=t1[:, 0:n], in0=t1[:, 0:n], in1=t2[:, 0:n], op=ALU.add),
                     reads=[t1, t2], writes=[t1])
                S.op("dve", lambda e, m=m: e.tensor_tensor(out=t2[:, 0:n], in0=pss[2][:, 0:n], in1=gt[:, 16 + m, 0:n], op=ALU.mult),
                     reads=[pss[2], gt], writes=[t2])
                S.op("pool", lambda e, m=m: e.tensor_tensor(out=mT[:, m, 0:n], in0=t1[:, 0:n], in1=t2[:, 0:n], op=ALU.add),
                     reads=[t1, t2], writes=[mT])
            for m in range(8):
                ps, _ = PS.next()
                for k in range(8):
                    S.mm(lambda e, k=k, m=m: e.matmul(ps[:, 0:n], lhsT=wo[:, k, m * 128:(m + 1) * 128], rhs=mT[:, k, 0:n],
                                                      start=(k == 0), stop=(k == 7)), reads=[wo, mT], writes=[ps], last=(k == 7))
                S.op("dve", lambda e, m=m: e.scalar_tensor_tensor(out=xb[:, m, 0:n], in0=ps[:, 0:n], scalar=modT[:, 16 + m, v:v + 1],
                                                                  in1=xb[:, m, 0:n], op0=ALU.mult, op1=ALU.add),
                     reads=[ps, modT, xb], writes=[xb])
            S.dma("act", xsem, T["x1T"][:][:, t0:t0 + n].rearrange("(k p) n -> p k n", p=128), xb[:, :, 0:n],
                  reads=[xb], writes=[T["x1T"]])
            S.op("act", lambda e: e.activation(out=sq[:, :, 0:n], in_=xb[:, :, 0:n], func=AF.Square), reads=[xb], writes=[sq])
            ps, _ = PS.next()
            for k in range(8):
                S.mm(lambda e, k=k: e.matmul(ps[:, 0:n], lhsT=ones_bf[:], rhs=sq[:, k, 0:n], start=(k == 0), stop=(k == 7)),
                     reads=[ones_bf, sq], writes=[ps], last=(k == 7))
            rstd_from_ps(S, ps, n, D, eps_t, rstd)
            h2, h2s = H2.next()
            for k in range(8):
                S.op("dve", lambda e, k=k: e.scalar_tensor_tensor(out=xs[:, 0:n], in0=xb[:, k, 0:n], scalar=A2[:, k, v:v + 1],
                                                                  in1=rstd[:, 0:n], op0=ALU.mult, op1=ALU.mult),
                     reads=[xb, A2, rstd], writes=[xs])
                S.op("act", lambda e, k=k: e.activation(out=h2[:, k, 0:n], in_=xs[:, 0:n], func=AF.Identity,
                                                        bias=modT[:, 24 + k, v:v + 1]), reads=[xs, modT], writes=[h2])
            S.dma("act", h2s, T["h2T"][:][:, :, t0:t0 + n], h2[:, :, 0:n], reads=[h2], writes=[T["h2T"]])


def phase_C2(p, L, Xout, final_out=None):
    S, T, C = p.S, p.T, p.C
    eps_t, ones_bf = p.eps_t, p.ones_bf
    with S.scope():
        ld = S.dma_sem("ld")
        modT = S.sbuf("modT", [128, 48, 2], F32)
        S.dma("sp", ld, modT[:], T["modT"][:], reads=[T["modT"]], writes=[modT])
        cw = S.sbuf("cw", [128, 44, 3], F32)
        cb = S.sbuf("cb", [128, 44], F32)
        S.dma("sp", ld, cw[:], L["conv_w"], writes=[cw])
        S.dma("sp", ld, cb[:], L["conv_b"], writes=[cb])
        fn = S.sbuf("fn", [128, 8], F32)
        S.dma("sp", ld, fn[:], C["final_norm"], writes=[fn])
        hsem = S.dma_sem("h2e")
        wdsem = S.dma_sem("wd")
        with S.scope():
            PS = Rot(S, "ps", 6, [128, 512], F32, psum=True, dma=False)
            PSH = Rot(S, "psh", 2, [128, 4], F32, psum=True, dma=False)
            h2e = S.sbuf("h2e", [128, 8, 2306], BF16)
            WU = Rot(S, "wu", 2, [128, 8, 256], BF16)
            UB = Rot(S, "ub", 4, [128, 514], F32, dma=False)
            ACC = Rot(S, "acc", 4, [128, 512], F32, dma=False)
            AO = Rot(S, "ao", 3, [128, 512], BF16)
            for seg in SEGS:
                base = seg[0][0]
                tot = sum(n for _, n in seg)
                seqs = []
                if base == 0:
                    seqs = [(0, CTX), (CTX, tot)]
                else:
                    seqs = [(base, base + tot)]
                S.op("dve", lambda e: e.memset(h2e[:, :, 0:1], 0.0), writes=[h2e])
                S.op("dve", lambda e: e.memset(h2e[:, :, tot + 1:tot + 2], 0.0), writes=[h2e])
                lo_tok = base - 1 if base > CTX else base
                hi_tok = base + tot + 1 if base + tot < NTOT else base + tot
                S.dma("sp", hsem, h2e[:, :, lo_tok - base + 1:hi_tok - base + 1], T["h2T"][:][:, :, lo_tok:hi_tok],
                      reads=[T["h2T"]], writes=[h2e])
                for c in range(22):
                    wu, wus = WU.next()
                    S.dma("pool", wus, wu[:, :, 0:128], L["w_up"][:, c * 128:(c + 1) * 128].rearrange("(k p) n -> p k n", p=128),
                          writes=[wu])
                    S.dma("pool", wus, wu[:, :, 128:256],
                          L["w_up"][:, FFN + c * 128:FFN + (c + 1) * 128].rearrange("(k p) n -> p k n", p=128), writes=[wu])
                    for (t0, n) in seg:
                        lo = t0 - base + 1
                        accs = []
                        for half in range(2):
                            ci = c + 22 * half
                            ps, _ = PS.next()
                            for k in range(8):
                                S.mm(lambda e, k=k, half=half: e.matmul(ps[:, 0:n], lhsT=wu[:, k, half * 128:(half + 1) * 128],
                                                                        rhs=h2e[:, k, lo:lo + n], start=(k == 0), stop=(k == 7)),
                                     reads=[wu, h2e], writes=[ps], last=(k == 7))
                            psh, _ = PSH.next()
                            lcol = lo - 1
                            rcol = lo + n
                            lz = (t0 == 0) or (t0 == CTX)
                            rz = (t0 + n == CTX) or (t0 + n == NTOT)
                            for k in range(8):
                                S.mm(lambda e, k=k, half=half: e.matmul(psh[:, 0:1], lhsT=wu[:, k, half * 128:(half + 1) * 128],
                                                                        rhs=h2e[:, k, lcol:lcol + 1], start=(k == 0), stop=(k == 7)),
                                     reads=[wu, h2e], writes=[psh], last=False)
                            for k in range(8):
                                S.mm(lambda e, k=k, half=half: e.matmul(psh[:, 1:2], lhsT=wu[:, k, half * 128:(half + 1) * 128],
                                                                        rhs=h2e[:, k, rcol:rcol + 1], start=(k == 0), stop=(k == 7)),
                                     reads=[wu, h2e], writes=[psh], last=(k == 7))
                            ub, _ = UB.next()
                            evac(S, "act", ub[:, 1:n + 1], ps[:, 0:n], [ps], [ub])
                            if lz:
                                S.op("dve", lambda e: e.memset(ub[:, 0:1], 0.0), writes=[ub])
                            else:
                                S.op("dve", lambda e: e.tensor_copy(out=ub[:, 0:1], in_=psh[:, 0:1]), reads=[psh], writes=[ub])
                            if rz:
                                S.op("dve", lambda e: e.memset(ub[:, n + 1:n + 2], 0.0), writes=[ub])
                            else:
                                S.op("dve", lambda e: e.tensor_copy(out=ub[:, n + 1:n + 2], in_=psh[:, 1:2]), reads=[psh], writes=[ub])
                            acc, _ = ACC.next()
                            eng = "dve" if half == 0 else "pool"
                            S.op(eng, lambda e, ci=ci: e.tensor_scalar(out=acc[:, 0:n], in0=ub[:, 0:n], scalar1=cw[:, ci, 0:1],
                                                                       scalar2=cb[:, ci:ci + 1], op0=ALU.mult, op1=ALU.add),
                                 reads=[ub, cw, cb], writes=[acc])
                            S.op("dve", lambda e, ci=ci: e.scalar_tensor_tensor(out=acc[:, 0:n], in0=ub[:, 1:n + 1], scalar=cw[:, ci, 1:2],
                                                                                in1=acc[:, 0:n], op0=ALU.mult, op1=ALU.add),
                                 reads=[ub, cw, acc], writes=[acc])
                            S.op("dve", lambda e, ci=ci: e.scalar_tensor_tensor(out=acc[:, 0:n], in0=ub[:, 2:n + 2], scalar=cw[:, ci, 2:3],
                                                                                in1=acc[:, 0:n], op0=ALU.mult, op1=ALU.add),
                                 reads=[ub, cw, acc], writes=[acc])
                            accs.append(acc)
                        S.op("act", lambda e: e.activation(out=accs[0][:, 0:n], in_=accs[0][:, 0:n], func=AF.Silu),
                             reads=[accs[0]], writes=[accs[0]])
                        ao, aos = AO.next()
                        S.op("pool", lambda e: e.tensor_tensor(out=ao[:, 0:n], in0=accs[0][:, 0:n], in1=accs[1][:, 0:n], op=ALU.mult),
                             reads=[accs[0], accs[1]], writes=[ao])
                        S.dma("act", aos, T["aT"][:][:, c, t0:t0 + n], ao[:, 0:n], reads=[ao], writes=[T["aT"]])
        with S.scope():
            PS = Rot(S, "ps", 7, [128, 512], F32, psum=True, dma=False)
            wd = S.sbuf("wd", [128, 22, D], BF16)
            for k0 in range(0, 22, 2):
                S.dma("pool", wdsem, wd[:, k0:k0 + 2, :], L["w_down"].rearrange("(k p) n -> p k n", p=128)[:, k0:k0 + 2, :], writes=[wd])
            AT = Rot(S, "at", 2, [128, 22, 512], BF16)
            XT = Rot(S, "xt", 2, [128, 8, 512], F32)
            sq = S.sbuf("sq", [128, 8, 512], BF16)
            rstd = S.sbuf("rstd", [128, 512], F32)
            for (t0, n) in TT:
                v = 1 if t0 == 0 else 0
                at, ats = AT.next()
                for i in range(0, 22, 6):
                    i2 = min(22, i + 6)
                    S.dma("sp", ats, at[:, i:i2, 0:n], T["aT"][:][:, i:i2, t0:t0 + n], reads=[T["aT"]], writes=[at])
                xb, xsem = XT.next()
                S.dma("sp", xsem, xb[:, :, 0:n], T["x1T"][:][:, t0:t0 + n].rearrange("(k p) n -> p k n", p=128),
                      reads=[T["x1T"]], writes=[xb])
                for m in range(8):
                    ps, _ = PS.next()
                    for k in range(22):
                        S.mm(lambda e, k=k, m=m: e.matmul(ps[:, 0:n], lhsT=wd[:, k, m * 128:(m + 1) * 128], rhs=at[:, k, 0:n],
                                                          start=(k == 0), stop=(k == 21)), reads=[wd, at], writes=[ps], last=(k == 21))
                    S.op("dve", lambda e, m=m: e.scalar_tensor_tensor(out=xb[:, m, 0:n], in0=ps[:, 0:n], scalar=modT[:, 40 + m, v:v + 1],
                                                                      in1=xb[:, m, 0:n], op0=ALU.mult, op1=ALU.add),
                         reads=[ps, modT, xb], writes=[xb])
                if final_out is None:
                    S.dma("act", xsem, Xout[:][:, t0:t0 + n].rearrange("(k p) n -> p k n", p=128), xb[:, :, 0:n],
                          reads=[xb], writes=[Xout])
                elif t0 >= CTX:
                    S.op("act", lambda e: e.activation(out=sq[:, :, 0:n], in_=xb[:, :, 0:n], func=AF.Square), reads=[xb], writes=[sq])
                    ps, _ = PS.next()
                    for k in range(8):
                        S.mm(lambda e, k=k: e.matmul(ps[:, 0:n], lhsT=ones_bf[:], rhs=sq[:, k, 0:n], start=(k == 0), stop=(k == 7)),
                             reads=[ones_bf, sq], writes=[ps], last=(k == 7))
                    rstd_from_ps(S, ps, n, D, eps_t, rstd)
                    for k in range(8):
                        S.op("dve", lambda e, k=k: e.scalar_tensor_tensor(out=xb[:, k, 0:n], in0=xb[:, k, 0:n], scalar=fn[:, k:k + 1],
                                                                          in1=rstd[:, 0:n], op0=ALU.mult, op1=ALU.mult),
                             reads=[xb, fn, rstd], writes=[xb])
                    S.dma("act", xsem, final_out[:][:, t0 - CTX:t0 - CTX + n].rearrange("(k p) n -> p k n", p=128), xb[:, :, 0:n],
                          reads=[xb], writes=[final_out])


PHASES_ALL = ("A", "B1", "B2", "B3", "C1", "C2")
PHASE_W = {"A": ("w_mod", "b_mod", "norm_mix", "w_in", "bqn", "bkvn", "w_uq", "w_ukv", "w_gate", "b_gate"),
           "B1": ("a_sink",), "B2": (), "B3": ("c_hn",), "C1": ("norm_ffn", "w_br_a", "w_br_b", "w_br_c", "w_out"),
           "C2": ("conv_w", "conv_b", "w_up", "w_down")}


def needed_weights(phases):
    need = set()
    for ph in phases:
        need.update(PHASE_W[ph])
    return [w for w in LAYER_W if w[0] in need]


def build_program(nlayers, final, phases=PHASES_ALL, ext_scratch=(), stop=None):
    nc = bass.Bass("TRN2", target_bir_lowering=False)
    xin_ap = dram_in(nc, "xT", (D, NTOT), F32)
    Cap = {n: dram_in(nc, n, s, d) for n, s, d in CONSTS}
    Lw = [{n: dram_in(nc, "%s_%d" % (n, l), s, d) for n, s, d in needed_weights(phases)} for l in range(nlayers)]
    if final:
        out_ap = dram_out(nc, "outT", (D, SEQ), F32)
    else:
        out_ap = dram_out(nc, "xT_out", (D, NTOT), F32)
    with contextlib.ExitStack() as st:
        S = Sched(nc, st)
        p = P()
        p.nc, p.S, p.C = nc, S, Cap
        p.stop = stop
        p.T = {}
        for n, s, d in SCRATCH:
            if n in ext_scratch:
                kind = ext_scratch[n]
                ap = dram_in(nc, n, s, d) if kind == "in" else dram_out(nc, n, s, d)
                p.T[n] = Buf(ap, n)
            else:
                p.T[n] = S.dram(n, s, d)
        Xext = Buf(xin_ap, "xT")
        Oext = Buf(out_ap, "out")
        ld = S.dma_sem("ldc")
        p.eps_t = S.sbuf("eps", [128, 1], F32)
        p.one_t = S.sbuf("one", [128, 1], F32)
        p.ones_bf = S.sbuf("ones_bf", [128, 128], BF16)
        S.op("dve", lambda e: e.memset(p.eps_t[:], EPS), writes=[p.eps_t])
        S.op("dve", lambda e: e.memset(p.one_t[:], 1.0), writes=[p.one_t])
        S.op("dve", lambda e: e.memset(p.ones_bf[:], 1.0), writes=[p.ones_bf])
        for nm, shape in (("permA", [128, 128]), ("permB", [96, 96]), ("permK", [32, 32]), ("sel", [32, 96]),
                          ("maskLo", [128, 512]), ("maskHi", [128, 512]), ("triF", [64, 512]), ("triB", [64, 512]),
                          ("ident", [64, 64]), ("sv", [1, 128])):
            b_ = S.sbuf(nm, shape, BF16)
            S.dma("sp", ld, b_[:], Cap[nm], writes=[b_])
            setattr(p, nm, b_)
        p.cv = S.sbuf("cv", [128, 8, 2], F32)
        S.dma("sp", ld, p.cv[:], Cap["cvec"], writes=[p.cv])
        for l in range(nlayers):
            Xin = Xext if l == 0 else p.T["XA" if l % 2 == 1 else "XB"]
            last = (l == nlayers - 1)
            Xout = Oext if (last and not final) else p.T["XA" if (l + 1) % 2 == 1 else "XB"]
            if "A" in phases:
                phase_A(p, Lw[l], Xin)
            if "B1" in phases:
                phase_B1(p, Lw[l])
            if "B2" in phases:
                phase_B2(p, Lw[l])
            if "B3" in phases:
                phase_B3(p, Lw[l])
            if "C1" in phases:
                phase_C1(p, Lw[l], Xin)
            if "C2" in phases:
                phase_C2(p, Lw[l], Xout, final_out=(Oext if (last and final) else None))
        S.finish("sp")
        p.ninst = S.ninst
    nc._ninst = p.ninst
    return nc


_PROGS = {}
_CONSTS = {}


def _pk(v):
    v = np.asarray(v)
    k = v.shape[0] // 128
    out = v.reshape((k, 128) + v.shape[1:])
    return np.ascontiguousarray(np.moveaxis(out, 0, 1))


def host_consts():
    if "c" in _CONSTS:
        return _CONSTS["c"]
    pos = np.concatenate([np.zeros(CTX, np.int64), np.arange(SEQ)])
    is_ctx = np.concatenate([np.ones(CTX, bool), np.zeros(SEQ, bool)])
    c = dict(_rope_tables(pos, is_ctx))
    c.update(_perm_consts())
    j = np.arange(128)[:, None]
    i = (np.arange(512) % 128)[None, :]
    c["maskLo"] = (j >= i).astype(np.float32).astype(NPBF)
    c["maskHi"] = (j <= i).astype(np.float32).astype(NPBF)
    j = np.arange(64)[:, None]
    i = (np.arange(512) % 64)[None, :]
    c["triF"] = (j <= i).astype(np.float32).astype(NPBF)
    c["triB"] = (j >= i).astype(np.float32).astype(NPBF)
    rm = np.ones((64, NTOT), np.float32)
    rm[:, ::64] = 0.0
    c["rmask"] = rm
    c["ident"] = np.eye(64, dtype=np.float32).astype(NPBF)
    sv = np.zeros((1, 128), np.float32)
    sv[0, 64:] = 1.0
    c["sv"] = sv.astype(NPBF)
    _CONSTS["c"] = c
    return c


def layer_inputs(W, l, names):
    m = {}
    g = {
        "w_mod": lambda: W["w_mod"][l], "b_mod": lambda: _pk(W["b_mod"][l]), "norm_mix": lambda: _pk(W["norm_mix"][l]),
        "norm_ffn": lambda: _pk(W["norm_ffn"][l]), "w_in": lambda: W["w_in"][l],
        "a_sink": lambda: np.ascontiguousarray(W["a_sink"][l].reshape(1, 8)),
        "bqn": lambda: _pk(W["b_q_norm"][l]), "bkvn": lambda: _pk(W["b_kv_norm"][l]),
        "w_uq": lambda: W["b_w_uq"][l], "w_ukv": lambda: W["b_w_ukv"][l], "w_gate": lambda: W["c_w_gate"][l],
        "b_gate": lambda: np.ascontiguousarray(W["c_b_gate"][l].reshape(2, 2, 128).transpose(2, 0, 1)),
        "c_hn": lambda: np.ascontiguousarray(W["c_head_norm"][l].reshape(4, 128).T),
        "w_br_a": lambda: W["w_br_a"][l], "w_br_b": lambda: W["w_br_b"][l], "w_br_c": lambda: W["w_br_c"][l],
        "w_out": lambda: W["w_out"][l], "w_up": lambda: W["w_up"][l],
        "conv_w": lambda: np.ascontiguousarray(W["conv_w"][l].reshape(3, 44, 128).transpose(2, 1, 0)),
        "conv_b": lambda: _pk(W["conv_b"][l]), "w_down": lambda: W["w_down"][l],
    }
    for n in names:
        m[n] = np.ascontiguousarray(g[n]())
    return m


def core_inputs(W, b, layers, phases=PHASES_ALL):
    c = dict(host_consts())
    c["cvec"] = _pk(np.stack([W["c"][b], W["c_ctx"]], axis=1))
    c["final_norm"] = _pk(W["final_norm"])
    m = {n: c[n] for n, _, _ in CONSTS}
    names = [w[0] for w in needed_weights(phases)]
    for i, l in enumerate(layers):
        for n, v in layer_inputs(W, l, names).items():
            m["%s_%d" % (n, i)] = v
    return m


def kernel(**W):
    W = {k: np.asarray(v) for k, v in W.items()}
    if "fused" not in _PROGS:
        _PROGS["fused"] = build_program(DEPTH, True)
    maps = []
    for b in range(2):
        m = core_inputs(W, b, list(range(DEPTH)))
        m["xT"] = np.ascontiguousarray(np.concatenate([W["ctx"][b], W["x"][b]], axis=0).T)
        maps.append(m)
    res = run_bass_kernel_spmd(_PROGS["fused"], maps, core_ids=[0, 1]).results
    out = np.stack([np.ascontiguousarray(np.asarray(res[b]["outT"]).T) for b in range(2)], axis=0)
    return out.astype(np.float32)
```

```python
import contextlib
import numpy as np
import ml_dtypes
import concourse.bass as bass
import concourse.mybir as mybir
from concourse.bass_utils import run_bass_kernel_spmd

F32 = mybir.dt.float32
BF16 = mybir.dt.bfloat16
AF = mybir.ActivationFunctionType
ALU = mybir.AluOpType
NPBF = ml_dtypes.bfloat16

NCORES = 8
D = 1024
SEQ = 8192
CTX = 256
DEPTH = 4
NTOT = CTX + SEQ
NKB = NTOT // 128
EPS = 1e-6
IN_DIM = 5824
O_AQ, O_AK, O_AV, O_BQ, O_BKV, O_BKR, O_CQ, O_CK, O_CV, O_CR, O_CG, O_GATE = (
    0, 512, 640, 768, 1024, 1152, 1184, 1440, 1696, 2208, 2720, 2752)
FFN = 2816
TT = [(0, CTX)] + [(CTX + 512 * i, 512) for i in range(16)]
SEGS = [TT[0:5], TT[5:9], TT[9:13], TT[13:17]]

SAME_ENGINE_SYNC = True
_STOP = None


class _Stop(Exception):
    pass


def chk(name):
    if _STOP == name:
        raise _Stop()


class Buf:
    __slots__ = ("t", "name", "lw", "rd")

    def __init__(self, t, name=""):
        self.t = t
        self.name = name
        self.lw = {}
        self.rd = {}

    def __getitem__(self, idx):
        return self.t[idx]


class Sched:
    def __init__(self, nc, stack):
        self.nc = nc
        self.stack = stack
        self.root = stack
        self.engs = {"pe": nc.tensor, "act": nc.scalar, "dve": nc.vector,
                     "pool": nc.gpsimd, "sp": nc.sync}
        self.sems = {}
        self.cnt = {}
        self.seen = {e: {} for e in self.engs}
        for e in ("pe", "act", "dve", "pool"):
            self.sems[e] = stack.enter_context(nc.semaphore("s_" + e))
            self.cnt[e] = 0
        self.ninst = 0
        self._uid = 0
        self.sem_bufs = {}
        self._free = []
        self._scoped = [[]]

    def uid(self, p):
        self._uid += 1
        return "%s_%d" % (p, self._uid)

    @contextlib.contextmanager
    def scope(self):
        old = self.stack
        self._scoped.append([])
        with contextlib.ExitStack() as st:
            self.stack = st
            try:
                yield
            finally:
                self.barrier()
                self.stack = old
                self._free.extend(self._scoped.pop())

    def sbuf(self, name, shape, dt):
        return Buf(self.stack.enter_context(self.nc.sbuf_tensor(self.uid(name), shape, dt)), name)

    def psum(self, name, shape, dt):
        return Buf(self.stack.enter_context(self.nc.psum_tensor(self.uid(name), shape, dt)), name)

    def dram(self, name, shape, dt):
        return Buf(self.nc.dram_tensor(self.uid(name), list(shape), dt).ap(), name)

    def dma_sem(self, name):
        if self._free:
            key = self._free.pop()
            for sfx in ("~hw", "~sw"):
                if key + sfx in self.sem_bufs:
                    self.sem_bufs[key + sfx] = []
        else:
            key = self.uid("d")
        self._scoped[-1].append(key)
        return key

    def _sem(self, key):
        if key not in self.sems:
            self.sems[key] = self.root.enter_context(self.nc.semaphore(key.replace("~", "_")))
            self.cnt[key] = 0
            self.sem_bufs[key] = []
        return self.sems[key]

    def _wait(self, eng, deps):
        e = self.engs[eng]
        for (k, c) in sorted(deps, key=lambda x: str(x[0])):
            if k == eng and not SAME_ENGINE_SYNC:
                continue
            if k == "pe" and eng == "pe":
                continue
            if self.seen[eng].get(k, 0) >= c:
                continue
            e.wait_ge(self.sems[k], c)
            self.seen[eng][k] = c

    @staticmethod
    def _deps(reads, writes):
        deps = set()
        for b in reads:
            deps.update(b.lw.items())
        for b in writes:
            deps.update(b.lw.items())
            deps.update(b.rd.items())
        return deps

    def op(self, eng, fn, reads=(), writes=()):
        self._wait(eng, self._deps(reads, writes))
        ins = fn(self.engs[eng])
        self.cnt[eng] += 1
        ins.then_inc(self.sems[eng], 1)
        for b in reads:
            b.rd[eng] = self.cnt[eng]
        for b in writes:
            b.lw[eng] = self.cnt[eng]
            b.rd = {}
        self.ninst += 1
        return ins

    def mm(self, fn, reads=(), writes=(), last=True):
        self._wait("pe", self._deps(reads, writes))
        ins = fn(self.engs["pe"])
        self.ninst += 1
        if last:
            self.cnt["pe"] += 1
            ins.then_inc(self.sems["pe"], 1)
            for b in writes:
                b.lw["pe"] = self.cnt["pe"]
                b.rd = {}
        for b in reads:
            b.rd["pe"] = self.cnt["pe"] + (0 if last else 1)
        return ins

    def dma(self, q, semkey, out_ap, in_ap, reads=(), writes=(), **kw):
        semkey = semkey + ("~sw" if q == "pool" else "~hw")
        sem = self._sem(semkey)
        self._wait(q, self._deps(reads, writes))
        ins = self.engs[q].dma_start(out=out_ap, in_=in_ap, **kw)
        self.cnt[semkey] += 16
        c = self.cnt[semkey]
        ins.then_inc(sem, 16)
        for b in self.sem_bufs[semkey]:
            if semkey in b.lw:
                b.lw[semkey] = c
            if semkey in b.rd:
                b.rd[semkey] = c
        for b in reads:
            b.rd[semkey] = c
            if b not in self.sem_bufs[semkey]:
                self.sem_bufs[semkey].append(b)
        for b in writes:
            b.lw[semkey] = c
            b.rd = {}
            if b not in self.sem_bufs[semkey]:
                self.sem_bufs[semkey].append(b)
        return ins

    def barrier(self):
        snap = [(k, c) for k, c in self.cnt.items() if c > 0]
        for eng in ("pe", "act", "dve", "pool", "sp"):
            e = self.engs[eng]
            for (k, c) in snap:
                if self.seen[eng].get(k, 0) >= c:
                    continue
                e.wait_ge(self.sems[k], c)
                self.seen[eng][k] = c

    def finish(self, eng="sp"):
        for k, c in self.cnt.items():
            if c > 0:
                self._wait(eng, {(k, c)})

    def wait_all(self, eng, bufs):
        deps = set()
        for b in bufs:
            deps.update(b.lw.items())
            deps.update(b.rd.items())
        self._wait(eng, deps)


class Rot:
    def __init__(self, S, name, n, shape, dt, psum=False, dma=True):
        self.bufs = [(S.psum if psum else S.sbuf)(name + str(i), shape, dt) for i in range(n)]
        self.sems = [S.dma_sem(name + str(i)) for i in range(n)] if dma else [None] * n
        self.i = 0

    def next(self):
        b, s = self.bufs[self.i], self.sems[self.i]
        self.i = (self.i + 1) % len(self.bufs)
        return b, s


class Out:
    def __init__(self):
        self.bufs = []

    def add(self, b):
        if b not in self.bufs:
            self.bufs.append(b)


def dram_in(nc, name, shape, dt):
    return nc.dram_tensor(name, list(shape), dt, kind="ExternalInput").ap()


def dram_out(nc, name, shape, dt):
    return nc.dram_tensor(name, list(shape), dt, kind="ExternalOutput").ap()


def _rope_tables(pos, is_ctx):
    n = pos.shape[0]
    row = (pos // 64).astype(np.float32)
    col = (pos % 64).astype(np.float32)

    def tab(nfreq):
        inv = (np.float32(10000.0) ** (-np.arange(nfreq, dtype=np.float32) / np.float32(nfreq))).astype(np.float32)
        ang = np.concatenate([row[:, None] * inv, col[:, None] * inv], axis=-1).astype(np.float32)
        c, s = np.cos(ang).astype(np.float32), np.sin(ang).astype(np.float32)
        c[is_ctx] = 1.0
        s[is_ctx] = 0.0
        return c, s

    ca, sa = tab(16)
    cb, sb = tab(8)
    cosA = np.empty((128, n), np.float32)
    sinA = np.empty((128, n), np.float32)
    for p in range(128):
        d = p % 64
        i = d % 32
        cosA[p] = ca[:, i]
        sinA[p] = (-sa[:, i]) if d < 32 else sa[:, i]
    cosB = np.ones((96, n), np.float32)
    sinB = np.zeros((96, n), np.float32)
    cosK = np.empty((32, n), np.float32)
    sinK = np.empty((32, n), np.float32)
    for r in range(32):
        i = r % 16
        cosK[r] = cb[:, i]
        sinK[r] = (-sb[:, i]) if r < 16 else sb[:, i]
    cosB[64:96] = cosK
    sinB[64:96] = sinK
    return dict(cosA=cosA, sinA=sinA, cosB=cosB, sinB=sinB, cosK=cosK, sinK=sinK)


def _perm_consts():
    permA = np.zeros((128, 128), np.float32)
    for m in range(128):
        d = m % 64
        permA[m + 32 if d < 32 else m - 32, m] = 1.0
    permB = np.zeros((96, 96), np.float32)
    for m in range(64, 96):
        r = m - 64
        permB[m + 16 if r < 16 else m - 16, m] = 1.0
    permK = np.zeros((32, 32), np.float32)
    for m in range(32):
        permK[m + 16 if m < 16 else m - 16, m] = 1.0
    sel = np.zeros((32, 96), np.float32)
    for k in range(32):
        sel[k, 64 + k] = 1.0
    return dict(permA=permA.astype(NPBF), permB=permB.astype(NPBF), permK=permK.astype(NPBF),
                sel=sel.astype(NPBF))


LAYER_W = [
    ("w_mod", (D, 6 * D), F32), ("b_mod", (128, 48), F32), ("norm_mix", (128, 8), F32), ("norm_ffn", (128, 8), F32),
    ("w_in", (D, IN_DIM), F32), ("a_sink", (1, 8), F32), ("bqn", (128, 2), F32), ("bkvn", (128, 1), F32),
    ("w_uq", (256, 768), F32), ("w_ukv", (128, 1024), F32), ("w_gate", (2, 16, 256), F32), ("b_gate", (128, 2, 2), F32),
    ("c_hn", (128, 4), F32), ("w_br_a", (512, D), F32), ("w_br_b", (512, D), F32), ("w_br_c", (512, D), F32),
    ("w_out", (D, D), F32), ("w_up", (D, 2 * FFN), F32), ("conv_w", (128, 44, 3), F32), ("conv_b", (128, 44), F32),
    ("w_down", (FFN, D), F32),
]
CONSTS = [
    ("cvec", (128, 8, 2), F32), ("final_norm", (128, 8), F32),
    ("cosA", (128, NTOT), F32), ("sinA", (128, NTOT), F32), ("cosB", (96, NTOT), F32), ("sinB", (96, NTOT), F32),
    ("cosK", (32, NTOT), F32), ("sinK", (32, NTOT), F32),
    ("permA", (128, 128), BF16), ("permB", (96, 96), BF16), ("permK", (32, 32), BF16), ("sel", (32, 96), BF16),
    ("maskLo", (128, 512), BF16), ("maskHi", (128, 512), BF16), ("triF", (64, 512), BF16), ("triB", (64, 512), BF16),
    ("rmask", (64, NTOT), F32), ("ident", (64, 64), BF16), ("sv", (1, 128), BF16),
]
SCRATCH = [
    ("modT", (128, 48, 2), F32),
    ("qaT", (128, 4, NTOT), BF16), ("kaT", (128, NTOT), BF16), ("vaA", (NTOT, 2, 128), BF16),
    ("qbT", (96, 8, NTOT), BF16), ("kbT", (96, 8, NTOT), BF16), ("vbA", (NTOT, 8, 128), BF16),
    ("cqT", (128, 2, NTOT), BF16), ("ckT", (128, 2, NTOT), BF16), ("cvv", (NTOT, 512), BF16),
    ("crT", (128, 4, NTOT), BF16), ("gfT", (128, 2, NTOT), F32), ("gbT", (128, 2, NTOT), F32),
    ("gatesT", (128, 24, NTOT), BF16),
    ("yaT", (128, 4, NTOT), BF16), ("ybT", (128, 4, NTOT), BF16), ("ycT", (128, 4, NTOT), BF16),
    ("x1T", (D, NTOT), F32), ("h2T", (128, 8, NTOT), BF16), ("aT", (128, 22, NTOT), BF16),
    ("XA", (D, NTOT), F32), ("XB", (D, NTOT), F32),
]


class P:
    pass


def evac(S, eng, out_ap, in_ap, reads, writes, func=None, scale=1.0):
    if eng == "act":
        return S.op("act", lambda e: e.activation(out=out_ap, in_=in_ap, func=func or AF.Copy, scale=scale),
                    reads=reads, writes=writes)
    return S.op(eng, lambda e: e.tensor_copy(out=out_ap, in_=in_ap), reads=reads, writes=writes)


def rstd_from_ps(S, ps, n, dim, eps_t, out):
    S.op("act", lambda e: e.activation(out=out[:, 0:n], in_=ps[:, 0:n], func=AF.Ln, scale=1.0 / dim, bias=eps_t[:]),
         reads=[ps, eps_t], writes=[out])
    S.op("act", lambda e: e.activation(out=out[:, 0:n], in_=out[:, 0:n], func=AF.Exp, scale=-0.5), reads=[out], writes=[out])


def phase_A(p, L, Xin):
    S, T, C = p.S, p.T, p.C
    nc = p.nc
    eps_t, one_t, ones_bf = p.eps_t, p.one_t, p.ones_bf
    with S.scope():
        ld = S.dma_sem("ldc")
        bmod = S.sbuf("bmod", [128, 48], F32)
        nmix = S.sbuf("nmix", [128, 8], F32)
        bqn = S.sbuf("bqn", [128, 2], F32)
        bkvn = S.sbuf("bkvn", [128, 1], F32)
        bg = S.sbuf("bg", [128, 2, 2], F32)
        for b_, n_ in ((bmod, "b_mod"), (nmix, "norm_mix"), (bqn, "bqn"), (bkvn, "bkvn"), (bg, "b_gate")):
            S.dma("sp", ld, b_[:], L[n_], writes=[b_])
        wuq = S.sbuf("wuq", [128, 2, 768], BF16)
        S.dma("pool", ld, wuq[:], L["w_uq"].rearrange("(k p) n -> p k n", p=128), writes=[wuq])
        wkn = S.sbuf("wkn", [128, 8, 96], BF16)
        S.op("dve", lambda e: e.memset(wkn[:], 0.0), writes=[wkn])
        S.dma("pool", ld, wkn[:, :, 0:64], L["w_ukv"].rearrange("p (h c) -> p h c", c=128)[:, :, 0:64], writes=[wkn])
        wvb = S.sbuf("wvb", [128, 8, 64], BF16)
        S.dma("pool", ld, wvb[:], L["w_ukv"].rearrange("p (h c) -> p h c", c=128)[:, :, 64:128], writes=[wvb])
        wg = S.sbuf("wg", [32, 2, 256], BF16)
        S.op("dve", lambda e: e.memset(wg[:], 0.0), writes=[wg])
        S.dma("pool", ld, wg[0:16, 0, :], L["w_gate"][0], writes=[wg])
        S.dma("pool", ld, wg[16:32, 1, :], L["w_gate"][1], writes=[wg])
        nbg = S.sbuf("nbg", [128, 2, 2], F32)
        S.op("dve", lambda e: e.tensor_scalar_mul(out=nbg[:], in0=bg[:], scalar1=-1.0), reads=[bg], writes=[nbg])

        PS = Rot(S, "ps", 7, [128, 512], F32, psum=True, dma=False)

        modT = S.sbuf("modT", [128, 48, 2], F32)
        sc = S.sbuf("silu_c", [128, 8, 2], F32)
        tmp8 = S.sbuf("tmp8", [128, 8, 2], F32)
        cv = p.cv
        S.op("act", lambda e: e.activation(out=tmp8[:], in_=cv[:], func=AF.Exp, scale=-1.0), reads=[cv], writes=[tmp8])
        S.op("dve", lambda e: e.tensor_scalar_add(out=tmp8[:], in0=tmp8[:], scalar1=1.0), reads=[tmp8], writes=[tmp8])
        S.op("dve", lambda e: e.reciprocal(out=tmp8[:], in_=tmp8[:]), reads=[tmp8], writes=[tmp8])
        S.op("dve", lambda e: e.tensor_tensor(out=sc[:], in0=tmp8[:], in1=cv[:], op=ALU.mult), reads=[tmp8, cv], writes=[sc])
        scb = S.sbuf("silu_cb", [128, 8, 2], BF16)
        S.op("dve", lambda e: e.tensor_copy(out=scb[:], in_=sc[:]), reads=[sc], writes=[scb])
        with S.scope():
            WM = Rot(S, "wm", 2, [128, 8, 512], BF16)
            for piece in range(12):
                wb, ws = WM.next()
                S.dma("pool", ws, wb[:], L["w_mod"][:, piece * 512:(piece + 1) * 512].rearrange("(k p) n -> p k n", p=128),
                      writes=[wb])
                ps, _ = PS.next()
                for j in range(4):
                    for k in range(8):
                        S.mm(lambda e, j=j, k=k: e.matmul(ps[:, j * 2:j * 2 + 2], lhsT=wb[:, k, j * 128:(j + 1) * 128],
                                                          rhs=scb[:, k, :], start=(k == 0), stop=(k == 7)),
                             reads=[wb, scb], writes=[ps], last=(j == 3 and k == 7))
                S.op("dve", lambda e, piece=piece: e.tensor_tensor(
                    out=modT[:, piece * 4:(piece + 1) * 4, :], in0=ps[:, 0:8].rearrange("p (j v) -> p j v", v=2),
                    in1=bmod[:, piece * 4:(piece + 1) * 4].unsqueeze(2).to_broadcast([128, 4, 2]), op=ALU.add),
                    reads=[ps, bmod], writes=[modT])
        msem = S.dma_sem("modst")
        if p.stop == "mod":
            S.dma("act", msem, T["modT"][:], modT[:], reads=[modT], writes=[T["modT"]])
            return
        S.dma("act", msem, T["modT"][:], modT[:], reads=[modT], writes=[T["modT"]])
        A1 = S.sbuf("A1", [128, 8, 2], F32)
        S.op("dve", lambda e: e.tensor_scalar_add(out=A1[:], in0=modT[:, 8:16, :], scalar1=1.0), reads=[modT], writes=[A1])
        S.op("dve", lambda e: e.tensor_tensor(out=A1[:], in0=A1[:], in1=nmix[:].unsqueeze(2).to_broadcast([128, 8, 2]),
                                              op=ALU.mult), reads=[A1, nmix], writes=[A1])

        hT = S.sbuf("hT", [128, 8, 2304], BF16)
        WP = Rot(S, "wp", 3, [128, 8, 512], BF16)
        STB = Rot(S, "stb", 4, [128, 512], BF16)
        STF = Rot(S, "stf", 3, [128, 512], F32)
        TAB = Rot(S, "tab", 4, [128, 512], F32)
        XT = Rot(S, "xt", 2, [128, 8, 512], F32)
        sq = S.sbuf("sq", [128, 8, 512], BF16)
        rstd = S.sbuf("rstd", [128, 512], F32)
        xs = S.sbuf("xs", [128, 512], F32)
        cq = S.sbuf("cq", [128, 2, 512], F32)
        sq2 = S.sbuf("sq2", [128, 2, 512], BF16)
        rs2 = S.sbuf("rs2", [128, 512], F32)
        cqn = S.sbuf("cqn", [128, 2, 512], BF16)
        ckvn = S.sbuf("ckvn", [128, 512], BF16)
        vst = S.sbuf("vst", [128, 8, 128], BF16)
        S.op("dve", lambda e: e.memset(vst[:], 1.0), writes=[vst])
        vsem = S.dma_sem("vst")
        cvst = S.sbuf("cvst", [128, 512], BF16)
        csem = S.dma_sem("cvst")
        cg = S.sbuf("cg", [32, 512], BF16)
        krb = S.sbuf("krb", [32, 512], BF16)

        def load_w(col0, ncols):
            wb, ws = WP.next()
            S.dma("pool", ws, wb[:, :, 0:ncols], L["w_in"][:, col0:col0 + ncols].rearrange("(k p) n -> p k n", p=128),
                  writes=[wb])
            return wb

        def store(dst, dst_ap, buf, src_ap, sem):
            S.dma("act", sem, dst_ap, src_ap, reads=[buf], writes=[dst])

        for seg in SEGS:
            loc = {}
            l0 = 0
            for (t0, n) in seg:
                loc[t0] = l0
                l0 += n

            for (t0, n) in seg:
                v = 1 if t0 == 0 else 0
                lo = loc[t0]
                xb, xsem = XT.next()
                S.dma("sp", xsem, xb[:, :, 0:n], Xin[:][:, t0:t0 + n].rearrange("(k p) n -> p k n", p=128),
                      reads=[Xin], writes=[xb])
                S.op("act", lambda e: e.activation(out=sq[:, :, 0:n], in_=xb[:, :, 0:n], func=AF.Square), reads=[xb], writes=[sq])
                ps, _ = PS.next()
                for k in range(8):
                    S.mm(lambda e, k=k: e.matmul(ps[:, 0:n], lhsT=ones_bf[:], rhs=sq[:, k, 0:n], start=(k == 0), stop=(k == 7)),
                         reads=[ones_bf, sq], writes=[ps], last=(k == 7))
                rstd_from_ps(S, ps, n, D, eps_t, rstd)
                for k in range(8):
                    S.op("dve", lambda e, k=k: e.scalar_tensor_tensor(out=xs[:, 0:n], in0=xb[:, k, 0:n], scalar=A1[:, k, v:v + 1],
                                                                      in1=rstd[:, 0:n], op0=ALU.mult, op1=ALU.mult),
                         reads=[xb, A1, rstd], writes=[xs])
                    S.op("act", lambda e, k=k: e.activation(out=hT[:, k, lo:lo + n], in_=xs[:, 0:n], func=AF.Identity,
                                                            bias=modT[:, k, v:v + 1]),
                         reads=[xs, modT], writes=[hT])

            def proj_fm(ps, wb, c0, m, t0, n):
                lo = loc[t0]
                for k in range(8):
                    S.mm(lambda e, k=k: e.matmul(ps[0:m, 0:n], lhsT=wb[:, k, c0:c0 + m], rhs=hT[:, k, lo:lo + n],
                                                 start=(k == 0), stop=(k == 7)),
                         reads=[wb, hT], writes=[ps], last=(k == 7))

            def rope(ps, m, t0, n, perm, cosn, sinn, dst, dst_ap, obuf=None):
                pre, _ = STB.next()
                evac(S, "act", pre[0:m, 0:n], ps[0:m, 0:n], [ps], [pre])
                ct, cs = TAB.next()
                S.dma("sp", cs, ct[0:m, 0:n], C[cosn][:, t0:t0 + n], writes=[ct])
                stt, ss = TAB.next()
                S.dma("sp", ss, stt[0:m, 0:n], C[sinn][:, t0:t0 + n], writes=[stt])
                ps2, _ = PS.next()
                S.mm(lambda e: e.matmul(ps2[0:m, 0:n], lhsT=perm[:], rhs=pre[0:m, 0:n], start=True, stop=True),
                     reads=[perm, pre], writes=[ps2])
                a, _ = STF.next()
                S.op("dve", lambda e: e.tensor_tensor(out=a[0:m, 0:n], in0=pre[0:m, 0:n], in1=ct[0:m, 0:n], op=ALU.mult),
                     reads=[pre, ct], writes=[a])
                b, _ = STF.next()
                S.op("dve", lambda e: e.tensor_tensor(out=b[0:m, 0:n], in0=ps2[0:m, 0:n], in1=stt[0:m, 0:n], op=ALU.mult),
                     reads=[ps2, stt], writes=[b])
                if obuf is not None:
                    o, osm = obuf, None
                else:
                    o, osm = STB.next()
                S.op("dve", lambda e: e.tensor_tensor(out=o[0:m, 0:n], in0=a[0:m, 0:n], in1=b[0:m, 0:n], op=ALU.add),
                     reads=[a, b], writes=[o])
                if dst is not None:
                    store(dst, dst_ap, o, o[0:m, 0:n], osm)
                return o

            def simple_group(col0, nchunks, dst, func=AF.Copy, scale=1.0):
                for c0 in range(0, nchunks, 4):
                    ncn = min(4, nchunks - c0)
                    wb = load_w(col0 + c0 * 128, ncn * 128)
                    for (t0, n) in seg:
                        for j in range(ncn):
                            ps, _ = PS.next()
                            proj_fm(ps, wb, j * 128, 128, t0, n)
                            o, osm = STB.next()
                            evac(S, "act", o[:, 0:n], ps[:, 0:n], [ps], [o], func=func, scale=scale)
                            store(dst, dst[:][:, c0 + j, t0:t0 + n], o, o[:, 0:n], osm)

            if p.stop == "hT":
                return
            wb, ws = WP.next()
            for half in range(2):
                for g in range(4):
                    c0 = O_AQ + (half * 4 + g) * 64
                    S.dma("pool", ws, wb[:, :, g * 128 + half * 64:g * 128 + (half + 1) * 64],
                          L["w_in"][:, c0:c0 + 64].rearrange("(k p) c -> p k c", p=128), writes=[wb])
            for (t0, n) in seg:
                for g in range(4):
                    ps, _ = PS.next()
                    proj_fm(ps, wb, g * 128, 128, t0, n)
                    rope(ps, 128, t0, n, p.permA, "cosA", "sinA", T["qaT"], T["qaT"][:][:, g, t0:t0 + n])
            if p.stop == "G1":
                return
            wb = load_w(O_AK, 256)
            for (t0, n) in seg:
                lo = loc[t0]
                ps, _ = PS.next()
                proj_fm(ps, wb, 0, 128, t0, n)
                rope(ps, 128, t0, n, p.permA, "cosA", "sinA", T["kaT"], T["kaT"][:][:, t0:t0 + n])
                for s0 in range(0, n, 128):
                    ps, _ = PS.next()
                    for k in range(8):
                        S.mm(lambda e, k=k: e.matmul(ps[:, 0:128], lhsT=hT[:, k, lo + s0:lo + s0 + 128], rhs=wb[:, k, 128:256],
                                                     start=(k == 0), stop=(k == 7)),
                             reads=[wb, hT], writes=[ps], last=(k == 7))
                    S.op("act", lambda e: e.activation(out=vst[:, 0:2, 0:64],
                                                       in_=ps[:, 0:128].rearrange("p (h c) -> p h c", c=64), func=AF.Copy),
                         reads=[ps], writes=[vst])
                    store(T["vaA"], T["vaA"][:][t0 + s0:t0 + s0 + 128], vst, vst[:, 0:2, :], vsem)
            if p.stop == "G3":
                return
            wb = load_w(O_BQ, 256)
            wb2 = load_w(O_BKV, 256)
            for (t0, n) in seg:
                for c in range(2):
                    ps, _ = PS.next()
                    proj_fm(ps, wb, c * 128, 128, t0, n)
                    S.op("dve", lambda e, c=c: e.tensor_copy(out=cq[:, c, 0:n], in_=ps[:, 0:n]), reads=[ps], writes=[cq])
                    S.op("act", lambda e, c=c: e.activation(out=sq2[:, c, 0:n], in_=cq[:, c, 0:n], func=AF.Square),
                         reads=[cq], writes=[sq2])
                ps, _ = PS.next()
                for c in range(2):
                    S.mm(lambda e, c=c: e.matmul(ps[:, 0:n], lhsT=ones_bf[:], rhs=sq2[:, c, 0:n], start=(c == 0), stop=(c == 1)),
                         reads=[ones_bf, sq2], writes=[ps], last=(c == 1))
                rstd_from_ps(S, ps, n, 256, eps_t, rs2)
                for c in range(2):
                    S.op("dve", lambda e, c=c: e.scalar_tensor_tensor(out=cqn[:, c, 0:n], in0=cq[:, c, 0:n], scalar=bqn[:, c:c + 1],
                                                                      in1=rs2[:, 0:n], op0=ALU.mult, op1=ALU.mult),
                         reads=[cq, bqn, rs2], writes=[cqn])
                for h in range(8):
                    ps, _ = PS.next()
                    for c in range(2):
                        S.mm(lambda e, c=c, h=h: e.matmul(ps[0:96, 0:n], lhsT=wuq[:, c, h * 96:(h + 1) * 96], rhs=cqn[:, c, 0:n],
                                                          start=(c == 0), stop=(c == 1)),
                             reads=[wuq, cqn], writes=[ps], last=(c == 1))
                    rope(ps, 96, t0, n, p.permB, "cosB", "sinB", T["qbT"], T["qbT"][:][:, h, t0:t0 + n])
                if p.stop == "G4":
                    return
                ps, _ = PS.next()
                if p.stop == "G5x":
                    return
                proj_fm(ps, wb2, 0, 128, t0, n)
                if p.stop == "G5a0":
                    return
                S.op("dve", lambda e: e.tensor_copy(out=cq[:, 0, 0:n], in_=ps[:, 0:n]), reads=[ps], writes=[cq])
                S.op("act", lambda e: e.activation(out=sq2[:, 0, 0:n], in_=cq[:, 0, 0:n], func=AF.Square), reads=[cq], writes=[sq2])
                if p.stop in ("G5a1", "G5nosq"):
                    return
                ps, _ = PS.next()
                S.mm(lambda e: e.matmul(ps[:, 0:n], lhsT=ones_bf[:], rhs=sq2[:, 0, 0:n], start=True, stop=True),
                     reads=[ones_bf, sq2], writes=[ps])
                if p.stop == "G5a2":
                    return
                rstd_from_ps(S, ps, n, 128, eps_t, rs2)
                if p.stop == "G5a3":
                    return
                S.op("dve", lambda e: e.scalar_tensor_tensor(out=ckvn[:, 0:n], in0=cq[:, 0, 0:n], scalar=bkvn[:, 0:1],
                                                             in1=rs2[:, 0:n], op0=ALU.mult, op1=ALU.mult),
                     reads=[cq, bkvn, rs2], writes=[ckvn])
                if p.stop == "G5a":
                    return
                ps, _ = PS.next()
                proj_fm(ps, wb2, 128, 128, t0, n)
                if p.stop == "G5":
                    return
                kr = rope(ps, 32, t0, n, p.permK, "cosK", "sinK", None, None, obuf=krb)
                if p.stop == "G5b":
                    return
                for h in range(8):
                    ps, _ = PS.next()
                    S.mm(lambda e, h=h: e.matmul(ps[0:96, 0:n], lhsT=wkn[:, h, :], rhs=ckvn[:, 0:n], start=True, stop=False),
                         reads=[wkn, ckvn], writes=[ps], last=False)
                    S.mm(lambda e: e.matmul(ps[0:96, 0:n], lhsT=p.sel[:], rhs=kr[0:32, 0:n], start=False, stop=True),
                         reads=[p.sel, kr], writes=[ps], last=True)
                    o, osm = STB.next()
                    evac(S, "act", o[0:96, 0:n], ps[0:96, 0:n], [ps], [o])
                    store(T["kbT"], T["kbT"][:][:, h, t0:t0 + n], o, o[0:96, 0:n], osm)
                if p.stop == "G5c":
                    return
                for s0 in range(0, n, 128):
                    ps, _ = PS.next()
                    S.mm(lambda e: e.matmul(ps[:, 0:512], lhsT=ckvn[:, s0:s0 + 128], rhs=wvb[:].rearrange("p h c -> p (h c)"),
                                            start=True, stop=True), reads=[ckvn, wvb], writes=[ps])
                    S.op("act", lambda e: e.activation(out=vst[:, :, 0:64],
                                                       in_=ps[:, 0:512].rearrange("p (h c) -> p h c", c=64), func=AF.Copy),
                         reads=[ps], writes=[vst])
                    store(T["vbA"], T["vbA"][:][t0 + s0:t0 + s0 + 128], vst, vst[:, :, :], vsem)
            if p.stop == "G6":
                return
            simple_group(O_CQ, 2, T["cqT"], scale=0.125)
            simple_group(O_CK, 2, T["ckT"])
            wb = load_w(O_CV, 512)
            for (t0, n) in seg:
                lo = loc[t0]
                for s0 in range(0, n, 128):
                    ps, _ = PS.next()
                    for k in range(8):
                        S.mm(lambda e, k=k: e.matmul(ps[:, 0:512], lhsT=hT[:, k, lo + s0:lo + s0 + 128], rhs=wb[:, k, 0:512],
                                                     start=(k == 0), stop=(k == 7)),
                             reads=[wb, hT], writes=[ps], last=(k == 7))
                    evac(S, "act", cvst[:, :], ps[:, 0:512], [ps], [cvst])
                    store(T["cvv"], T["cvv"][:][t0 + s0:t0 + s0 + 128, :], cvst, cvst[:, :], csem)
            wb = load_w(O_CG, 128)
            for (t0, n) in seg:
                ps, _ = PS.next()
                proj_fm(ps, wb, 0, 128, t0, n)
                evac(S, "act", cg[:, 0:n], ps[0:32, 0:n], [ps], [cg])
                for dr, dst in ((0, T["gfT"]), (1, T["gbT"])):
                    for pr in range(2):
                        ps, _ = PS.next()
                        S.mm(lambda e, dr=dr, pr=pr: e.matmul(ps[:, 0:n], lhsT=wg[:, dr, pr * 128:(pr + 1) * 128], rhs=cg[:, 0:n],
                                                              start=True, stop=True), reads=[wg, cg], writes=[ps])
                        o, osm = STF.next()
                        S.op("act", lambda e, dr=dr, pr=pr: e.activation(out=o[:, 0:n], in_=ps[:, 0:n], func=AF.Exp, scale=-1.0,
                                                                         bias=nbg[:, dr, pr:pr + 1]), reads=[ps, nbg], writes=[o])
                        S.op("act", lambda e: e.activation(out=o[:, 0:n], in_=o[:, 0:n], func=AF.Ln, bias=one_t[:]),
                             reads=[o, one_t], writes=[o])
                        S.op("dve", lambda e: e.tensor_scalar_mul(out=o[:, 0:n], in0=o[:, 0:n], scalar1=-1.0 / 16.0),
                             reads=[o], writes=[o])
                        store(dst, dst[:][:, pr, t0:t0 + n], o, o[:, 0:n], osm)
            if p.stop == "G11":
                return
            simple_group(O_CR, 4, T["crT"], func=AF.Silu)
            simple_group(O_GATE, 24, T["gatesT"], func=AF.Sigmoid)


def phase_B1(p, L):
    S, T, C = p.S, p.T, p.C
    with S.scope():
        ld = S.dma_sem("ld")
        ka = S.sbuf("ka", [128, NTOT], BF16)
        va = S.sbuf("va", [128, NKB, 2, 128], BF16)
        S.dma("sp", ld, ka[:], T["kaT"][:], reads=[T["kaT"]], writes=[ka])
        for i in range(0, NKB, 6):
            S.dma("sp", ld, va[:, i:i + 6], T["vaA"][:].rearrange("(b p) h c -> p b h c", p=128)[:, i:i + 6],
                  reads=[T["vaA"]], writes=[va])
        sk = S.sbuf("sk", [1, 8], F32)
        S.dma("sp", ld, sk[:], L["a_sink"], writes=[sk])
        zrow = S.sbuf("zrow", [1, 128], F32)
        S.op("dve", lambda e: e.memset(zrow[:], 0.0), writes=[zrow])
        esrow = S.sbuf("esrow", [1, 2, 512], BF16)
        for h in range(8):
            S.op("act", lambda e, h=h: e.activation(out=esrow[0:1, h // 4, (h % 4) * 128:(h % 4 + 1) * 128], in_=zrow[:],
                                                    func=AF.Exp, bias=sk[0:1, h:h + 1]), reads=[zrow, sk], writes=[esrow])
        PSS = Rot(S, "pss", 5, [128, 512], F32, psum=True, dma=False)
        PSO = Rot(S, "pso", 2, [128, 512], F32, psum=True, dma=False)
        QT = Rot(S, "qt", 2, [128, 4, 512], BF16)
        PT = Rot(S, "pt", 6, [128, 512], BF16, dma=False)
        RC = Rot(S, "rc", 2, [64, 512], F32, dma=False)
        OS = Rot(S, "os", 3, [64, 512], BF16)
        scale = 64 ** -0.5

        for (t0, n) in TT:
            qt, qsem = QT.next()
            S.dma("sp", qsem, qt[:, :, 0:n], T["qaT"][:][:, :, t0:t0 + n], reads=[T["qaT"]], writes=[qt])
            for qb in range(n // 128):
                q0 = t0 + qb * 128
                blk = q0 // 128
                if t0 == 0:
                    kbs = [(0, None), (1, None)]
                else:
                    kbs = [(0, None), (1, None)]
                    if blk - 1 >= 2:
                        kbs.append((blk - 1, "maskLo"))
                    kbs.append((blk, None))
                    if blk + 1 < NKB:
                        kbs.append((blk + 1, "maskHi"))
                for kvh in range(2):
                    pb = kvh * 64
                    pso, _ = PSO.next()
                    pend = []
                    for i, (kb, msk) in enumerate(kbs):
                        pss, _ = PSS.next()
                        S.mm(lambda e, kb=kb, pss=pss: e.matmul(pss[:, 0:512].rearrange("p (g q) -> p g q", g=4),
                                                                lhsT=ka[pb:pb + 64, kb * 128:(kb + 1) * 128],
                                                                rhs=qt[pb:pb + 64, :, qb * 128:(qb + 1) * 128], start=True, stop=True),
                             reads=[ka, qt], writes=[pss])
                        pt, _ = PT.next()
                        S.op("act", lambda e, pss=pss, pt=pt: e.activation(out=pt[:, :], in_=pss[:, :], func=AF.Exp, scale=scale),
                             reads=[pss], writes=[pt])
                        if msk is not None:
                            mk = p.maskLo if msk == "maskLo" else p.maskHi
                            S.op("dve", lambda e, mk=mk, pt=pt: e.tensor_tensor(out=pt[:, :], in0=pt[:, :], in1=mk[:, :], op=ALU.mult),
                                 reads=[pt, mk], writes=[pt])
                        pend.append((kb, pt))
                    for i, (kb, pt) in enumerate(pend):
                        S.mm(lambda e, kb=kb, i=i, pt=pt: e.matmul(pso[:, :], lhsT=va[:, kb, kvh, :], rhs=pt[:, :], start=(i == 0), stop=False),
                             reads=[va, pt], writes=[pso], last=False)
                    S.mm(lambda e: e.matmul(pso[:, :], lhsT=p.sv[:], rhs=esrow[0:1, kvh, :], start=False, stop=True),
                         reads=[p.sv, esrow], writes=[pso], last=True)
                    rc, _ = RC.next()
                    S.op("dve", lambda e: e.reciprocal(out=rc[:, :], in_=pso[64:128, :]), reads=[pso], writes=[rc])
                    o, osm = OS.next()
                    S.op("dve", lambda e: e.tensor_tensor(out=o[:, :], in0=pso[0:64, :], in1=rc[:, :], op=ALU.mult),
                         reads=[pso, rc], writes=[o])
                    for gp in range(2):
                        S.dma("act", osm, T["yaT"][:][gp * 64:(gp + 1) * 64, kvh * 2:kvh * 2 + 2, q0:q0 + 128],
                              o[:, :].rearrange("p (g2 gp q) -> p g2 gp q", g2=2, gp=2)[:, :, gp, :],
                              reads=[o], writes=[T["yaT"]])


def phase_B2(p, L):
    S, T, C = p.S, p.T, p.C
    with S.scope():
        KS = Rot(S, "ks", 2, [96, NTOT], BF16)
        VS = Rot(S, "vs", 2, [128, NKB, 128], BF16)
        PSS = Rot(S, "pss", 5, [128, 512], F32, psum=True, dma=False)
        PSO = Rot(S, "pso", 2, [128, 512], F32, psum=True, dma=False)
        QT = Rot(S, "qt", 2, [96, 512], BF16)
        PT = Rot(S, "pt", 6, [128, 512], BF16, dma=False)
        RC = Rot(S, "rc", 2, [64, 512], F32, dma=False)
        OS = Rot(S, "os", 3, [64, 512], BF16)
        scale = 96 ** -0.5
        for h in range(8):
            ks, ksem = KS.next()
            vs, vsem = VS.next()
            S.dma("sp", ksem, ks[:], T["kbT"][:][:, h, :], reads=[T["kbT"]], writes=[ks])
            for i in range(0, NKB, 6):
                S.dma("sp", vsem, vs[:, i:i + 6], T["vbA"][:].rearrange("(b p) h c -> p b h c", p=128)[:, i:i + 6, h, :],
                      reads=[T["vbA"]], writes=[vs])
            for (t0, n) in TT:
                qt, qsem = QT.next()
                S.dma("sp", qsem, qt[:, 0:n], T["qbT"][:][:, h, t0:t0 + n], reads=[T["qbT"]], writes=[qt])
                nkb = 2 if t0 == 0 else NKB
                pso, _ = PSO.next()
                LOOK = 3
                pend = []
                for kb in range(nkb + LOOK):
                    if kb < nkb:
                        pss, _ = PSS.next()
                        S.mm(lambda e, kb=kb, pss=pss: e.matmul(pss[:, 0:n], lhsT=ks[:, kb * 128:(kb + 1) * 128], rhs=qt[:, 0:n],
                                                                start=True, stop=True), reads=[ks, qt], writes=[pss])
                        pt, _ = PT.next()
                        S.op("act", lambda e, pss=pss, pt=pt: e.activation(out=pt[:, 0:n], in_=pss[:, 0:n], func=AF.Exp, scale=scale),
                             reads=[pss], writes=[pt])
                        pend.append(pt)
                    if kb >= LOOK:
                        k2 = kb - LOOK
                        pt2 = pend.pop(0)
                        S.mm(lambda e, k2=k2, pt2=pt2: e.matmul(pso[:, 0:n], lhsT=vs[:, k2, :], rhs=pt2[:, 0:n], start=(k2 == 0),
                                                                stop=(k2 == nkb - 1)), reads=[vs, pt2], writes=[pso], last=(k2 == nkb - 1))
                rc, _ = RC.next()
                S.op("dve", lambda e: e.reciprocal(out=rc[:, 0:n], in_=pso[64:128, 0:n]), reads=[pso], writes=[rc])
                o, osm = OS.next()
                S.op("dve", lambda e: e.tensor_tensor(out=o[:, 0:n], in0=pso[0:64, 0:n], in1=rc[:, 0:n], op=ALU.mult),
                     reads=[pso, rc], writes=[o])
                S.dma("act", osm, T["ybT"][:][(h % 2) * 64:(h % 2 + 1) * 64, h // 2, t0:t0 + n], o[:, 0:n],
                      reads=[o], writes=[T["ybT"]])


def phase_B3(p, L):
    S, T, C = p.S, p.T, p.C
    eps_t, ones_bf = p.eps_t, p.ones_bf
    NCH = NTOT // 64
    PIECE = 2112
    with S.scope():
        ld = S.dma_sem("ld")
        hn = S.sbuf("hn", [128, 4], F32)
        S.dma("sp", ld, hn[:], L["c_hn"], writes=[hn])
        vsb = S.sbuf("vsb", [64, NCH, 128], BF16)
        qtb = S.sbuf("qtb", [64, NTOT], BF16)
        ktT = S.sbuf("ktT", [64, NCH, 64], BF16)
        attT = S.sbuf("attT", [64, NCH, 64], BF16)
        ebl = S.sbuf("ebl", [64, NCH], F32)
        oT = S.sbuf("oT", [128, NTOT], F32)
        qp = S.sbuf("qp", [64, PIECE], BF16)
        kp = S.sbuf("kp", [64, PIECE], BF16)
        gp = S.sbuf("gp", [64, PIECE], F32)
        bp = S.sbuf("bp", [64, PIECE], F32)
        rm = S.sbuf("rm", [64, PIECE], F32)
        ep = S.sbuf("ep", [64, PIECE], BF16)
        ktp = S.sbuf("ktp", [64, PIECE], BF16)
        S.dma("sp", ld, rm[:], C["rmask"][:, 0:PIECE], writes=[rm])
        Sf = [S.sbuf("Sf%d" % i, [64, 128], F32) for i in range(2)]
        Sb = [S.sbuf("Sb%d" % i, [64, 128], BF16) for i in range(2)]
        S1 = S.sbuf("S1", [64, 128], F32)
        PST = Rot(S, "pst", 2, [64, 512], BF16, psum=True, dma=False)
        PSA = Rot(S, "psa", 2, [64, 512], F32, psum=True, dma=False)
        PSO = Rot(S, "pso", 2, [128, 512], F32, psum=True, dma=False)
        PSK = Rot(S, "psk", 2, [64, 128], F32, psum=True, dma=False)
        sqb = S.sbuf("sqb", [128, 512], BF16)
        rs = S.sbuf("rs", [128, 512], F32)
        tt = S.sbuf("tt", [128, 512], F32)
        CR = Rot(S, "cr", 2, [128, 512], BF16)
        YO = Rot(S, "yo", 2, [128, 512], BF16)

        for h in range(4):
            hb = (h % 2) * 64
            for i in range(0, NCH, 11):
                S.dma("sp", ld, vsb[:, i:i + 11, :],
                      T["cvv"][:].rearrange("(c j) (h e) -> j c h e", j=64, e=128)[:, i:i + 11, h, :],
                      reads=[T["cvv"]], writes=[vsb])
            for dr in range(2):
                gsrc = T["gfT"] if dr == 0 else T["gbT"]
                tri = p.triF if dr == 0 else p.triB
                for pc in range(NTOT // PIECE):
                    a0 = pc * PIECE
                    c0 = a0 // 64
                    S.dma("sp", ld, qp[:], T["cqT"][:][hb:hb + 64, h // 2, a0:a0 + PIECE], reads=[T["cqT"]], writes=[qp])
                    S.dma("sp", ld, kp[:], T["ckT"][:][hb:hb + 64, h // 2, a0:a0 + PIECE], reads=[T["ckT"]], writes=[kp])
                    S.dma("sp", ld, gp[:], gsrc[:][hb:hb + 64, h // 2, a0:a0 + PIECE], reads=[gsrc], writes=[gp])
                    S.op("dve", lambda e: e.tensor_tensor_scan(out=bp[:], data0=rm[:], data1=gp[:], initial=0.0,
                                                               op0=ALU.mult, op1=ALU.add), reads=[rm, gp], writes=[bp])
                    if dr == 1:
                        b3 = bp[:].rearrange("p (c j) -> p c j", j=64)
                        S.op("dve", lambda e: e.tensor_tensor(out=gp[:], in0=gp[:], in1=bp[:], op=ALU.subtract),
                             reads=[gp, bp], writes=[gp])
                        S.op("dve", lambda e: e.tensor_tensor(out=bp[:].rearrange("p (c j) -> p c j", j=64),
                                                              in0=gp[:].rearrange("p (c j) -> p c j", j=64),
                                                              in1=b3[:, :, 63:64].to_broadcast([64, PIECE // 64, 64]), op=ALU.add),
                             reads=[gp, bp], writes=[bp])
                    last_col = 63 if dr == 0 else 0
                    S.op("act", lambda e: e.activation(out=ebl[:, c0:c0 + PIECE // 64],
                                                       in_=bp[:].rearrange("p (c j) -> p c j", j=64)[:, :, last_col],
                                                       func=AF.Exp), reads=[bp], writes=[ebl])
                    S.op("act", lambda e: e.activation(out=ep[:], in_=bp[:], func=AF.Exp), reads=[bp], writes=[ep])
                    S.op("dve", lambda e: e.tensor_tensor(out=qtb[:, a0:a0 + PIECE], in0=qp[:], in1=ep[:], op=ALU.mult),
                         reads=[qp, ep], writes=[qtb])
                    S.op("act", lambda e: e.activation(out=ep[:], in_=bp[:], func=AF.Exp, scale=-1.0), reads=[bp], writes=[ep])
                    S.op("dve", lambda e: e.tensor_tensor(out=ktp[:], in0=kp[:], in1=ep[:], op=ALU.mult),
                         reads=[kp, ep], writes=[ktp])
                    for c8 in range(0, PIECE // 64, 8):
                        nn = min(8, PIECE // 64 - c8)
                        pst, _ = PST.next()
                        psa, _ = PSA.next()
                        for j in range(nn):
                            cl = c8 + j
                            S.mm(lambda e, cl=cl, j=j: e.transpose(out=pst[:, j * 64:(j + 1) * 64], in_=ktp[:, cl * 64:(cl + 1) * 64],
                                                                   identity=p.ident[:]),
                                 reads=[ktp, p.ident], writes=[pst], last=(j == nn - 1))
                        for j in range(nn):
                            cl = c8 + j
                            S.mm(lambda e, cl=cl, j=j: e.matmul(psa[:, j * 64:(j + 1) * 64], lhsT=ktp[:, cl * 64:(cl + 1) * 64],
                                                                rhs=qtb[:, a0 + cl * 64:a0 + (cl + 1) * 64], start=True, stop=True),
                                 reads=[ktp, qtb], writes=[psa], last=(j == nn - 1))
                        S.op("act", lambda e: e.activation(out=ktT[:, c0 + c8:c0 + c8 + nn, :].rearrange("p c d -> p (c d)"),
                                                           in_=pst[:, 0:nn * 64], func=AF.Copy), reads=[pst], writes=[ktT])
                        S.op("dve", lambda e: e.tensor_tensor(out=attT[:, c0 + c8:c0 + c8 + nn, :].rearrange("p c d -> p (c d)"),
                                                              in0=psa[:, 0:nn * 64], in1=tri[:, 0:nn * 64], op=ALU.mult),
                             reads=[psa, tri], writes=[attT])
                order = list(range(NCH)) if dr == 0 else ([3, 2, 1, 0] + list(range(NCH - 1, 3, -1)))
                cur = 0
                S.op("dve", lambda e: e.memset(Sf[0][:], 0.0), writes=[Sf[0]])
                S.op("dve", lambda e: e.memset(Sb[0][:], 0.0), writes=[Sb[0]])
                groups = [order[0:4]] + [order[i:i + 8] for i in range(4, NCH, 8)]
                for grp in groups:
                    ng = len(grp)
                    lo_c = min(grp)
                    pso, _ = PSO.next()
                    for j, c in enumerate(grp):
                        col = (c - lo_c) * 64
                        S.mm(lambda e, c=c, col=col: e.matmul(pso[:, col:col + 64], lhsT=vsb[:, c, :], rhs=attT[:, c, :],
                                                              start=True, stop=False),
                             reads=[vsb, attT], writes=[pso], last=False)
                        S.mm(lambda e, c=c, col=col, cur=cur: e.matmul(pso[:, col:col + 64], lhsT=Sb[cur][:],
                                                                       rhs=qtb[:, c * 64:(c + 1) * 64], start=False, stop=True),
                             reads=[Sb[cur], qtb], writes=[pso], last=(j == ng - 1))
                        psk, _ = PSK.next()
                        S.mm(lambda e, c=c: e.matmul(psk[:, :], lhsT=ktT[:, c, :], rhs=vsb[:, c, :], start=True, stop=True),
                             reads=[ktT, vsb], writes=[psk])
                        nxt = 1 - cur
                        S.op("dve", lambda e, cur=cur: e.tensor_tensor(out=S1[:], in0=psk[:, :], in1=Sf[cur][:], op=ALU.add),
                             reads=[psk, Sf[cur]], writes=[S1])
                        S.op("dve", lambda e, c=c, nxt=nxt: e.tensor_scalar_mul(out=Sf[nxt][:], in0=S1[:], scalar1=ebl[:, c:c + 1]),
                             reads=[S1, ebl], writes=[Sf[nxt]])
                        S.op("act", lambda e, c=c, nxt=nxt: e.activation(out=Sb[nxt][:], in_=S1[:], func=AF.Copy,
                                                                         scale=ebl[:, c:c + 1]),
                             reads=[S1, ebl], writes=[Sb[nxt]])
                        cur = nxt
                    w = ng * 64
                    if dr == 0:
                        evac(S, "act", oT[:, lo_c * 64:lo_c * 64 + w], pso[:, 0:w], [pso], [oT])
                    else:
                        S.op("dve", lambda e, lo_c=lo_c, w=w: e.tensor_tensor(out=oT[:, lo_c * 64:lo_c * 64 + w],
                                                                             in0=oT[:, lo_c * 64:lo_c * 64 + w],
                                                                             in1=pso[:, 0:w], op=ALU.add),
                             reads=[oT, pso], writes=[oT])
            for (t0, n) in TT:
                S.op("act", lambda e: e.activation(out=sqb[:, 0:n], in_=oT[:, t0:t0 + n], func=AF.Square), reads=[oT], writes=[sqb])
                pso, _ = PSO.next()
                S.mm(lambda e: e.matmul(pso[:, 0:n], lhsT=ones_bf[:], rhs=sqb[:, 0:n], start=True, stop=True),
                     reads=[ones_bf, sqb], writes=[pso])
                rstd_from_ps(S, pso, n, 128, eps_t, rs)
                cr, crs = CR.next()
                S.dma("sp", crs, cr[:, 0:n], T["crT"][:][:, h, t0:t0 + n], reads=[T["crT"]], writes=[cr])
                S.op("dve", lambda e: e.scalar_tensor_tensor(out=tt[:, 0:n], in0=oT[:, t0:t0 + n], scalar=hn[:, h:h + 1],
                                                             in1=rs[:, 0:n], op0=ALU.mult, op1=ALU.mult),
                     reads=[oT, hn, rs], writes=[tt])
                yo, ys = YO.next()
                S.op("dve", lambda e: e.tensor_tensor(out=yo[:, 0:n], in0=tt[:, 0:n], in1=cr[:, 0:n], op=ALU.mult),
                     reads=[tt, cr], writes=[yo])
                S.dma("act", ys, T["ycT"][:][:, h, t0:t0 + n], yo[:, 0:n], reads=[yo], writes=[T["ycT"]])


def phase_C1(p, L, Xin):
    S, T, C = p.S, p.T, p.C
    eps_t, ones_bf = p.eps_t, p.ones_bf
    with S.scope():
        ld = S.dma_sem("ld")
        modT = S.sbuf("modT", [128, 48, 2], F32)
        S.dma("sp", ld, modT[:], T["modT"][:], reads=[T["modT"]], writes=[modT])
        nffn = S.sbuf("nffn", [128, 8], F32)
        S.dma("sp", ld, nffn[:], L["norm_ffn"], writes=[nffn])
        wbr = [S.sbuf("wbr%d" % i, [128, 4, D], BF16) for i in range(3)]
        for i, nm in enumerate(("w_br_a", "w_br_b", "w_br_c")):
            S.dma("pool", ld, wbr[i][:], L[nm].rearrange("(k p) n -> p k n", p=128), writes=[wbr[i]])
        wo = S.sbuf("wo", [128, 8, D], BF16)
        for k0 in range(0, 8, 4):
            S.dma("pool", ld, wo[:, k0:k0 + 4, :], L["w_out"].rearrange("(k p) n -> p k n", p=128)[:, k0:k0 + 4, :], writes=[wo])
        A2 = S.sbuf("A2", [128, 8, 2], F32)
        S.op("dve", lambda e: e.tensor_scalar_add(out=A2[:], in0=modT[:, 32:40, :], scalar1=1.0), reads=[modT], writes=[A2])
        S.op("dve", lambda e: e.tensor_tensor(out=A2[:], in0=A2[:], in1=nffn[:].unsqueeze(2).to_broadcast([128, 8, 2]),
                                              op=ALU.mult), reads=[A2, nffn], writes=[A2])
        PS = Rot(S, "ps", 7, [128, 512], F32, psum=True, dma=False)
        YA = Rot(S, "ya", 2, [128, 12, 512], BF16)
        GT = Rot(S, "gt", 1, [128, 24, 512], BF16)
        XT = Rot(S, "xt", 2, [128, 8, 512], F32)
        mT = S.sbuf("mT", [128, 8, 512], BF16)
        t1 = S.sbuf("t1", [128, 512], F32)
        t2 = S.sbuf("t2", [128, 512], F32)
        sq = S.sbuf("sq", [128, 8, 512], BF16)
        rstd = S.sbuf("rstd", [128, 512], F32)
        xs = S.sbuf("xs", [128, 512], F32)
        H2 = Rot(S, "h2", 1, [128, 8, 512], BF16)
        for (t0, n) in TT:
            v = 1 if t0 == 0 else 0
            ya, yas = YA.next()
            for i, nm in enumerate(("yaT", "ybT", "ycT")):
                S.dma("sp", yas, ya[:, i * 4:(i + 1) * 4, 0:n], T[nm][:][:, :, t0:t0 + n], reads=[T[nm]], writes=[ya])
            gt, gts = GT.next()
            for i in range(3):
                S.dma("sp", gts, gt[:, i * 8:(i + 1) * 8, 0:n], T["gatesT"][:][:, i * 8:(i + 1) * 8, t0:t0 + n],
                      reads=[T["gatesT"]], writes=[gt])
            xb, xsem = XT.next()
            S.dma("sp", xsem, xb[:, :, 0:n], Xin[:][:, t0:t0 + n].rearrange("(k p) n -> p k n", p=128), reads=[Xin], writes=[xb])
            for m in range(8):
                pss = []
                for i in range(3):
                    ps, _ = PS.next()
                    for k in range(4):
                        S.mm(lambda e, i=i, k=k, m=m: e.matmul(ps[:, 0:n], lhsT=wbr[i][:, k, m * 128:(m + 1) * 128],
                                                               rhs=ya[:, i * 4 + k, 0:n], start=(k == 0), stop=(k == 3)),
                             reads=[wbr[i], ya], writes=[ps], last=(k == 3))
                    pss.append(ps)
                S.op("dve", lambda e, m=m: e.tensor_tensor(out=t1[:, 0:n], in0=pss[0][:, 0:n], in1=gt[:, m, 0:n], op=ALU.mult),
                     reads=[pss[0], gt], writes=[t1])
                S.op("dve", lambda e, m=m: e.tensor_tensor(out=t2[:, 0:n], in0=pss[1][:, 0:n], in1=gt[:, 8 + m, 0:n], op=ALU.mult),
                     reads=[pss[1], gt], writes=[t2])
                S.op("pool", lambda e: e.tensor_tensor(out=t1[:, 0:n], in0=t1[:, 0:n], in1=t2[:, 0:n], op=ALU.add),
                     reads=[t1, t2], writes=[t1])
                S.op("dve", lambda e, m=m: e.tensor_tensor(out=t2[:, 0:n], in0=pss[2][:, 0:n], in1=gt[:, 16 + m, 0:n], op=ALU.mult),
                     reads=[pss[2], gt], writes=[t2])
                S.op("pool", lambda e, m=m: e.tensor_tensor(out=mT[:, m, 0:n], in0=t1[:, 0:n], in1=t2[:, 0:n], op=ALU.add),
                     reads=[t1, t2], writes=[mT])
            for m in range(8):
                ps, _ = PS.next()
                for k in range(8):
                    S.mm(lambda e, k=k, m=m: e.matmul(ps[:, 0:n], lhsT=wo[:, k, m * 128:(m + 1) * 128], rhs=mT[:, k, 0:n],
                                                      start=(k == 0), stop=(k == 7)), reads=[wo, mT], writes=[ps], last=(k == 7))
                S.op("dve", lambda e, m=m: e.scalar_tensor_tensor(out=xb[:, m, 0:n], in0=ps[:, 0:n], scalar=modT[:, 16 + m, v:v + 1],
                                                                  in1=xb[:, m, 0:n], op0=ALU.mult, op1=ALU.add),
                     reads=[ps, modT, xb], writes=[xb])
            S.dma("act", xsem, T["x1T"][:][:, t0:t0 + n].rearrange("(k p) n -> p k n", p=128), xb[:, :, 0:n],
                  reads=[xb], writes=[T["x1T"]])
            S.op("act", lambda e: e.activation(out=sq[:, :, 0:n], in_=xb[:, :, 0:n], func=AF.Square), reads=[xb], writes=[sq])
            ps, _ = PS.next()
            for k in range(8):
                S.mm(lambda e, k=k: e.matmul(ps[:, 0:n], lhsT=ones_bf[:], rhs=sq[:, k, 0:n], start=(k == 0), stop=(k == 7)),
                     reads=[ones_bf, sq], writes=[ps], last=(k == 7))
            rstd_from_ps(S, ps, n, D, eps_t, rstd)
            h2, h2s = H2.next()
            for k in range(8):
                S.op("dve", lambda e, k=k: e.scalar_tensor_tensor(out=xs[:, 0:n], in0=xb[:, k, 0:n], scalar=A2[:, k, v:v + 1],
                                                                  in1=rstd[:, 0:n], op0=ALU.mult, op1=ALU.mult),
                     reads=[xb, A2, rstd], writes=[xs])
                S.op("act", lambda e, k=k: e.activation(out=h2[:, k, 0:n], in_=xs[:, 0:n], func=AF.Identity,
                                                        bias=modT[:, 24 + k, v:v + 1]), reads=[xs, modT], writes=[h2])
            S.dma("act", h2s, T["h2T"][:][:, :, t0:t0 + n], h2[:, :, 0:n], reads=[h2], writes=[T["h2T"]])


def phase_C2(p, L, Xout, final_out=None):
    S, T, C = p.S, p.T, p.C
    eps_t, ones_bf = p.eps_t, p.ones_bf
    with S.scope():
        ld = S.dma_sem("ld")
        modT = S.sbuf("modT", [128, 48, 2], F32)
        S.dma("sp", ld, modT[:], T["modT"][:], reads=[T["modT"]], writes=[modT])
        cw = S.sbuf("cw", [128, 44, 3], F32)
        cb = S.sbuf("cb", [128, 44], F32)
        S.dma("sp", ld, cw[:], L["conv_w"], writes=[cw])
        S.dma("sp", ld, cb[:], L["conv_b"], writes=[cb])
        fn = S.sbuf("fn", [128, 8], F32)
        S.dma("sp", ld, fn[:], C["final_norm"], writes=[fn])
        hsem = S.dma_sem("h2e")
        wdsem = S.dma_sem("wd")
        with S.scope():
            PS = Rot(S, "ps", 6, [128, 512], F32, psum=True, dma=False)
            PSH = Rot(S, "psh", 2, [128, 4], F32, psum=True, dma=False)
            h2e = S.sbuf("h2e", [128, 8, 2306], BF16)
            WU = Rot(S, "wu", 2, [128, 8, 256], BF16)
            UB = Rot(S, "ub", 4, [128, 514], F32, dma=False)
            ACC = Rot(S, "acc", 4, [128, 512], F32, dma=False)
            AO = Rot(S, "ao", 3, [128, 512], BF16)
            for seg in SEGS:
                base = seg[0][0]
                tot = sum(n for _, n in seg)
                seqs = []
                if base == 0:
                    seqs = [(0, CTX), (CTX, tot)]
                else:
                    seqs = [(base, base + tot)]
                S.op("dve", lambda e: e.memset(h2e[:, :, 0:1], 0.0), writes=[h2e])
                S.op("dve", lambda e: e.memset(h2e[:, :, tot + 1:tot + 2], 0.0), writes=[h2e])
                lo_tok = base - 1 if base > CTX else base
                hi_tok = base + tot + 1 if base + tot < NTOT else base + tot
                S.dma("sp", hsem, h2e[:, :, lo_tok - base + 1:hi_tok - base + 1], T["h2T"][:][:, :, lo_tok:hi_tok],
                      reads=[T["h2T"]], writes=[h2e])
                for c in range(22):
                    wu, wus = WU.next()
                    S.dma("pool", wus, wu[:, :, 0:128], L["w_up"][:, c * 128:(c + 1) * 128].rearrange("(k p) n -> p k n", p=128),
                          writes=[wu])
                    S.dma("pool", wus, wu[:, :, 128:256],
                          L["w_up"][:, FFN + c * 128:FFN + (c + 1) * 128].rearrange("(k p) n -> p k n", p=128), writes=[wu])
                    for (t0, n) in seg:
                        lo = t0 - base + 1
                        accs = []
                        for half in range(2):
                            ci = c + 22 * half
                            ps, _ = PS.next()
                            for k in range(8):
                                S.mm(lambda e, k=k, half=half: e.matmul(ps[:, 0:n], lhsT=wu[:, k, half * 128:(half + 1) * 128],
                                                                        rhs=h2e[:, k, lo:lo + n], start=(k == 0), stop=(k == 7)),
                                     reads=[wu, h2e], writes=[ps], last=(k == 7))
                            psh, _ = PSH.next()
                            lcol = lo - 1
                            rcol = lo + n
                            lz = (t0 == 0) or (t0 == CTX)
                            rz = (t0 + n == CTX) or (t0 + n == NTOT)
                            for k in range(8):
                                S.mm(lambda e, k=k, half=half: e.matmul(psh[:, 0:1], lhsT=wu[:, k, half * 128:(half + 1) * 128],
                                                                        rhs=h2e[:, k, lcol:lcol + 1], start=(k == 0), stop=(k == 7)),
                                     reads=[wu, h2e], writes=[psh], last=False)
                            for k in range(8):
                                S.mm(lambda e, k=k, half=half: e.matmul(psh[:, 1:2], lhsT=wu[:, k, half * 128:(half + 1) * 128],
                                                                        rhs=h2e[:, k, rcol:rcol + 1], start=(k == 0), stop=(k == 7)),
                                     reads=[wu, h2e], writes=[psh], last=(k == 7))
                            ub, _ = UB.next()
                            evac(S, "act", ub[:, 1:n + 1], ps[:, 0:n], [ps], [ub])
                            if lz:
                                S.op("dve", lambda e: e.memset(ub[:, 0:1], 0.0), writes=[ub])
                            else:
                                S.op("dve", lambda e: e.tensor_copy(out=ub[:, 0:1], in_=psh[:, 0:1]), reads=[psh], writes=[ub])
                            if rz:
                                S.op("dve", lambda e: e.memset(ub[:, n + 1:n + 2], 0.0), writes=[ub])
                            else:
                                S.op("dve", lambda e: e.tensor_copy(out=ub[:, n + 1:n + 2], in_=psh[:, 1:2]), reads=[psh], writes=[ub])
                            acc, _ = ACC.next()
                            eng = "dve" if half == 0 else "pool"
                            S.op(eng, lambda e, ci=ci: e.tensor_scalar(out=acc[:, 0:n], in0=ub[:, 0:n], scalar1=cw[:, ci, 0:1],
                                                                       scalar2=cb[:, ci:ci + 1], op0=ALU.mult, op1=ALU.add),
                                 reads=[ub, cw, cb], writes=[acc])
                            S.op("dve", lambda e, ci=ci: e.scalar_tensor_tensor(out=acc[:, 0:n], in0=ub[:, 1:n + 1], scalar=cw[:, ci, 1:2],
                                                                                in1=acc[:, 0:n], op0=ALU.mult, op1=ALU.add),
                                 reads=[ub, cw, acc], writes=[acc])
                            S.op("dve", lambda e, ci=ci: e.scalar_tensor_tensor(out=acc[:, 0:n], in0=ub[:, 2:n + 2], scalar=cw[:, ci, 2:3],
                                                                                in1=acc[:, 0:n], op0=ALU.mult, op1=ALU.add),
                                 reads=[ub, cw, acc], writes=[acc])
                            accs.append(acc)
                        S.op("act", lambda e: e.activation(out=accs[0][:, 0:n], in_=accs[0][:, 0:n], func=AF.Silu),
                             reads=[accs[0]], writes=[accs[0]])
                        ao, aos = AO.next()
                        S.op("pool", lambda e: e.tensor_tensor(out=ao[:, 0:n], in0=accs[0][:, 0:n], in1=accs[1][:, 0:n], op=ALU.mult),
                             reads=[accs[0], accs[1]], writes=[ao])
                        S.dma("act", aos, T["aT"][:][:, c, t0:t0 + n], ao[:, 0:n], reads=[ao], writes=[T["aT"]])
        with S.scope():
            PS = Rot(S, "ps", 7, [128, 512], F32, psum=True, dma=False)
            wd = S.sbuf("wd", [128, 22, D], BF16)
            for k0 in range(0, 22, 2):
                S.dma("pool", wdsem, wd[:, k0:k0 + 2, :], L["w_down"].rearrange("(k p) n -> p k n", p=128)[:, k0:k0 + 2, :], writes=[wd])
            AT = Rot(S, "at", 2, [128, 22, 512], BF16)
            XT = Rot(S, "xt", 2, [128, 8, 512], F32)
            sq = S.sbuf("sq", [128, 8, 512], BF16)
            rstd = S.sbuf("rstd", [128, 512], F32)
            for (t0, n) in TT:
                v = 1 if t0 == 0 else 0
                at, ats = AT.next()
                for i in range(0, 22, 6):
                    i2 = min(22, i + 6)
                    S.dma("sp", ats, at[:, i:i2, 0:n], T["aT"][:][:, i:i2, t0:t0 + n], reads=[T["aT"]], writes=[at])
                xb, xsem = XT.next()
                S.dma("sp", xsem, xb[:, :, 0:n], T["x1T"][:][:, t0:t0 + n].rearrange("(k p) n -> p k n", p=128),
                      reads=[T["x1T"]], writes=[xb])
                for m in range(8):
                    ps, _ = PS.next()
                    for k in range(22):
                        S.mm(lambda e, k=k, m=m: e.matmul(ps[:, 0:n], lhsT=wd[:, k, m * 128:(m + 1) * 128], rhs=at[:, k, 0:n],
                                                          start=(k == 0), stop=(k == 21)), reads=[wd, at], writes=[ps], last=(k == 21))
                    S.op("dve", lambda e, m=m: e.scalar_tensor_tensor(out=xb[:, m, 0:n], in0=ps[:, 0:n], scalar=modT[:, 40 + m, v:v + 1],
                                                                      in1=xb[:, m, 0:n], op0=ALU.mult, op1=ALU.add),
                         reads=[ps, modT, xb], writes=[xb])
                if final_out is None:
                    S.dma("act", xsem, Xout[:][:, t0:t0 + n].rearrange("(k p) n -> p k n", p=128), xb[:, :, 0:n],
                          reads=[xb], writes=[Xout])
                elif t0 >= CTX:
                    S.op("act", lambda e: e.activation(out=sq[:, :, 0:n], in_=xb[:, :, 0:n], func=AF.Square), reads=[xb], writes=[sq])
                    ps, _ = PS.next()
                    for k in range(8):
                        S.mm(lambda e, k=k: e.matmul(ps[:, 0:n], lhsT=ones_bf[:], rhs=sq[:, k, 0:n], start=(k == 0), stop=(k == 7)),
                             reads=[ones_bf, sq], writes=[ps], last=(k == 7))
                    rstd_from_ps(S, ps, n, D, eps_t, rstd)
                    for k in range(8):
                        S.op("dve", lambda e, k=k: e.scalar_tensor_tensor(out=xb[:, k, 0:n], in0=xb[:, k, 0:n], scalar=fn[:, k:k + 1],
                                                                          in1=rstd[:, 0:n], op0=ALU.mult, op1=ALU.mult),
                             reads=[xb, fn, rstd], writes=[xb])
                    S.dma("act", xsem, final_out[:][:, t0 - CTX:t0 - CTX + n].rearrange("(k p) n -> p k n", p=128), xb[:, :, 0:n],
                          reads=[xb], writes=[final_out])


PHASES_ALL = ("A", "B1", "B2", "B3", "C1", "C2")
PHASE_W = {"A": ("w_mod", "b_mod", "norm_mix", "w_in", "bqn", "bkvn", "w_uq", "w_ukv", "w_gate", "b_gate"),
           "B1": ("a_sink",), "B2": (), "B3": ("c_hn",), "C1": ("norm_ffn", "w_br_a", "w_br_b", "w_br_c", "w_out"),
           "C2": ("conv_w", "conv_b", "w_up", "w_down")}


def needed_weights(phases):
    need = set()
    for ph in phases:
        need.update(PHASE_W[ph])
    return [w for w in LAYER_W if w[0] in need]


def build_program(nlayers, final, phases=PHASES_ALL, ext_scratch=(), stop=None):
    nc = bass.Bass("TRN2", target_bir_lowering=False)
    xin_ap = dram_in(nc, "xT", (D, NTOT), F32)
    Cap = {n: dram_in(nc, n, s, d) for n, s, d in CONSTS}
    Lw = [{n: dram_in(nc, "%s_%d" % (n, l), s, d) for n, s, d in needed_weights(phases)} for l in range(nlayers)]
    if final:
        out_ap = dram_out(nc, "outT", (D, SEQ), F32)
    else:
        out_ap = dram_out(nc, "xT_out", (D, NTOT), F32)
    with contextlib.ExitStack() as st:
        S = Sched(nc, st)
        p = P()
        p.nc, p.S, p.C = nc, S, Cap
        p.stop = stop
        p.T = {}
        for n, s, d in SCRATCH:
            if n in ext_scratch:
                kind = ext_scratch[n]
                ap = dram_in(nc, n, s, d) if kind == "in" else dram_out(nc, n, s, d)
                p.T[n] = Buf(ap, n)
            else:
                p.T[n] = S.dram(n, s, d)
        Xext = Buf(xin_ap, "xT")
        Oext = Buf(out_ap, "out")
        ld = S.dma_sem("ldc")
        p.eps_t = S.sbuf("eps", [128, 1], F32)
        p.one_t = S.sbuf("one", [128, 1], F32)
        p.ones_bf = S.sbuf("ones_bf", [128, 128], BF16)
        S.op("dve", lambda e: e.memset(p.eps_t[:], EPS), writes=[p.eps_t])
        S.op("dve", lambda e: e.memset(p.one_t[:], 1.0), writes=[p.one_t])
        S.op("dve", lambda e: e.memset(p.ones_bf[:], 1.0), writes=[p.ones_bf])
        for nm, shape in (("permA", [128, 128]), ("permB", [96, 96]), ("permK", [32, 32]), ("sel", [32, 96]),
                          ("maskLo", [128, 512]), ("maskHi", [128, 512]), ("triF", [64, 512]), ("triB", [64, 512]),
                          ("ident", [64, 64]), ("sv", [1, 128])):
            b_ = S.sbuf(nm, shape, BF16)
            S.dma("sp", ld, b_[:], Cap[nm], writes=[b_])
            setattr(p, nm, b_)
        p.cv = S.sbuf("cv", [128, 8, 2], F32)
        S.dma("sp", ld, p.cv[:], Cap["cvec"], writes=[p.cv])
        for l in range(nlayers):
            Xin = Xext if l == 0 else p.T["XA" if l % 2 == 1 else "XB"]
            last = (l == nlayers - 1)
            Xout = Oext if (last and not final) else p.T["XA" if (l + 1) % 2 == 1 else "XB"]
            if "A" in phases:
                phase_A(p, Lw[l], Xin)
            if "B1" in phases:
                phase_B1(p, Lw[l])
            if "B2" in phases:
                phase_B2(p, Lw[l])
            if "B3" in phases:
                phase_B3(p, Lw[l])
            if "C1" in phases:
                phase_C1(p, Lw[l], Xin)
            if "C2" in phases:
                phase_C2(p, Lw[l], Xout, final_out=(Oext if (last and final) else None))
        S.finish("sp")
        p.ninst = S.ninst
    nc._ninst = p.ninst
    return nc


_PROGS = {}
_CONSTS = {}


def _pk(v):
    v = np.asarray(v)
    k = v.shape[0] // 128
    out = v.reshape((k, 128) + v.shape[1:])
    return np.ascontiguousarray(np.moveaxis(out, 0, 1))


def host_consts():
    if "c" in _CONSTS:
        return _CONSTS["c"]
    pos = np.concatenate([np.zeros(CTX, np.int64), np.arange(SEQ)])
    is_ctx = np.concatenate([np.ones(CTX, bool), np.zeros(SEQ, bool)])
    c = dict(_rope_tables(pos, is_ctx))
    c.update(_perm_consts())
    j = np.arange(128)[:, None]
    i = (np.arange(512) % 128)[None, :]
    c["maskLo"] = (j >= i).astype(np.float32).astype(NPBF)
    c["maskHi"] = (j <= i).astype(np.float32).astype(NPBF)
    j = np.arange(64)[:, None]
    i = (np.arange(512) % 64)[None, :]
    c["triF"] = (j <= i).astype(np.float32).astype(NPBF)
    c["triB"] = (j >= i).astype(np.float32).astype(NPBF)
    rm = np.ones((64, NTOT), np.float32)
    rm[:, ::64] = 0.0
    c["rmask"] = rm
    c["ident"] = np.eye(64, dtype=np.float32).astype(NPBF)
    sv = np.zeros((1, 128), np.float32)
    sv[0, 64:] = 1.0
    c["sv"] = sv.astype(NPBF)
    _CONSTS["c"] = c
    return c


def layer_inputs(W, l, names):
    m = {}
    g = {
        "w_mod": lambda: W["w_mod"][l], "b_mod": lambda: _pk(W["b_mod"][l]), "norm_mix": lambda: _pk(W["norm_mix"][l]),
        "norm_ffn": lambda: _pk(W["norm_ffn"][l]), "w_in": lambda: W["w_in"][l],
        "a_sink": lambda: np.ascontiguousarray(W["a_sink"][l].reshape(1, 8)),
        "bqn": lambda: _pk(W["b_q_norm"][l]), "bkvn": lambda: _pk(W["b_kv_norm"][l]),
        "w_uq": lambda: W["b_w_uq"][l], "w_ukv": lambda: W["b_w_ukv"][l], "w_gate": lambda: W["c_w_gate"][l],
        "b_gate": lambda: np.ascontiguousarray(W["c_b_gate"][l].reshape(2, 2, 128).transpose(2, 0, 1)),
        "c_hn": lambda: np.ascontiguousarray(W["c_head_norm"][l].reshape(4, 128).T),
        "w_br_a": lambda: W["w_br_a"][l], "w_br_b": lambda: W["w_br_b"][l], "w_br_c": lambda: W["w_br_c"][l],
        "w_out": lambda: W["w_out"][l], "w_up": lambda: W["w_up"][l],
        "conv_w": lambda: np.ascontiguousarray(W["conv_w"][l].reshape(3, 44, 128).transpose(2, 1, 0)),
        "conv_b": lambda: _pk(W["conv_b"][l]), "w_down": lambda: W["w_down"][l],
    }
    for n in names:
        m[n] = np.ascontiguousarray(g[n]())
    return m


def core_inputs(W, b, layers, phases=PHASES_ALL):
    c = dict(host_consts())
    c["cvec"] = _pk(np.stack([W["c"][b], W["c_ctx"]], axis=1))
    c["final_norm"] = _pk(W["final_norm"])
    m = {n: c[n] for n, _, _ in CONSTS}
    names = [w[0] for w in needed_weights(phases)]
    for i, l in enumerate(layers):
        for n, v in layer_inputs(W, l, names).items():
            m["%s_%d" % (n, i)] = v
    return m


def kernel(**W):
    W = {k: np.asarray(v) for k, v in W.items()}
    if "fused" not in _PROGS:
        _PROGS["fused"] = build_program(DEPTH, True)
    maps = []
    for b in range(2):
        m = core_inputs(W, b, list(range(DEPTH)))
        m["xT"] = np.ascontiguousarray(np.concatenate([W["ctx"][b], W["x"][b]], axis=0).T)
        maps.append(m)
    res = run_bass_kernel_spmd(_PROGS["fused"], maps, core_ids=[0, 1]).results
    out = np.stack([np.ascontiguousarray(np.asarray(res[b]["outT"]).T) for b in range(2)], axis=0)
    return out.astype(np.float32)
```

```python
import contextlib
import numpy as np
import ml_dtypes
import concourse.bass as bass
import concourse.mybir as mybir
from concourse.bass_utils import run_bass_kernel_spmd

F32 = mybir.dt.float32
BF16 = mybir.dt.bfloat16
AF = mybir.ActivationFunctionType
ALU = mybir.AluOpType
NPBF = ml_dtypes.bfloat16

NCORES = 8
D = 1024
SEQ = 8192
CTX = 256
DEPTH = 4
NTOT = CTX + SEQ
NKB = NTOT // 128
EPS = 1e-6
IN_DIM = 5824
O_AQ, O_AK, O_AV, O_BQ, O_BKV, O_BKR, O_CQ, O_CK, O_CV, O_CR, O_CG, O_GATE = (
    0, 512, 640, 768, 1024, 1152, 1184, 1440, 1696, 2208, 2720, 2752)
FFN = 2816
TT = [(0, CTX)] + [(CTX + 512 * i, 512) for i in range(16)]
SEGS = [TT[0:5], TT[5:9], TT[9:13], TT[13:17]]

SAME_ENGINE_SYNC = True
_STOP = None


class _Stop(Exception):
    pass


def chk(name):
    if _STOP == name:
        raise _Stop()


class Buf:
    __slots__ = ("t", "name", "lw", "rd")

    def __init__(self, t, name=""):
        self.t = t
        self.name = name
        self.lw = {}
        self.rd = {}

    def __getitem__(self, idx):
        return self.t[idx]


class Sched:
    def __init__(self, nc, stack):
        self.nc = nc
        self.stack = stack
        self.root = stack
        self.engs = {"pe": nc.tensor, "act": nc.scalar, "dve": nc.vector,
                     "pool": nc.gpsimd, "sp": nc.sync}
        self.sems = {}
        self.cnt = {}
        self.seen = {e: {} for e in self.engs}
        for e in ("pe", "act", "dve", "pool"):
            self.sems[e] = stack.enter_context(nc.semaphore("s_" + e))
            self.cnt[e] = 0
        self.ninst = 0
        self._uid = 0
        self.sem_bufs = {}
        self._free = []
        self._scoped = [[]]

    def uid(self, p):
        self._uid += 1
        return "%s_%d" % (p, self._uid)

    @contextlib.contextmanager
    def scope(self):
        old = self.stack
        self._scoped.append([])
        with contextlib.ExitStack() as st:
            self.stack = st
            try:
                yield
            finally:
                self.barrier()
                self.stack = old
                self._free.extend(self._scoped.pop())

    def sbuf(self, name, shape, dt):
        return Buf(self.stack.enter_context(self.nc.sbuf_tensor(self.uid(name), shape, dt)), name)

    def psum(self, name, shape, dt):
        return Buf(self.stack.enter_context(self.nc.psum_tensor(self.uid(name), shape, dt)), name)

    def dram(self, name, shape, dt):
        return Buf(self.nc.dram_tensor(self.uid(name), list(shape), dt).ap(), name)

    def dma_sem(self, name):
        if self._free:
            key = self._free.pop()
            for sfx in ("~hw", "~sw"):
                if key + sfx in self.sem_bufs:
                    self.sem_bufs[key + sfx] = []
        else:
            key = self.uid("d")
        self._scoped[-1].append(key)
        return key

    def _sem(self, key):
        if key not in self.sems:
            self.sems[key] = self.root.enter_context(self.nc.semaphore(key.replace("~", "_")))
            self.cnt[key] = 0
            self.sem_bufs[key] = []
        return self.sems[key]

    def _wait(self, eng, deps):
        e = self.engs[eng]
        for (k, c) in sorted(deps, key=lambda x: str(x[0])):
            if k == eng and not SAME_ENGINE_SYNC:
                continue
            if k == "pe" and eng == "pe":
                continue
            if self.seen[eng].get(k, 0) >= c:
                continue
            e.wait_ge(self.sems[k], c)
            self.seen[eng][k] = c

    @staticmethod
    def _deps(reads, writes):
        deps = set()
        for b in reads:
            deps.update(b.lw.items())
        for b in writes:
            deps.update(b.lw.items())
            deps.update(b.rd.items())
        return deps

    def op(self, eng, fn, reads=(), writes=()):
        self._wait(eng, self._deps(reads, writes))
        ins = fn(self.engs[eng])
        self.cnt[eng] += 1
        ins.then_inc(self.sems[eng], 1)
        for b in reads:
            b.rd[eng] = self.cnt[eng]
        for b in writes:
            b.lw[eng] = self.cnt[eng]
            b.rd = {}
        self.ninst += 1
        return ins

    def mm(self, fn, reads=(), writes=(), last=True):
        self._wait("pe", self._deps(reads, writes))
        ins = fn(self.engs["pe"])
        self.ninst += 1
        if last:
            self.cnt["pe"] += 1
            ins.then_inc(self.sems["pe"], 1)
            for b in writes:
                b.lw["pe"] = self.cnt["pe"]
                b.rd = {}
        for b in reads:
            b.rd["pe"] = self.cnt["pe"] + (0 if last else 1)
        return ins

    def dma(self, q, semkey, out_ap, in_ap, reads=(), writes=(), **kw):
        semkey = semkey + ("~sw" if q == "pool" else "~hw")
        sem = self._sem(semkey)
        self._wait(q, self._deps(reads, writes))
        ins = self.engs[q].dma_start(out=out_ap, in_=in_ap, **kw)
        self.cnt[semkey] += 16
        c = self.cnt[semkey]
        ins.then_inc(sem, 16)
        for b in self.sem_bufs[semkey]:
            if semkey in b.lw:
                b.lw[semkey] = c
            if semkey in b.rd:
                b.rd[semkey] = c
        for b in reads:
            b.rd[semkey] = c
            if b not in self.sem_bufs[semkey]:
                self.sem_bufs[semkey].append(b)
        for b in writes:
            b.lw[semkey] = c
            b.rd = {}
            if b not in self.sem_bufs[semkey]:
                self.sem_bufs[semkey].append(b)
        return ins

    def barrier(self):
        snap = [(k, c) for k, c in self.cnt.items() if c > 0]
        for eng in ("pe", "act", "dve", "pool", "sp"):
            e = self.engs[eng]
            for (k, c) in snap:
                if self.seen[eng].get(k, 0) >= c:
                    continue
                e.wait_ge(self.sems[k], c)
                self.seen[eng][k] = c

    def finish(self, eng="sp"):
        for k, c in self.cnt.items():
            if c > 0:
                self._wait(eng, {(k, c)})

    def wait_all(self, eng, bufs):
        deps = set()
        for b in bufs:
            deps.update(b.lw.items())
            deps.update(b.rd.items())
        self._wait(eng, deps)


class Rot:
    def __init__(self, S, name, n, shape, dt, psum=False, dma=True):
        self.bufs = [(S.psum if psum else S.sbuf)(name + str(i), shape, dt) for i in range(n)]
        self.sems = [S.dma_sem(name + str(i)) for i in range(n)] if dma else [None] * n
        self.i = 0

    def next(self):
        b, s = self.bufs[self.i], self.sems[self.i]
        self.i = (self.i + 1) % len(self.bufs)
        return b, s


class Out:
    def __init__(self):
        self.bufs = []

    def add(self, b):
        if b not in self.bufs:
            self.bufs.append(b)


def dram_in(nc, name, shape, dt):
    return nc.dram_tensor(name, list(shape), dt, kind="ExternalInput").ap()


def dram_out(nc, name, shape, dt):
    return nc.dram_tensor(name, list(shape), dt, kind="ExternalOutput").ap()


def _rope_tables(pos, is_ctx):
    n = pos.shape[0]
    row = (pos // 64).astype(np.float32)
    col = (pos % 64).astype(np.float32)

    def tab(nfreq):
        inv = (np.float32(10000.0) ** (-np.arange(nfreq, dtype=np.float32) / np.float32(nfreq))).astype(np.float32)
        ang = np.concatenate([row[:, None] * inv, col[:, None] * inv], axis=-1).astype(np.float32)
        c, s = np.cos(ang).astype(np.float32), np.sin(ang).astype(np.float32)
        c[is_ctx] = 1.0
        s[is_ctx] = 0.0
        return c, s

    ca, sa = tab(16)
    cb, sb = tab(8)
    cosA = np.empty((128, n), np.float32)
    sinA = np.empty((128, n), np.float32)
    for p in range(128):
        d = p % 64
        i = d % 32
        cosA[p] = ca[:, i]
        sinA[p] = (-sa[:, i]) if d < 32 else sa[:, i]
    cosB = np.ones((96, n), np.float32)
    sinB = np.zeros((96, n), np.float32)
    cosK = np.empty((32, n), np.float32)
    sinK = np.empty((32, n), np.float32)
    for r in range(32):
        i = r % 16
        cosK[r] = cb[:, i]
        sinK[r] = (-sb[:, i]) if r < 16 else sb[:, i]
    cosB[64:96] = cosK
    sinB[64:96] = sinK
    return dict(cosA=cosA, sinA=sinA, cosB=cosB, sinB=sinB, cosK=cosK, sinK=sinK)


def _perm_consts():
    permA = np.zeros((128, 128), np.float32)
    for m in range(128):
        d = m % 64
        permA[m + 32 if d < 32 else m - 32, m] = 1.0
    permB = np.zeros((96, 96), np.float32)
    for m in range(64, 96):
        r = m - 64
        permB[m + 16 if r < 16 else m - 16, m] = 1.0
    permK = np.zeros((32, 32), np.float32)
    for m in range(32):
        permK[m + 16 if m < 16 else m - 16, m] = 1.0
    sel = np.zeros((32, 96), np.float32)
    for k in range(32):
        sel[k, 64 + k] = 1.0
    return dict(permA=permA.astype(NPBF), permB=permB.astype(NPBF), permK=permK.astype(NPBF),
                sel=sel.astype(NPBF))


LAYER_W = [
    ("w_mod", (D, 6 * D), F32), ("b_mod", (128, 48), F32), ("norm_mix", (128, 8), F32), ("norm_ffn", (128, 8), F32),
    ("w_in", (D, IN_DIM), F32), ("a_sink", (1, 8), F32), ("bqn", (128, 2), F32), ("bkvn", (128, 1), F32),
    ("w_uq", (256, 768), F32), ("w_ukv", (128, 1024), F32), ("w_gate", (2, 16, 256), F32), ("b_gate", (128, 2, 2), F32),
    ("c_hn", (128, 4), F32), ("w_br_a", (512, D), F32), ("w_br_b", (512, D), F32), ("w_br_c", (512, D), F32),
    ("w_out", (D, D), F32), ("w_up", (D, 2 * FFN), F32), ("conv_w", (128, 44, 3), F32), ("conv_b", (128, 44), F32),
    ("w_down", (FFN, D), F32),
]
CONSTS = [
    ("cvec", (128, 8, 2), F32), ("final_norm", (128, 8), F32),
    ("cosA", (128, NTOT), F32), ("sinA", (128, NTOT), F32), ("cosB", (96, NTOT), F32), ("sinB", (96, NTOT), F32),
    ("cosK", (32, NTOT), F32), ("sinK", (32, NTOT), F32),
    ("permA", (128, 128), BF16), ("permB", (96, 96), BF16), ("permK", (32, 32), BF16), ("sel", (32, 96), BF16),
    ("maskLo", (128, 512), BF16), ("maskHi", (128, 512), BF16), ("triF", (64, 512), BF16), ("triB", (64, 512), BF16),
    ("rmask", (64, NTOT), F32), ("ident", (64, 64), BF16), ("sv", (1, 128), BF16),
]
SCRATCH = [
    ("modT", (128, 48, 2), F32),
    ("qaT", (128, 4, NTOT), BF16), ("kaT", (128, NTOT), BF16), ("vaA", (NTOT, 2, 128), BF16),
    ("qbT", (96, 8, NTOT), BF16), ("kbT", (96, 8, NTOT), BF16), ("vbA", (NTOT, 8, 128), BF16),
    ("cqT", (128, 2, NTOT), BF16), ("ckT", (128, 2, NTOT), BF16), ("cvv", (NTOT, 512), BF16),
    ("crT", (128, 4, NTOT), BF16), ("gfT", (128, 2, NTOT), F32), ("gbT", (128, 2, NTOT), F32),
    ("gatesT", (128, 24, NTOT), BF16),
    ("yaT", (128, 4, NTOT), BF16), ("ybT", (128, 4, NTOT), BF16), ("ycT", (128, 4, NTOT), BF16),
    ("x1T", (D, NTOT), F32), ("h2T", (128, 8, NTOT), BF16), ("aT", (128, 22, NTOT), BF16),
    ("XA", (D, NTOT), F32), ("XB", (D, NTOT), F32),
]


class P:
    pass


def evac(S, eng, out_ap, in_ap, reads, writes, func=None, scale=1.0):
    if eng == "act":
        return S.op("act", lambda e: e.activation(out=out_ap, in_=in_ap, func=func or AF.Copy, scale=scale),
                    reads=reads, writes=writes)
    return S.op(eng, lambda e: e.tensor_copy(out=out_ap, in_=in_ap), reads=reads, writes=writes)


def rstd_from_ps(S, ps, n, dim, eps_t, out):
    S.op("act", lambda e: e.activation(out=out[:, 0:n], in_=ps[:, 0:n], func=AF.Ln, scale=1.0 / dim, bias=eps_t[:]),
         reads=[ps, eps_t], writes=[out])
    S.op("act", lambda e: e.activation(out=out[:, 0:n], in_=out[:, 0:n], func=AF.Exp, scale=-0.5), reads=[out], writes=[out])


def phase_A(p, L, Xin):
    S, T, C = p.S, p.T, p.C
    nc = p.nc
    eps_t, one_t, ones_bf = p.eps_t, p.one_t, p.ones_bf
    with S.scope():
        ld = S.dma_sem("ldc")
        bmod = S.sbuf("bmod", [128, 48], F32)
        nmix = S.sbuf("nmix", [128, 8], F32)
        bqn = S.sbuf("bqn", [128, 2], F32)
        bkvn = S.sbuf("bkvn", [128, 1], F32)
        bg = S.sbuf("bg", [128, 2, 2], F32)
        for b_, n_ in ((bmod, "b_mod"), (nmix, "norm_mix"), (bqn, "bqn"), (bkvn, "bkvn"), (bg, "b_gate")):
            S.dma("sp", ld, b_[:], L[n_], writes=[b_])
        wuq = S.sbuf("wuq", [128, 2, 768], BF16)
        S.dma("pool", ld, wuq[:], L["w_uq"].rearrange("(k p) n -> p k n", p=128), writes=[wuq])
        wkn = S.sbuf("wkn", [128, 8, 96], BF16)
        S.op("dve", lambda e: e.memset(wkn[:], 0.0), writes=[wkn])
        S.dma("pool", ld, wkn[:, :, 0:64], L["w_ukv"].rearrange("p (h c) -> p h c", c=128)[:, :, 0:64], writes=[wkn])
        wvb = S.sbuf("wvb", [128, 8, 64], BF16)
        S.dma("pool", ld, wvb[:], L["w_ukv"].rearrange("p (h c) -> p h c", c=128)[:, :, 64:128], writes=[wvb])
        wg = S.sbuf("wg", [32, 2, 256], BF16)
        S.op("dve", lambda e: e.memset(wg[:], 0.0), writes=[wg])
        S.dma("pool", ld, wg[0:16, 0, :], L["w_gate"][0], writes=[wg])
        S.dma("pool", ld, wg[16:32, 1, :], L["w_gate"][1], writes=[wg])
        nbg = S.sbuf("nbg", [128, 2, 2], F32)
        S.op("dve", lambda e: e.tensor_scalar_mul(out=nbg[:], in0=bg[:], scalar1=-1.0), reads=[bg], writes=[nbg])

        PS = Rot(S, "ps", 7, [128, 512], F32, psum=True, dma=False)

        modT = S.sbuf("modT", [128, 48, 2], F32)
        sc = S.sbuf("silu_c", [128, 8, 2], F32)
        tmp8 = S.sbuf("tmp8", [128, 8, 2], F32)
        cv = p.cv
        S.op("act", lambda e: e.activation(out=tmp8[:], in_=cv[:], func=AF.Exp, scale=-1.0), reads=[cv], writes=[tmp8])
        S.op("dve", lambda e: e.tensor_scalar_add(out=tmp8[:], in0=tmp8[:], scalar1=1.0), reads=[tmp8], writes=[tmp8])
        S.op("dve", lambda e: e.reciprocal(out=tmp8[:], in_=tmp8[:]), reads=[tmp8], writes=[tmp8])
        S.op("dve", lambda e: e.tensor_tensor(out=sc[:], in0=tmp8[:], in1=cv[:], op=ALU.mult), reads=[tmp8, cv], writes=[sc])
        scb = S.sbuf("silu_cb", [128, 8, 2], BF16)
        S.op("dve", lambda e: e.tensor_copy(out=scb[:], in_=sc[:]), reads=[sc], writes=[scb])
        with S.scope():
            WM = Rot(S, "wm", 2, [128, 8, 512], BF16)
            for piece in range(12):
                wb, ws = WM.next()
                S.dma("pool", ws, wb[:], L["w_mod"][:, piece * 512:(piece + 1) * 512].rearrange("(k p) n -> p k n", p=128),
                      writes=[wb])
                ps, _ = PS.next()
                for j in range(4):
                    for k in range(8):
                        S.mm(lambda e, j=j, k=k: e.matmul(ps[:, j * 2:j * 2 + 2], lhsT=wb[:, k, j * 128:(j + 1) * 128],
                                                          rhs=scb[:, k, :], start=(k == 0), stop=(k == 7)),
                             reads=[wb, scb], writes=[ps], last=(j == 3 and k == 7))
                S.op("dve", lambda e, piece=piece: e.tensor_tensor(
                    out=modT[:, piece * 4:(piece + 1) * 4, :], in0=ps[:, 0:8].rearrange("p (j v) -> p j v", v=2),
                    in1=bmod[:, piece * 4:(piece + 1) * 4].unsqueeze(2).to_broadcast([128, 4, 2]), op=ALU.add),
                    reads=[ps, bmod], writes=[modT])
        msem = S.dma_sem("modst")
        if p.stop == "mod":
            S.dma("act", msem, T["modT"][:], modT[:], reads=[modT], writes=[T["modT"]])
            return
        S.dma("act", msem, T["modT"][:], modT[:], reads=[modT], writes=[T["modT"]])
        A1 = S.sbuf("A1", [128, 8, 2], F32)
        S.op("dve", lambda e: e.tensor_scalar_add(out=A1[:], in0=modT[:, 8:16, :], scalar1=1.0), reads=[modT], writes=[A1])
        S.op("dve", lambda e: e.tensor_tensor(out=A1[:], in0=A1[:], in1=nmix[:].unsqueeze(2).to_broadcast([128, 8, 2]),
                                              op=ALU.mult), reads=[A1, nmix], writes=[A1])

        hT = S.sbuf("hT", [128, 8, 2304], BF16)
        WP = Rot(S, "wp", 3, [128, 8, 512], BF16)
        STB = Rot(S, "stb", 4, [128, 512], BF16)
        STF = Rot(S, "stf", 3, [128, 512], F32)
        TAB = Rot(S, "tab", 4, [128, 512], F32)
        XT = Rot(S, "xt", 2, [128, 8, 512], F32)
        sq = S.sbuf("sq", [128, 8, 512], BF16)
        rstd = S.sbuf("rstd", [128, 512], F32)
        xs = S.sbuf("xs", [128, 512], F32)
        cq = S.sbuf("cq", [128, 2, 512], F32)
        sq2 = S.sbuf("sq2", [128, 2, 512], BF16)
        rs2 = S.sbuf("rs2", [128, 512], F32)
        cqn = S.sbuf("cqn", [128, 2, 512], BF16)
        ckvn = S.sbuf("ckvn", [128, 512], BF16)
        vst = S.sbuf("vst", [128, 8, 128], BF16)
        S.op("dve", lambda e: e.memset(vst[:], 1.0), writes=[vst])
        vsem = S.dma_sem("vst")
        cvst = S.sbuf("cvst", [128, 512], BF16)
        csem = S.dma_sem("cvst")
        cg = S.sbuf("cg", [32, 512], BF16)
        krb = S.sbuf("krb", [32, 512], BF16)

        def load_w(col0, ncols):
            wb, ws = WP.next()
            S.dma("pool", ws, wb[:, :, 0:ncols], L["w_in"][:, col0:col0 + ncols].rearrange("(k p) n -> p k n", p=128),
                  writes=[wb])
            return wb

        def store(dst, dst_ap, buf, src_ap, sem):
            S.dma("act", sem, dst_ap, src_ap, reads=[buf], writes=[dst])

        for seg in SEGS:
            loc = {}
            l0 = 0
            for (t0, n) in seg:
                loc[t0] = l0
                l0 += n

            for (t0, n) in seg:
                v = 1 if t0 == 0 else 0
                lo = loc[t0]
                xb, xsem = XT.next()
                S.dma("sp", xsem, xb[:, :, 0:n], Xin[:][:, t0:t0 + n].rearrange("(k p) n -> p k n", p=128),
                      reads=[Xin], writes=[xb])
                S.op("act", lambda e: e.activation(out=sq[:, :, 0:n], in_=xb[:, :, 0:n], func=AF.Square), reads=[xb], writes=[sq])
                ps, _ = PS.next()
                for k in range(8):
                    S.mm(lambda e, k=k: e.matmul(ps[:, 0:n], lhsT=ones_bf[:], rhs=sq[:, k, 0:n], start=(k == 0), stop=(k == 7)),
                         reads=[ones_bf, sq], writes=[ps], last=(k == 7))
                rstd_from_ps(S, ps, n, D, eps_t, rstd)
                for k in range(8):
                    S.op("dve", lambda e, k=k: e.scalar_tensor_tensor(out=xs[:, 0:n], in0=xb[:, k, 0:n], scalar=A1[:, k, v:v + 1],
                                                                      in1=rstd[:, 0:n], op0=ALU.mult, op1=ALU.mult),
                         reads=[xb, A1, rstd], writes=[xs])
                    S.op("act", lambda e, k=k: e.activation(out=hT[:, k, lo:lo + n], in_=xs[:, 0:n], func=AF.Identity,
                                                            bias=modT[:, k, v:v + 1]),
                         reads=[xs, modT], writes=[hT])

            def proj_fm(ps, wb, c0, m, t0, n):
                lo = loc[t0]
                for k in range(8):
                    S.mm(lambda e, k=k: e.matmul(ps[0:m, 0:n], lhsT=wb[:, k, c0:c0 + m], rhs=hT[:, k, lo:lo + n],
                                                 start=(k == 0), stop=(k == 7)),
                         reads=[wb, hT], writes=[ps], last=(k == 7))

            def rope(ps, m, t0, n, perm, cosn, sinn, dst, dst_ap, obuf=None):
                pre, _ = STB.next()
                evac(S, "act", pre[0:m, 0:n], ps[0:m, 0:n], [ps], [pre])
                ct, cs = TAB.next()
                S.dma("sp", cs, ct[0:m, 0:n], C[cosn][:, t0:t0 + n], writes=[ct])
                stt, ss = TAB.next()
                S.dma("sp", ss, stt[0:m, 0:n], C[sinn][:, t0:t0 + n], writes=[stt])
                ps2, _ = PS.next()
                S.mm(lambda e: e.matmul(ps2[0:m, 0:n], lhsT=perm[:], rhs=pre[0:m, 0:n], start=True, stop=True),
                     reads=[perm, pre], writes=[ps2])
                a, _ = STF.next()
                S.op("dve", lambda e: e.tensor_tensor(out=a[0:m, 0:n], in0=pre[0:m, 0:n], in1=ct[0:m, 0:n], op=ALU.mult),
                     reads=[pre, ct], writes=[a])
                b, _ = STF.next()
                S.op("dve", lambda e: e.tensor_tensor(out=b[0:m, 0:n], in0=ps2[0:m, 0:n], in1=stt[0:m, 0:n], op=ALU.mult),
                     reads=[ps2, stt], writes=[b])
                if obuf is not None:
                    o, osm = obuf, None
                else:
                    o, osm = STB.next()
                S.op("dve", lambda e: e.tensor_tensor(out=o[0:m, 0:n], in0=a[0:m, 0:n], in1=b[0:m, 0:n], op=ALU.add),
                     reads=[a, b], writes=[o])
                if dst is not None:
                    store(dst, dst_ap, o, o[0:m, 0:n], osm)
                return o

            def simple_group(col0, nchunks, dst, func=AF.Copy, scale=1.0):
                for c0 in range(0, nchunks, 4):
                    ncn = min(4, nchunks - c0)
                    wb = load_w(col0 + c0 * 128, ncn * 128)
                    for (t0, n) in seg:
                        for j in range(ncn):
                            ps, _ = PS.next()
                            proj_fm(ps, wb, j * 128, 128, t0, n)
                            o, osm = STB.next()
                            evac(S, "act", o[:, 0:n], ps[:, 0:n], [ps], [o], func=func, scale=scale)
                            store(dst, dst[:][:, c0 + j, t0:t0 + n], o, o[:, 0:n], osm)

            if p.stop == "hT":
                return
            wb, ws = WP.next()
            for half in range(2):
                for g in range(4):
                    c0 = O_AQ + (half * 4 + g) * 64
                    S.dma("pool", ws, wb[:, :, g * 128 + half * 64:g * 128 + (half + 1) * 64],
                          L["w_in"][:, c0:c0 + 64].rearrange("(k p) c -> p k c", p=128), writes=[wb])
            for (t0, n) in seg:
                for g in range(4):
                    ps, _ = PS.next()
                    proj_fm(ps, wb, g * 128, 128, t0, n)
                    rope(ps, 128, t0, n, p.permA, "cosA", "sinA", T["qaT"], T["qaT"][:][:, g, t0:t0 + n])
            if p.stop == "G1":
                return
            wb = load_w(O_AK, 256)
            for (t0, n) in seg:
                lo = loc[t0]
                ps, _ = PS.next()
                proj_fm(ps, wb, 0, 128, t0, n)
                rope(ps, 128, t0, n, p.permA, "cosA", "sinA", T["kaT"], T["kaT"][:][:, t0:t0 + n])
                for s0 in range(0, n, 128):
                    ps, _ = PS.next()
                    for k in range(8):
                        S.mm(lambda e, k=k: e.matmul(ps[:, 0:128], lhsT=hT[:, k, lo + s0:lo + s0 + 128], rhs=wb[:, k, 128:256],
                                                     start=(k == 0), stop=(k == 7)),
                             reads=[wb, hT], writes=[ps], last=(k == 7))
                    S.op("act", lambda e: e.activation(out=vst[:, 0:2, 0:64],
                                                       in_=ps[:, 0:128].rearrange("p (h c) -> p h c", c=64), func=AF.Copy),
                         reads=[ps], writes=[vst])
                    store(T["vaA"], T["vaA"][:][t0 + s0:t0 + s0 + 128], vst, vst[:, 0:2, :], vsem)
            if p.stop == "G3":
                return
            wb = load_w(O_BQ, 256)
            wb2 = load_w(O_BKV, 256)
            for (t0, n) in seg:
                for c in range(2):
                    ps, _ = PS.next()
                    proj_fm(ps, wb, c * 128, 128, t0, n)
                    S.op("dve", lambda e, c=c: e.tensor_copy(out=cq[:, c, 0:n], in_=ps[:, 0:n]), reads=[ps], writes=[cq])
                    S.op("act", lambda e, c=c: e.activation(out=sq2[:, c, 0:n], in_=cq[:, c, 0:n], func=AF.Square),
                         reads=[cq], writes=[sq2])
                ps, _ = PS.next()
                for c in range(2):
                    S.mm(lambda e, c=c: e.matmul(ps[:, 0:n], lhsT=ones_bf[:], rhs=sq2[:, c, 0:n], start=(c == 0), stop=(c == 1)),
                         reads=[ones_bf, sq2], writes=[ps], last=(c == 1))
                rstd_from_ps(S, ps, n, 256, eps_t, rs2)
                for c in range(2):
                    S.op("dve", lambda e, c=c: e.scalar_tensor_tensor(out=cqn[:, c, 0:n], in0=cq[:, c, 0:n], scalar=bqn[:, c:c + 1],
                                                                      in1=rs2[:, 0:n], op0=ALU.mult, op1=ALU.mult),
                         reads=[cq, bqn, rs2], writes=[cqn])
                for h in range(8):
                    ps, _ = PS.next()
                    for c in range(2):
                        S.mm(lambda e, c=c, h=h: e.matmul(ps[0:96, 0:n], lhsT=wuq[:, c, h * 96:(h + 1) * 96], rhs=cqn[:, c, 0:n],
                                                          start=(c == 0), stop=(c == 1)),
                             reads=[wuq, cqn], writes=[ps], last=(c == 1))
                    rope(ps, 96, t0, n, p.permB, "cosB", "sinB", T["qbT"], T["qbT"][:][:, h, t0:t0 + n])
                if p.stop == "G4":
                    return
                ps, _ = PS.next()
                if p.stop == "G5x":
                    return
                proj_fm(ps, wb2, 0, 128, t0, n)
                if p.stop == "G5a0":
                    return
                S.op("dve", lambda e: e.tensor_copy(out=cq[:, 0, 0:n], in_=ps[:, 0:n]), reads=[ps], writes=[cq])
                S.op("act", lambda e: e.activation(out=sq2[:, 0, 0:n], in_=cq[:, 0, 0:n], func=AF.Square), reads=[cq], writes=[sq2])
                if p.stop in ("G5a1", "G5nosq"):
                    return
                ps, _ = PS.next()
                S.mm(lambda e: e.matmul(ps[:, 0:n], lhsT=ones_bf[:], rhs=sq2[:, 0, 0:n], start=True, stop=True),
                     reads=[ones_bf, sq2], writes=[ps])
                if p.stop == "G5a2":
                    return
                rstd_from_ps(S, ps, n, 128, eps_t, rs2)
                if p.stop == "G5a3":
                    return
                S.op("dve", lambda e: e.scalar_tensor_tensor(out=ckvn[:, 0:n], in0=cq[:, 0, 0:n], scalar=bkvn[:, 0:1],
                                                             in1=rs2[:, 0:n], op0=ALU.mult, op1=ALU.mult),
                     reads=[cq, bkvn, rs2], writes=[ckvn])
                if p.stop == "G5a":
                    return
                ps, _ = PS.next()
                proj_fm(ps, wb2, 128, 128, t0, n)
                if p.stop == "G5":
                    return
                kr = rope(ps, 32, t0, n, p.permK, "cosK", "sinK", None, None, obuf=krb)
                if p.stop == "G5b":
                    return
                for h in range(8):
                    ps, _ = PS.next()
                    S.mm(lambda e, h=h: e.matmul(ps[0:96, 0:n], lhsT=wkn[:, h, :], rhs=ckvn[:, 0:n], start=True, stop=False),
                         reads=[wkn, ckvn], writes=[ps], last=False)
                    S.mm(lambda e: e.matmul(ps[0:96, 0:n], lhsT=p.sel[:], rhs=kr[0:32, 0:n], start=False, stop=True),
                         reads=[p.sel, kr], writes=[ps], last=True)
                    o, osm = STB.next()
                    evac(S, "act", o[0:96, 0:n], ps[0:96, 0:n], [ps], [o])
                    store(T["kbT"], T["kbT"][:][:, h, t0:t0 + n], o, o[0:96, 0:n], osm)
                if p.stop == "G5c":
                    return
                for s0 in range(0, n, 128):
                    ps, _ = PS.next()
                    S.mm(lambda e: e.matmul(ps[:, 0:512], lhsT=ckvn[:, s0:s0 + 128], rhs=wvb[:].rearrange("p h c -> p (h c)"),
                                            start=True, stop=True), reads=[ckvn, wvb], writes=[ps])
                    S.op("act", lambda e: e.activation(out=vst[:, :, 0:64],
                                                       in_=ps[:, 0:512].rearrange("p (h c) -> p h c", c=64), func=AF.Copy),
                         reads=[ps], writes=[vst])
                    store(T["vbA"], T["vbA"][:][t0 + s0:t0 + s0 + 128], vst, vst[:, :, :], vsem)
            if p.stop == "G6":
                return
            simple_group(O_CQ, 2, T["cqT"], scale=0.125)
            simple_group(O_CK, 2, T["ckT"])
            wb = load_w(O_CV, 512)
            for (t0, n) in seg:
                lo = loc[t0]
                for s0 in range(0, n, 128):
                    ps, _ = PS.next()
                    for k in range(8):
                        S.mm(lambda e, k=k: e.matmul(ps[:, 0:512], lhsT=hT[:, k, lo + s0:lo + s0 + 128], rhs=wb[:, k, 0:512],
                                                     start=(k == 0), stop=(k == 7)),
                             reads=[wb, hT], writes=[ps], last=(k == 7))
                    evac(S, "act", cvst[:, :], ps[:, 0:512], [ps], [cvst])
                    store(T["cvv"], T["cvv"][:][t0 + s0:t0 + s0 + 128, :], cvst, cvst[:, :], csem)
            wb = load_w(O_CG, 128)
            for (t0, n) in seg:
                ps, _ = PS.next()
                proj_fm(ps, wb, 0, 128, t0, n)
                evac(S, "act", cg[:, 0:n], ps[0:32, 0:n], [ps], [cg])
                for dr, dst in ((0, T["gfT"]), (1, T["gbT"])):
                    for pr in range(2):
                        ps, _ = PS.next()
                        S.mm(lambda e, dr=dr, pr=pr: e.matmul(ps[:, 0:n], lhsT=wg[:, dr, pr * 128:(pr + 1) * 128], rhs=cg[:, 0:n],
                                                              start=True, stop=True), reads=[wg, cg], writes=[ps])
                        o, osm = STF.next()
                        S.op("act", lambda e, dr=dr, pr=pr: e.activation(out=o[:, 0:n], in_=ps[:, 0:n], func=AF.Exp, scale=-1.0,
                                                                         bias=nbg[:, dr, pr:pr + 1]), reads=[ps, nbg], writes=[o])
                        S.op("act", lambda e: e.activation(out=o[:, 0:n], in_=o[:, 0:n], func=AF.Ln, bias=one_t[:]),
                             reads=[o, one_t], writes=[o])
                        S.op("dve", lambda e: e.tensor_scalar_mul(out=o[:, 0:n], in0=o[:, 0:n], scalar1=-1.0 / 16.0),
                             reads=[o], writes=[o])
                        store(dst, dst[:][:, pr, t0:t0 + n], o, o[:, 0:n], osm)
            if p.stop == "G11":
                return
            simple_group(O_CR, 4, T["crT"], func=AF.Silu)
            simple_group(O_GATE, 24, T["gatesT"], func=AF.Sigmoid)


def phase_B1(p, L):
    S, T, C = p.S, p.T, p.C
    with S.scope():
        ld = S.dma_sem("ld")
        ka = S.sbuf("ka", [128, NTOT], BF16)
        va = S.sbuf("va", [128, NKB, 2, 128], BF16)
        S.dma("sp", ld, ka[:], T["kaT"][:], reads=[T["kaT"]], writes=[ka])
        for i in range(0, NKB, 6):
            S.dma("sp", ld, va[:, i:i + 6], T["vaA"][:].rearrange("(b p) h c -> p b h c", p=128)[:, i:i + 6],
                  reads=[T["vaA"]], writes=[va])
        sk = S.sbuf("sk", [1, 8], F32)
        S.dma("sp", ld, sk[:], L["a_sink"], writes=[sk])
        zrow = S.sbuf("zrow", [1, 128], F32)
        S.op("dve", lambda e: e.memset(zrow[:], 0.0), writes=[zrow])
        esrow = S.sbuf("esrow", [1, 2, 512], BF16)
        for h in range(8):
            S.op("act", lambda e, h=h: e.activation(out=esrow[0:1, h // 4, (h % 4) * 128:(h % 4 + 1) * 128], in_=zrow[:],
                                                    func=AF.Exp, bias=sk[0:1, h:h + 1]), reads=[zrow, sk], writes=[esrow])
        PSS = Rot(S, "pss", 5, [128, 512], F32, psum=True, dma=False)
        PSO = Rot(S, "pso", 2, [128, 512], F32, psum=True, dma=False)
        QT = Rot(S, "qt", 2, [128, 4, 512], BF16)
        PT = Rot(S, "pt", 6, [128, 512], BF16, dma=False)
        RC = Rot(S, "rc", 2, [64, 512], F32, dma=False)
        OS = Rot(S, "os", 3, [64, 512], BF16)
        scale = 64 ** -0.5

        for (t0, n) in TT:
            qt, qsem = QT.next()
            S.dma("sp", qsem, qt[:, :, 0:n], T["qaT"][:][:, :, t0:t0 + n], reads=[T["qaT"]], writes=[qt])
            for qb in range(n // 128):
                q0 = t0 + qb * 128
                blk = q0 // 128
                if t0 == 0:
                    kbs = [(0, None), (1, None)]
                else:
                    kbs = [(0, None), (1, None)]
                    if blk - 1 >= 2:
                        kbs.append((blk - 1, "maskLo"))
                    kbs.append((blk, None))
                    if blk + 1 < NKB:
                        kbs.append((blk + 1, "maskHi"))
                for kvh in range(2):
                    pb = kvh * 64
                    pso, _ = PSO.next()
                    pend = []
                    for i, (kb, msk) in enumerate(kbs):
                        pss, _ = PSS.next()
                        S.mm(lambda e, kb=kb, pss=pss: e.matmul(pss[:, 0:512].rearrange("p (g q) -> p g q", g=4),
                                                                lhsT=ka[pb:pb + 64, kb * 128:(kb + 1) * 128],
                                                                rhs=qt[pb:pb + 64, :, qb * 128:(qb + 1) * 128], start=True, stop=True),
                             reads=[ka, qt], writes=[pss])
                        pt, _ = PT.next()
                        S.op("act", lambda e, pss=pss, pt=pt: e.activation(out=pt[:, :], in_=pss[:, :], func=AF.Exp, scale=scale),
                             reads=[pss], writes=[pt])
                        if msk is not None:
                            mk = p.maskLo if msk == "maskLo" else p.maskHi
                            S.op("dve", lambda e, mk=mk, pt=pt: e.tensor_tensor(out=pt[:, :], in0=pt[:, :], in1=mk[:, :], op=ALU.mult),
                                 reads=[pt, mk], writes=[pt])
                        pend.append((kb, pt))
                    for i, (kb, pt) in enumerate(pend):
                        S.mm(lambda e, kb=kb, i=i, pt=pt: e.matmul(pso[:, :], lhsT=va[:, kb, kvh, :], rhs=pt[:, :], start=(i == 0), stop=False),
                             reads=[va, pt], writes=[pso], last=False)
                    S.mm(lambda e: e.matmul(pso[:, :], lhsT=p.sv[:], rhs=esrow[0:1, kvh, :], start=False, stop=True),
                         reads=[p.sv, esrow], writes=[pso], last=True)
                    rc, _ = RC.next()
                    S.op("dve", lambda e: e.reciprocal(out=rc[:, :], in_=pso[64:128, :]), reads=[pso], writes=[rc])
                    o, osm = OS.next()
                    S.op("dve", lambda e: e.tensor_tensor(out=o[:, :], in0=pso[0:64, :], in1=rc[:, :], op=ALU.mult),
                         reads=[pso, rc], writes=[o])
                    for gp in range(2):
                        S.dma("act", osm, T["yaT"][:][gp * 64:(gp + 1) * 64, kvh * 2:kvh * 2 + 2, q0:q0 + 128],
                              o[:, :].rearrange("p (g2 gp q) -> p g2 gp q", g2=2, gp=2)[:, :, gp, :],
                              reads=[o], writes=[T["yaT"]])


def phase_B2(p, L):
    S, T, C = p.S, p.T, p.C
    with S.scope():
        KS = Rot(S, "ks", 2, [96, NTOT], BF16)
        VS = Rot(S, "vs", 2, [128, NKB, 128], BF16)
        PSS = Rot(S, "pss", 5, [128, 512], F32, psum=True, dma=False)
        PSO = Rot(S, "pso", 2, [128, 512], F32, psum=True, dma=False)
        QT = Rot(S, "qt", 2, [96, 512], BF16)
        PT = Rot(S, "pt", 6, [128, 512], BF16, dma=False)
        RC = Rot(S, "rc", 2, [64, 512], F32, dma=False)
        OS = Rot(S, "os", 3, [64, 512], BF16)
        scale = 96 ** -0.5
        for h in range(8):
            ks, ksem = KS.next()
            vs, vsem = VS.next()
            S.dma("sp", ksem, ks[:], T["kbT"][:][:, h, :], reads=[T["kbT"]], writes=[ks])
            for i in range(0, NKB, 6):
                S.dma("sp", vsem, vs[:, i:i + 6], T["vbA"][:].rearrange("(b p) h c -> p b h c", p=128)[:, i:i + 6, h, :],
                      reads=[T["vbA"]], writes=[vs])
            for (t0, n) in TT:
                qt, qsem = QT.next()
                S.dma("sp", qsem, qt[:, 0:n], T["qbT"][:][:, h, t0:t0 + n], reads=[T["qbT"]], writes=[qt])
                nkb = 2 if t0 == 0 else NKB
                pso, _ = PSO.next()
                LOOK = 3
                pend = []
                for kb in range(nkb + LOOK):
                    if kb < nkb:
                        pss, _ = PSS.next()
                        S.mm(lambda e, kb=kb, pss=pss: e.matmul(pss[:, 0:n], lhsT=ks[:, kb * 128:(kb + 1) * 128], rhs=qt[:, 0:n],
                                                                start=True, stop=True), reads=[ks, qt], writes=[pss])
                        pt, _ = PT.next()
                        S.op("act", lambda e, pss=pss, pt=pt: e.activation(out=pt[:, 0:n], in_=pss[:, 0:n], func=AF.Exp, scale=scale),
                             reads=[pss], writes=[pt])
                        pend.append(pt)
                    if kb >= LOOK:
                        k2 = kb - LOOK
                        pt2 = pend.pop(0)
                        S.mm(lambda e, k2=k2, pt2=pt2: e.matmul(pso[:, 0:n], lhsT=vs[:, k2, :], rhs=pt2[:, 0:n], start=(k2 == 0),
                                                                stop=(k2 == nkb - 1)), reads=[vs, pt2], writes=[pso], last=(k2 == nkb - 1))
                rc, _ = RC.next()
                S.op("dve", lambda e: e.reciprocal(out=rc[:, 0:n], in_=pso[64:128, 0:n]), reads=[pso], writes=[rc])
                o, osm = OS.next()
                S.op("dve", lambda e: e.tensor_tensor(out=o[:, 0:n], in0=pso[0:64, 0:n], in1=rc[:, 0:n], op=ALU.mult),
                     reads=[pso, rc], writes=[o])
                S.dma("act", osm, T["ybT"][:][(h % 2) * 64:(h % 2 + 1) * 64, h // 2, t0:t0 + n], o[:, 0:n],
                      reads=[o], writes=[T["ybT"]])


def phase_B3(p, L):
    S, T, C = p.S, p.T, p.C
    eps_t, ones_bf = p.eps_t, p.ones_bf
    NCH = NTOT // 64
    PIECE = 2112
    with S.scope():
        ld = S.dma_sem("ld")
        hn = S.sbuf("hn", [128, 4], F32)
        S.dma("sp", ld, hn[:], L["c_hn"], writes=[hn])
        vsb = S.sbuf("vsb", [64, NCH, 128], BF16)
        qtb = S.sbuf("qtb", [64, NTOT], BF16)
        ktT = S.sbuf("ktT", [64, NCH, 64], BF16)
        attT = S.sbuf("attT", [64, NCH, 64], BF16)
        ebl = S.sbuf("ebl", [64, NCH], F32)
        oT = S.sbuf("oT", [128, NTOT], F32)
        qp = S.sbuf("qp", [64, PIECE], BF16)
        kp = S.sbuf("kp", [64, PIECE], BF16)
        gp = S.sbuf("gp", [64, PIECE], F32)
        bp = S.sbuf("bp", [64, PIECE], F32)
        rm = S.sbuf("rm", [64, PIECE], F32)
        ep = S.sbuf("ep", [64, PIECE], BF16)
        ktp = S.sbuf("ktp", [64, PIECE], BF16)
        S.dma("sp", ld, rm[:], C["rmask"][:, 0:PIECE], writes=[rm])
        Sf = [S.sbuf("Sf%d" % i, [64, 128], F32) for i in range(2)]
        Sb = [S.sbuf("Sb%d" % i, [64, 128], BF16) for i in range(2)]
        S1l = [S.sbuf("S1%d" % i, [64, 128], F32) for i in range(2)]
        PST = Rot(S, "pst", 2, [64, 512], BF16, psum=True, dma=False)
        PSA = Rot(S, "psa", 2, [64, 512], F32, psum=True, dma=False)
        PSO = Rot(S, "pso", 2, [128, 512], F32, psum=True, dma=False)
        PSK = Rot(S, "psk", 2, [64, 128], F32, psum=True, dma=False)
        sqb = S.sbuf("sqb", [128, 512], BF16)
        rs = S.sbuf("rs", [128, 512], F32)
        tt = S.sbuf("tt", [128, 512], F32)
        CR = Rot(S, "cr", 2, [128, 512], BF16)
        YO = Rot(S, "yo", 2, [128, 512], BF16)

        for h in range(4):
            hb = (h % 2) * 64
            for i in range(0, NCH, 11):
                S.dma("sp", ld, vsb[:, i:i + 11, :],
                      T["cvv"][:].rearrange("(c j) (h e) -> j c h e", j=64, e=128)[:, i:i + 11, h, :],
                      reads=[T["cvv"]], writes=[vsb])
            for dr in range(2):
                gsrc = T["gfT"] if dr == 0 else T["gbT"]
                tri = p.triF if dr == 0 else p.triB
                for pc in range(NTOT // PIECE):
                    a0 = pc * PIECE
                    c0 = a0 // 64
                    S.dma("sp", ld, qp[:], T["cqT"][:][hb:hb + 64, h // 2, a0:a0 + PIECE], reads=[T["cqT"]], writes=[qp])
                    S.dma("sp", ld, kp[:], T["ckT"][:][hb:hb + 64, h // 2, a0:a0 + PIECE], reads=[T["ckT"]], writes=[kp])
                    S.dma("sp", ld, gp[:], gsrc[:][hb:hb + 64, h // 2, a0:a0 + PIECE], reads=[gsrc], writes=[gp])
                    S.op("dve", lambda e: e.tensor_tensor_scan(out=bp[:], data0=rm[:], data1=gp[:], initial=0.0,
                                                               op0=ALU.mult, op1=ALU.add), reads=[rm, gp], writes=[bp])
                    if dr == 1:
                        b3 = bp[:].rearrange("p (c j) -> p c j", j=64)
                        S.op("dve", lambda e: e.tensor_tensor(out=gp[:], in0=gp[:], in1=bp[:], op=ALU.subtract),
                             reads=[gp, bp], writes=[gp])
                        S.op("dve", lambda e: e.tensor_tensor(out=bp[:].rearrange("p (c j) -> p c j", j=64),
                                                              in0=gp[:].rearrange("p (c j) -> p c j", j=64),
                                                              in1=b3[:, :, 63:64].to_broadcast([64, PIECE // 64, 64]), op=ALU.add),
                             reads=[gp, bp], writes=[bp])
                    last_col = 63 if dr == 0 else 0
                    S.op("act", lambda e: e.activation(out=ebl[:, c0:c0 + PIECE // 64],
                                                       in_=bp[:].rearrange("p (c j) -> p c j", j=64)[:, :, last_col],
                                                       func=AF.Exp), reads=[bp], writes=[ebl])
                    S.op("act", lambda e: e.activation(out=ep[:], in_=bp[:], func=AF.Exp), reads=[bp], writes=[ep])
                    S.op("dve", lambda e: e.tensor_tensor(out=qtb[:, a0:a0 + PIECE], in0=qp[:], in1=ep[:], op=ALU.mult),
                         reads=[qp, ep], writes=[qtb])
                    S.op("act", lambda e: e.activation(out=ep[:], in_=bp[:], func=AF.Exp, scale=-1.0), reads=[bp], writes=[ep])
                    S.op("dve", lambda e: e.tensor_tensor(out=ktp[:], in0=kp[:], in1=ep[:], op=ALU.mult),
                         reads=[kp, ep], writes=[ktp])
                    for c8 in range(0, PIECE // 64, 8):
                        nn = min(8, PIECE // 64 - c8)
                        pst, _ = PST.next()
                        psa, _ = PSA.next()
                        for j in range(nn):
                            cl = c8 + j
                            S.mm(lambda e, cl=cl, j=j: e.transpose(out=pst[:, j * 64:(j + 1) * 64], in_=ktp[:, cl * 64:(cl + 1) * 64],
                                                                   identity=p.ident[:]),
                                 reads=[ktp, p.ident], writes=[pst], last=(j == nn - 1))
                        for j in range(nn):
                            cl = c8 + j
                            S.mm(lambda e, cl=cl, j=j: e.matmul(psa[:, j * 64:(j + 1) * 64], lhsT=ktp[:, cl * 64:(cl + 1) * 64],
                                                                rhs=qtb[:, a0 + cl * 64:a0 + (cl + 1) * 64], start=True, stop=True),
                                 reads=[ktp, qtb], writes=[psa], last=(j == nn - 1))
                        S.op("act", lambda e: e.activation(out=ktT[:, c0 + c8:c0 + c8 + nn, :].rearrange("p c d -> p (c d)"),
                                                           in_=pst[:, 0:nn * 64], func=AF.Copy), reads=[pst], writes=[ktT])
                        S.op("dve", lambda e: e.tensor_tensor(out=attT[:, c0 + c8:c0 + c8 + nn, :].rearrange("p c d -> p (c d)"),
                                                              in0=psa[:, 0:nn * 64], in1=tri[:, 0:nn * 64], op=ALU.mult),
                             reads=[psa, tri], writes=[attT])
                order = list(range(NCH)) if dr == 0 else ([3, 2, 1, 0] + list(range(NCH - 1, 3, -1)))
                cur = 0
                S.op("dve", lambda e: e.memset(Sf[0][:], 0.0), writes=[Sf[0]])
                S.op("dve", lambda e: e.memset(Sb[0][:], 0.0), writes=[Sb[0]])
                groups = [order[0:4]] + [order[i:i + 8] for i in range(4, NCH, 8)]
                for grp in groups:
                    ng = len(grp)
                    lo_c = min(grp)
                    pso, _ = PSO.next()
                    for j, c in enumerate(grp):
                        col = (c - lo_c) * 64
                        psk, _ = PSK.next()
                        S.mm(lambda e, c=c, psk=psk: e.matmul(psk[:, :], lhsT=ktT[:, c, :], rhs=vsb[:, c, :], start=True, stop=True),
                             reads=[ktT, vsb], writes=[psk])
                        S.mm(lambda e, c=c, col=col: e.matmul(pso[:, col:col + 64], lhsT=vsb[:, c, :], rhs=attT[:, c, :],
                                                              start=True, stop=False),
                             reads=[vsb, attT], writes=[pso], last=False)
                        S.mm(lambda e, c=c, col=col, cur=cur: e.matmul(pso[:, col:col + 64], lhsT=Sb[cur][:],
                                                                       rhs=qtb[:, c * 64:(c + 1) * 64], start=False, stop=True),
                             reads=[Sb[cur], qtb], writes=[pso], last=True)
                        nxt = 1 - cur
                        S1 = S1l[cur]
                        S.op("dve", lambda e, cur=cur, S1=S1, psk=psk: e.tensor_tensor(out=S1[:], in0=psk[:, :], in1=Sf[cur][:], op=ALU.add),
                             reads=[psk, Sf[cur]], writes=[S1])
                        S.op("dve", lambda e, c=c, nxt=nxt, S1=S1: e.tensor_scalar_mul(out=Sf[nxt][:], in0=S1[:], scalar1=ebl[:, c:c + 1]),
                             reads=[S1, ebl], writes=[Sf[nxt]])
                        S.op("act", lambda e, c=c, nxt=nxt, S1=S1: e.activation(out=Sb[nxt][:], in_=S1[:], func=AF.Copy,
                                                                                scale=ebl[:, c:c + 1]),
                             reads=[S1, ebl], writes=[Sb[nxt]])
                        cur = nxt
                    w = ng * 64
                    if dr == 0:
                        evac(S, "act", oT[:, lo_c * 64:lo_c * 64 + w], pso[:, 0:w], [pso], [oT])
                    else:
                        S.op("dve", lambda e, lo_c=lo_c, w=w: e.tensor_tensor(out=oT[:, lo_c * 64:lo_c * 64 + w],
                                                                             in0=oT[:, lo_c * 64:lo_c * 64 + w],
                                                                             in1=pso[:, 0:w], op=ALU.add),
                             reads=[oT, pso], writes=[oT])
            for (t0, n) in TT:
                S.op("act", lambda e: e.activation(out=sqb[:, 0:n], in_=oT[:, t0:t0 + n], func=AF.Square), reads=[oT], writes=[sqb])
                pso, _ = PSO.next()
                S.mm(lambda e: e.matmul(pso[:, 0:n], lhsT=ones_bf[:], rhs=sqb[:, 0:n], start=True, stop=True),
                     reads=[ones_bf, sqb], writes=[pso])
                rstd_from_ps(S, pso, n, 128, eps_t, rs)
                cr, crs = CR.next()
                S.dma("sp", crs, cr[:, 0:n], T["crT"][:][:, h, t0:t0 + n], reads=[T["crT"]], writes=[cr])
                S.op("dve", lambda e: e.scalar_tensor_tensor(out=tt[:, 0:n], in0=oT[:, t0:t0 + n], scalar=hn[:, h:h + 1],
                                                             in1=rs[:, 0:n], op0=ALU.mult, op1=ALU.mult),
                     reads=[oT, hn, rs], writes=[tt])
                yo, ys = YO.next()
                S.op("dve", lambda e: e.tensor_tensor(out=yo[:, 0:n], in0=tt[:, 0:n], in1=cr[:, 0:n], op=ALU.mult),
                     reads=[tt, cr], writes=[yo])
                S.dma("act", ys, T["ycT"][:][:, h, t0:t0 + n], yo[:, 0:n], reads=[yo], writes=[T["ycT"]])


def phase_C1(p, L, Xin):
    S, T, C = p.S, p.T, p.C
    eps_t, ones_bf = p.eps_t, p.ones_bf
    with S.scope():
        ld = S.dma_sem("ld")
        modT = S.sbuf("modT", [128, 48, 2], F32)
        S.dma("sp", ld, modT[:], T["modT"][:], reads=[T["modT"]], writes=[modT])
        nffn = S.sbuf("nffn", [128, 8], F32)
        S.dma("sp", ld, nffn[:], L["norm_ffn"], writes=[nffn])
        wbr = [S.sbuf("wbr%d" % i, [128, 4, D], BF16) for i in range(3)]
        for i, nm in enumerate(("w_br_a", "w_br_b", "w_br_c")):
            S.dma("pool", ld, wbr[i][:], L[nm].rearrange("(k p) n -> p k n", p=128), writes=[wbr[i]])
        wo = S.sbuf("wo", [128, 8, D], BF16)
        for k0 in range(0, 8, 4):
            S.dma("pool", ld, wo[:, k0:k0 + 4, :], L["w_out"].rearrange("(k p) n -> p k n", p=128)[:, k0:k0 + 4, :], writes=[wo])
        A2 = S.sbuf("A2", [128, 8, 2], F32)
        S.op("dve", lambda e: e.tensor_scalar_add(out=A2[:], in0=modT[:, 32:40, :], scalar1=1.0), reads=[modT], writes=[A2])
        S.op("dve", lambda e: e.tensor_tensor(out=A2[:], in0=A2[:], in1=nffn[:].unsqueeze(2).to_broadcast([128, 8, 2]),
                                              op=ALU.mult), reads=[A2, nffn], writes=[A2])
        PS = Rot(S, "ps", 7, [128, 512], F32, psum=True, dma=False)
        YA = Rot(S, "ya", 2, [128, 12, 512], BF16)
        GT = Rot(S, "gt", 1, [128, 24, 512], BF16)
        XT = Rot(S, "xt", 2, [128, 8, 512], F32)
        mT = S.sbuf("mT", [128, 8, 512], BF16)
        t1 = S.sbuf("t1", [128, 512], F32)
        t2 = S.sbuf("t2", [128, 512], F32)
        sq = S.sbuf("sq", [128, 8, 512], BF16)
        rstd = S.sbuf("rstd", [128, 512], F32)
        xs = S.sbuf("xs", [128, 512], F32)
        H2 = Rot(S, "h2", 1, [128, 8, 512], BF16)
        for (t0, n) in TT:
            v = 1 if t0 == 0 else 0
            ya, yas = YA.next()
            for i, nm in enumerate(("yaT", "ybT", "ycT")):
                S.dma("sp", yas, ya[:, i * 4:(i + 1) * 4, 0:n], T[nm][:][:, :, t0:t0 + n], reads=[T[nm]], writes=[ya])
            gt, gts = GT.next()
            for i in range(3):
                S.dma("sp", gts, gt[:, i * 8:(i + 1) * 8, 0:n], T["gatesT"][:][:, i * 8:(i + 1) * 8, t0:t0 + n],
                      reads=[T["gatesT"]], writes=[gt])
            xb, xsem = XT.next()
            S.dma("sp", xsem, xb[:, :, 0:n], Xin[:][:, t0:t0 + n].rearrange("(k p) n -> p k n", p=128), reads=[Xin], writes=[xb])
            for m in range(8):
                pss = []
                for i in range(3):
                    ps, _ = PS.next()
                    for k in range(4):
                        S.mm(lambda e, i=i, k=k, m=m: e.matmul(ps[:, 0:n], lhsT=wbr[i][:, k, m * 128:(m + 1) * 128],
                                                               rhs=ya[:, i * 4 + k, 0:n], start=(k == 0), stop=(k == 3)),
                             reads=[wbr[i], ya], writes=[ps], last=(k == 3))
                    pss.append(ps)
                S.op("dve", lambda e, m=m: e.tensor_tensor(out=t1[:, 0:n], in0=pss[0][:, 0:n], in1=gt[:, m, 0:n], op=ALU.mult),
                     reads=[pss[0], gt], writes=[t1])
                S.op("dve", lambda e, m=m: e.tensor_tensor(out=t2[:, 0:n], in0=pss[1][:, 0:n], in1=gt[:, 8 + m, 0:n], op=ALU.mult),
                     reads=[pss[1], gt], writes=[t2])
                S.op("pool", lambda e: e.tensor_tensor(out=t1[:, 0:n], in0=t1[:, 0:n], in1=t2[:, 0:n], op=ALU.add),
                     reads=[t1, t2], writes=[t1])
                S.op("dve", lambda e, m=m: e.tensor_tensor(out=t2[:, 0:n], in0=pss[2][:, 0:n], in1=gt[:, 16 + m, 0:n], op=ALU.mult),
                     reads=[pss[2], gt], writes=[t2])
                S.op("pool", lambda e, m=m: e.tensor_tensor(out=mT[:, m, 0:n], in0=t1[:, 0:n], in1=t2[:, 0:n], op=ALU.add),
                     reads=[t1, t2], writes=[mT])
            for m in range(8):
                ps, _ = PS.next()
                for k in range(8):
                    S.mm(lambda e, k=k, m=m: e.matmul(ps[:, 0:n], lhsT=wo[:, k, m * 128:(m + 1) * 128], rhs=mT[:, k, 0:n],
                                                      start=(k == 0), stop=(k == 7)), reads=[wo, mT], writes=[ps], last=(k == 7))
                S.op("dve", lambda e, m=m: e.scalar_tensor_tensor(out=xb[:, m, 0:n], in0=ps[:, 0:n], scalar=modT[:, 16 + m, v:v + 1],
                                                                  in1=xb[:, m, 0:n], op0=ALU.mult, op1=ALU.add),
                     reads=[ps, modT, xb], writes=[xb])
            S.dma("act", xsem, T["x1T"][:][:, t0:t0 + n].rearrange("(k p) n -> p k n", p=128), xb[:, :, 0:n],
                  reads=[xb], writes=[T["x1T"]])
            S.op("act", lambda e: e.activation(out=sq[:, :, 0:n], in_=xb[:, :, 0:n], func=AF.Square), reads=[xb], writes=[sq])
            ps, _ = PS.next()
            for k in range(8):
                S.mm(lambda e, k=k: e.matmul(ps[:, 0:n], lhsT=ones_bf[:], rhs=sq[:, k, 0:n], start=(k == 0), stop=(k == 7)),
                     reads=[ones_bf, sq], writes=[ps], last=(k == 7))
            rstd_from_ps(S, ps, n, D, eps_t, rstd)
            h2, h2s = H2.next()
            for k in range(8):
                S.op("dve", lambda e, k=k: e.scalar_tensor_tensor(out=xs[:, 0:n], in0=xb[:, k, 0:n], scalar=A2[:, k, v:v + 1],
                                                                  in1=rstd[:, 0:n], op0=ALU.mult, op1=ALU.mult),
                     reads=[xb, A2, rstd], writes=[xs])
                S.op("act", lambda e, k=k: e.activation(out=h2[:, k, 0:n], in_=xs[:, 0:n], func=AF.Identity,
                                                        bias=modT[:, 24 + k, v:v + 1]), reads=[xs, modT], writes=[h2])
            S.dma("act", h2s, T["h2T"][:][:, :, t0:t0 + n], h2[:, :, 0:n], reads=[h2], writes=[T["h2T"]])


def phase_C2(p, L, Xout, final_out=None):
    S, T, C = p.S, p.T, p.C
    eps_t, ones_bf = p.eps_t, p.ones_bf
    with S.scope():
        ld = S.dma_sem("ld")
        modT = S.sbuf("modT", [128, 48, 2], F32)
        S.dma("sp", ld, modT[:], T["modT"][:], reads=[T["modT"]], writes=[modT])
        cw = S.sbuf("cw", [128, 44, 3], F32)
        cb = S.sbuf("cb", [128, 44], F32)
        S.dma("sp", ld, cw[:], L["conv_w"], writes=[cw])
        S.dma("sp", ld, cb[:], L["conv_b"], writes=[cb])
        fn = S.sbuf("fn", [128, 8], F32)
        S.dma("sp", ld, fn[:], C["final_norm"], writes=[fn])
        hsem = S.dma_sem("h2e")
        wdsem = S.dma_sem("wd")
        with S.scope():
            PS = Rot(S, "ps", 6, [128, 512], F32, psum=True, dma=False)
            PSH = Rot(S, "psh", 2, [128, 4], F32, psum=True, dma=False)
            h2e = S.sbuf("h2e", [128, 8, 2306], BF16)
            WU = Rot(S, "wu", 2, [128, 8, 256], BF16)
            UB = Rot(S, "ub", 4, [128, 514], F32, dma=False)
            ACC = Rot(S, "acc", 4, [128, 512], F32, dma=False)
            AO = Rot(S, "ao", 3, [128, 512], BF16)
            for seg in SEGS:
                base = seg[0][0]
                tot = sum(n for _, n in seg)
                seqs = []
                if base == 0:
                    seqs = [(0, CTX), (CTX, tot)]
                else:
                    seqs = [(base, base + tot)]
                S.op("dve", lambda e: e.memset(h2e[:, :, 0:1], 0.0), writes=[h2e])
                S.op("dve", lambda e: e.memset(h2e[:, :, tot + 1:tot + 2], 0.0), writes=[h2e])
                lo_tok = base - 1 if base > CTX else base
                hi_tok = base + tot + 1 if base + tot < NTOT else base + tot
                S.dma("sp", hsem, h2e[:, :, lo_tok - base + 1:hi_tok - base + 1], T["h2T"][:][:, :, lo_tok:hi_tok],
                      reads=[T["h2T"]], writes=[h2e])
                for c in range(22):
                    wu, wus = WU.next()
                    S.dma("pool", wus, wu[:, :, 0:128], L["w_up"][:, c * 128:(c + 1) * 128].rearrange("(k p) n -> p k n", p=128),
                          writes=[wu])
                    S.dma("pool", wus, wu[:, :, 128:256],
                          L["w_up"][:, FFN + c * 128:FFN + (c + 1) * 128].rearrange("(k p) n -> p k n", p=128), writes=[wu])
                    for (t0, n) in seg:
                        lo = t0 - base + 1
                        accs = []
                        for half in range(2):
                            ci = c + 22 * half
                            ps, _ = PS.next()
                            for k in range(8):
                                S.mm(lambda e, k=k, half=half: e.matmul(ps[:, 0:n], lhsT=wu[:, k, half * 128:(half + 1) * 128],
                                                                        rhs=h2e[:, k, lo:lo + n], start=(k == 0), stop=(k == 7)),
                                     reads=[wu, h2e], writes=[ps], last=(k == 7))
                            psh, _ = PSH.next()
                            lcol = lo - 1
                            rcol = lo + n
                            lz = (t0 == 0) or (t0 == CTX)
                            rz = (t0 + n == CTX) or (t0 + n == NTOT)
                            for k in range(8):
                                S.mm(lambda e, k=k, half=half: e.matmul(psh[:, 0:1], lhsT=wu[:, k, half * 128:(half + 1) * 128],
                                                                        rhs=h2e[:, k, lcol:lcol + 1], start=(k == 0), stop=(k == 7)),
                                     reads=[wu, h2e], writes=[psh], last=False)
                            for k in range(8):
                                S.mm(lambda e, k=k, half=half: e.matmul(psh[:, 1:2], lhsT=wu[:, k, half * 128:(half + 1) * 128],
                                                                        rhs=h2e[:, k, rcol:rcol + 1], start=(k == 0), stop=(k == 7)),
                                     reads=[wu, h2e], writes=[psh], last=(k == 7))
                            ub, _ = UB.next()
                            evac(S, "act", ub[:, 1:n + 1], ps[:, 0:n], [ps], [ub])
                            if lz:
                                S.op("dve", lambda e: e.memset(ub[:, 0:1], 0.0), writes=[ub])
                            else:
                                S.op("dve", lambda e: e.tensor_copy(out=ub[:, 0:1], in_=psh[:, 0:1]), reads=[psh], writes=[ub])
                            if rz:
                                S.op("dve", lambda e: e.memset(ub[:, n + 1:n + 2], 0.0), writes=[ub])
                            else:
                                S.op("dve", lambda e: e.tensor_copy(out=ub[:, n + 1:n + 2], in_=psh[:, 1:2]), reads=[psh], writes=[ub])
                            acc, _ = ACC.next()
                            eng = "dve" if half == 0 else "pool"
                            S.op(eng, lambda e, ci=ci: e.tensor_scalar(out=acc[:, 0:n], in0=ub[:, 0:n], scalar1=cw[:, ci, 0:1],
                                                                       scalar2=cb[:, ci:ci + 1], op0=ALU.mult, op1=ALU.add),
                                 reads=[ub, cw, cb], writes=[acc])
                            S.op("dve", lambda e, ci=ci: e.scalar_tensor_tensor(out=acc[:, 0:n], in0=ub[:, 1:n + 1], scalar=cw[:, ci, 1:2],
                                                                                in1=acc[:, 0:n], op0=ALU.mult, op1=ALU.add),
                                 reads=[ub, cw, acc], writes=[acc])
                            S.op("dve", lambda e, ci=ci: e.scalar_tensor_tensor(out=acc[:, 0:n], in0=ub[:, 2:n + 2], scalar=cw[:, ci, 2:3],
                                                                                in1=acc[:, 0:n], op0=ALU.mult, op1=ALU.add),
                                 reads=[ub, cw, acc], writes=[acc])
                            accs.append(acc)
                        S.op("act", lambda e: e.activation(out=accs[0][:, 0:n], in_=accs[0][:, 0:n], func=AF.Silu),
                             reads=[accs[0]], writes=[accs[0]])
                        ao, aos = AO.next()
                        S.op("pool", lambda e: e.tensor_tensor(out=ao[:, 0:n], in0=accs[0][:, 0:n], in1=accs[1][:, 0:n], op=ALU.mult),
                             reads=[accs[0], accs[1]], writes=[ao])
                        S.dma("act", aos, T["aT"][:][:, c, t0:t0 + n], ao[:, 0:n], reads=[ao], writes=[T["aT"]])
        with S.scope():
            PS = Rot(S, "ps", 7, [128, 512], F32, psum=True, dma=False)
            wd = S.sbuf("wd", [128, 22, D], BF16)
            for k0 in range(0, 22, 2):
                S.dma("pool", wdsem, wd[:, k0:k0 + 2, :], L["w_down"].rearrange("(k p) n -> p k n", p=128)[:, k0:k0 + 2, :], writes=[wd])
            AT = Rot(S, "at", 2, [128, 22, 512], BF16)
            XT = Rot(S, "xt", 2, [128, 8, 512], F32)
            sq = S.sbuf("sq", [128, 8, 512], BF16)
            rstd = S.sbuf("rstd", [128, 512], F32)
            for (t0, n) in TT:
                v = 1 if t0 == 0 else 0
                at, ats = AT.next()
                for i in range(0, 22, 6):
                    i2 = min(22, i + 6)
                    S.dma("sp", ats, at[:, i:i2, 0:n], T["aT"][:][:, i:i2, t0:t0 + n], reads=[T["aT"]], writes=[at])
                xb, xsem = XT.next()
                S.dma("sp", xsem, xb[:, :, 0:n], T["x1T"][:][:, t0:t0 + n].rearrange("(k p) n -> p k n", p=128),
                      reads=[T["x1T"]], writes=[xb])
                for m in range(8):
                    ps, _ = PS.next()
                    for k in range(22):
                        S.mm(lambda e, k=k, m=m: e.matmul(ps[:, 0:n], lhsT=wd[:, k, m * 128:(m + 1) * 128], rhs=at[:, k, 0:n],
                                                          start=(k == 0), stop=(k == 21)), reads=[wd, at], writes=[ps], last=(k == 21))
                    S.op("dve", lambda e, m=m: e.scalar_tensor_tensor(out=xb[:, m, 0:n], in0=ps[:, 0:n], scalar=modT[:, 40 + m, v:v + 1],
                                                                      in1=xb[:, m, 0:n], op0=ALU.mult, op1=ALU.add),
                         reads=[ps, modT, xb], writes=[xb])
                if final_out is None:
                    S.dma("act", xsem, Xout[:][:, t0:t0 + n].rearrange("(k p) n -> p k n", p=128), xb[:, :, 0:n],
                          reads=[xb], writes=[Xout])
                elif t0 >= CTX:
                    S.op("act", lambda e: e.activation(out=sq[:, :, 0:n], in_=xb[:, :, 0:n], func=AF.Square), reads=[xb], writes=[sq])
                    ps, _ = PS.next()
                    for k in range(8):
                        S.mm(lambda e, k=k: e.matmul(ps[:, 0:n], lhsT=ones_bf[:], rhs=sq[:, k, 0:n], start=(k == 0), stop=(k == 7)),
                             reads=[ones_bf, sq], writes=[ps], last=(k == 7))
                    rstd_from_ps(S, ps, n, D, eps_t, rstd)
                    for k in range(8):
                        S.op("dve", lambda e, k=k: e.scalar_tensor_tensor(out=xb[:, k, 0:n], in0=xb[:, k, 0:n], scalar=fn[:, k:k + 1],
                                                                          in1=rstd[:, 0:n], op0=ALU.mult, op1=ALU.mult),
                             reads=[xb, fn, rstd], writes=[xb])
                    S.dma("act", xsem, final_out[:][:, t0 - CTX:t0 - CTX + n].rearrange("(k p) n -> p k n", p=128), xb[:, :, 0:n],
                          reads=[xb], writes=[final_out])


PHASES_ALL = ("A", "B1", "B2", "B3", "C1", "C2")
PHASE_W = {"A": ("w_mod", "b_mod", "norm_mix", "w_in", "bqn", "bkvn", "w_uq", "w_ukv", "w_gate", "b_gate"),
           "B1": ("a_sink",), "B2": (), "B3": ("c_hn",), "C1": ("norm_ffn", "w_br_a", "w_br_b", "w_br_c", "w_out"),
           "C2": ("conv_w", "conv_b", "w_up", "w_down")}


def needed_weights(phases):
    need = set()
    for ph in phases:
        need.update(PHASE_W[ph])
    return [w for w in LAYER_W if w[0] in need]


def build_program(nlayers, final, phases=PHASES_ALL, ext_scratch=(), stop=None):
    nc = bass.Bass("TRN2", target_bir_lowering=False)
    xin_ap = dram_in(nc, "xT", (D, NTOT), F32)
    Cap = {n: dram_in(nc, n, s, d) for n, s, d in CONSTS}
    Lw = [{n: dram_in(nc, "%s_%d" % (n, l), s, d) for n, s, d in needed_weights(phases)} for l in range(nlayers)]
    if final:
        out_ap = dram_out(nc, "outT", (D, SEQ), F32)
    else:
        out_ap = dram_out(nc, "xT_out", (D, NTOT), F32)
    with contextlib.ExitStack() as st:
        S = Sched(nc, st)
        p = P()
        p.nc, p.S, p.C = nc, S, Cap
        p.stop = stop
        p.T = {}
        for n, s, d in SCRATCH:
            if n in ext_scratch:
                kind = ext_scratch[n]
                ap = dram_in(nc, n, s, d) if kind == "in" else dram_out(nc, n, s, d)
                p.T[n] = Buf(ap, n)
            else:
                p.T[n] = S.dram(n, s, d)
        Xext = Buf(xin_ap, "xT")
        Oext = Buf(out_ap, "out")
        ld = S.dma_sem("ldc")
        p.eps_t = S.sbuf("eps", [128, 1], F32)
        p.one_t = S.sbuf("one", [128, 1], F32)
        p.ones_bf = S.sbuf("ones_bf", [128, 128], BF16)
        S.op("dve", lambda e: e.memset(p.eps_t[:], EPS), writes=[p.eps_t])
        S.op("dve", lambda e: e.memset(p.one_t[:], 1.0), writes=[p.one_t])
        S.op("dve", lambda e: e.memset(p.ones_bf[:], 1.0), writes=[p.ones_bf])
        for nm, shape in (("permA", [128, 128]), ("permB", [96, 96]), ("permK", [32, 32]), ("sel", [32, 96]),
                          ("maskLo", [128, 512]), ("maskHi", [128, 512]), ("triF", [64, 512]), ("triB", [64, 512]),
                          ("ident", [64, 64]), ("sv", [1, 128])):
            b_ = S.sbuf(nm, shape, BF16)
            S.dma("sp", ld, b_[:], Cap[nm], writes=[b_])
            setattr(p, nm, b_)
        p.cv = S.sbuf("cv", [128, 8, 2], F32)
        S.dma("sp", ld, p.cv[:], Cap["cvec"], writes=[p.cv])
        for l in range(nlayers):
            Xin = Xext if l == 0 else p.T["XA" if l % 2 == 1 else "XB"]
            last = (l == nlayers - 1)
            Xout = Oext if (last and not final) else p.T["XA" if (l + 1) % 2 == 1 else "XB"]
            if "A" in phases:
                phase_A(p, Lw[l], Xin)
            if "B1" in phases:
                phase_B1(p, Lw[l])
            if "B2" in phases:
                phase_B2(p, Lw[l])
            if "B3" in phases:
                phase_B3(p, Lw[l])
            if "C1" in phases:
                phase_C1(p, Lw[l], Xin)
            if "C2" in phases:
                phase_C2(p, Lw[l], Xout, final_out=(Oext if (last and final) else None))
        S.finish("sp")
        p.ninst = S.ninst
    nc._ninst = p.ninst
    return nc


_PROGS = {}
_CONSTS = {}


def _pk(v):
    v = np.asarray(v)
    k = v.shape[0] // 128
    out = v.reshape((k, 128) + v.shape[1:])
    return np.ascontiguousarray(np.moveaxis(out, 0, 1))


def host_consts():
    if "c" in _CONSTS:
        return _CONSTS["c"]
    pos = np.concatenate([np.zeros(CTX, np.int64), np.arange(SEQ)])
    is_ctx = np.concatenate([np.ones(CTX, bool), np.zeros(SEQ, bool)])
    c = dict(_rope_tables(pos, is_ctx))
    c.update(_perm_consts())
    j = np.arange(128)[:, None]
    i = (np.arange(512) % 128)[None, :]
    c["maskLo"] = (j >= i).astype(np.float32).astype(NPBF)
    c["maskHi"] = (j <= i).astype(np.float32).astype(NPBF)
    j = np.arange(64)[:, None]
    i = (np.arange(512) % 64)[None, :]
    c["triF"] = (j <= i).astype(np.float32).astype(NPBF)
    c["triB"] = (j >= i).astype(np.float32).astype(NPBF)
    rm = np.ones((64, NTOT), np.float32)
    rm[:, ::64] = 0.0
    c["rmask"] = rm
    c["ident"] = np.eye(64, dtype=np.float32).astype(NPBF)
    sv = np.zeros((1, 128), np.float32)
    sv[0, 64:] = 1.0
    c["sv"] = sv.astype(NPBF)
    _CONSTS["c"] = c
    return c


def layer_inputs(W, l, names):
    m = {}
    g = {
        "w_mod": lambda: W["w_mod"][l], "b_mod": lambda: _pk(W["b_mod"][l]), "norm_mix": lambda: _pk(W["norm_mix"][l]),
        "norm_ffn": lambda: _pk(W["norm_ffn"][l]), "w_in": lambda: W["w_in"][l],
        "a_sink": lambda: np.ascontiguousarray(W["a_sink"][l].reshape(1, 8)),
        "bqn": lambda: _pk(W["b_q_norm"][l]), "bkvn": lambda: _pk(W["b_kv_norm"][l]),
        "w_uq": lambda: W["b_w_uq"][l], "w_ukv": lambda: W["b_w_ukv"][l], "w_gate": lambda: W["c_w_gate"][l],
        "b_gate": lambda: np.ascontiguousarray(W["c_b_gate"][l].reshape(2, 2, 128).transpose(2, 0, 1)),
        "c_hn": lambda: np.ascontiguousarray(W["c_head_norm"][l].reshape(4, 128).T),
        "w_br_a": lambda: W["w_br_a"][l], "w_br_b": lambda: W["w_br_b"][l], "w_br_c": lambda: W["w_br_c"][l],
        "w_out": lambda: W["w_out"][l], "w_up": lambda: W["w_up"][l],
        "conv_w": lambda: np.ascontiguousarray(W["conv_w"][l].reshape(3, 44, 128).transpose(2, 1, 0)),
        "conv_b": lambda: _pk(W["conv_b"][l]), "w_down": lambda: W["w_down"][l],
    }
    for n in names:
        m[n] = np.ascontiguousarray(g[n]())
    return m


def core_inputs(W, b, layers, phases=PHASES_ALL):
    c = dict(host_consts())
    c["cvec"] = _pk(np.stack([W["c"][b], W["c_ctx"]], axis=1))
    c["final_norm"] = _pk(W["final_norm"])
    m = {n: c[n] for n, _, _ in CONSTS}
    names = [w[0] for w in needed_weights(phases)]
    for i, l in enumerate(layers):
        for n, v in layer_inputs(W, l, names).items():
            m["%s_%d" % (n, i)] = v
    return m


def kernel(**W):
    W = {k: np.asarray(v) for k, v in W.items()}
    if "fused" not in _PROGS:
        _PROGS["fused"] = build_program(DEPTH, True)
    maps = []
    for b in range(2):
        m = core_inputs(W, b, list(range(DEPTH)))
        m["xT"] = np.ascontiguousarray(np.concatenate([W["ctx"][b], W["x"][b]], axis=0).T)
        maps.append(m)
    res = run_bass_kernel_spmd(_PROGS["fused"], maps, core_ids=[0, 1]).results
    out = np.stack([np.ascontiguousarray(np.asarray(res[b]["outT"]).T) for b in range(2)], axis=0)
    return out.astype(np.float32)
```

```python
import contextlib
import numpy as np
import ml_dtypes
import concourse.bass as bass
import concourse.mybir as mybir
from concourse.bass_utils import run_bass_kernel_spmd

F32 = mybir.dt.float32
BF16 = mybir.dt.bfloat16
AF = mybir.ActivationFunctionType
ALU = mybir.AluOpType
NPBF = ml_dtypes.bfloat16

NCORES = 8
D = 1024
SEQ = 8192
CTX = 256
DEPTH = 4
NTOT = CTX + SEQ
NKB = NTOT // 128
EPS = 1e-6
IN_DIM = 5824
O_AQ, O_AK, O_AV, O_BQ, O_BKV, O_BKR, O_CQ, O_CK, O_CV, O_CR, O_CG, O_GATE = (
    0, 512, 640, 768, 1024, 1152, 1184, 1440, 1696, 2208, 2720, 2752)
FFN = 2816
TT = [(0, CTX)] + [(CTX + 512 * i, 512) for i in range(16)]
SEGS = [TT[0:5], TT[5:9], TT[9:13], TT[13:17]]

SAME_ENGINE_SYNC = True
_STOP = None


class _Stop(Exception):
    pass


def chk(name):
    if _STOP == name:
        raise _Stop()


class Buf:
    __slots__ = ("t", "name", "lw", "rd")

    def __init__(self, t, name=""):
        self.t = t
        self.name = name
        self.lw = {}
        self.rd = {}

    def __getitem__(self, idx):
        return self.t[idx]


class Sched:
    def __init__(self, nc, stack):
        self.nc = nc
        self.stack = stack
        self.root = stack
        self.engs = {"pe": nc.tensor, "act": nc.scalar, "dve": nc.vector,
                     "pool": nc.gpsimd, "sp": nc.sync}
        self.sems = {}
        self.cnt = {}
        self.seen = {e: {} for e in self.engs}
        for e in ("pe", "act", "dve", "pool"):
            self.sems[e] = stack.enter_context(nc.semaphore("s_" + e))
            self.cnt[e] = 0
        self.ninst = 0
        self._uid = 0
        self.sem_bufs = {}
        self._free = []
        self._scoped = [[]]

    def uid(self, p):
        self._uid += 1
        return "%s_%d" % (p, self._uid)

    @contextlib.contextmanager
    def scope(self):
        old = self.stack
        self._scoped.append([])
        with contextlib.ExitStack() as st:
            self.stack = st
            try:
                yield
            finally:
                self.barrier()
                self.stack = old
                self._free.extend(self._scoped.pop())

    def sbuf(self, name, shape, dt):
        return Buf(self.stack.enter_context(self.nc.sbuf_tensor(self.uid(name), shape, dt)), name)

    def psum(self, name, shape, dt):
        return Buf(self.stack.enter_context(self.nc.psum_tensor(self.uid(name), shape, dt)), name)

    def dram(self, name, shape, dt):
        return Buf(self.nc.dram_tensor(self.uid(name), list(shape), dt).ap(), name)

    def dma_sem(self, name):
        if self._free:
            key = self._free.pop()
            for sfx in ("~hw", "~sw"):
                if key + sfx in self.sem_bufs:
                    self.sem_bufs[key + sfx] = []
        else:
            key = self.uid("d")
        self._scoped[-1].append(key)
        return key

    def _sem(self, key):
        if key not in self.sems:
            self.sems[key] = self.root.enter_context(self.nc.semaphore(key.replace("~", "_")))
            self.cnt[key] = 0
            self.sem_bufs[key] = []
        return self.sems[key]

    def _wait(self, eng, deps):
        e = self.engs[eng]
        for (k, c) in sorted(deps, key=lambda x: str(x[0])):
            if k == eng and not SAME_ENGINE_SYNC:
                continue
            if k == "pe" and eng == "pe":
                continue
            if self.seen[eng].get(k, 0) >= c:
                continue
            e.wait_ge(self.sems[k], c)
            self.seen[eng][k] = c

    @staticmethod
    def _deps(reads, writes):
        deps = set()
        for b in reads:
            deps.update(b.lw.items())
        for b in writes:
            deps.update(b.lw.items())
            deps.update(b.rd.items())
        return deps

    def op(self, eng, fn, reads=(), writes=()):
        self._wait(eng, self._deps(reads, writes))
        ins = fn(self.engs[eng])
        self.cnt[eng] += 1
        ins.then_inc(self.sems[eng], 1)
        for b in reads:
            b.rd[eng] = self.cnt[eng]
        for b in writes:
            b.lw[eng] = self.cnt[eng]
            b.rd = {}
        self.ninst += 1
        return ins

    def mm(self, fn, reads=(), writes=(), last=True):
        self._wait("pe", self._deps(reads, writes))
        ins = fn(self.engs["pe"])
        self.ninst += 1
        if last:
            self.cnt["pe"] += 1
            ins.then_inc(self.sems["pe"], 1)
            for b in writes:
                b.lw["pe"] = self.cnt["pe"]
                b.rd = {}
        for b in reads:
            b.rd["pe"] = self.cnt["pe"] + (0 if last else 1)
        return ins

    def dma(self, q, semkey, out_ap, in_ap, reads=(), writes=(), **kw):
        semkey = semkey + ("~sw" if q == "pool" else "~hw")
        sem = self._sem(semkey)
        self._wait(q, self._deps(reads, writes))
        ins = self.engs[q].dma_start(out=out_ap, in_=in_ap, **kw)
        self.cnt[semkey] += 16
        c = self.cnt[semkey]
        ins.then_inc(sem, 16)
        for b in self.sem_bufs[semkey]:
            if semkey in b.lw:
                b.lw[semkey] = c
            if semkey in b.rd:
                b.rd[semkey] = c
        for b in reads:
            b.rd[semkey] = c
            if b not in self.sem_bufs[semkey]:
                self.sem_bufs[semkey].append(b)
        for b in writes:
            b.lw[semkey] = c
            b.rd = {}
            if b not in self.sem_bufs[semkey]:
                self.sem_bufs[semkey].append(b)
        return ins

    def barrier(self):
        snap = [(k, c) for k, c in self.cnt.items() if c > 0]
        for eng in ("pe", "act", "dve", "pool", "sp"):
            e = self.engs[eng]
            for (k, c) in snap:
                if self.seen[eng].get(k, 0) >= c:
                    continue
                e.wait_ge(self.sems[k], c)
                self.seen[eng][k] = c

    def finish(self, eng="sp"):
        for k, c in self.cnt.items():
            if c > 0:
                self._wait(eng, {(k, c)})

    def wait_all(self, eng, bufs):
        deps = set()
        for b in bufs:
            deps.update(b.lw.items())
            deps.update(b.rd.items())
        self._wait(eng, deps)


class Rot:
    def __init__(self, S, name, n, shape, dt, psum=False, dma=True):
        self.bufs = [(S.psum if psum else S.sbuf)(name + str(i), shape, dt) for i in range(n)]
        self.sems = [S.dma_sem(name + str(i)) for i in range(n)] if dma else [None] * n
        self.i = 0

    def next(self):
        b, s = self.bufs[self.i], self.sems[self.i]
        self.i = (self.i + 1) % len(self.bufs)
        return b, s


class Out:
    def __init__(self):
        self.bufs = []

    def add(self, b):
        if b not in self.bufs:
            self.bufs.append(b)


def dram_in(nc, name, shape, dt):
    return nc.dram_tensor(name, list(shape), dt, kind="ExternalInput").ap()


def dram_out(nc, name, shape, dt):
    return nc.dram_tensor(name, list(shape), dt, kind="ExternalOutput").ap()


def _rope_tables(pos, is_ctx):
    n = pos.shape[0]
    row = (pos // 64).astype(np.float32)
    col = (pos % 64).astype(np.float32)

    def tab(nfreq):
        inv = (np.float32(10000.0) ** (-np.arange(nfreq, dtype=np.float32) / np.float32(nfreq))).astype(np.float32)
        ang = np.concatenate([row[:, None] * inv, col[:, None] * inv], axis=-1).astype(np.float32)
        c, s = np.cos(ang).astype(np.float32), np.sin(ang).astype(np.float32)
        c[is_ctx] = 1.0
        s[is_ctx] = 0.0
        return c, s

    ca, sa = tab(16)
    cb, sb = tab(8)
    cosA = np.empty((128, n), np.float32)
    sinA = np.empty((128, n), np.float32)
    for p in range(128):
        d = p % 64
        i = d % 32
        cosA[p] = ca[:, i]
        sinA[p] = (-sa[:, i]) if d < 32 else sa[:, i]
    cosB = np.ones((96, n), np.float32)
    sinB = np.zeros((96, n), np.float32)
    cosK = np.empty((32, n), np.float32)
    sinK = np.empty((32, n), np.float32)
    for r in range(32):
        i = r % 16
        cosK[r] = cb[:, i]
        sinK[r] = (-sb[:, i]) if r < 16 else sb[:, i]
    cosB[64:96] = cosK
    sinB[64:96] = sinK
    return dict(cosA=cosA, sinA=sinA, cosB=cosB, sinB=sinB, cosK=cosK, sinK=sinK)


def _perm_consts():
    permA = np.zeros((128, 128), np.float32)
    for m in range(128):
        d = m % 64
        permA[m + 32 if d < 32 else m - 32, m] = 1.0
    permB = np.zeros((96, 96), np.float32)
    for m in range(64, 96):
        r = m - 64
        permB[m + 16 if r < 16 else m - 16, m] = 1.0
    permK = np.zeros((32, 32), np.float32)
    for m in range(32):
        permK[m + 16 if m < 16 else m - 16, m] = 1.0
    sel = np.zeros((32, 96), np.float32)
    for k in range(32):
        sel[k, 64 + k] = 1.0
    return dict(permA=permA.astype(NPBF), permB=permB.astype(NPBF), permK=permK.astype(NPBF),
                sel=sel.astype(NPBF))


LAYER_W = [
    ("w_mod", (D, 6 * D), F32), ("b_mod", (128, 48), F32), ("norm_mix", (128, 8), F32), ("norm_ffn", (128, 8), F32),
    ("w_in", (D, IN_DIM), F32), ("a_sink", (1, 8), F32), ("bqn", (128, 2), F32), ("bkvn", (128, 1), F32),
    ("w_uq", (256, 768), F32), ("w_ukv", (128, 1024), F32), ("w_gate", (2, 16, 256), F32), ("b_gate", (128, 2, 2), F32),
    ("c_hn", (128, 4), F32), ("w_br_a", (512, D), F32), ("w_br_b", (512, D), F32), ("w_br_c", (512, D), F32),
    ("w_out", (D, D), F32), ("w_up", (D, 2 * FFN), F32), ("conv_w", (128, 44, 3), F32), ("conv_b", (128, 44), F32),
    ("w_down", (FFN, D), F32),
]
CONSTS = [
    ("cvec", (128, 8, 2), F32), ("final_norm", (128, 8), F32),
    ("cosA", (128, NTOT), F32), ("sinA", (128, NTOT), F32), ("cosB", (96, NTOT), F32), ("sinB", (96, NTOT), F32),
    ("cosK", (32, NTOT), F32), ("sinK", (32, NTOT), F32),
    ("permA", (128, 128), BF16), ("permB", (96, 96), BF16), ("permK", (32, 32), BF16), ("sel", (32, 96), BF16),
    ("maskLo", (128, 512), BF16), ("maskHi", (128, 512), BF16), ("triF", (64, 512), BF16), ("triB", (64, 512), BF16),
    ("rmask", (64, NTOT), F32), ("ident", (64, 64), BF16), ("sv", (1, 128), BF16),
]
SCRATCH = [
    ("modT", (128, 48, 2), F32),
    ("qaT", (128, 4, NTOT), BF16), ("kaT", (128, NTOT), BF16), ("vaA", (NTOT, 2, 128), BF16),
    ("qbT", (96, 8, NTOT), BF16), ("kbT", (96, 8, NTOT), BF16), ("vbA", (NTOT, 8, 128), BF16),
    ("cqT", (128, 2, NTOT), BF16), ("ckT", (128, 2, NTOT), BF16), ("cvv", (NTOT, 512), BF16),
    ("crT", (128, 4, NTOT), BF16), ("gfT", (128, 2, NTOT), F32), ("gbT", (128, 2, NTOT), F32),
    ("gatesT", (128, 24, NTOT), BF16),
    ("yaT", (128, 4, NTOT), BF16), ("ybT", (128, 4, NTOT), BF16), ("ycT", (128, 4, NTOT), BF16),
    ("x1T", (D, NTOT), F32), ("h2T", (128, 8, NTOT), BF16), ("aT", (128, 22, NTOT), BF16),
    ("XA", (D, NTOT), F32), ("XB", (D, NTOT), F32),
]


class P:
    pass


def evac(S, eng, out_ap, in_ap, reads, writes, func=None, scale=1.0):
    if eng == "act":
        return S.op("act", lambda e: e.activation(out=out_ap, in_=in_ap, func=func or AF.Copy, scale=scale),
                    reads=reads, writes=writes)
    return S.op(eng, lambda e: e.tensor_copy(out=out_ap, in_=in_ap), reads=reads, writes=writes)


def rstd_from_ps(S, ps, n, dim, eps_t, out):
    S.op("act", lambda e: e.activation(out=out[:, 0:n], in_=ps[:, 0:n], func=AF.Ln, scale=1.0 / dim, bias=eps_t[:]),
         reads=[ps, eps_t], writes=[out])
    S.op("act", lambda e: e.activation(out=out[:, 0:n], in_=out[:, 0:n], func=AF.Exp, scale=-0.5), reads=[out], writes=[out])


def phase_A(p, L, Xin):
    S, T, C = p.S, p.T, p.C
    nc = p.nc
    eps_t, one_t, ones_bf = p.eps_t, p.one_t, p.ones_bf
    with S.scope():
        ld = S.dma_sem("ldc")
        bmod = S.sbuf("bmod", [128, 48], F32)
        nmix = S.sbuf("nmix", [128, 8], F32)
        bqn = S.sbuf("bqn", [128, 2], F32)
        bkvn = S.sbuf("bkvn", [128, 1], F32)
        bg = S.sbuf("bg", [128, 2, 2], F32)
        for b_, n_ in ((bmod, "b_mod"), (nmix, "norm_mix"), (bqn, "bqn"), (bkvn, "bkvn"), (bg, "b_gate")):
            S.dma("sp", ld, b_[:], L[n_], writes=[b_])
        wuq = S.sbuf("wuq", [128, 2, 768], BF16)
        S.dma("pool", ld, wuq[:], L["w_uq"].rearrange("(k p) n -> p k n", p=128), writes=[wuq])
        wkn = S.sbuf("wkn", [128, 8, 96], BF16)
        S.op("dve", lambda e: e.memset(wkn[:], 0.0), writes=[wkn])
        S.dma("pool", ld, wkn[:, :, 0:64], L["w_ukv"].rearrange("p (h c) -> p h c", c=128)[:, :, 0:64], writes=[wkn])
        wvb = S.sbuf("wvb", [128, 8, 64], BF16)
        S.dma("pool", ld, wvb[:], L["w_ukv"].rearrange("p (h c) -> p h c", c=128)[:, :, 64:128], writes=[wvb])
        wg = S.sbuf("wg", [32, 2, 256], BF16)
        S.op("dve", lambda e: e.memset(wg[:], 0.0), writes=[wg])
        S.dma("pool", ld, wg[0:16, 0, :], L["w_gate"][0], writes=[wg])
        S.dma("pool", ld, wg[16:32, 1, :], L["w_gate"][1], writes=[wg])
        nbg = S.sbuf("nbg", [128, 2, 2], F32)
        S.op("dve", lambda e: e.tensor_scalar_mul(out=nbg[:], in0=bg[:], scalar1=-1.0), reads=[bg], writes=[nbg])

        PS = Rot(S, "ps", 7, [128, 512], F32, psum=True, dma=False)

        modT = S.sbuf("modT", [128, 48, 2], F32)
        sc = S.sbuf("silu_c", [128, 8, 2], F32)
        tmp8 = S.sbuf("tmp8", [128, 8, 2], F32)
        cv = p.cv
        S.op("act", lambda e: e.activation(out=tmp8[:], in_=cv[:], func=AF.Exp, scale=-1.0), reads=[cv], writes=[tmp8])
        S.op("dve", lambda e: e.tensor_scalar_add(out=tmp8[:], in0=tmp8[:], scalar1=1.0), reads=[tmp8], writes=[tmp8])
        S.op("dve", lambda e: e.reciprocal(out=tmp8[:], in_=tmp8[:]), reads=[tmp8], writes=[tmp8])
        S.op("dve", lambda e: e.tensor_tensor(out=sc[:], in0=tmp8[:], in1=cv[:], op=ALU.mult), reads=[tmp8, cv], writes=[sc])
        scb = S.sbuf("silu_cb", [128, 8, 2], BF16)
        S.op("dve", lambda e: e.tensor_copy(out=scb[:], in_=sc[:]), reads=[sc], writes=[scb])
        with S.scope():
            WM = Rot(S, "wm", 2, [128, 8, 512], BF16)
            for piece in range(12):
                wb, ws = WM.next()
                S.dma("pool", ws, wb[:], L["w_mod"][:, piece * 512:(piece + 1) * 512].rearrange("(k p) n -> p k n", p=128),
                      writes=[wb])
                ps, _ = PS.next()
                for j in range(4):
                    for k in range(8):
                        S.mm(lambda e, j=j, k=k: e.matmul(ps[:, j * 2:j * 2 + 2], lhsT=wb[:, k, j * 128:(j + 1) * 128],
                                                          rhs=scb[:, k, :], start=(k == 0), stop=(k == 7)),
                             reads=[wb, scb], writes=[ps], last=(j == 3 and k == 7))
                S.op("dve", lambda e, piece=piece: e.tensor_tensor(
                    out=modT[:, piece * 4:(piece + 1) * 4, :], in0=ps[:, 0:8].rearrange("p (j v) -> p j v", v=2),
                    in1=bmod[:, piece * 4:(piece + 1) * 4].unsqueeze(2).to_broadcast([128, 4, 2]), op=ALU.add),
                    reads=[ps, bmod], writes=[modT])
        msem = S.dma_sem("modst")
        if p.stop == "mod":
            S.dma("act", msem, T["modT"][:], modT[:], reads=[modT], writes=[T["modT"]])
            return
        S.dma("act", msem, T["modT"][:], modT[:], reads=[modT], writes=[T["modT"]])
        A1 = S.sbuf("A1", [128, 8, 2], F32)
        S.op("dve", lambda e: e.tensor_scalar_add(out=A1[:], in0=modT[:, 8:16, :], scalar1=1.0), reads=[modT], writes=[A1])
        S.op("dve", lambda e: e.tensor_tensor(out=A1[:], in0=A1[:], in1=nmix[:].unsqueeze(2).to_broadcast([128, 8, 2]),
                                              op=ALU.mult), reads=[A1, nmix], writes=[A1])

        hT = S.sbuf("hT", [128, 8, 2304], BF16)
        WP = Rot(S, "wp", 3, [128, 8, 512], BF16)
        STB = Rot(S, "stb", 4, [128, 512], BF16)
        STF = Rot(S, "stf", 3, [128, 512], F32)
        TAB = Rot(S, "tab", 4, [128, 512], F32)
        XT = Rot(S, "xt", 2, [128, 8, 512], F32)
        sq = S.sbuf("sq", [128, 8, 512], BF16)
        rstd = S.sbuf("rstd", [128, 512], F32)
        xs = S.sbuf("xs", [128, 512], F32)
        cq = S.sbuf("cq", [128, 2, 512], F32)
        sq2 = S.sbuf("sq2", [128, 2, 512], BF16)
        rs2 = S.sbuf("rs2", [128, 512], F32)
        cqn = S.sbuf("cqn", [128, 2, 512], BF16)
        ckvn = S.sbuf("ckvn", [128, 512], BF16)
        vst = S.sbuf("vst", [128, 8, 128], BF16)
        S.op("dve", lambda e: e.memset(vst[:], 1.0), writes=[vst])
        vsem = S.dma_sem("vst")
        cvst = S.sbuf("cvst", [128, 512], BF16)
        csem = S.dma_sem("cvst")
        cg = S.sbuf("cg", [32, 512], BF16)
        krb = S.sbuf("krb", [32, 512], BF16)

        def load_w(col0, ncols):
            wb, ws = WP.next()
            S.dma("pool", ws, wb[:, :, 0:ncols], L["w_in"][:, col0:col0 + ncols].rearrange("(k p) n -> p k n", p=128),
                  writes=[wb])
            return wb

        def store(dst, dst_ap, buf, src_ap, sem):
            S.dma("act", sem, dst_ap, src_ap, reads=[buf], writes=[dst])

        for seg in SEGS:
            loc = {}
            l0 = 0
            for (t0, n) in seg:
                loc[t0] = l0
                l0 += n

            for (t0, n) in seg:
                v = 1 if t0 == 0 else 0
                lo = loc[t0]
                xb, xsem = XT.next()
                S.dma("sp", xsem, xb[:, :, 0:n], Xin[:][:, t0:t0 + n].rearrange("(k p) n -> p k n", p=128),
                      reads=[Xin], writes=[xb])
                S.op("act", lambda e: e.activation(out=sq[:, :, 0:n], in_=xb[:, :, 0:n], func=AF.Square), reads=[xb], writes=[sq])
                ps, _ = PS.next()
                for k in range(8):
                    S.mm(lambda e, k=k: e.matmul(ps[:, 0:n], lhsT=ones_bf[:], rhs=sq[:, k, 0:n], start=(k == 0), stop=(k == 7)),
                         reads=[ones_bf, sq], writes=[ps], last=(k == 7))
                rstd_from_ps(S, ps, n, D, eps_t, rstd)
                for k in range(8):
                    S.op("dve", lambda e, k=k: e.scalar_tensor_tensor(out=xs[:, 0:n], in0=xb[:, k, 0:n], scalar=A1[:, k, v:v + 1],
                                                                      in1=rstd[:, 0:n], op0=ALU.mult, op1=ALU.mult),
                         reads=[xb, A1, rstd], writes=[xs])
                    S.op("act", lambda e, k=k: e.activation(out=hT[:, k, lo:lo + n], in_=xs[:, 0:n], func=AF.Identity,
                                                            bias=modT[:, k, v:v + 1]),
                         reads=[xs, modT], writes=[hT])

            def proj_fm(ps, wb, c0, m, t0, n):
                lo = loc[t0]
                for k in range(8):
                    S.mm(lambda e, k=k: e.matmul(ps[0:m, 0:n], lhsT=wb[:, k, c0:c0 + m], rhs=hT[:, k, lo:lo + n],
                                                 start=(k == 0), stop=(k == 7)),
                         reads=[wb, hT], writes=[ps], last=(k == 7))

            def rope(ps, m, t0, n, perm, cosn, sinn, dst, dst_ap, obuf=None):
                pre, _ = STB.next()
                evac(S, "act", pre[0:m, 0:n], ps[0:m, 0:n], [ps], [pre])
                ct, cs = TAB.next()
                S.dma("sp", cs, ct[0:m, 0:n], C[cosn][:, t0:t0 + n], writes=[ct])
                stt, ss = TAB.next()
                S.dma("sp", ss, stt[0:m, 0:n], C[sinn][:, t0:t0 + n], writes=[stt])
                ps2, _ = PS.next()
                S.mm(lambda e: e.matmul(ps2[0:m, 0:n], lhsT=perm[:], rhs=pre[0:m, 0:n], start=True, stop=True),
                     reads=[perm, pre], writes=[ps2])
                a, _ = STF.next()
                S.op("dve", lambda e: e.tensor_tensor(out=a[0:m, 0:n], in0=pre[0:m, 0:n], in1=ct[0:m, 0:n], op=ALU.mult),
                     reads=[pre, ct], writes=[a])
                b, _ = STF.next()
                S.op("dve", lambda e: e.tensor_tensor(out=b[0:m, 0:n], in0=ps2[0:m, 0:n], in1=stt[0:m, 0:n], op=ALU.mult),
                     reads=[ps2, stt], writes=[b])
                if obuf is not None:
                    o, osm = obuf, None
                else:
                    o, osm = STB.next()
                S.op("dve", lambda e: e.tensor_tensor(out=o[0:m, 0:n], in0=a[0:m, 0:n], in1=b[0:m, 0:n], op=ALU.add),
                     reads=[a, b], writes=[o])
                if dst is not None:
                    store(dst, dst_ap, o, o[0:m, 0:n], osm)
                return o

            def simple_group(col0, nchunks, dst, func=AF.Copy, scale=1.0):
                for c0 in range(0, nchunks, 4):
                    ncn = min(4, nchunks - c0)
                    wb = load_w(col0 + c0 * 128, ncn * 128)
                    for (t0, n) in seg:
                        for j in range(ncn):
                            ps, _ = PS.next()
                            proj_fm(ps, wb, j * 128, 128, t0, n)
                            o, osm = STB.next()
                            evac(S, "act", o[:, 0:n], ps[:, 0:n], [ps], [o], func=func, scale=scale)
                            store(dst, dst[:][:, c0 + j, t0:t0 + n], o, o[:, 0:n], osm)

            if p.stop == "hT":
                return
            wb, ws = WP.next()
            for half in range(2):
                for g in range(4):
                    c0 = O_AQ + (half * 4 + g) * 64
                    S.dma("pool", ws, wb[:, :, g * 128 + half * 64:g * 128 + (half + 1) * 64],
                          L["w_in"][:, c0:c0 + 64].rearrange("(k p) c -> p k c", p=128), writes=[wb])
            for (t0, n) in seg:
                for g in range(4):
                    ps, _ = PS.next()
                    proj_fm(ps, wb, g * 128, 128, t0, n)
                    rope(ps, 128, t0, n, p.permA, "cosA", "sinA", T["qaT"], T["qaT"][:][:, g, t0:t0 + n])
            if p.stop == "G1":
                return
            wb = load_w(O_AK, 256)
            for (t0, n) in seg:
                lo = loc[t0]
                ps, _ = PS.next()
                proj_fm(ps, wb, 0, 128, t0, n)
                rope(ps, 128, t0, n, p.permA, "cosA", "sinA", T["kaT"], T["kaT"][:][:, t0:t0 + n])
                for s0 in range(0, n, 128):
                    ps, _ = PS.next()
                    for k in range(8):
                        S.mm(lambda e, k=k: e.matmul(ps[:, 0:128], lhsT=hT[:, k, lo + s0:lo + s0 + 128], rhs=wb[:, k, 128:256],
                                                     start=(k == 0), stop=(k == 7)),
                             reads=[wb, hT], writes=[ps], last=(k == 7))
                    S.op("act", lambda e: e.activation(out=vst[:, 0:2, 0:64],
                                                       in_=ps[:, 0:128].rearrange("p (h c) -> p h c", c=64), func=AF.Copy),
                         reads=[ps], writes=[vst])
                    store(T["vaA"], T["vaA"][:][t0 + s0:t0 + s0 + 128], vst, vst[:, 0:2, :], vsem)
            if p.stop == "G3":
                return
            wb = load_w(O_BQ, 256)
            wb2 = load_w(O_BKV, 256)
            for (t0, n) in seg:
                for c in range(2):
                    ps, _ = PS.next()
                    proj_fm(ps, wb, c * 128, 128, t0, n)
                    S.op("dve", lambda e, c=c: e.tensor_copy(out=cq[:, c, 0:n], in_=ps[:, 0:n]), reads=[ps], writes=[cq])
                    S.op("act", lambda e, c=c: e.activation(out=sq2[:, c, 0:n], in_=cq[:, c, 0:n], func=AF.Square),
                         reads=[cq], writes=[sq2])
                ps, _ = PS.next()
                for c in range(2):
                    S.mm(lambda e, c=c: e.matmul(ps[:, 0:n], lhsT=ones_bf[:], rhs=sq2[:, c, 0:n], start=(c == 0), stop=(c == 1)),
                         reads=[ones_bf, sq2], writes=[ps], last=(c == 1))
                rstd_from_ps(S, ps, n, 256, eps_t, rs2)
                for c in range(2):
                    S.op("dve", lambda e, c=c: e.scalar_tensor_tensor(out=cqn[:, c, 0:n], in0=cq[:, c, 0:n], scalar=bqn[:, c:c + 1],
                                                                      in1=rs2[:, 0:n], op0=ALU.mult, op1=ALU.mult),
                         reads=[cq, bqn, rs2], writes=[cqn])
                for h in range(8):
                    ps, _ = PS.next()
                    for c in range(2):
                        S.mm(lambda e, c=c, h=h: e.matmul(ps[0:96, 0:n], lhsT=wuq[:, c, h * 96:(h + 1) * 96], rhs=cqn[:, c, 0:n],
                                                          start=(c == 0), stop=(c == 1)),
                             reads=[wuq, cqn], writes=[ps], last=(c == 1))
                    rope(ps, 96, t0, n, p.permB, "cosB", "sinB", T["qbT"], T["qbT"][:][:, h, t0:t0 + n])
                if p.stop == "G4":
                    return
                ps, _ = PS.next()
                if p.stop == "G5x":
                    return
                proj_fm(ps, wb2, 0, 128, t0, n)
                if p.stop == "G5a0":
                    return
                S.op("dve", lambda e: e.tensor_copy(out=cq[:, 0, 0:n], in_=ps[:, 0:n]), reads=[ps], writes=[cq])
                S.op("act", lambda e: e.activation(out=sq2[:, 0, 0:n], in_=cq[:, 0, 0:n], func=AF.Square), reads=[cq], writes=[sq2])
                if p.stop in ("G5a1", "G5nosq"):
                    return
                ps, _ = PS.next()
                S.mm(lambda e: e.matmul(ps[:, 0:n], lhsT=ones_bf[:], rhs=sq2[:, 0, 0:n], start=True, stop=True),
                     reads=[ones_bf, sq2], writes=[ps])
                if p.stop == "G5a2":
                    return
                rstd_from_ps(S, ps, n, 128, eps_t, rs2)
                if p.stop == "G5a3":
                    return
                S.op("dve", lambda e: e.scalar_tensor_tensor(out=ckvn[:, 0:n], in0=cq[:, 0, 0:n], scalar=bkvn[:, 0:1],
                                                             in1=rs2[:, 0:n], op0=ALU.mult, op1=ALU.mult),
                     reads=[cq, bkvn, rs2], writes=[ckvn])
                if p.stop == "G5a":
                    return
                ps, _ = PS.next()
                proj_fm(ps, wb2, 128, 128, t0, n)
                if p.stop == "G5":
                    return
                kr = rope(ps, 32, t0, n, p.permK, "cosK", "sinK", None, None, obuf=krb)
                if p.stop == "G5b":
                    return
                for h in range(8):
                    ps, _ = PS.next()
                    S.mm(lambda e, h=h: e.matmul(ps[0:96, 0:n], lhsT=wkn[:, h, :], rhs=ckvn[:, 0:n], start=True, stop=False),
                         reads=[wkn, ckvn], writes=[ps], last=False)
                    S.mm(lambda e: e.matmul(ps[0:96, 0:n], lhsT=p.sel[:], rhs=kr[0:32, 0:n], start=False, stop=True),
                         reads=[p.sel, kr], writes=[ps], last=True)
                    o, osm = STB.next()
                    evac(S, "act", o[0:96, 0:n], ps[0:96, 0:n], [ps], [o])
                    store(T["kbT"], T["kbT"][:][:, h, t0:t0 + n], o, o[0:96, 0:n], osm)
                if p.stop == "G5c":
                    return
                for s0 in range(0, n, 128):
                    ps, _ = PS.next()
                    S.mm(lambda e: e.matmul(ps[:, 0:512], lhsT=ckvn[:, s0:s0 + 128], rhs=wvb[:].rearrange("p h c -> p (h c)"),
                                            start=True, stop=True), reads=[ckvn, wvb], writes=[ps])
                    S.op("act", lambda e: e.activation(out=vst[:, :, 0:64],
                                                       in_=ps[:, 0:512].rearrange("p (h c) -> p h c", c=64), func=AF.Copy),
                         reads=[ps], writes=[vst])
                    store(T["vbA"], T["vbA"][:][t0 + s0:t0 + s0 + 128], vst, vst[:, :, :], vsem)
            if p.stop == "G6":
                return
            simple_group(O_CQ, 2, T["cqT"], scale=0.125)
            simple_group(O_CK, 2, T["ckT"])
            wb = load_w(O_CV, 512)
            for (t0, n) in seg:
                lo = loc[t0]
                for s0 in range(0, n, 128):
                    ps, _ = PS.next()
                    for k in range(8):
                        S.mm(lambda e, k=k: e.matmul(ps[:, 0:512], lhsT=hT[:, k, lo + s0:lo + s0 + 128], rhs=wb[:, k, 0:512],
                                                     start=(k == 0), stop=(k == 7)),
                             reads=[wb, hT], writes=[ps], last=(k == 7))
                    evac(S, "act", cvst[:, :], ps[:, 0:512], [ps], [cvst])
                    store(T["cvv"], T["cvv"][:][t0 + s0:t0 + s0 + 128, :], cvst, cvst[:, :], csem)
            wb = load_w(O_CG, 128)
            for (t0, n) in seg:
                ps, _ = PS.next()
                proj_fm(ps, wb, 0, 128, t0, n)
                evac(S, "act", cg[:, 0:n], ps[0:32, 0:n], [ps], [cg])
                for dr, dst in ((0, T["gfT"]), (1, T["gbT"])):
                    for pr in range(2):
                        ps, _ = PS.next()
                        S.mm(lambda e, dr=dr, pr=pr: e.matmul(ps[:, 0:n], lhsT=wg[:, dr, pr * 128:(pr + 1) * 128], rhs=cg[:, 0:n],
                                                              start=True, stop=True), reads=[wg, cg], writes=[ps])
                        o, osm = STF.next()
                        S.op("act", lambda e, dr=dr, pr=pr: e.activation(out=o[:, 0:n], in_=ps[:, 0:n], func=AF.Exp, scale=-1.0,
                                                                         bias=nbg[:, dr, pr:pr + 1]), reads=[ps, nbg], writes=[o])
                        S.op("act", lambda e: e.activation(out=o[:, 0:n], in_=o[:, 0:n], func=AF.Ln, bias=one_t[:]),
                             reads=[o, one_t], writes=[o])
                        S.op("dve", lambda e: e.tensor_scalar_mul(out=o[:, 0:n], in0=o[:, 0:n], scalar1=-1.0 / 16.0),
                             reads=[o], writes=[o])
                        store(dst, dst[:][:, pr, t0:t0 + n], o, o[:, 0:n], osm)
            if p.stop == "G11":
                return
            simple_group(O_CR, 4, T["crT"], func=AF.Silu)
            simple_group(O_GATE, 24, T["gatesT"], func=AF.Sigmoid)


def phase_B1(p, L):
    S, T, C = p.S, p.T, p.C
    with S.scope():
        ld = S.dma_sem("ld")
        ka = S.sbuf("ka", [128, NTOT], BF16)
        va = S.sbuf("va", [128, NKB, 2, 128], BF16)
        S.dma("sp", ld, ka[:], T["kaT"][:], reads=[T["kaT"]], writes=[ka])
        for i in range(0, NKB, 6):
            S.dma("sp", ld, va[:, i:i + 6], T["vaA"][:].rearrange("(b p) h c -> p b h c", p=128)[:, i:i + 6],
                  reads=[T["vaA"]], writes=[va])
        sk = S.sbuf("sk", [1, 8], F32)
        S.dma("sp", ld, sk[:], L["a_sink"], writes=[sk])
        zrow = S.sbuf("zrow", [1, 128], F32)
        S.op("dve", lambda e: e.memset(zrow[:], 0.0), writes=[zrow])
        esrow = S.sbuf("esrow", [1, 2, 512], BF16)
        for h in range(8):
            S.op("act", lambda e, h=h: e.activation(out=esrow[0:1, h // 4, (h % 4) * 128:(h % 4 + 1) * 128], in_=zrow[:],
                                                    func=AF.Exp, bias=sk[0:1, h:h + 1]), reads=[zrow, sk], writes=[esrow])
        PSS = Rot(S, "pss", 5, [128, 512], F32, psum=True, dma=False)
        PSO = Rot(S, "pso", 2, [128, 512], F32, psum=True, dma=False)
        QT = Rot(S, "qt", 2, [128, 4, 512], BF16)
        PT = Rot(S, "pt", 6, [128, 512], BF16, dma=False)
        RC = Rot(S, "rc", 2, [64, 512], F32, dma=False)
        OS = Rot(S, "os", 3, [64, 512], BF16)
        scale = 64 ** -0.5

        for (t0, n) in TT:
            qt, qsem = QT.next()
            S.dma("sp", qsem, qt[:, :, 0:n], T["qaT"][:][:, :, t0:t0 + n], reads=[T["qaT"]], writes=[qt])
            for qb in range(n // 128):
                q0 = t0 + qb * 128
                blk = q0 // 128
                if t0 == 0:
                    kbs = [(0, None), (1, None)]
                else:
                    kbs = [(0, None), (1, None)]
                    if blk - 1 >= 2:
                        kbs.append((blk - 1, "maskLo"))
                    kbs.append((blk, None))
                    if blk + 1 < NKB:
                        kbs.append((blk + 1, "maskHi"))
                for kvh in range(2):
                    pb = kvh * 64
                    pso, _ = PSO.next()
                    pend = []
                    for i, (kb, msk) in enumerate(kbs):
                        pss, _ = PSS.next()
                        S.mm(lambda e, kb=kb, pss=pss: e.matmul(pss[:, 0:512].rearrange("p (g q) -> p g q", g=4),
                                                                lhsT=ka[pb:pb + 64, kb * 128:(kb + 1) * 128],
                                                                rhs=qt[pb:pb + 64, :, qb * 128:(qb + 1) * 128], start=True, stop=True),
                             reads=[ka, qt], writes=[pss])
                        pt, _ = PT.next()
                        S.op("act", lambda e, pss=pss, pt=pt: e.activation(out=pt[:, :], in_=pss[:, :], func=AF.Exp, scale=scale),
                             reads=[pss], writes=[pt])
                        if msk is not None:
                            mk = p.maskLo if msk == "maskLo" else p.maskHi
                            S.op("dve", lambda e, mk=mk, pt=pt: e.tensor_tensor(out=pt[:, :], in0=pt[:, :], in1=mk[:, :], op=ALU.mult),
                                 reads=[pt, mk], writes=[pt])
                        pend.append((kb, pt))
                    for i, (kb, pt) in enumerate(pend):
                        S.mm(lambda e, kb=kb, i=i, pt=pt: e.matmul(pso[:, :], lhsT=va[:, kb, kvh, :], rhs=pt[:, :], start=(i == 0), stop=False),
                             reads=[va, pt], writes=[pso], last=False)
                    S.mm(lambda e: e.matmul(pso[:, :], lhsT=p.sv[:], rhs=esrow[0:1, kvh, :], start=False, stop=True),
                         reads=[p.sv, esrow], writes=[pso], last=True)
                    rc, _ = RC.next()
                    S.op("dve", lambda e: e.reciprocal(out=rc[:, :], in_=pso[64:128, :]), reads=[pso], writes=[rc])
                    o, osm = OS.next()
                    S.op("dve", lambda e: e.tensor_tensor(out=o[:, :], in0=pso[0:64, :], in1=rc[:, :], op=ALU.mult),
                         reads=[pso, rc], writes=[o])
                    for gp in range(2):
                        S.dma("act", osm, T["yaT"][:][gp * 64:(gp + 1) * 64, kvh * 2:kvh * 2 + 2, q0:q0 + 128],
                              o[:, :].rearrange("p (g2 gp q) -> p g2 gp q", g2=2, gp=2)[:, :, gp, :],
                              reads=[o], writes=[T["yaT"]])


def phase_B2(p, L):
    S, T, C = p.S, p.T, p.C
    with S.scope():
        KS = Rot(S, "ks", 2, [96, NTOT], BF16)
        VS = Rot(S, "vs", 2, [128, NKB, 128], BF16)
        PSS = Rot(S, "pss", 5, [128, 512], F32, psum=True, dma=False)
        PSO = Rot(S, "pso", 2, [128, 512], F32, psum=True, dma=False)
        QT = Rot(S, "qt", 2, [96, 512], BF16)
        PT = Rot(S, "pt", 6, [128, 512], BF16, dma=False)
        RC = Rot(S, "rc", 2, [64, 512], F32, dma=False)
        OS = Rot(S, "os", 3, [64, 512], BF16)
        scale = 96 ** -0.5
        for h in range(8):
            ks, ksem = KS.next()
            vs, vsem = VS.next()
            S.dma("sp", ksem, ks[:], T["kbT"][:][:, h, :], reads=[T["kbT"]], writes=[ks])
            for i in range(0, NKB, 6):
                S.dma("sp", vsem, vs[:, i:i + 6], T["vbA"][:].rearrange("(b p) h c -> p b h c", p=128)[:, i:i + 6, h, :],
                      reads=[T["vbA"]], writes=[vs])
            for (t0, n) in TT:
                qt, qsem = QT.next()
                S.dma("sp", qsem, qt[:, 0:n], T["qbT"][:][:, h, t0:t0 + n], reads=[T["qbT"]], writes=[qt])
                nkb = 2 if t0 == 0 else NKB
                pso, _ = PSO.next()
                LOOK = 3
                pend = []
                for kb in range(nkb + LOOK):
                    if kb < nkb:
                        pss, _ = PSS.next()
                        S.mm(lambda e, kb=kb, pss=pss: e.matmul(pss[:, 0:n], lhsT=ks[:, kb * 128:(kb + 1) * 128], rhs=qt[:, 0:n],
                                                                start=True, stop=True), reads=[ks, qt], writes=[pss])
                        pt, _ = PT.next()
                        S.op("act", lambda e, pss=pss, pt=pt: e.activation(out=pt[:, 0:n], in_=pss[:, 0:n], func=AF.Exp, scale=scale),
                             reads=[pss], writes=[pt])
                        pend.append(pt)
                    if kb >= LOOK:
                        k2 = kb - LOOK
                        pt2 = pend.pop(0)
                        S.mm(lambda e, k2=k2, pt2=pt2: e.matmul(pso[:, 0:n], lhsT=vs[:, k2, :], rhs=pt2[:, 0:n], start=(k2 == 0),
                                                                stop=(k2 == nkb - 1)), reads=[vs, pt2], writes=[pso], last=(k2 == nkb - 1))
                rc, _ = RC.next()
                S.op("dve", lambda e: e.reciprocal(out=rc[:, 0:n], in_=pso[64:128, 0:n]), reads=[pso], writes=[rc])
                o, osm = OS.next()
                S.op("dve", lambda e: e.tensor_tensor(out=o[:, 0:n], in0=pso[0:64, 0:n], in1=rc[:, 0:n], op=ALU.mult),
                     reads=[pso, rc], writes=[o])
                S.dma("act", osm, T["ybT"][:][(h % 2) * 64:(h % 2 + 1) * 64, h // 2, t0:t0 + n], o[:, 0:n],
                      reads=[o], writes=[T["ybT"]])


def phase_B3(p, L):
    S, T, C = p.S, p.T, p.C
    eps_t, ones_bf = p.eps_t, p.ones_bf
    NCH = NTOT // 64
    PIECE = 2112
    with S.scope():
        ld = S.dma_sem("ld")
        hn = S.sbuf("hn", [128, 4], F32)
        S.dma("sp", ld, hn[:], L["c_hn"], writes=[hn])
        vsb = S.sbuf("vsb", [64, NCH, 128], BF16)
        qtb = S.sbuf("qtb", [64, NTOT], BF16)
        ktT = S.sbuf("ktT", [64, NCH, 64], BF16)
        attT = S.sbuf("attT", [64, NCH, 64], BF16)
        ebl = S.sbuf("ebl", [64, NCH], F32)
        oT = S.sbuf("oT", [128, NTOT], F32)
        qp = S.sbuf("qp", [64, PIECE], BF16)
        kp = S.sbuf("kp", [64, PIECE], BF16)
        gp = S.sbuf("gp", [64, PIECE], F32)
        bp = S.sbuf("bp", [64, PIECE], F32)
        rm = S.sbuf("rm", [64, PIECE], F32)
        ep = S.sbuf("ep", [64, PIECE], BF16)
        ktp = S.sbuf("ktp", [64, PIECE], BF16)
        S.dma("sp", ld, rm[:], C["rmask"][:, 0:PIECE], writes=[rm])
        Sf = [S.sbuf("Sf%d" % i, [64, 128], F32) for i in range(2)]
        Sb = [S.sbuf("Sb%d" % i, [64, 128], BF16) for i in range(2)]
        S1l = [S.sbuf("S1%d" % i, [64, 128], F32) for i in range(2)]
        PST = Rot(S, "pst", 2, [64, 512], BF16, psum=True, dma=False)
        PSA = Rot(S, "psa", 2, [64, 512], F32, psum=True, dma=False)
        PSO = Rot(S, "pso", 2, [128, 512], F32, psum=True, dma=False)
        PSK = Rot(S, "psk", 2, [64, 128], F32, psum=True, dma=False)
        sqb = S.sbuf("sqb", [128, 512], BF16)
        rs = S.sbuf("rs", [128, 512], F32)
        tt = S.sbuf("tt", [128, 512], F32)
        CR = Rot(S, "cr", 2, [128, 512], BF16)
        YO = Rot(S, "yo", 2, [128, 512], BF16)

        for h in range(4):
            hb = (h % 2) * 64
            for i in range(0, NCH, 11):
                S.dma("sp", ld, vsb[:, i:i + 11, :],
                      T["cvv"][:].rearrange("(c j) (h e) -> j c h e", j=64, e=128)[:, i:i + 11, h, :],
                      reads=[T["cvv"]], writes=[vsb])
            for dr in range(2):
                gsrc = T["gfT"] if dr == 0 else T["gbT"]
                tri = p.triF if dr == 0 else p.triB
                for pc in range(NTOT // PIECE):
                    a0 = pc * PIECE
                    c0 = a0 // 64
                    S.dma("sp", ld, qp[:], T["cqT"][:][hb:hb + 64, h // 2, a0:a0 + PIECE], reads=[T["cqT"]], writes=[qp])
                    S.dma("sp", ld, kp[:], T["ckT"][:][hb:hb + 64, h // 2, a0:a0 + PIECE], reads=[T["ckT"]], writes=[kp])
                    S.dma("sp", ld, gp[:], gsrc[:][hb:hb + 64, h // 2, a0:a0 + PIECE], reads=[gsrc], writes=[gp])
                    S.op("dve", lambda e: e.tensor_tensor_scan(out=bp[:], data0=rm[:], data1=gp[:], initial=0.0,
                                                               op0=ALU.mult, op1=ALU.add), reads=[rm, gp], writes=[bp])
                    if dr == 1:
                        b3 = bp[:].rearrange("p (c j) -> p c j", j=64)
                        S.op("dve", lambda e: e.tensor_tensor(out=gp[:], in0=gp[:], in1=bp[:], op=ALU.subtract),
                             reads=[gp, bp], writes=[gp])
                        S.op("dve", lambda e: e.tensor_tensor(out=bp[:].rearrange("p (c j) -> p c j", j=64),
                                                              in0=gp[:].rearrange("p (c j) -> p c j", j=64),
                                                              in1=b3[:, :, 63:64].to_broadcast([64, PIECE // 64, 64]), op=ALU.add),
                             reads=[gp, bp], writes=[bp])
                    last_col = 63 if dr == 0 else 0
                    S.op("act", lambda e: e.activation(out=ebl[:, c0:c0 + PIECE // 64],
                                                       in_=bp[:].rearrange("p (c j) -> p c j", j=64)[:, :, last_col],
                                                       func=AF.Exp), reads=[bp], writes=[ebl])
                    S.op("act", lambda e: e.activation(out=ep[:], in_=bp[:], func=AF.Exp), reads=[bp], writes=[ep])
                    S.op("dve", lambda e: e.tensor_tensor(out=qtb[:, a0:a0 + PIECE], in0=qp[:], in1=ep[:], op=ALU.mult),
                         reads=[qp, ep], writes=[qtb])
                    S.op("act", lambda e: e.activation(out=ep[:], in_=bp[:], func=AF.Exp, scale=-1.0), reads=[bp], writes=[ep])
                    S.op("dve", lambda e: e.tensor_tensor(out=ktp[:], in0=kp[:], in1=ep[:], op=ALU.mult),
                         reads=[kp, ep], writes=[ktp])
                    for c8 in range(0, PIECE // 64, 8):
                        nn = min(8, PIECE // 64 - c8)
                        pst, _ = PST.next()
                        psa, _ = PSA.next()
                        for j in range(nn):
                            cl = c8 + j
                            S.mm(lambda e, cl=cl, j=j: e.transpose(out=pst[:, j * 64:(j + 1) * 64], in_=ktp[:, cl * 64:(cl + 1) * 64],
                                                                   identity=p.ident[:]),
                                 reads=[ktp, p.ident], writes=[pst], last=(j == nn - 1))
                        for j in range(nn):
                            cl = c8 + j
                            S.mm(lambda e, cl=cl, j=j: e.matmul(psa[:, j * 64:(j + 1) * 64], lhsT=ktp[:, cl * 64:(cl + 1) * 64],
                                                                rhs=qtb[:, a0 + cl * 64:a0 + (cl + 1) * 64], start=True, stop=True),
                                 reads=[ktp, qtb], writes=[psa], last=(j == nn - 1))
                        S.op("act", lambda e: e.activation(out=ktT[:, c0 + c8:c0 + c8 + nn, :].rearrange("p c d -> p (c d)"),
                                                           in_=pst[:, 0:nn * 64], func=AF.Copy), reads=[pst], writes=[ktT])
                        S.op("dve", lambda e: e.tensor_tensor(out=attT[:, c0 + c8:c0 + c8 + nn, :].rearrange("p c d -> p (c d)"),
                                                              in0=psa[:, 0:nn * 64], in1=tri[:, 0:nn * 64], op=ALU.mult),
                             reads=[psa, tri], writes=[attT])
                order = list(range(NCH)) if dr == 0 else ([3, 2, 1, 0] + list(range(NCH - 1, 3, -1)))
                cur = 0
                S.op("dve", lambda e: e.memset(Sf[0][:], 0.0), writes=[Sf[0]])
                S.op("dve", lambda e: e.memset(Sb[0][:], 0.0), writes=[Sb[0]])
                groups = [order[0:4]] + [order[i:i + 8] for i in range(4, NCH, 8)]
                for grp in groups:
                    ng = len(grp)
                    lo_c = min(grp)
                    pso, _ = PSO.next()
                    for j, c in enumerate(grp):
                        col = (c - lo_c) * 64
                        psk, _ = PSK.next()
                        S.mm(lambda e, c=c, psk=psk: e.matmul(psk[:, :], lhsT=ktT[:, c, :], rhs=vsb[:, c, :], start=True, stop=True),
                             reads=[ktT, vsb], writes=[psk])
                        S.mm(lambda e, c=c, col=col: e.matmul(pso[:, col:col + 64], lhsT=vsb[:, c, :], rhs=attT[:, c, :],
                                                              start=True, stop=False),
                             reads=[vsb, attT], writes=[pso], last=False)
                        S.mm(lambda e, c=c, col=col, cur=cur: e.matmul(pso[:, col:col + 64], lhsT=Sb[cur][:],
                                                                       rhs=qtb[:, c * 64:(c + 1) * 64], start=False, stop=True),
                             reads=[Sb[cur], qtb], writes=[pso], last=True)
                        nxt = 1 - cur
                        S1 = S1l[cur]
                        S.op("dve", lambda e, cur=cur, S1=S1, psk=psk: e.tensor_tensor(out=S1[:], in0=psk[:, :], in1=Sf[cur][:], op=ALU.add),
                             reads=[psk, Sf[cur]], writes=[S1])
                        S.op("dve", lambda e, c=c, nxt=nxt, S1=S1: e.tensor_scalar_mul(out=Sf[nxt][:], in0=S1[:], scalar1=ebl[:, c:c + 1]),
                             reads=[S1, ebl], writes=[Sf[nxt]])
                        S.op("act", lambda e, c=c, nxt=nxt, S1=S1: e.activation(out=Sb[nxt][:], in_=S1[:], func=AF.Copy,
                                                                                scale=ebl[:, c:c + 1]),
                             reads=[S1, ebl], writes=[Sb[nxt]])
                        cur = nxt
                    w = ng * 64
                    if dr == 0:
                        evac(S, "act", oT[:, lo_c * 64:lo_c * 64 + w], pso[:, 0:w], [pso], [oT])
                    else:
                        S.op("dve", lambda e, lo_c=lo_c, w=w: e.tensor_tensor(out=oT[:, lo_c * 64:lo_c * 64 + w],
                                                                             in0=oT[:, lo_c * 64:lo_c * 64 + w],
                                                                             in1=pso[:, 0:w], op=ALU.add),
                             reads=[oT, pso], writes=[oT])
            for (t0, n) in TT:
                S.op("act", lambda e: e.activation(out=sqb[:, 0:n], in_=oT[:, t0:t0 + n], func=AF.Square), reads=[oT], writes=[sqb])
                pso, _ = PSO.next()
                S.mm(lambda e: e.matmul(pso[:, 0:n], lhsT=ones_bf[:], rhs=sqb[:, 0:n], start=True, stop=True),
                     reads=[ones_bf, sqb], writes=[pso])
                rstd_from_ps(S, pso, n, 128, eps_t, rs)
                cr, crs = CR.next()
                S.dma("sp", crs, cr[:, 0:n], T["crT"][:][:, h, t0:t0 + n], reads=[T["crT"]], writes=[cr])
                S.op("dve", lambda e: e.scalar_tensor_tensor(out=tt[:, 0:n], in0=oT[:, t0:t0 + n], scalar=hn[:, h:h + 1],
                                                             in1=rs[:, 0:n], op0=ALU.mult, op1=ALU.mult),
                     reads=[oT, hn, rs], writes=[tt])
                yo, ys = YO.next()
                S.op("dve", lambda e: e.tensor_tensor(out=yo[:, 0:n], in0=tt[:, 0:n], in1=cr[:, 0:n], op=ALU.mult),
                     reads=[tt, cr], writes=[yo])
                S.dma("act", ys, T["ycT"][:][:, h, t0:t0 + n], yo[:, 0:n], reads=[yo], writes=[T["ycT"]])


def phase_C1(p, L, Xin):
    S, T, C = p.S, p.T, p.C
    eps_t, ones_bf = p.eps_t, p.ones_bf
    with S.scope():
        ld = S.dma_sem("ld")
        modT = S.sbuf("modT", [128, 48, 2], F32)
        S.dma("sp", ld, modT[:], T["modT"][:], reads=[T["modT"]], writes=[modT])
        nffn = S.sbuf("nffn", [128, 8], F32)
        S.dma("sp", ld, nffn[:], L["norm_ffn"], writes=[nffn])
        wbr = [S.sbuf("wbr%d" % i, [128, 4, D], BF16) for i in range(3)]
        for i, nm in enumerate(("w_br_a", "w_br_b", "w_br_c")):
            S.dma("pool", ld, wbr[i][:], L[nm].rearrange("(k p) n -> p k n", p=128), writes=[wbr[i]])
        wo = S.sbuf("wo", [128, 8, D], BF16)
        for k0 in range(0, 8, 4):
            S.dma("pool", ld, wo[:, k0:k0 + 4, :], L["w_out"].rearrange("(k p) n -> p k n", p=128)[:, k0:k0 + 4, :], writes=[wo])
        A2 = S.sbuf("A2", [128, 8, 2], F32)
        S.op("dve", lambda e: e.tensor_scalar_add(out=A2[:], in0=modT[:, 32:40, :], scalar1=1.0), reads=[modT], writes=[A2])
        S.op("dve", lambda e: e.tensor_tensor(out=A2[:], in0=A2[:], in1=nffn[:].unsqueeze(2).to_broadcast([128, 8, 2]),
                                              op=ALU.mult), reads=[A2, nffn], writes=[A2])
        PS = Rot(S, "ps", 7, [128, 512], F32, psum=True, dma=False)
        YA = Rot(S, "ya", 2, [128, 12, 512], BF16)
        GT = Rot(S, "gt", 1, [128, 24, 512], BF16)
        XT = Rot(S, "xt", 2, [128, 8, 512], F32)
        mT = S.sbuf("mT", [128, 8, 512], BF16)
        t1 = S.sbuf("t1", [128, 512], F32)
        t2 = S.sbuf("t2", [128, 512], F32)
        sq = S.sbuf("sq", [128, 8, 512], BF16)
        rstd = S.sbuf("rstd", [128, 512], F32)
        xs = S.sbuf("xs", [128, 512], F32)
        H2 = Rot(S, "h2", 1, [128, 8, 512], BF16)
        for (t0, n) in TT:
            v = 1 if t0 == 0 else 0
            ya, yas = YA.next()
            for i, nm in enumerate(("yaT", "ybT", "ycT")):
                S.dma("sp", yas, ya[:, i * 4:(i + 1) * 4, 0:n], T[nm][:][:, :, t0:t0 + n], reads=[T[nm]], writes=[ya])
            gt, gts = GT.next()
            for i in range(3):
                S.dma("sp", gts, gt[:, i * 8:(i + 1) * 8, 0:n], T["gatesT"][:][:, i * 8:(i + 1) * 8, t0:t0 + n],
                      reads=[T["gatesT"]], writes=[gt])
            xb, xsem = XT.next()
            S.dma("sp", xsem, xb[:, :, 0:n], Xin[:][:, t0:t0 + n].rearrange("(k p) n -> p k n", p=128), reads=[Xin], writes=[xb])
            for m in range(8):
                pss = []
                for i in range(3):
                    ps, _ = PS.next()
                    for k in range(4):
                        S.mm(lambda e, i=i, k=k, m=m: e.matmul(ps[:, 0:n], lhsT=wbr[i][:, k, m * 128:(m + 1) * 128],
                                                               rhs=ya[:, i * 4 + k, 0:n], start=(k == 0), stop=(k == 3)),
                             reads=[wbr[i], ya], writes=[ps], last=(k == 3))
                    pss.append(ps)
                S.op("dve", lambda e, m=m: e.tensor_tensor(out=t1[:, 0:n], in0=pss[0][:, 0:n], in1=gt[:, m, 0:n], op=ALU.mult),
                     reads=[pss[0], gt], writes=[t1])
                S.op("dve", lambda e, m=m: e.tensor_tensor(out=t2[:, 0:n], in0=pss[1][:, 0:n], in1=gt[:, 8 + m, 0:n], op=ALU.mult),
                     reads=[pss[1], gt], writes=[t2])
                S.op("pool", lambda e: e.tensor_tensor(out=t1[:, 0:n], in0=t1[:, 0:n], in1=t2[:, 0:n], op=ALU.add),
                     reads=[t1, t2], writes=[t1])
                S.op("dve", lambda e, m=m: e.tensor_tensor(out=t2[:, 0:n], in0=pss[2][:, 0:n], in1=gt[:, 16 + m, 0:n], op=ALU.mult),
                     reads=[pss[2], gt], writes=[t2])
                S.op("pool", lambda e, m=m: e.tensor_tensor(out=mT[:, m, 0:n], in0=t1[:, 0:n], in1=t2[:, 0:n], op=ALU.add),
                     reads=[t1, t2], writes=[mT])
            for m in range(8):
                ps, _ = PS.next()
                for k in range(8):
                    S.mm(lambda e, k=k, m=m: e.matmul(ps[:, 0:n], lhsT=wo[:, k, m * 128:(m + 1) * 128], rhs=mT[:, k, 0:n],
                                                      start=(k == 0), stop=(k == 7)), reads=[wo, mT], writes=[ps], last=(k == 7))
                S.op("dve", lambda e, m=m: e.scalar_tensor_tensor(out=xb[:, m, 0:n], in0=ps[:, 0:n], scalar=modT[:, 16 + m, v:v + 1],
                                                                  in1=xb[:, m, 0:n], op0=ALU.mult, op1=ALU.add),
                     reads=[ps, modT, xb], writes=[xb])
            S.dma("act", xsem, T["x1T"][:][:, t0:t0 + n].rearrange("(k p) n -> p k n", p=128), xb[:, :, 0:n],
                  reads=[xb], writes=[T["x1T"]])
            S.op("act", lambda e: e.activation(out=sq[:, :, 0:n], in_=xb[:, :, 0:n], func=AF.Square), reads=[xb], writes=[sq])
            ps, _ = PS.next()
            for k in range(8):
                S.mm(lambda e, k=k: e.matmul(ps[:, 0:n], lhsT=ones_bf[:], rhs=sq[:, k, 0:n], start=(k == 0), stop=(k == 7)),
                     reads=[ones_bf, sq], writes=[ps], last=(k == 7))
            rstd_from_ps(S, ps, n, D, eps_t, rstd)
            h2, h2s = H2.next()
            for k in range(8):
                S.op("dve", lambda e, k=k: e.scalar_tensor_tensor(out=xs[:, 0:n], in0=xb[:, k, 0:n], scalar=A2[:, k, v:v + 1],
                                                                  in1=rstd[:, 0:n], op0=ALU.mult, op1=ALU.mult),
                     reads=[xb, A2, rstd], writes=[xs])
                S.op("act", lambda e, k=k: e.activation(out=h2[:, k, 0:n], in_=xs[:, 0:n], func=AF.Identity,
                                                        bias=modT[:, 24 + k, v:v + 1]), reads=[xs, modT], writes=[h2])
            S.dma("act", h2s, T["h2T"][:][:, :, t0:t0 + n], h2[:, :, 0:n], reads=[h2], writes=[T["h2T"]])


def phase_C2(p, L, Xout, final_out=None):
    S, T, C = p.S, p.T, p.C
    eps_t, ones_bf = p.eps_t, p.ones_bf
    with S.scope():
        ld = S.dma_sem("ld")
        modT = S.sbuf("modT", [128, 48, 2], F32)
        S.dma("sp", ld, modT[:], T["modT"][:], reads=[T["modT"]], writes=[modT])
        cw = S.sbuf("cw", [128, 44, 3], F32)
        cb = S.sbuf("cb", [128, 44], F32)
        S.dma("sp", ld, cw[:], L["conv_w"], writes=[cw])
        S.dma("sp", ld, cb[:], L["conv_b"], writes=[cb])
        fn = S.sbuf("fn", [128, 8], F32)
        S.dma("sp", ld, fn[:], C["final_norm"], writes=[fn])
        hsem = S.dma_sem("h2e")
        wdsem = S.dma_sem("wd")
        with S.scope():
            PS = Rot(S, "ps", 6, [128, 512], F32, psum=True, dma=False)
            PSH = Rot(S, "psh", 2, [128, 4], F32, psum=True, dma=False)
            h2e = S.sbuf("h2e", [128, 8, 4354], BF16)
            WU = Rot(S, "wu", 3, [128, 8, 256], BF16)
            UB = Rot(S, "ub", 4, [128, 514], F32, dma=False)
            ACC = Rot(S, "acc", 4, [128, 512], F32, dma=False)
            AO = Rot(S, "ao", 3, [128, 512], BF16)
            for seg in (TT[0:9], TT[9:17]):
                base = seg[0][0]
                tot = sum(n for _, n in seg)
                seqs = []
                if base == 0:
                    seqs = [(0, CTX), (CTX, tot)]
                else:
                    seqs = [(base, base + tot)]
                S.op("dve", lambda e: e.memset(h2e[:, :, 0:1], 0.0), writes=[h2e])
                S.op("dve", lambda e: e.memset(h2e[:, :, tot + 1:tot + 2], 0.0), writes=[h2e])
                lo_tok = base - 1 if base > CTX else base
                hi_tok = base + tot + 1 if base + tot < NTOT else base + tot
                S.dma("sp", hsem, h2e[:, :, lo_tok - base + 1:hi_tok - base + 1], T["h2T"][:][:, :, lo_tok:hi_tok],
                      reads=[T["h2T"]], writes=[h2e])
                for c in range(22):
                    wu, wus = WU.next()
                    S.dma("pool", wus, wu[:, :, 0:128], L["w_up"][:, c * 128:(c + 1) * 128].rearrange("(k p) n -> p k n", p=128),
                          writes=[wu])
                    S.dma("pool", wus, wu[:, :, 128:256],
                          L["w_up"][:, FFN + c * 128:FFN + (c + 1) * 128].rearrange("(k p) n -> p k n", p=128), writes=[wu])
                    for (t0, n) in seg:
                        lo = t0 - base + 1
                        accs = []
                        for half in range(2):
                            ci = c + 22 * half
                            ps, _ = PS.next()
                            for k in range(8):
                                S.mm(lambda e, k=k, half=half: e.matmul(ps[:, 0:n], lhsT=wu[:, k, half * 128:(half + 1) * 128],
                                                                        rhs=h2e[:, k, lo:lo + n], start=(k == 0), stop=(k == 7)),
                                     reads=[wu, h2e], writes=[ps], last=(k == 7))
                            psh, _ = PSH.next()
                            lcol = lo - 1
                            rcol = lo + n
                            lz = (t0 == 0) or (t0 == CTX)
                            rz = (t0 + n == CTX) or (t0 + n == NTOT)
                            for k in range(8):
                                S.mm(lambda e, k=k, half=half: e.matmul(psh[:, 0:2], lhsT=wu[:, k, half * 128:(half + 1) * 128],
                                                                        rhs=h2e[:, k, lcol:rcol + 1:n + 1], start=(k == 0), stop=(k == 7)),
                                     reads=[wu, h2e], writes=[psh], last=(k == 7))
                            ub, _ = UB.next()
                            evac(S, "act", ub[:, 1:n + 1], ps[:, 0:n], [ps], [ub])
                            if lz:
                                S.op("dve", lambda e: e.memset(ub[:, 0:1], 0.0), writes=[ub])
                            else:
                                S.op("dve", lambda e: e.tensor_copy(out=ub[:, 0:1], in_=psh[:, 0:1]), reads=[psh], writes=[ub])
                            if rz:
                                S.op("dve", lambda e: e.memset(ub[:, n + 1:n + 2], 0.0), writes=[ub])
                            else:
                                S.op("dve", lambda e: e.tensor_copy(out=ub[:, n + 1:n + 2], in_=psh[:, 1:2]), reads=[psh], writes=[ub])
                            acc, _ = ACC.next()
                            eng = "dve" if half == 0 else "pool"
                            S.op(eng, lambda e, ci=ci: e.tensor_scalar(out=acc[:, 0:n], in0=ub[:, 0:n], scalar1=cw[:, ci, 0:1],
                                                                       scalar2=cb[:, ci:ci + 1], op0=ALU.mult, op1=ALU.add),
                                 reads=[ub, cw, cb], writes=[acc])
                            S.op("dve", lambda e, ci=ci: e.scalar_tensor_tensor(out=acc[:, 0:n], in0=ub[:, 1:n + 1], scalar=cw[:, ci, 1:2],
                                                                                in1=acc[:, 0:n], op0=ALU.mult, op1=ALU.add),
                                 reads=[ub, cw, acc], writes=[acc])
                            S.op("dve", lambda e, ci=ci: e.scalar_tensor_tensor(out=acc[:, 0:n], in0=ub[:, 2:n + 2], scalar=cw[:, ci, 2:3],
                                                                                in1=acc[:, 0:n], op0=ALU.mult, op1=ALU.add),
                                 reads=[ub, cw, acc], writes=[acc])
                            accs.append(acc)
                        S.op("act", lambda e: e.activation(out=accs[0][:, 0:n], in_=accs[0][:, 0:n], func=AF.Silu),
                             reads=[accs[0]], writes=[accs[0]])
                        ao, aos = AO.next()
                        S.op("pool", lambda e: e.tensor_tensor(out=ao[:, 0:n], in0=accs[0][:, 0:n], in1=accs[1][:, 0:n], op=ALU.mult),
                             reads=[accs[0], accs[1]], writes=[ao])
                        S.dma("act", aos, T["aT"][:][:, c, t0:t0 + n], ao[:, 0:n], reads=[ao], writes=[T["aT"]])
        with S.scope():
            PS = Rot(S, "ps", 7, [128, 512], F32, psum=True, dma=False)
            wd = S.sbuf("wd", [128, 22, D], BF16)
            for k0 in range(0, 22, 2):
                S.dma("pool", wdsem, wd[:, k0:k0 + 2, :], L["w_down"].rearrange("(k p) n -> p k n", p=128)[:, k0:k0 + 2, :], writes=[wd])
            AT = Rot(S, "at", 2, [128, 22, 512], BF16)
            XT = Rot(S, "xt", 2, [128, 8, 512], F32)
            sq = S.sbuf("sq", [128, 8, 512], BF16)
            rstd = S.sbuf("rstd", [128, 512], F32)
            for (t0, n) in TT:
                v = 1 if t0 == 0 else 0
                at, ats = AT.next()
                for i in range(0, 22, 6):
                    i2 = min(22, i + 6)
                    S.dma("sp", ats, at[:, i:i2, 0:n], T["aT"][:][:, i:i2, t0:t0 + n], reads=[T["aT"]], writes=[at])
                xb, xsem = XT.next()
                S.dma("sp", xsem, xb[:, :, 0:n], T["x1T"][:][:, t0:t0 + n].rearrange("(k p) n -> p k n", p=128),
                      reads=[T["x1T"]], writes=[xb])
                for m in range(8):
                    ps, _ = PS.next()
                    for k in range(22):
                        S.mm(lambda e, k=k, m=m: e.matmul(ps[:, 0:n], lhsT=wd[:, k, m * 128:(m + 1) * 128], rhs=at[:, k, 0:n],
                                                          start=(k == 0), stop=(k == 21)), reads=[wd, at], writes=[ps], last=(k == 21))
                    S.op("dve", lambda e, m=m: e.scalar_tensor_tensor(out=xb[:, m, 0:n], in0=ps[:, 0:n], scalar=modT[:, 40 + m, v:v + 1],
                                                                      in1=xb[:, m, 0:n], op0=ALU.mult, op1=ALU.add),
                         reads=[ps, modT, xb], writes=[xb])
                if final_out is None:
                    S.dma("act", xsem, Xout[:][:, t0:t0 + n].rearrange("(k p) n -> p k n", p=128), xb[:, :, 0:n],
                          reads=[xb], writes=[Xout])
                elif t0 >= CTX:
                    S.op("act", lambda e: e.activation(out=sq[:, :, 0:n], in_=xb[:, :, 0:n], func=AF.Square), reads=[xb], writes=[sq])
                    ps, _ = PS.next()
                    for k in range(8):
                        S.mm(lambda e, k=k: e.matmul(ps[:, 0:n], lhsT=ones_bf[:], rhs=sq[:, k, 0:n], start=(k == 0), stop=(k == 7)),
                             reads=[ones_bf, sq], writes=[ps], last=(k == 7))
                    rstd_from_ps(S, ps, n, D, eps_t, rstd)
                    for k in range(8):
                        S.op("dve", lambda e, k=k: e.scalar_tensor_tensor(out=xb[:, k, 0:n], in0=xb[:, k, 0:n], scalar=fn[:, k:k + 1],
                                                                          in1=rstd[:, 0:n], op0=ALU.mult, op1=ALU.mult),
                             reads=[xb, fn, rstd], writes=[xb])
                    S.dma("act", xsem, final_out[:][:, t0 - CTX:t0 - CTX + n].rearrange("(k p) n -> p k n", p=128), xb[:, :, 0:n],
                          reads=[xb], writes=[final_out])


PHASES_ALL = ("A", "B1", "B2", "B3", "C1", "C2")
PHASE_W = {"A": ("w_mod", "b_mod", "norm_mix", "w_in", "bqn", "bkvn", "w_uq", "w_ukv", "w_gate", "b_gate"),
           "B1": ("a_sink",), "B2": (), "B3": ("c_hn",), "C1": ("norm_ffn", "w_br_a", "w_br_b", "w_br_c", "w_out"),
           "C2": ("conv_w", "conv_b", "w_up", "w_down")}


def needed_weights(phases):
    need = set()
    for ph in phases:
        need.update(PHASE_W[ph])
    return [w for w in LAYER_W if w[0] in need]


def build_program(nlayers, final, phases=PHASES_ALL, ext_scratch=(), stop=None):
    nc = bass.Bass("TRN2", target_bir_lowering=False)
    xin_ap = dram_in(nc, "xT", (D, NTOT), F32)
    Cap = {n: dram_in(nc, n, s, d) for n, s, d in CONSTS}
    Lw = [{n: dram_in(nc, "%s_%d" % (n, l), s, d) for n, s, d in needed_weights(phases)} for l in range(nlayers)]
    if final:
        out_ap = dram_out(nc, "outT", (D, SEQ), F32)
    else:
        out_ap = dram_out(nc, "xT_out", (D, NTOT), F32)
    with contextlib.ExitStack() as st:
        S = Sched(nc, st)
        p = P()
        p.nc, p.S, p.C = nc, S, Cap
        p.stop = stop
        p.T = {}
        for n, s, d in SCRATCH:
            if n in ext_scratch:
                kind = ext_scratch[n]
                ap = dram_in(nc, n, s, d) if kind == "in" else dram_out(nc, n, s, d)
                p.T[n] = Buf(ap, n)
            else:
                p.T[n] = S.dram(n, s, d)
        Xext = Buf(xin_ap, "xT")
        Oext = Buf(out_ap, "out")
        ld = S.dma_sem("ldc")
        p.eps_t = S.sbuf("eps", [128, 1], F32)
        p.one_t = S.sbuf("one", [128, 1], F32)
        p.ones_bf = S.sbuf("ones_bf", [128, 128], BF16)
        S.op("dve", lambda e: e.memset(p.eps_t[:], EPS), writes=[p.eps_t])
        S.op("dve", lambda e: e.memset(p.one_t[:], 1.0), writes=[p.one_t])
        S.op("dve", lambda e: e.memset(p.ones_bf[:], 1.0), writes=[p.ones_bf])
        for nm, shape in (("permA", [128, 128]), ("permB", [96, 96]), ("permK", [32, 32]), ("sel", [32, 96]),
                          ("maskLo", [128, 512]), ("maskHi", [128, 512]), ("triF", [64, 512]), ("triB", [64, 512]),
                          ("ident", [64, 64]), ("sv", [1, 128])):
            b_ = S.sbuf(nm, shape, BF16)
            S.dma("sp", ld, b_[:], Cap[nm], writes=[b_])
            setattr(p, nm, b_)
        p.cv = S.sbuf("cv", [128, 8, 2], F32)
        S.dma("sp", ld, p.cv[:], Cap["cvec"], writes=[p.cv])
        for l in range(nlayers):
            Xin = Xext if l == 0 else p.T["XA" if l % 2 == 1 else "XB"]
            last = (l == nlayers - 1)
            Xout = Oext if (last and not final) else p.T["XA" if (l + 1) % 2 == 1 else "XB"]
            if "A" in phases:
                phase_A(p, Lw[l], Xin)
            if "B1" in phases:
                phase_B1(p, Lw[l])
            if "B2" in phases:
                phase_B2(p, Lw[l])
            if "B3" in phases:
                phase_B3(p, Lw[l])
            if "C1" in phases:
                phase_C1(p, Lw[l], Xin)
            if "C2" in phases:
                phase_C2(p, Lw[l], Xout, final_out=(Oext if (last and final) else None))
        S.finish("sp")
        p.ninst = S.ninst
    nc._ninst = p.ninst
    return nc


_PROGS = {}
_CONSTS = {}


def _pk(v):
    v = np.asarray(v)
    k = v.shape[0] // 128
    out = v.reshape((k, 128) + v.shape[1:])
    return np.ascontiguousarray(np.moveaxis(out, 0, 1))


def host_consts():
    if "c" in _CONSTS:
        return _CONSTS["c"]
    pos = np.concatenate([np.zeros(CTX, np.int64), np.arange(SEQ)])
    is_ctx = np.concatenate([np.ones(CTX, bool), np.zeros(SEQ, bool)])
    c = dict(_rope_tables(pos, is_ctx))
    c.update(_perm_consts())
    j = np.arange(128)[:, None]
    i = (np.arange(512) % 128)[None, :]
    c["maskLo"] = (j >= i).astype(np.float32).astype(NPBF)
    c["maskHi"] = (j <= i).astype(np.float32).astype(NPBF)
    j = np.arange(64)[:, None]
    i = (np.arange(512) % 64)[None, :]
    c["triF"] = (j <= i).astype(np.float32).astype(NPBF)
    c["triB"] = (j >= i).astype(np.float32).astype(NPBF)
    rm = np.ones((64, NTOT), np.float32)
    rm[:, ::64] = 0.0
    c["rmask"] = rm
    c["ident"] = np.eye(64, dtype=np.float32).astype(NPBF)
    sv = np.zeros((1, 128), np.float32)
    sv[0, 64:] = 1.0
    c["sv"] = sv.astype(NPBF)
    _CONSTS["c"] = c
    return c


def layer_inputs(W, l, names):
    m = {}
    g = {
        "w_mod": lambda: W["w_mod"][l], "b_mod": lambda: _pk(W["b_mod"][l]), "norm_mix": lambda: _pk(W["norm_mix"][l]),
        "norm_ffn": lambda: _pk(W["norm_ffn"][l]), "w_in": lambda: W["w_in"][l],
        "a_sink": lambda: np.ascontiguousarray(W["a_sink"][l].reshape(1, 8)),
        "bqn": lambda: _pk(W["b_q_norm"][l]), "bkvn": lambda: _pk(W["b_kv_norm"][l]),
        "w_uq": lambda: W["b_w_uq"][l], "w_ukv": lambda: W["b_w_ukv"][l], "w_gate": lambda: W["c_w_gate"][l],
        "b_gate": lambda: np.ascontiguousarray(W["c_b_gate"][l].reshape(2, 2, 128).transpose(2, 0, 1)),
        "c_hn": lambda: np.ascontiguousarray(W["c_head_norm"][l].reshape(4, 128).T),
        "w_br_a": lambda: W["w_br_a"][l], "w_br_b": lambda: W["w_br_b"][l], "w_br_c": lambda: W["w_br_c"][l],
        "w_out": lambda: W["w_out"][l], "w_up": lambda: W["w_up"][l],
        "conv_w": lambda: np.ascontiguousarray(W["conv_w"][l].reshape(3, 44, 128).transpose(2, 1, 0)),
        "conv_b": lambda: _pk(W["conv_b"][l]), "w_down": lambda: W["w_down"][l],
    }
    for n in names:
        m[n] = np.ascontiguousarray(g[n]())
    return m


def core_inputs(W, b, layers, phases=PHASES_ALL):
    c = dict(host_consts())
    c["cvec"] = _pk(np.stack([W["c"][b], W["c_ctx"]], axis=1))
    c["final_norm"] = _pk(W["final_norm"])
    m = {n: c[n] for n, _, _ in CONSTS}
    names = [w[0] for w in needed_weights(phases)]
    for i, l in enumerate(layers):
        for n, v in layer_inputs(W, l, names).items():
            m["%s_%d" % (n, i)] = v
    return m


def kernel(**W):
    W = {k: np.asarray(v) for k, v in W.items()}
    if "fused" not in _PROGS:
        _PROGS["fused"] = build_program(DEPTH, True)
    maps = []
    for b in range(2):
        m = core_inputs(W, b, list(range(DEPTH)))
        m["xT"] = np.ascontiguousarray(np.concatenate([W["ctx"][b], W["x"][b]], axis=0).T)
        maps.append(m)
    res = run_bass_kernel_spmd(_PROGS["fused"], maps, core_ids=[0, 1]).results
    out = np.stack([np.ascontiguousarray(np.asarray(res[b]["outT"]).T) for b in range(2)], axis=0)
    return out.astype(np.float32)
```

```python
import contextlib
import numpy as np
import ml_dtypes
import concourse.bass as bass
import concourse.mybir as mybir
from concourse.bass_utils import run_bass_kernel_spmd

F32 = mybir.dt.float32
BF16 = mybir.dt.bfloat16
AF = mybir.ActivationFunctionType
ALU = mybir.AluOpType
NPBF = ml_dtypes.bfloat16

NCORES = 8
D = 1024
SEQ = 8192
CTX = 256
DEPTH = 4
NTOT = CTX + SEQ
NKB = NTOT // 128
EPS = 1e-6
IN_DIM = 5824
O_AQ, O_AK, O_AV, O_BQ, O_BKV, O_BKR, O_CQ, O_CK, O_CV, O_CR, O_CG, O_GATE = (
    0, 512, 640, 768, 1024, 1152, 1184, 1440, 1696, 2208, 2720, 2752)
FFN = 2816
TT = [(0, CTX)] + [(CTX + 512 * i, 512) for i in range(16)]
SEGS = [TT[0:5], TT[5:9], TT[9:13], TT[13:17]]

SAME_ENGINE_SYNC = True
_STOP = None


class _Stop(Exception):
    pass


def chk(name):
    if _STOP == name:
        raise _Stop()


class Buf:
    __slots__ = ("t", "name", "lw", "rd")

    def __init__(self, t, name=""):
        self.t = t
        self.name = name
        self.lw = {}
        self.rd = {}

    def __getitem__(self, idx):
        return self.t[idx]


class Sched:
    def __init__(self, nc, stack):
        self.nc = nc
        self.stack = stack
        self.root = stack
        self.engs = {"pe": nc.tensor, "act": nc.scalar, "dve": nc.vector,
                     "pool": nc.gpsimd, "sp": nc.sync}
        self.sems = {}
        self.cnt = {}
        self.seen = {e: {} for e in self.engs}
        for e in ("pe", "act", "dve", "pool"):
            self.sems[e] = stack.enter_context(nc.semaphore("s_" + e))
            self.cnt[e] = 0
        self.ninst = 0
        self._uid = 0
        self.sem_bufs = {}
        self._free = []
        self._scoped = [[]]

    def uid(self, p):
        self._uid += 1
        return "%s_%d" % (p, self._uid)

    @contextlib.contextmanager
    def scope(self):
        old = self.stack
        self._scoped.append([])
        with contextlib.ExitStack() as st:
            self.stack = st
            try:
                yield
            finally:
                self.barrier()
                self.stack = old
                self._free.extend(self._scoped.pop())

    def sbuf(self, name, shape, dt):
        return Buf(self.stack.enter_context(self.nc.sbuf_tensor(self.uid(name), shape, dt)), name)

    def psum(self, name, shape, dt):
        return Buf(self.stack.enter_context(self.nc.psum_tensor(self.uid(name), shape, dt)), name)

    def dram(self, name, shape, dt):
        return Buf(self.nc.dram_tensor(self.uid(name), list(shape), dt).ap(), name)

    def dma_sem(self, name):
        if self._free:
            key = self._free.pop()
            for sfx in ("~hw", "~sw"):
                if key + sfx in self.sem_bufs:
                    self.sem_bufs[key + sfx] = []
        else:
            key = self.uid("d")
        self._scoped[-1].append(key)
        return key

    def _sem(self, key):
        if key not in self.sems:
            self.sems[key] = self.root.enter_context(self.nc.semaphore(key.replace("~", "_")))
            self.cnt[key] = 0
            self.sem_bufs[key] = []
        return self.sems[key]

    def _wait(self, eng, deps):
        e = self.engs[eng]
        for (k, c) in sorted(deps, key=lambda x: str(x[0])):
            if k == eng and not SAME_ENGINE_SYNC:
                continue
            if k == "pe" and eng == "pe":
                continue
            if self.seen[eng].get(k, 0) >= c:
                continue
            e.wait_ge(self.sems[k], c)
            self.seen[eng][k] = c

    @staticmethod
    def _deps(reads, writes):
        deps = set()
        for b in reads:
            deps.update(b.lw.items())
        for b in writes:
            deps.update(b.lw.items())
            deps.update(b.rd.items())
        return deps

    def op(self, eng, fn, reads=(), writes=()):
        self._wait(eng, self._deps(reads, writes))
        ins = fn(self.engs[eng])
        self.cnt[eng] += 1
        ins.then_inc(self.sems[eng], 1)
        for b in reads:
            b.rd[eng] = self.cnt[eng]
        for b in writes:
            b.lw[eng] = self.cnt[eng]
            b.rd = {}
        self.ninst += 1
        return ins

    def mm(self, fn, reads=(), writes=(), last=True):
        self._wait("pe", self._deps(reads, writes))
        ins = fn(self.engs["pe"])
        self.ninst += 1
        if last:
            self.cnt["pe"] += 1
            ins.then_inc(self.sems["pe"], 1)
            for b in writes:
                b.lw["pe"] = self.cnt["pe"]
                b.rd = {}
        for b in reads:
            b.rd["pe"] = self.cnt["pe"] + (0 if last else 1)
        return ins

    def dma(self, q, semkey, out_ap, in_ap, reads=(), writes=(), **kw):
        semkey = semkey + ("~sw" if q == "pool" else "~hw")
        sem = self._sem(semkey)
        self._wait(q, self._deps(reads, writes))
        ins = self.engs[q].dma_start(out=out_ap, in_=in_ap, **kw)
        self.cnt[semkey] += 16
        c = self.cnt[semkey]
        ins.then_inc(sem, 16)
        for b in self.sem_bufs[semkey]:
            if semkey in b.lw:
                b.lw[semkey] = c
            if semkey in b.rd:
                b.rd[semkey] = c
        for b in reads:
            b.rd[semkey] = c
            if b not in self.sem_bufs[semkey]:
                self.sem_bufs[semkey].append(b)
        for b in writes:
            b.lw[semkey] = c
            b.rd = {}
            if b not in self.sem_bufs[semkey]:
                self.sem_bufs[semkey].append(b)
        return ins

    def barrier(self):
        snap = [(k, c) for k, c in self.cnt.items() if c > 0]
        for eng in ("pe", "act", "dve", "pool", "sp"):
            e = self.engs[eng]
            for (k, c) in snap:
                if self.seen[eng].get(k, 0) >= c:
                    continue
                e.wait_ge(self.sems[k], c)
                self.seen[eng][k] = c

    def finish(self, eng="sp"):
        for k, c in self.cnt.items():
            if c > 0:
                self._wait(eng, {(k, c)})

    def wait_all(self, eng, bufs):
        deps = set()
        for b in bufs:
            deps.update(b.lw.items())
            deps.update(b.rd.items())
        self._wait(eng, deps)


class Rot:
    def __init__(self, S, name, n, shape, dt, psum=False, dma=True):
        self.bufs = [(S.psum if psum else S.sbuf)(name + str(i), shape, dt) for i in range(n)]
        self.sems = [S.dma_sem(name + str(i)) for i in range(n)] if dma else [None] * n
        self.i = 0

    def next(self):
        b, s = self.bufs[self.i], self.sems[self.i]
        self.i = (self.i + 1) % len(self.bufs)
        return b, s


class Out:
    def __init__(self):
        self.bufs = []

    def add(self, b):
        if b not in self.bufs:
            self.bufs.append(b)


def dram_in(nc, name, shape, dt):
    return nc.dram_tensor(name, list(shape), dt, kind="ExternalInput").ap()


def dram_out(nc, name, shape, dt):
    return nc.dram_tensor(name, list(shape), dt, kind="ExternalOutput").ap()


def _rope_tables(pos, is_ctx):
    n = pos.shape[0]
    row = (pos // 64).astype(np.float32)
    col = (pos % 64).astype(np.float32)

    def tab(nfreq):
        inv = (np.float32(10000.0) ** (-np.arange(nfreq, dtype=np.float32) / np.float32(nfreq))).astype(np.float32)
        ang = np.concatenate([row[:, None] * inv, col[:, None] * inv], axis=-1).astype(np.float32)
        c, s = np.cos(ang).astype(np.float32), np.sin(ang).astype(np.float32)
        c[is_ctx] = 1.0
        s[is_ctx] = 0.0
        return c, s

    ca, sa = tab(16)
    cb, sb = tab(8)
    cosA = np.empty((128, n), np.float32)
    sinA = np.empty((128, n), np.float32)
    for p in range(128):
        d = p % 64
        i = d % 32
        cosA[p] = ca[:, i]
        sinA[p] = (-sa[:, i]) if d < 32 else sa[:, i]
    cosB = np.ones((96, n), np.float32)
    sinB = np.zeros((96, n), np.float32)
    cosK = np.empty((32, n), np.float32)
    sinK = np.empty((32, n), np.float32)
    for r in range(32):
        i = r % 16
        cosK[r] = cb[:, i]
        sinK[r] = (-sb[:, i]) if r < 16 else sb[:, i]
    cosB[64:96] = cosK
    sinB[64:96] = sinK
    return dict(cosA=cosA, sinA=sinA, cosB=cosB, sinB=sinB, cosK=cosK, sinK=sinK)


def _perm_consts():
    permA = np.zeros((128, 128), np.float32)
    for m in range(128):
        d = m % 64
        permA[m + 32 if d < 32 else m - 32, m] = 1.0
    permB = np.zeros((96, 96), np.float32)
    for m in range(64, 96):
        r = m - 64
        permB[m + 16 if r < 16 else m - 16, m] = 1.0
    permK = np.zeros((32, 32), np.float32)
    for m in range(32):
        permK[m + 16 if m < 16 else m - 16, m] = 1.0
    sel = np.zeros((32, 96), np.float32)
    for k in range(32):
        sel[k, 64 + k] = 1.0
    return dict(permA=permA.astype(NPBF), permB=permB.astype(NPBF), permK=permK.astype(NPBF),
                sel=sel.astype(NPBF))


LAYER_W = [
    ("w_mod", (D, 6 * D), F32), ("b_mod", (128, 48), F32), ("norm_mix", (128, 8), F32), ("norm_ffn", (128, 8), F32),
    ("w_in", (D, IN_DIM), F32), ("a_sink", (1, 8), F32), ("bqn", (128, 2), F32), ("bkvn", (128, 1), F32),
    ("w_uq", (256, 768), F32), ("w_ukv", (128, 1024), F32), ("w_gate", (2, 16, 256), F32), ("b_gate", (128, 2, 2), F32),
    ("c_hn", (128, 4), F32), ("w_br_a", (512, D), F32), ("w_br_b", (512, D), F32), ("w_br_c", (512, D), F32),
    ("w_out", (D, D), F32), ("w_up", (D, 2 * FFN), F32), ("conv_w", (128, 44, 3), F32), ("conv_b", (128, 44), F32),
    ("w_down", (FFN, D), F32),
]
CONSTS = [
    ("cvec", (128, 8, 2), F32), ("final_norm", (128, 8), F32),
    ("cosA", (128, NTOT), F32), ("sinA", (128, NTOT), F32), ("cosB", (96, NTOT), F32), ("sinB", (96, NTOT), F32),
    ("cosK", (32, NTOT), F32), ("sinK", (32, NTOT), F32),
    ("permA", (128, 128), BF16), ("permB", (96, 96), BF16), ("permK", (32, 32), BF16), ("sel", (32, 96), BF16),
    ("maskLo", (128, 512), BF16), ("maskHi", (128, 512), BF16), ("triF", (64, 512), BF16), ("triB", (64, 512), BF16),
    ("rmask", (64, NTOT), F32), ("ident", (64, 64), BF16), ("sv", (1, 128), BF16),
]
SCRATCH = [
    ("modT", (128, 48, 2), F32),
    ("qaT", (128, 4, NTOT), BF16), ("kaT", (128, NTOT), BF16), ("vaA", (NTOT, 2, 128), BF16),
    ("qbT", (96, 8, NTOT), BF16), ("kbT", (96, 8, NTOT), BF16), ("vbA", (NTOT, 8, 128), BF16),
    ("cqT", (128, 2, NTOT), BF16), ("ckT", (128, 2, NTOT), BF16), ("cvv", (NTOT, 512), BF16),
    ("crT", (128, 4, NTOT), BF16), ("gfT", (128, 2, NTOT), F32), ("gbT", (128, 2, NTOT), F32),
    ("gatesT", (128, 24, NTOT), BF16),
    ("yaT", (128, 4, NTOT), BF16), ("ybT", (128, 4, NTOT), BF16), ("ycT", (128, 4, NTOT), BF16),
    ("x1T", (D, NTOT), F32), ("h2T", (128, 8, NTOT), BF16), ("aT", (128, 22, NTOT), BF16),
    ("XA", (D, NTOT), F32), ("XB", (D, NTOT), F32),
]


class P:
    pass


def evac(S, eng, out_ap, in_ap, reads, writes, func=None, scale=1.0):
    if eng == "act":
        return S.op("act", lambda e: e.activation(out=out_ap, in_=in_ap, func=func or AF.Copy, scale=scale),
                    reads=reads, writes=writes)
    return S.op(eng, lambda e: e.tensor_copy(out=out_ap, in_=in_ap), reads=reads, writes=writes)


def rstd_from_ps(S, ps, n, dim, eps_t, out):
    S.op("act", lambda e: e.activation(out=out[:, 0:n], in_=ps[:, 0:n], func=AF.Ln, scale=1.0 / dim, bias=eps_t[:]),
         reads=[ps, eps_t], writes=[out])
    S.op("act", lambda e: e.activation(out=out[:, 0:n], in_=out[:, 0:n], func=AF.Exp, scale=-0.5), reads=[out], writes=[out])


def phase_A(p, L, Xin):
    S, T, C = p.S, p.T, p.C
    nc = p.nc
    eps_t, one_t, ones_bf = p.eps_t, p.one_t, p.ones_bf
    with S.scope():
        ld = S.dma_sem("ldc")
        bmod = S.sbuf("bmod", [128, 48], F32)
        nmix = S.sbuf("nmix", [128, 8], F32)
        bqn = S.sbuf("bqn", [128, 2], F32)
        bkvn = S.sbuf("bkvn", [128, 1], F32)
        bg = S.sbuf("bg", [128, 2, 2], F32)
        for b_, n_ in ((bmod, "b_mod"), (nmix, "norm_mix"), (bqn, "bqn"), (bkvn, "bkvn"), (bg, "b_gate")):
            S.dma("sp", ld, b_[:], L[n_], writes=[b_])
        wuq = S.sbuf("wuq", [128, 2, 768], BF16)
        S.dma("pool", ld, wuq[:], L["w_uq"].rearrange("(k p) n -> p k n", p=128), writes=[wuq])
        wkn = S.sbuf("wkn", [128, 8, 96], BF16)
        S.op("dve", lambda e: e.memset(wkn[:], 0.0), writes=[wkn])
        S.dma("pool", ld, wkn[:, :, 0:64], L["w_ukv"].rearrange("p (h c) -> p h c", c=128)[:, :, 0:64], writes=[wkn])
        wvb = S.sbuf("wvb", [128, 8, 64], BF16)
        S.dma("pool", ld, wvb[:], L["w_ukv"].rearrange("p (h c) -> p h c", c=128)[:, :, 64:128], writes=[wvb])
        wg = S.sbuf("wg", [32, 2, 256], BF16)
        S.op("dve", lambda e: e.memset(wg[:], 0.0), writes=[wg])
        S.dma("pool", ld, wg[0:16, 0, :], L["w_gate"][0], writes=[wg])
        S.dma("pool", ld, wg[16:32, 1, :], L["w_gate"][1], writes=[wg])
        nbg = S.sbuf("nbg", [128, 2, 2], F32)
        S.op("dve", lambda e: e.tensor_scalar_mul(out=nbg[:], in0=bg[:], scalar1=-1.0), reads=[bg], writes=[nbg])

        PS = Rot(S, "ps", 7, [128, 512], F32, psum=True, dma=False)

        modT = S.sbuf("modT", [128, 48, 2], F32)
        sc = S.sbuf("silu_c", [128, 8, 2], F32)
        tmp8 = S.sbuf("tmp8", [128, 8, 2], F32)
        cv = p.cv
        S.op("act", lambda e: e.activation(out=tmp8[:], in_=cv[:], func=AF.Exp, scale=-1.0), reads=[cv], writes=[tmp8])
        S.op("dve", lambda e: e.tensor_scalar_add(out=tmp8[:], in0=tmp8[:], scalar1=1.0), reads=[tmp8], writes=[tmp8])
        S.op("dve", lambda e: e.reciprocal(out=tmp8[:], in_=tmp8[:]), reads=[tmp8], writes=[tmp8])
        S.op("dve", lambda e: e.tensor_tensor(out=sc[:], in0=tmp8[:], in1=cv[:], op=ALU.mult), reads=[tmp8, cv], writes=[sc])
        scb = S.sbuf("silu_cb", [128, 8, 2], BF16)
        S.op("dve", lambda e: e.tensor_copy(out=scb[:], in_=sc[:]), reads=[sc], writes=[scb])
        with S.scope():
            WM = Rot(S, "wm", 2, [128, 8, 512], BF16)
            for piece in range(12):
                wb, ws = WM.next()
                S.dma("pool", ws, wb[:], L["w_mod"][:, piece * 512:(piece + 1) * 512].rearrange("(k p) n -> p k n", p=128),
                      writes=[wb])
                ps, _ = PS.next()
                for j in range(4):
                    for k in range(8):
                        S.mm(lambda e, j=j, k=k: e.matmul(ps[:, j * 2:j * 2 + 2], lhsT=wb[:, k, j * 128:(j + 1) * 128],
                                                          rhs=scb[:, k, :], start=(k == 0), stop=(k == 7)),
                             reads=[wb, scb], writes=[ps], last=(j == 3 and k == 7))
                S.op("dve", lambda e, piece=piece: e.tensor_tensor(
                    out=modT[:, piece * 4:(piece + 1) * 4, :], in0=ps[:, 0:8].rearrange("p (j v) -> p j v", v=2),
                    in1=bmod[:, piece * 4:(piece + 1) * 4].unsqueeze(2).to_broadcast([128, 4, 2]), op=ALU.add),
                    reads=[ps, bmod], writes=[modT])
        msem = S.dma_sem("modst")
        if p.stop == "mod":
            S.dma("act", msem, T["modT"][:], modT[:], reads=[modT], writes=[T["modT"]])
            return
        S.dma("act", msem, T["modT"][:], modT[:], reads=[modT], writes=[T["modT"]])
        A1 = S.sbuf("A1", [128, 8, 2], F32)
        S.op("dve", lambda e: e.tensor_scalar_add(out=A1[:], in0=modT[:, 8:16, :], scalar1=1.0), reads=[modT], writes=[A1])
        S.op("dve", lambda e: e.tensor_tensor(out=A1[:], in0=A1[:], in1=nmix[:].unsqueeze(2).to_broadcast([128, 8, 2]),
                                              op=ALU.mult), reads=[A1, nmix], writes=[A1])

        hT = S.sbuf("hT", [128, 8, 2304], BF16)
        WP = Rot(S, "wp", 3, [128, 8, 512], BF16)
        STB = Rot(S, "stb", 4, [128, 512], BF16)
        STF = Rot(S, "stf", 3, [128, 512], F32)
        TAB = Rot(S, "tab", 4, [128, 512], F32)
        XT = Rot(S, "xt", 2, [128, 8, 512], F32)
        sq = S.sbuf("sq", [128, 8, 512], BF16)
        rstd = S.sbuf("rstd", [128, 512], F32)
        xs = S.sbuf("xs", [128, 512], F32)
        cq = S.sbuf("cq", [128, 2, 512], F32)
        sq2 = S.sbuf("sq2", [128, 2, 512], BF16)
        rs2 = S.sbuf("rs2", [128, 512], F32)
        cqn = S.sbuf("cqn", [128, 2, 512], BF16)
        ckvn = S.sbuf("ckvn", [128, 512], BF16)
        vst = S.sbuf("vst", [128, 8, 128], BF16)
        S.op("dve", lambda e: e.memset(vst[:], 1.0), writes=[vst])
        vsem = S.dma_sem("vst")
        cvst = S.sbuf("cvst", [128, 512], BF16)
        csem = S.dma_sem("cvst")
        cg = S.sbuf("cg", [32, 512], BF16)
        krb = S.sbuf("krb", [32, 512], BF16)

        def load_w(col0, ncols):
            wb, ws = WP.next()
            S.dma("pool", ws, wb[:, :, 0:ncols], L["w_in"][:, col0:col0 + ncols].rearrange("(k p) n -> p k n", p=128),
                  writes=[wb])
            return wb

        def store(dst, dst_ap, buf, src_ap, sem):
            S.dma("act", sem, dst_ap, src_ap, reads=[buf], writes=[dst])

        for seg in SEGS:
            loc = {}
            l0 = 0
            for (t0, n) in seg:
                loc[t0] = l0
                l0 += n

            for (t0, n) in seg:
                v = 1 if t0 == 0 else 0
                lo = loc[t0]
                xb, xsem = XT.next()
                S.dma("sp", xsem, xb[:, :, 0:n], Xin[:][:, t0:t0 + n].rearrange("(k p) n -> p k n", p=128),
                      reads=[Xin], writes=[xb])
                S.op("act", lambda e: e.activation(out=sq[:, :, 0:n], in_=xb[:, :, 0:n], func=AF.Square), reads=[xb], writes=[sq])
                ps, _ = PS.next()
                for k in range(8):
                    S.mm(lambda e, k=k: e.matmul(ps[:, 0:n], lhsT=ones_bf[:], rhs=sq[:, k, 0:n], start=(k == 0), stop=(k == 7)),
                         reads=[ones_bf, sq], writes=[ps], last=(k == 7))
                rstd_from_ps(S, ps, n, D, eps_t, rstd)
                for k in range(8):
                    S.op("dve", lambda e, k=k: e.scalar_tensor_tensor(out=xs[:, 0:n], in0=xb[:, k, 0:n], scalar=A1[:, k, v:v + 1],
                                                                      in1=rstd[:, 0:n], op0=ALU.mult, op1=ALU.mult),
                         reads=[xb, A1, rstd], writes=[xs])
                    S.op("act", lambda e, k=k: e.activation(out=hT[:, k, lo:lo + n], in_=xs[:, 0:n], func=AF.Identity,
                                                            bias=modT[:, k, v:v + 1]),
                         reads=[xs, modT], writes=[hT])

            def proj_fm(ps, wb, c0, m, t0, n):
                lo = loc[t0]
                for k in range(8):
                    S.mm(lambda e, k=k: e.matmul(ps[0:m, 0:n], lhsT=wb[:, k, c0:c0 + m], rhs=hT[:, k, lo:lo + n],
                                                 start=(k == 0), stop=(k == 7)),
                         reads=[wb, hT], writes=[ps], last=(k == 7))

            def rope(ps, m, t0, n, perm, cosn, sinn, dst, dst_ap, obuf=None):
                pre, _ = STB.next()
                evac(S, "act", pre[0:m, 0:n], ps[0:m, 0:n], [ps], [pre])
                ct, cs = TAB.next()
                S.dma("sp", cs, ct[0:m, 0:n], C[cosn][:, t0:t0 + n], writes=[ct])
                stt, ss = TAB.next()
                S.dma("sp", ss, stt[0:m, 0:n], C[sinn][:, t0:t0 + n], writes=[stt])
                ps2, _ = PS.next()
                S.mm(lambda e: e.matmul(ps2[0:m, 0:n], lhsT=perm[:], rhs=pre[0:m, 0:n], start=True, stop=True),
                     reads=[perm, pre], writes=[ps2])
                a, _ = STF.next()
                S.op("dve", lambda e: e.tensor_tensor(out=a[0:m, 0:n], in0=pre[0:m, 0:n], in1=ct[0:m, 0:n], op=ALU.mult),
                     reads=[pre, ct], writes=[a])
                b, _ = STF.next()
                S.op("dve", lambda e: e.tensor_tensor(out=b[0:m, 0:n], in0=ps2[0:m, 0:n], in1=stt[0:m, 0:n], op=ALU.mult),
                     reads=[ps2, stt], writes=[b])
                if obuf is not None:
                    o, osm = obuf, None
                else:
                    o, osm = STB.next()
                S.op("dve", lambda e: e.tensor_tensor(out=o[0:m, 0:n], in0=a[0:m, 0:n], in1=b[0:m, 0:n], op=ALU.add),
                     reads=[a, b], writes=[o])
                if dst is not None:
                    store(dst, dst_ap, o, o[0:m, 0:n], osm)
                return o

            def simple_group(col0, nchunks, dst, func=AF.Copy, scale=1.0):
                for c0 in range(0, nchunks, 4):
                    ncn = min(4, nchunks - c0)
                    wb = load_w(col0 + c0 * 128, ncn * 128)
                    for (t0, n) in seg:
                        for j in range(ncn):
                            ps, _ = PS.next()
                            proj_fm(ps, wb, j * 128, 128, t0, n)
                            o, osm = STB.next()
                            evac(S, "act", o[:, 0:n], ps[:, 0:n], [ps], [o], func=func, scale=scale)
                            store(dst, dst[:][:, c0 + j, t0:t0 + n], o, o[:, 0:n], osm)

            if p.stop == "hT":
                return
            wb, ws = WP.next()
            for half in range(2):
                for g in range(4):
                    c0 = O_AQ + (half * 4 + g) * 64
                    S.dma("pool", ws, wb[:, :, g * 128 + half * 64:g * 128 + (half + 1) * 64],
                          L["w_in"][:, c0:c0 + 64].rearrange("(k p) c -> p k c", p=128), writes=[wb])
            for (t0, n) in seg:
                for g in range(4):
                    ps, _ = PS.next()
                    proj_fm(ps, wb, g * 128, 128, t0, n)
                    rope(ps, 128, t0, n, p.permA, "cosA", "sinA", T["qaT"], T["qaT"][:][:, g, t0:t0 + n])
            if p.stop == "G1":
                return
            wb = load_w(O_AK, 256)
            for (t0, n) in seg:
                lo = loc[t0]
                ps, _ = PS.next()
                proj_fm(ps, wb, 0, 128, t0, n)
                rope(ps, 128, t0, n, p.permA, "cosA", "sinA", T["kaT"], T["kaT"][:][:, t0:t0 + n])
                for s0 in range(0, n, 128):
                    ps, _ = PS.next()
                    for k in range(8):
                        S.mm(lambda e, k=k: e.matmul(ps[:, 0:128], lhsT=hT[:, k, lo + s0:lo + s0 + 128], rhs=wb[:, k, 128:256],
                                                     start=(k == 0), stop=(k == 7)),
                             reads=[wb, hT], writes=[ps], last=(k == 7))
                    S.op("act", lambda e: e.activation(out=vst[:, 0:2, 0:64],
                                                       in_=ps[:, 0:128].rearrange("p (h c) -> p h c", c=64), func=AF.Copy),
                         reads=[ps], writes=[vst])
                    store(T["vaA"], T["vaA"][:][t0 + s0:t0 + s0 + 128], vst, vst[:, 0:2, :], vsem)
            if p.stop == "G3":
                return
            wb = load_w(O_BQ, 256)
            wb2 = load_w(O_BKV, 256)
            for (t0, n) in seg:
                for c in range(2):
                    ps, _ = PS.next()
                    proj_fm(ps, wb, c * 128, 128, t0, n)
                    S.op("dve", lambda e, c=c: e.tensor_copy(out=cq[:, c, 0:n], in_=ps[:, 0:n]), reads=[ps], writes=[cq])
                    S.op("act", lambda e, c=c: e.activation(out=sq2[:, c, 0:n], in_=cq[:, c, 0:n], func=AF.Square),
                         reads=[cq], writes=[sq2])
                ps, _ = PS.next()
                for c in range(2):
                    S.mm(lambda e, c=c: e.matmul(ps[:, 0:n], lhsT=ones_bf[:], rhs=sq2[:, c, 0:n], start=(c == 0), stop=(c == 1)),
                         reads=[ones_bf, sq2], writes=[ps], last=(c == 1))
                rstd_from_ps(S, ps, n, 256, eps_t, rs2)
                for c in range(2):
                    S.op("dve", lambda e, c=c: e.scalar_tensor_tensor(out=cqn[:, c, 0:n], in0=cq[:, c, 0:n], scalar=bqn[:, c:c + 1],
                                                                      in1=rs2[:, 0:n], op0=ALU.mult, op1=ALU.mult),
                         reads=[cq, bqn, rs2], writes=[cqn])
                for h in range(8):
                    ps, _ = PS.next()
                    for c in range(2):
                        S.mm(lambda e, c=c, h=h: e.matmul(ps[0:96, 0:n], lhsT=wuq[:, c, h * 96:(h + 1) * 96], rhs=cqn[:, c, 0:n],
                                                          start=(c == 0), stop=(c == 1)),
                             reads=[wuq, cqn], writes=[ps], last=(c == 1))
                    rope(ps, 96, t0, n, p.permB, "cosB", "sinB", T["qbT"], T["qbT"][:][:, h, t0:t0 + n])
                if p.stop == "G4":
                    return
                ps, _ = PS.next()
                if p.stop == "G5x":
                    return
                proj_fm(ps, wb2, 0, 128, t0, n)
                if p.stop == "G5a0":
                    return
                S.op("dve", lambda e: e.tensor_copy(out=cq[:, 0, 0:n], in_=ps[:, 0:n]), reads=[ps], writes=[cq])
                S.op("act", lambda e: e.activation(out=sq2[:, 0, 0:n], in_=cq[:, 0, 0:n], func=AF.Square), reads=[cq], writes=[sq2])
                if p.stop in ("G5a1", "G5nosq"):
                    return
                ps, _ = PS.next()
                S.mm(lambda e: e.matmul(ps[:, 0:n], lhsT=ones_bf[:], rhs=sq2[:, 0, 0:n], start=True, stop=True),
                     reads=[ones_bf, sq2], writes=[ps])
                if p.stop == "G5a2":
                    return
                rstd_from_ps(S, ps, n, 128, eps_t, rs2)
                if p.stop == "G5a3":
                    return
                S.op("dve", lambda e: e.scalar_tensor_tensor(out=ckvn[:, 0:n], in0=cq[:, 0, 0:n], scalar=bkvn[:, 0:1],
                                                             in1=rs2[:, 0:n], op0=ALU.mult, op1=ALU.mult),
                     reads=[cq, bkvn, rs2], writes=[ckvn])
                if p.stop == "G5a":
                    return
                ps, _ = PS.next()
                proj_fm(ps, wb2, 128, 128, t0, n)
                if p.stop == "G5":
                    return
                kr = rope(ps, 32, t0, n, p.permK, "cosK", "sinK", None, None, obuf=krb)
                if p.stop == "G5b":
                    return
                for h in range(8):
                    ps, _ = PS.next()
                    S.mm(lambda e, h=h: e.matmul(ps[0:96, 0:n], lhsT=wkn[:, h, :], rhs=ckvn[:, 0:n], start=True, stop=False),
                         reads=[wkn, ckvn], writes=[ps], last=False)
                    S.mm(lambda e: e.matmul(ps[0:96, 0:n], lhsT=p.sel[:], rhs=kr[0:32, 0:n], start=False, stop=True),
                         reads=[p.sel, kr], writes=[ps], last=True)
                    o, osm = STB.next()
                    evac(S, "act", o[0:96, 0:n], ps[0:96, 0:n], [ps], [o])
                    store(T["kbT"], T["kbT"][:][:, h, t0:t0 + n], o, o[0:96, 0:n], osm)
                if p.stop == "G5c":
                    return
                for s0 in range(0, n, 128):
                    ps, _ = PS.next()
                    S.mm(lambda e: e.matmul(ps[:, 0:512], lhsT=ckvn[:, s0:s0 + 128], rhs=wvb[:].rearrange("p h c -> p (h c)"),
                                            start=True, stop=True), reads=[ckvn, wvb], writes=[ps])
                    S.op("act", lambda e: e.activation(out=vst[:, :, 0:64],
                                                       in_=ps[:, 0:512].rearrange("p (h c) -> p h c", c=64), func=AF.Copy),
                         reads=[ps], writes=[vst])
                    store(T["vbA"], T["vbA"][:][t0 + s0:t0 + s0 + 128], vst, vst[:, :, :], vsem)
            if p.stop == "G6":
                return
            simple_group(O_CQ, 2, T["cqT"], scale=0.125)
            simple_group(O_CK, 2, T["ckT"])
            wb = load_w(O_CV, 512)
            for (t0, n) in seg:
                lo = loc[t0]
                for s0 in range(0, n, 128):
                    ps, _ = PS.next()
                    for k in range(8):
                        S.mm(lambda e, k=k: e.matmul(ps[:, 0:512], lhsT=hT[:, k, lo + s0:lo + s0 + 128], rhs=wb[:, k, 0:512],
                                                     start=(k == 0), stop=(k == 7)),
                             reads=[wb, hT], writes=[ps], last=(k == 7))
                    evac(S, "act", cvst[:, :], ps[:, 0:512], [ps], [cvst])
                    store(T["cvv"], T["cvv"][:][t0 + s0:t0 + s0 + 128, :], cvst, cvst[:, :], csem)
            wb = load_w(O_CG, 128)
            for (t0, n) in seg:
                ps, _ = PS.next()
                proj_fm(ps, wb, 0, 128, t0, n)
                evac(S, "act", cg[:, 0:n], ps[0:32, 0:n], [ps], [cg])
                for dr, dst in ((0, T["gfT"]), (1, T["gbT"])):
                    for pr in range(2):
                        ps, _ = PS.next()
                        S.mm(lambda e, dr=dr, pr=pr: e.matmul(ps[:, 0:n], lhsT=wg[:, dr, pr * 128:(pr + 1) * 128], rhs=cg[:, 0:n],
                                                              start=True, stop=True), reads=[wg, cg], writes=[ps])
                        o, osm = STF.next()
                        S.op("act", lambda e, dr=dr, pr=pr: e.activation(out=o[:, 0:n], in_=ps[:, 0:n], func=AF.Exp, scale=-1.0,
                                                                         bias=nbg[:, dr, pr:pr + 1]), reads=[ps, nbg], writes=[o])
                        S.op("act", lambda e: e.activation(out=o[:, 0:n], in_=o[:, 0:n], func=AF.Ln, bias=one_t[:]),
                             reads=[o, one_t], writes=[o])
                        S.op("dve", lambda e: e.tensor_scalar_mul(out=o[:, 0:n], in0=o[:, 0:n], scalar1=-1.0 / 16.0),
                             reads=[o], writes=[o])
                        store(dst, dst[:][:, pr, t0:t0 + n], o, o[:, 0:n], osm)
            if p.stop == "G11":
                return
            simple_group(O_CR, 4, T["crT"], func=AF.Silu)
            simple_group(O_GATE, 24, T["gatesT"], func=AF.Sigmoid)


def phase_B1(p, L):
    S, T, C = p.S, p.T, p.C
    with S.scope():
        ld = S.dma_sem("ld")
        ka = S.sbuf("ka", [128, NTOT], BF16)
        va = S.sbuf("va", [128, NKB, 2, 128], BF16)
        S.dma("sp", ld, ka[:], T["kaT"][:], reads=[T["kaT"]], writes=[ka])
        for i in range(0, NKB, 6):
            S.dma("sp", ld, va[:, i:i + 6], T["vaA"][:].rearrange("(b p) h c -> p b h c", p=128)[:, i:i + 6],
                  reads=[T["vaA"]], writes=[va])
        sk = S.sbuf("sk", [1, 8], F32)
        S.dma("sp", ld, sk[:], L["a_sink"], writes=[sk])
        zrow = S.sbuf("zrow", [1, 128], F32)
        S.op("dve", lambda e: e.memset(zrow[:], 0.0), writes=[zrow])
        esrow = S.sbuf("esrow", [1, 2, 512], BF16)
        for h in range(8):
            S.op("act", lambda e, h=h: e.activation(out=esrow[0:1, h // 4, (h % 4) * 128:(h % 4 + 1) * 128], in_=zrow[:],
                                                    func=AF.Exp, bias=sk[0:1, h:h + 1]), reads=[zrow, sk], writes=[esrow])
        PSS = Rot(S, "pss", 5, [128, 512], F32, psum=True, dma=False)
        PSO = Rot(S, "pso", 2, [128, 512], F32, psum=True, dma=False)
        QT = Rot(S, "qt", 2, [128, 4, 512], BF16)
        PT = Rot(S, "pt", 6, [128, 512], BF16, dma=False)
        RC = Rot(S, "rc", 2, [64, 512], F32, dma=False)
        OS = Rot(S, "os", 3, [64, 512], BF16)
        scale = 64 ** -0.5

        for (t0, n) in TT:
            qt, qsem = QT.next()
            S.dma("sp", qsem, qt[:, :, 0:n], T["qaT"][:][:, :, t0:t0 + n], reads=[T["qaT"]], writes=[qt])
            for qb in range(n // 128):
                q0 = t0 + qb * 128
                blk = q0 // 128
                if t0 == 0:
                    kbs = [(0, None), (1, None)]
                else:
                    kbs = [(0, None), (1, None)]
                    if blk - 1 >= 2:
                        kbs.append((blk - 1, "maskLo"))
                    kbs.append((blk, None))
                    if blk + 1 < NKB:
                        kbs.append((blk + 1, "maskHi"))
                for kvh in range(2):
                    pb = kvh * 64
                    pso, _ = PSO.next()
                    pend = []
                    for i, (kb, msk) in enumerate(kbs):
                        pss, _ = PSS.next()
                        S.mm(lambda e, kb=kb, pss=pss: e.matmul(pss[:, 0:512].rearrange("p (g q) -> p g q", g=4),
                                                                lhsT=ka[pb:pb + 64, kb * 128:(kb + 1) * 128],
                                                                rhs=qt[pb:pb + 64, :, qb * 128:(qb + 1) * 128], start=True, stop=True),
                             reads=[ka, qt], writes=[pss])
                        pt, _ = PT.next()
                        S.op("act", lambda e, pss=pss, pt=pt: e.activation(out=pt[:, :], in_=pss[:, :], func=AF.Exp, scale=scale),
                             reads=[pss], writes=[pt])
                        if msk is not None:
                            mk = p.maskLo if msk == "maskLo" else p.maskHi
                            S.op("dve", lambda e, mk=mk, pt=pt: e.tensor_tensor(out=pt[:, :], in0=pt[:, :], in1=mk[:, :], op=ALU.mult),
                                 reads=[pt, mk], writes=[pt])
                        pend.append((kb, pt))
                    for i, (kb, pt) in enumerate(pend):
                        S.mm(lambda e, kb=kb, i=i, pt=pt: e.matmul(pso[:, :], lhsT=va[:, kb, kvh, :], rhs=pt[:, :], start=(i == 0), stop=False),
                             reads=[va, pt], writes=[pso], last=False)
                    S.mm(lambda e: e.matmul(pso[:, :], lhsT=p.sv[:], rhs=esrow[0:1, kvh, :], start=False, stop=True),
                         reads=[p.sv, esrow], writes=[pso], last=True)
                    rc, _ = RC.next()
                    S.op("dve", lambda e: e.reciprocal(out=rc[:, :], in_=pso[64:128, :]), reads=[pso], writes=[rc])
                    o, osm = OS.next()
                    S.op("dve", lambda e: e.tensor_tensor(out=o[:, :], in0=pso[0:64, :], in1=rc[:, :], op=ALU.mult),
                         reads=[pso, rc], writes=[o])
                    for gp in range(2):
                        S.dma("act", osm, T["yaT"][:][gp * 64:(gp + 1) * 64, kvh * 2:kvh * 2 + 2, q0:q0 + 128],
                              o[:, :].rearrange("p (g2 gp q) -> p g2 gp q", g2=2, gp=2)[:, :, gp, :],
                              reads=[o], writes=[T["yaT"]])


def phase_B2(p, L):
    S, T, C = p.S, p.T, p.C
    with S.scope():
        KS = Rot(S, "ks", 2, [96, NTOT], BF16)
        VS = Rot(S, "vs", 2, [128, NKB, 128], BF16)
        PSS = Rot(S, "pss", 5, [128, 512], F32, psum=True, dma=False)
        PSO = Rot(S, "pso", 2, [128, 512], F32, psum=True, dma=False)
        QT = Rot(S, "qt", 2, [96, 512], BF16)
        PT = Rot(S, "pt", 6, [128, 512], BF16, dma=False)
        RC = Rot(S, "rc", 2, [64, 512], F32, dma=False)
        OS = Rot(S, "os", 3, [64, 512], BF16)
        scale = 96 ** -0.5
        for h in range(8):
            ks, ksem = KS.next()
            vs, vsem = VS.next()
            S.dma("sp", ksem, ks[:], T["kbT"][:][:, h, :], reads=[T["kbT"]], writes=[ks])
            for i in range(0, NKB, 6):
                S.dma("sp", vsem, vs[:, i:i + 6], T["vbA"][:].rearrange("(b p) h c -> p b h c", p=128)[:, i:i + 6, h, :],
                      reads=[T["vbA"]], writes=[vs])
            for (t0, n) in TT:
                qt, qsem = QT.next()
                S.dma("sp", qsem, qt[:, 0:n], T["qbT"][:][:, h, t0:t0 + n], reads=[T["qbT"]], writes=[qt])
                nkb = 2 if t0 == 0 else NKB
                pso, _ = PSO.next()
                LOOK = 3
                pend = []
                for kb in range(nkb + LOOK):
                    if kb < nkb:
                        pss, _ = PSS.next()
                        S.mm(lambda e, kb=kb, pss=pss: e.matmul(pss[:, 0:n], lhsT=ks[:, kb * 128:(kb + 1) * 128], rhs=qt[:, 0:n],
                                                                start=True, stop=True), reads=[ks, qt], writes=[pss])
                        pt, _ = PT.next()
                        S.op("act", lambda e, pss=pss, pt=pt: e.activation(out=pt[:, 0:n], in_=pss[:, 0:n], func=AF.Exp, scale=scale),
                             reads=[pss], writes=[pt])
                        pend.append(pt)
                    if kb >= LOOK:
                        k2 = kb - LOOK
                        pt2 = pend.pop(0)
                        S.mm(lambda e, k2=k2, pt2=pt2: e.matmul(pso[:, 0:n], lhsT=vs[:, k2, :], rhs=pt2[:, 0:n], start=(k2 == 0),
                                                                stop=(k2 == nkb - 1)), reads=[vs, pt2], writes=[pso], last=(k2 == nkb - 1))
                rc, _ = RC.next()
                S.op("dve", lambda e: e.reciprocal(out=rc[:, 0:n], in_=pso[64:128, 0:n]), reads=[pso], writes=[rc])
                o, osm = OS.next()
                S.op("dve", lambda e: e.tensor_tensor(out=o[:, 0:n], in0=pso[0:64, 0:n], in1=rc[:, 0:n], op=ALU.mult),
                     reads=[pso, rc], writes=[o])
                S.dma("act", osm, T["ybT"][:][(h % 2) * 64:(h % 2 + 1) * 64, h // 2, t0:t0 + n], o[:, 0:n],
                      reads=[o], writes=[T["ybT"]])


def phase_B3(p, L):
    S, T, C = p.S, p.T, p.C
    eps_t, ones_bf = p.eps_t, p.ones_bf
    NCH = NTOT // 64
    PIECE = 2112
    with S.scope():
        ld = S.dma_sem("ld")
        hn = S.sbuf("hn", [128, 4], F32)
        S.dma("sp", ld, hn[:], L["c_hn"], writes=[hn])
        vsb = S.sbuf("vsb", [64, NCH, 128], BF16)
        qtb = S.sbuf("qtb", [64, NTOT], BF16)
        ktT = S.sbuf("ktT", [64, NCH, 64], BF16)
        attT = S.sbuf("attT", [64, NCH, 64], BF16)
        ebl = S.sbuf("ebl", [64, NCH], F32)
        oT = S.sbuf("oT", [128, NTOT], F32)
        qp = S.sbuf("qp", [64, PIECE], BF16)
        kp = S.sbuf("kp", [64, PIECE], BF16)
        gp = S.sbuf("gp", [64, PIECE], F32)
        bp = S.sbuf("bp", [64, PIECE], F32)
        rm = S.sbuf("rm", [64, PIECE], F32)
        ep = S.sbuf("ep", [64, PIECE], BF16)
        ktp = S.sbuf("ktp", [64, PIECE], BF16)
        S.dma("sp", ld, rm[:], C["rmask"][:, 0:PIECE], writes=[rm])
        Sf = [S.sbuf("Sf%d" % i, [64, 128], F32) for i in range(2)]
        Sb = [S.sbuf("Sb%d" % i, [64, 128], BF16) for i in range(2)]
        S1l = [S.sbuf("S1%d" % i, [64, 128], F32) for i in range(2)]
        PST = Rot(S, "pst", 2, [64, 512], BF16, psum=True, dma=False)
        PSA = Rot(S, "psa", 2, [64, 512], F32, psum=True, dma=False)
        PSO = Rot(S, "pso", 2, [128, 512], F32, psum=True, dma=False)
        PSK = Rot(S, "psk", 2, [64, 128], F32, psum=True, dma=False)
        sqb = S.sbuf("sqb", [128, 512], BF16)
        rs = S.sbuf("rs", [128, 512], F32)
        tt = S.sbuf("tt", [128, 512], F32)
        CR = Rot(S, "cr", 2, [128, 512], BF16)
        YO = Rot(S, "yo", 2, [128, 512], BF16)

        for h in range(4):
            hb = (h % 2) * 64
            for i in range(0, NCH, 11):
                S.dma("sp", ld, vsb[:, i:i + 11, :],
                      T["cvv"][:].rearrange("(c j) (h e) -> j c h e", j=64, e=128)[:, i:i + 11, h, :],
                      reads=[T["cvv"]], writes=[vsb])
            for dr in range(2):
                gsrc = T["gfT"] if dr == 0 else T["gbT"]
                tri = p.triF if dr == 0 else p.triB
                for pc in range(NTOT // PIECE):
                    a0 = pc * PIECE
                    c0 = a0 // 64
                    S.dma("sp", ld, qp[:], T["cqT"][:][hb:hb + 64, h // 2, a0:a0 + PIECE], reads=[T["cqT"]], writes=[qp])
                    S.dma("sp", ld, kp[:], T["ckT"][:][hb:hb + 64, h // 2, a0:a0 + PIECE], reads=[T["ckT"]], writes=[kp])
                    S.dma("sp", ld, gp[:], gsrc[:][hb:hb + 64, h // 2, a0:a0 + PIECE], reads=[gsrc], writes=[gp])
                    S.op("dve", lambda e: e.tensor_tensor_scan(out=bp[:], data0=rm[:], data1=gp[:], initial=0.0,
                                                               op0=ALU.mult, op1=ALU.add), reads=[rm, gp], writes=[bp])
                    if dr == 1:
                        b3 = bp[:].rearrange("p (c j) -> p c j", j=64)
                        S.op("dve", lambda e: e.tensor_tensor(out=gp[:], in0=gp[:], in1=bp[:], op=ALU.subtract),
                             reads=[gp, bp], writes=[gp])
                        S.op("dve", lambda e: e.tensor_tensor(out=bp[:].rearrange("p (c j) -> p c j", j=64),
                                                              in0=gp[:].rearrange("p (c j) -> p c j", j=64),
                                                              in1=b3[:, :, 63:64].to_broadcast([64, PIECE // 64, 64]), op=ALU.add),
                             reads=[gp, bp], writes=[bp])
                    last_col = 63 if dr == 0 else 0
                    S.op("act", lambda e: e.activation(out=ebl[:, c0:c0 + PIECE // 64],
                                                       in_=bp[:].rearrange("p (c j) -> p c j", j=64)[:, :, last_col],
                                                       func=AF.Exp), reads=[bp], writes=[ebl])
                    S.op("act", lambda e: e.activation(out=ep[:], in_=bp[:], func=AF.Exp), reads=[bp], writes=[ep])
                    S.op("dve", lambda e: e.tensor_tensor(out=qtb[:, a0:a0 + PIECE], in0=qp[:], in1=ep[:], op=ALU.mult),
                         reads=[qp, ep], writes=[qtb])
                    S.op("act", lambda e: e.activation(out=ep[:], in_=bp[:], func=AF.Exp, scale=-1.0), reads=[bp], writes=[ep])
                    S.op("dve", lambda e: e.tensor_tensor(out=ktp[:], in0=kp[:], in1=ep[:], op=ALU.mult),
                         reads=[kp, ep], writes=[ktp])
                    for c8 in range(0, PIECE // 64, 8):
                        nn = min(8, PIECE // 64 - c8)
                        pst, _ = PST.next()
                        psa, _ = PSA.next()
                        for j in range(nn):
                            cl = c8 + j
                            S.mm(lambda e, cl=cl, j=j: e.transpose(out=pst[:, j * 64:(j + 1) * 64], in_=ktp[:, cl * 64:(cl + 1) * 64],
                                                                   identity=p.ident[:]),
                                 reads=[ktp, p.ident], writes=[pst], last=(j == nn - 1))
                        for j in range(nn):
                            cl = c8 + j
                            S.mm(lambda e, cl=cl, j=j: e.matmul(psa[:, j * 64:(j + 1) * 64], lhsT=ktp[:, cl * 64:(cl + 1) * 64],
                                                                rhs=qtb[:, a0 + cl * 64:a0 + (cl + 1) * 64], start=True, stop=True),
                                 reads=[ktp, qtb], writes=[psa], last=(j == nn - 1))
                        S.op("act", lambda e: e.activation(out=ktT[:, c0 + c8:c0 + c8 + nn, :].rearrange("p c d -> p (c d)"),
                                                           in_=pst[:, 0:nn * 64], func=AF.Copy), reads=[pst], writes=[ktT])
                        S.op("dve", lambda e: e.tensor_tensor(out=attT[:, c0 + c8:c0 + c8 + nn, :].rearrange("p c d -> p (c d)"),
                                                              in0=psa[:, 0:nn * 64], in1=tri[:, 0:nn * 64], op=ALU.mult),
                             reads=[psa, tri], writes=[attT])
                order = list(range(NCH)) if dr == 0 else ([3, 2, 1, 0] + list(range(NCH - 1, 3, -1)))
                cur = 0
                S.op("dve", lambda e: e.memset(Sf[0][:], 0.0), writes=[Sf[0]])
                S.op("dve", lambda e: e.memset(Sb[0][:], 0.0), writes=[Sb[0]])
                groups = [order[0:4]] + [order[i:i + 8] for i in range(4, NCH, 8)]
                for grp in groups:
                    ng = len(grp)
                    lo_c = min(grp)
                    pso, _ = PSO.next()
                    for j, c in enumerate(grp):
                        col = (c - lo_c) * 64
                        psk, _ = PSK.next()
                        S.mm(lambda e, c=c, psk=psk: e.matmul(psk[:, :], lhsT=ktT[:, c, :], rhs=vsb[:, c, :], start=True, stop=True),
                             reads=[ktT, vsb], writes=[psk])
                        S.mm(lambda e, c=c, col=col: e.matmul(pso[:, col:col + 64], lhsT=vsb[:, c, :], rhs=attT[:, c, :],
                                                              start=True, stop=False),
                             reads=[vsb, attT], writes=[pso], last=False)
                        S.mm(lambda e, c=c, col=col, cur=cur: e.matmul(pso[:, col:col + 64], lhsT=Sb[cur][:],
                                                                       rhs=qtb[:, c * 64:(c + 1) * 64], start=False, stop=True),
                             reads=[Sb[cur], qtb], writes=[pso], last=True)
                        nxt = 1 - cur
                        S1 = S1l[cur]
                        S.op("dve", lambda e, cur=cur, S1=S1, psk=psk: e.tensor_tensor(out=S1[:], in0=psk[:, :], in1=Sf[cur][:], op=ALU.add),
                             reads=[psk, Sf[cur]], writes=[S1])
                        S.op("dve", lambda e, c=c, nxt=nxt, S1=S1: e.tensor_scalar_mul(out=Sf[nxt][:], in0=S1[:], scalar1=ebl[:, c:c + 1]),
                             reads=[S1, ebl], writes=[Sf[nxt]])
                        S.op("act", lambda e, c=c, nxt=nxt, S1=S1: e.activation(out=Sb[nxt][:], in_=S1[:], func=AF.Copy,
                                                                                scale=ebl[:, c:c + 1]),
                             reads=[S1, ebl], writes=[Sb[nxt]])
                        cur = nxt
                    w = ng * 64
                    if dr == 0:
                        evac(S, "act", oT[:, lo_c * 64:lo_c * 64 + w], pso[:, 0:w], [pso], [oT])
                    else:
                        S.op("dve", lambda e, lo_c=lo_c, w=w: e.tensor_tensor(out=oT[:, lo_c * 64:lo_c * 64 + w],
                                                                             in0=oT[:, lo_c * 64:lo_c * 64 + w],
                                                                             in1=pso[:, 0:w], op=ALU.add),
                             reads=[oT, pso], writes=[oT])
            for (t0, n) in TT:
                S.op("act", lambda e: e.activation(out=sqb[:, 0:n], in_=oT[:, t0:t0 + n], func=AF.Square), reads=[oT], writes=[sqb])
                pso, _ = PSO.next()
                S.mm(lambda e: e.matmul(pso[:, 0:n], lhsT=ones_bf[:], rhs=sqb[:, 0:n], start=True, stop=True),
                     reads=[ones_bf, sqb], writes=[pso])
                rstd_from_ps(S, pso, n, 128, eps_t, rs)
                cr, crs = CR.next()
                S.dma("sp", crs, cr[:, 0:n], T["crT"][:][:, h, t0:t0 + n], reads=[T["crT"]], writes=[cr])
                S.op("dve", lambda e: e.scalar_tensor_tensor(out=tt[:, 0:n], in0=oT[:, t0:t0 + n], scalar=hn[:, h:h + 1],
                                                             in1=rs[:, 0:n], op0=ALU.mult, op1=ALU.mult),
                     reads=[oT, hn, rs], writes=[tt])
                yo, ys = YO.next()
                S.op("dve", lambda e: e.tensor_tensor(out=yo[:, 0:n], in0=tt[:, 0:n], in1=cr[:, 0:n], op=ALU.mult),
                     reads=[tt, cr], writes=[yo])
                S.dma("act", ys, T["ycT"][:][:, h, t0:t0 + n], yo[:, 0:n], reads=[yo], writes=[T["ycT"]])


def phase_C1(p, L, Xin):
    S, T, C = p.S, p.T, p.C
    eps_t, ones_bf = p.eps_t, p.ones_bf
    with S.scope():
        ld = S.dma_sem("ld")
        modT = S.sbuf("modT", [128, 48, 2], F32)
        S.dma("sp", ld, modT[:], T["modT"][:], reads=[T["modT"]], writes=[modT])
        nffn = S.sbuf("nffn", [128, 8], F32)
        S.dma("sp", ld, nffn[:], L["norm_ffn"], writes=[nffn])
        wbr = [S.sbuf("wbr%d" % i, [128, 4, D], BF16) for i in range(3)]
        for i, nm in enumerate(("w_br_a", "w_br_b", "w_br_c")):
            S.dma("pool", ld, wbr[i][:], L[nm].rearrange("(k p) n -> p k n", p=128), writes=[wbr[i]])
        wo = S.sbuf("wo", [128, 8, D], BF16)
        for k0 in range(0, 8, 4):
            S.dma("pool", ld, wo[:, k0:k0 + 4, :], L["w_out"].rearrange("(k p) n -> p k n", p=128)[:, k0:k0 + 4, :], writes=[wo])
        A2 = S.sbuf("A2", [128, 8, 2], F32)
        S.op("dve", lambda e: e.tensor_scalar_add(out=A2[:], in0=modT[:, 32:40, :], scalar1=1.0), reads=[modT], writes=[A2])
        S.op("dve", lambda e: e.tensor_tensor(out=A2[:], in0=A2[:], in1=nffn[:].unsqueeze(2).to_broadcast([128, 8, 2]),
                                              op=ALU.mult), reads=[A2, nffn], writes=[A2])
        PS = Rot(S, "ps", 7, [128, 512], F32, psum=True, dma=False)
        YA = Rot(S, "ya", 2, [128, 12, 512], BF16)
        GT = Rot(S, "gt", 2, [128, 24, 512], BF16)
        XT = Rot(S, "xt", 2, [128, 8, 512], F32)
        mT = S.sbuf("mT", [128, 8, 512], BF16)
        t1 = S.sbuf("t1", [128, 512], F32)
        t2 = S.sbuf("t2", [128, 512], F32)
        sq = S.sbuf("sq", [128, 8, 512], BF16)
        rstd = S.sbuf("rstd", [128, 512], F32)
        xs = S.sbuf("xs", [128, 512], F32)
        H2 = Rot(S, "h2", 1, [128, 8, 512], BF16)
        for (t0, n) in TT:
            v = 1 if t0 == 0 else 0
            ya, yas = YA.next()
            for i, nm in enumerate(("yaT", "ybT", "ycT")):
                S.dma("sp", yas, ya[:, i * 4:(i + 1) * 4, 0:n], T[nm][:][:, :, t0:t0 + n], reads=[T[nm]], writes=[ya])
            gt, gts = GT.next()
            for i in range(3):
                S.dma("sp", gts, gt[:, i * 8:(i + 1) * 8, 0:n], T["gatesT"][:][:, i * 8:(i + 1) * 8, t0:t0 + n],
                      reads=[T["gatesT"]], writes=[gt])
            xb, xsem = XT.next()
            S.dma("sp", xsem, xb[:, :, 0:n], Xin[:][:, t0:t0 + n].rearrange("(k p) n -> p k n", p=128), reads=[Xin], writes=[xb])
            for m in range(8):
                pss = []
                for i in range(3):
                    ps, _ = PS.next()
                    for k in range(4):
                        S.mm(lambda e, i=i, k=k, m=m: e.matmul(ps[:, 0:n], lhsT=wbr[i][:, k, m * 128:(m + 1) * 128],
                                                               rhs=ya[:, i * 4 + k, 0:n], start=(k == 0), stop=(k == 3)),
                             reads=[wbr[i], ya], writes=[ps], last=(k == 3))
                    pss.append(ps)
                S.op("dve", lambda e, m=m: e.tensor_tensor(out=t1[:, 0:n], in0=pss[0][:, 0:n], in1=gt[:, m, 0:n], op=ALU.mult),
                     reads=[pss[0], gt], writes=[t1])
                S.op("dve", lambda e, m=m: e.tensor_tensor(out=t2[:, 0:n], in0=pss[1][:, 0:n], in1=gt[:, 8 + m, 0:n], op=ALU.mult),
                     reads=[pss[1], gt], writes=[t2])
                S.op("pool", lambda e: e.tensor_tensor(out=t1[:, 0:n], in0=t1[:, 0:n], in1=t2[:, 0:n], op=ALU.add),
                     reads=[t1, t2], writes=[t1])
                S.op("dve", lambda e, m=m: e.tensor_tensor(out=t2[:, 0:n], in0=pss[2][:, 0:n], in1=gt[:, 16 + m, 0:n], op=ALU.mult),
                     reads=[pss[2], gt], writes=[t2])
                S.op("pool", lambda e, m=m: e.tensor_tensor(out=mT[:, m, 0:n], in0=t1[:, 0:n], in1=t2[:, 0:n], op=ALU.add),
                     reads=[t1, t2], writes=[mT])
            for m in range(8):
                ps, _ = PS.next()
                for k in range(8):
                    S.mm(lambda e, k=k, m=m: e.matmul(ps[:, 0:n], lhsT=wo[:, k, m * 128:(m + 1) * 128], rhs=mT[:, k, 0:n],
                                                      start=(k == 0), stop=(k == 7)), reads=[wo, mT], writes=[ps], last=(k == 7))
                S.op("dve", lambda e, m=m: e.scalar_tensor_tensor(out=xb[:, m, 0:n], in0=ps[:, 0:n], scalar=modT[:, 16 + m, v:v + 1],
                                                                  in1=xb[:, m, 0:n], op0=ALU.mult, op1=ALU.add),
                     reads=[ps, modT, xb], writes=[xb])
            S.dma("act", xsem, T["x1T"][:][:, t0:t0 + n].rearrange("(k p) n -> p k n", p=128), xb[:, :, 0:n],
                  reads=[xb], writes=[T["x1T"]])
            S.op("act", lambda e: e.activation(out=sq[:, :, 0:n], in_=xb[:, :, 0:n], func=AF.Square), reads=[xb], writes=[sq])
            ps, _ = PS.next()
            for k in range(8):
                S.mm(lambda e, k=k: e.matmul(ps[:, 0:n], lhsT=ones_bf[:], rhs=sq[:, k, 0:n], start=(k == 0), stop=(k == 7)),
                     reads=[ones_bf, sq], writes=[ps], last=(k == 7))
            rstd_from_ps(S, ps, n, D, eps_t, rstd)
            h2, h2s = H2.next()
            for k in range(8):
                S.op("dve", lambda e, k=k: e.scalar_tensor_tensor(out=xs[:, 0:n], in0=xb[:, k, 0:n], scalar=A2[:, k, v:v + 1],
                                                                  in1=rstd[:, 0:n], op0=ALU.mult, op1=ALU.mult),
                     reads=[xb, A2, rstd], writes=[xs])
                S.op("act", lambda e, k=k: e.activation(out=h2[:, k, 0:n], in_=xs[:, 0:n], func=AF.Identity,
                                                        bias=modT[:, 24 + k, v:v + 1]), reads=[xs, modT], writes=[h2])
            S.dma("act", h2s, T["h2T"][:][:, :, t0:t0 + n], h2[:, :, 0:n], reads=[h2], writes=[T["h2T"]])


def phase_C2(p, L, Xout, final_out=None):
    S, T, C = p.S, p.T, p.C
    eps_t, ones_bf = p.eps_t, p.ones_bf
    with S.scope():
        ld = S.dma_sem("ld")
        modT = S.sbuf("modT", [128, 48, 2], F32)
        S.dma("sp", ld, modT[:], T["modT"][:], reads=[T["modT"]], writes=[modT])
        cw = S.sbuf("cw", [128, 44, 3], F32)
        cb = S.sbuf("cb", [128, 44], F32)
        S.dma("sp", ld, cw[:], L["conv_w"], writes=[cw])
        S.dma("sp", ld, cb[:], L["conv_b"], writes=[cb])
        fn = S.sbuf("fn", [128, 8], F32)
        S.dma("sp", ld, fn[:], C["final_norm"], writes=[fn])
        hsem = S.dma_sem("h2e")
        wdsem = S.dma_sem("wd")
        with S.scope():
            PS = Rot(S, "ps", 6, [128, 512], F32, psum=True, dma=False)
            PSH = Rot(S, "psh", 2, [128, 4], F32, psum=True, dma=False)
            h2e = S.sbuf("h2e", [128, 8, 4354], BF16)
            WU = Rot(S, "wu", 3, [128, 8, 256], BF16)
            UB = Rot(S, "ub", 4, [128, 514], F32, dma=False)
            ACC = Rot(S, "acc", 4, [128, 512], F32, dma=False)
            AO = Rot(S, "ao", 3, [128, 512], BF16)
            for seg in (TT[0:9], TT[9:17]):
                base = seg[0][0]
                tot = sum(n for _, n in seg)
                seqs = []
                if base == 0:
                    seqs = [(0, CTX), (CTX, tot)]
                else:
                    seqs = [(base, base + tot)]
                S.op("dve", lambda e: e.memset(h2e[:, :, 0:1], 0.0), writes=[h2e])
                S.op("dve", lambda e: e.memset(h2e[:, :, tot + 1:tot + 2], 0.0), writes=[h2e])
                lo_tok = base - 1 if base > CTX else base
                hi_tok = base + tot + 1 if base + tot < NTOT else base + tot
                S.dma("sp", hsem, h2e[:, :, lo_tok - base + 1:hi_tok - base + 1], T["h2T"][:][:, :, lo_tok:hi_tok],
                      reads=[T["h2T"]], writes=[h2e])
                for c in range(22):
                    wu, wus = WU.next()
                    S.dma("pool", wus, wu[:, :, 0:128], L["w_up"][:, c * 128:(c + 1) * 128].rearrange("(k p) n -> p k n", p=128),
                          writes=[wu])
                    S.dma("pool", wus, wu[:, :, 128:256],
                          L["w_up"][:, FFN + c * 128:FFN + (c + 1) * 128].rearrange("(k p) n -> p k n", p=128), writes=[wu])
                    for (t0, n) in seg:
                        lo = t0 - base + 1
                        accs = []
                        for half in range(2):
                            ci = c + 22 * half
                            ps, _ = PS.next()
                            for k in range(8):
                                S.mm(lambda e, k=k, half=half: e.matmul(ps[:, 0:n], lhsT=wu[:, k, half * 128:(half + 1) * 128],
                                                                        rhs=h2e[:, k, lo:lo + n], start=(k == 0), stop=(k == 7)),
                                     reads=[wu, h2e], writes=[ps], last=(k == 7))
                            psh, _ = PSH.next()
                            lcol = lo - 1
                            rcol = lo + n
                            lz = (t0 == 0) or (t0 == CTX)
                            rz = (t0 + n == CTX) or (t0 + n == NTOT)
                            for k in range(8):
                                S.mm(lambda e, k=k, half=half: e.matmul(psh[:, 0:2], lhsT=wu[:, k, half * 128:(half + 1) * 128],
                                                                        rhs=h2e[:, k, lcol:rcol + 1:n + 1], start=(k == 0), stop=(k == 7)),
                                     reads=[wu, h2e], writes=[psh], last=(k == 7))
                            ub, _ = UB.next()
                            evac(S, "act", ub[:, 1:n + 1], ps[:, 0:n], [ps], [ub])
                            if lz:
                                S.op("dve", lambda e: e.memset(ub[:, 0:1], 0.0), writes=[ub])
                            else:
                                S.op("dve", lambda e: e.tensor_copy(out=ub[:, 0:1], in_=psh[:, 0:1]), reads=[psh], writes=[ub])
                            if rz:
                                S.op("dve", lambda e: e.memset(ub[:, n + 1:n + 2], 0.0), writes=[ub])
                            else:
                                S.op("dve", lambda e: e.tensor_copy(out=ub[:, n + 1:n + 2], in_=psh[:, 1:2]), reads=[psh], writes=[ub])
                            acc, _ = ACC.next()
                            eng = "dve" if half == 0 else "pool"
                            S.op(eng, lambda e, ci=ci: e.tensor_scalar(out=acc[:, 0:n], in0=ub[:, 0:n], scalar1=cw[:, ci, 0:1],
                                                                       scalar2=cb[:, ci:ci + 1], op0=ALU.mult, op1=ALU.add),
                                 reads=[ub, cw, cb], writes=[acc])
                            S.op("dve", lambda e, ci=ci: e.scalar_tensor_tensor(out=acc[:, 0:n], in0=ub[:, 1:n + 1], scalar=cw[:, ci, 1:2],
                                                                                in1=acc[:, 0:n], op0=ALU.mult, op1=ALU.add),
                                 reads=[ub, cw, acc], writes=[acc])
                            S.op("dve", lambda e, ci=ci: e.scalar_tensor_tensor(out=acc[:, 0:n], in0=ub[:, 2:n + 2], scalar=cw[:, ci, 2:3],
                                                                                in1=acc[:, 0:n], op0=ALU.mult, op1=ALU.add),
                                 reads=[ub, cw, acc], writes=[acc])
                            accs.append(acc)
                        S.op("act", lambda e: e.activation(out=accs[0][:, 0:n], in_=accs[0][:, 0:n], func=AF.Silu),
                             reads=[accs[0]], writes=[accs[0]])
                        ao, aos = AO.next()
                        S.op("pool", lambda e: e.tensor_tensor(out=ao[:, 0:n], in0=accs[0][:, 0:n], in1=accs[1][:, 0:n], op=ALU.mult),
                             reads=[accs[0], accs[1]], writes=[ao])
                        S.dma("act", aos, T["aT"][:][:, c, t0:t0 + n], ao[:, 0:n], reads=[ao], writes=[T["aT"]])
        with S.scope():
            PS = Rot(S, "ps", 7, [128, 512], F32, psum=True, dma=False)
            wd = S.sbuf("wd", [128, 22, D], BF16)
            for k0 in range(0, 22, 2):
                S.dma("pool", wdsem, wd[:, k0:k0 + 2, :], L["w_down"].rearrange("(k p) n -> p k n", p=128)[:, k0:k0 + 2, :], writes=[wd])
            AT = Rot(S, "at", 2, [128, 22, 512], BF16)
            XT = Rot(S, "xt", 2, [128, 8, 512], F32)
            sq = S.sbuf("sq", [128, 8, 512], BF16)
            rstd = S.sbuf("rstd", [128, 512], F32)
            for (t0, n) in TT:
                v = 1 if t0 == 0 else 0
                at, ats = AT.next()
                for i in range(0, 22, 6):
                    i2 = min(22, i + 6)
                    S.dma("sp", ats, at[:, i:i2, 0:n], T["aT"][:][:, i:i2, t0:t0 + n], reads=[T["aT"]], writes=[at])
                xb, xsem = XT.next()
                S.dma("sp", xsem, xb[:, :, 0:n], T["x1T"][:][:, t0:t0 + n].rearrange("(k p) n -> p k n", p=128),
                      reads=[T["x1T"]], writes=[xb])
                for m in range(8):
                    ps, _ = PS.next()
                    for k in range(22):
                        S.mm(lambda e, k=k, m=m: e.matmul(ps[:, 0:n], lhsT=wd[:, k, m * 128:(m + 1) * 128], rhs=at[:, k, 0:n],
                                                          start=(k == 0), stop=(k == 21)), reads=[wd, at], writes=[ps], last=(k == 21))
                    S.op("dve", lambda e, m=m: e.scalar_tensor_tensor(out=xb[:, m, 0:n], in0=ps[:, 0:n], scalar=modT[:, 40 + m, v:v + 1],
                                                                      in1=xb[:, m, 0:n], op0=ALU.mult, op1=ALU.add),
                         reads=[ps, modT, xb], writes=[xb])
                if final_out is None:
                    S.dma("act", xsem, Xout[:][:, t0:t0 + n].rearrange("(k p) n -> p k n", p=128), xb[:, :, 0:n],
                          reads=[xb], writes=[Xout])
                elif t0 >= CTX:
                    S.op("act", lambda e: e.activation(out=sq[:, :, 0:n], in_=xb[:, :, 0:n], func=AF.Square), reads=[xb], writes=[sq])
                    ps, _ = PS.next()
                    for k in range(8):
                        S.mm(lambda e, k=k: e.matmul(ps[:, 0:n], lhsT=ones_bf[:], rhs=sq[:, k, 0:n], start=(k == 0), stop=(k == 7)),
                             reads=[ones_bf, sq], writes=[ps], last=(k == 7))
                    rstd_from_ps(S, ps, n, D, eps_t, rstd)
                    for k in range(8):
                        S.op("dve", lambda e, k=k: e.scalar_tensor_tensor(out=xb[:, k, 0:n], in0=xb[:, k, 0:n], scalar=fn[:, k:k + 1],
                                                                          in1=rstd[:, 0:n], op0=ALU.mult, op1=ALU.mult),
                             reads=[xb, fn, rstd], writes=[xb])
                    S.dma("act", xsem, final_out[:][:, t0 - CTX:t0 - CTX + n].rearrange("(k p) n -> p k n", p=128), xb[:, :, 0:n],
                          reads=[xb], writes=[final_out])


PHASES_ALL = ("A", "B1", "B2", "B3", "C1", "C2")
PHASE_W = {"A": ("w_mod", "b_mod", "norm_mix", "w_in", "bqn", "bkvn", "w_uq", "w_ukv", "w_gate", "b_gate"),
           "B1": ("a_sink",), "B2": (), "B3": ("c_hn",), "C1": ("norm_ffn", "w_br_a", "w_br_b", "w_br_c", "w_out"),
           "C2": ("conv_w", "conv_b", "w_up", "w_down")}


def needed_weights(phases):
    need = set()
    for ph in phases:
        need.update(PHASE_W[ph])
    return [w for w in LAYER_W if w[0] in need]


def build_program(nlayers, final, phases=PHASES_ALL, ext_scratch=(), stop=None):
    nc = bass.Bass("TRN2", target_bir_lowering=False)
    xin_ap = dram_in(nc, "xT", (D, NTOT), F32)
    Cap = {n: dram_in(nc, n, s, d) for n, s, d in CONSTS}
    Lw = [{n: dram_in(nc, "%s_%d" % (n, l), s, d) for n, s, d in needed_weights(phases)} for l in range(nlayers)]
    if final:
        out_ap = dram_out(nc, "outT", (D, SEQ), F32)
    else:
        out_ap = dram_out(nc, "xT_out", (D, NTOT), F32)
    with contextlib.ExitStack() as st:
        S = Sched(nc, st)
        p = P()
        p.nc, p.S, p.C = nc, S, Cap
        p.stop = stop
        p.T = {}
        for n, s, d in SCRATCH:
            if n in ext_scratch:
                kind = ext_scratch[n]
                ap = dram_in(nc, n, s, d) if kind == "in" else dram_out(nc, n, s, d)
                p.T[n] = Buf(ap, n)
            else:
                p.T[n] = S.dram(n, s, d)
        Xext = Buf(xin_ap, "xT")
        Oext = Buf(out_ap, "out")
        ld = S.dma_sem("ldc")
        p.eps_t = S.sbuf("eps", [128, 1], F32)
        p.one_t = S.sbuf("one", [128, 1], F32)
        p.ones_bf = S.sbuf("ones_bf", [128, 128], BF16)
        S.op("dve", lambda e: e.memset(p.eps_t[:], EPS), writes=[p.eps_t])
        S.op("dve", lambda e: e.memset(p.one_t[:], 1.0), writes=[p.one_t])
        S.op("dve", lambda e: e.memset(p.ones_bf[:], 1.0), writes=[p.ones_bf])
        for nm, shape in (("permA", [128, 128]), ("permB", [96, 96]), ("permK", [32, 32]), ("sel", [32, 96]),
                          ("maskLo", [128, 512]), ("maskHi", [128, 512]), ("triF", [64, 512]), ("triB", [64, 512]),
                          ("ident", [64, 64]), ("sv", [1, 128])):
            b_ = S.sbuf(nm, shape, BF16)
            S.dma("sp", ld, b_[:], Cap[nm], writes=[b_])
            setattr(p, nm, b_)
        p.cv = S.sbuf("cv", [128, 8, 2], F32)
        S.dma("sp", ld, p.cv[:], Cap["cvec"], writes=[p.cv])
        for l in range(nlayers):
            Xin = Xext if l == 0 else p.T["XA" if l % 2 == 1 else "XB"]
            last = (l == nlayers - 1)
            Xout = Oext if (last and not final) else p.T["XA" if (l + 1) % 2 == 1 else "XB"]
            if "A" in phases:
                phase_A(p, Lw[l], Xin)
            if "B1" in phases:
                phase_B1(p, Lw[l])
            if "B2" in phases:
                phase_B2(p, Lw[l])
            if "B3" in phases:
                phase_B3(p, Lw[l])
            if "C1" in phases:
                phase_C1(p, Lw[l], Xin)
            if "C2" in phases:
                phase_C2(p, Lw[l], Xout, final_out=(Oext if (last and final) else None))
        S.finish("sp")
        p.ninst = S.ninst
    nc._ninst = p.ninst
    return nc


_PROGS = {}
_CONSTS = {}


def _pk(v):
    v = np.asarray(v)
    k = v.shape[0] // 128
    out = v.reshape((k, 128) + v.shape[1:])
    return np.ascontiguousarray(np.moveaxis(out, 0, 1))


def host_consts():
    if "c" in _CONSTS:
        return _CONSTS["c"]
    pos = np.concatenate([np.zeros(CTX, np.int64), np.arange(SEQ)])
    is_ctx = np.concatenate([np.ones(CTX, bool), np.zeros(SEQ, bool)])
    c = dict(_rope_tables(pos, is_ctx))
    c.update(_perm_consts())
    j = np.arange(128)[:, None]
    i = (np.arange(512) % 128)[None, :]
    c["maskLo"] = (j >= i).astype(np.float32).astype(NPBF)
    c["maskHi"] = (j <= i).astype(np.float32).astype(NPBF)
    j = np.arange(64)[:, None]
    i = (np.arange(512) % 64)[None, :]
    c["triF"] = (j <= i).astype(np.float32).astype(NPBF)
    c["triB"] = (j >= i).astype(np.float32).astype(NPBF)
    rm = np.ones((64, NTOT), np.float32)
    rm[:, ::64] = 0.0
    c["rmask"] = rm
    c["ident"] = np.eye(64, dtype=np.float32).astype(NPBF)
    sv = np.zeros((1, 128), np.float32)
    sv[0, 64:] = 1.0
    c["sv"] = sv.astype(NPBF)
    _CONSTS["c"] = c
    return c


def layer_inputs(W, l, names):
    m = {}
    g = {
        "w_mod": lambda: W["w_mod"][l], "b_mod": lambda: _pk(W["b_mod"][l]), "norm_mix": lambda: _pk(W["norm_mix"][l]),
        "norm_ffn": lambda: _pk(W["norm_ffn"][l]), "w_in": lambda: W["w_in"][l],
        "a_sink": lambda: np.ascontiguousarray(W["a_sink"][l].reshape(1, 8)),
        "bqn": lambda: _pk(W["b_q_norm"][l]), "bkvn": lambda: _pk(W["b_kv_norm"][l]),
        "w_uq": lambda: W["b_w_uq"][l], "w_ukv": lambda: W["b_w_ukv"][l], "w_gate": lambda: W["c_w_gate"][l],
        "b_gate": lambda: np.ascontiguousarray(W["c_b_gate"][l].reshape(2, 2, 128).transpose(2, 0, 1)),
        "c_hn": lambda: np.ascontiguousarray(W["c_head_norm"][l].reshape(4, 128).T),
        "w_br_a": lambda: W["w_br_a"][l], "w_br_b": lambda: W["w_br_b"][l], "w_br_c": lambda: W["w_br_c"][l],
        "w_out": lambda: W["w_out"][l], "w_up": lambda: W["w_up"][l],
        "conv_w": lambda: np.ascontiguousarray(W["conv_w"][l].reshape(3, 44, 128).transpose(2, 1, 0)),
        "conv_b": lambda: _pk(W["conv_b"][l]), "w_down": lambda: W["w_down"][l],
    }
    for n in names:
        m[n] = np.ascontiguousarray(g[n]())
    return m


def core_inputs(W, b, layers, phases=PHASES_ALL):
    c = dict(host_consts())
    c["cvec"] = _pk(np.stack([W["c"][b], W["c_ctx"]], axis=1))
    c["final_norm"] = _pk(W["final_norm"])
    m = {n: c[n] for n, _, _ in CONSTS}
    names = [w[0] for w in needed_weights(phases)]
    for i, l in enumerate(layers):
        for n, v in layer_inputs(W, l, names).items():
            m["%s_%d" % (n, i)] = v
    return m


def kernel(**W):
    W = {k: np.asarray(v) for k, v in W.items()}
    if "fused" not in _PROGS:
        _PROGS["fused"] = build_program(DEPTH, True)
    maps = []
    for b in range(2):
        m = core_inputs(W, b, list(range(DEPTH)))
        m["xT"] = np.ascontiguousarray(np.concatenate([W["ctx"][b], W["x"][b]], axis=0).T)
        maps.append(m)
    res = run_bass_kernel_spmd(_PROGS["fused"], maps, core_ids=[0, 1]).results
    out = np.stack([np.ascontiguousarray(np.asarray(res[b]["outT"]).T) for b in range(2)], axis=0)
    return out.astype(np.float32)
```
